# Optimizing a Trainium2 kernel written in Bass

```python
import math
import jax, jax.numpy as jnp
from jax import lax
import numpy as np


D_MODEL = 1024
BATCH = 16
SEQ = 2048
DEPTH = 2

GRID_W = 64
CTX_LEN = 256
N_MIXERS = 2
N_ATT_LAYERS = (DEPTH + 1) // 2
N_SSM_LAYERS = DEPTH // 2

ATT_HEADS = 16
ATT_KV_HEADS = 4
HEAD_DIM = 64
ATT_GROUP = ATT_HEADS // ATT_KV_HEADS
ATT_DIM = ATT_HEADS * HEAD_DIM
ATT_IN_DIM = ATT_DIM + 2 * ATT_KV_HEADS * HEAD_DIM
WINDOW = 128
BLOCK = 128
SPAN = BLOCK + 2 * WINDOW
ROPE_BASE = 10000.0
ROPE_AXIS_DIM = HEAD_DIM // 2

SSM_D_INNER = 2 * D_MODEL
SSM_HEAD_DIM = 64
SSM_HEADS = SSM_D_INNER // SSM_HEAD_DIM
SSM_GROUPS = 4
SSM_GROUP_HEADS = SSM_HEADS // SSM_GROUPS
SSM_STATE = 128
SSM_CONV_W = 5
SSM_CHUNK = 128
SSM_CONV_DIM = SSM_D_INNER + 2 * SSM_GROUPS * SSM_STATE
SSM_IN_DIM = SSM_D_INNER + SSM_CONV_DIM + 2 * SSM_HEADS
SSM_NORM_GROUPS = SSM_GROUPS

FF_DIM = 4 * D_MODEL
N_MOD = 6
ALPHA = (2.0 * DEPTH) ** 0.25
BETA = (8.0 * DEPTH) ** -0.25
LN_EPS = 1e-5
RMS_EPS = 1e-5
NEG_INF = -1e30

kernel_name = 'hybrid_swa_ssd_flow_backbone'


def layer_norm(x, g, b):
    xf = x.astype(jnp.float32)
    mu = jnp.mean(xf, axis=-1, keepdims=True)
    var = jnp.mean(jnp.square(xf - mu), axis=-1, keepdims=True)
    y = (xf - mu) * lax.rsqrt(var + LN_EPS) * g.astype(jnp.float32) + b.astype(jnp.float32)
    return y.astype(x.dtype)


def modulate(h, shift, scale):
    return h * (1.0 + scale) + shift


def squared_relu_mlp(h, w1, w2):
    return jnp.square(jax.nn.relu(h @ w1)) @ w2


def axial_rope_tables(rows):
    row = jnp.repeat(jnp.arange(rows), GRID_W).astype(jnp.float32)
    col = jnp.tile(jnp.arange(GRID_W), rows).astype(jnp.float32)
    inv_freq = ROPE_BASE ** (-jnp.arange(0, ROPE_AXIS_DIM, 2, dtype=jnp.float32) / ROPE_AXIS_DIM)
    ang_r = row[:, None] * inv_freq[None, :]
    ang_c = col[:, None] * inv_freq[None, :]
    return (jnp.cos(ang_r), jnp.sin(ang_r), jnp.cos(ang_c), jnp.sin(ang_c))


def rotate_half(x, cos, sin):
    x1, x2 = jnp.split(x, 2, axis=-1)
    return jnp.concatenate([x1 * cos - x2 * sin, x2 * cos + x1 * sin], axis=-1)


def apply_axial_rope(x, tables):
    cos_r, sin_r, cos_c, sin_c = tables
    bshape = (x.shape[1],) + (1,) * (x.ndim - 3) + (-1,)
    xf = x.astype(jnp.float32)
    xr = rotate_half(xf[..., :ROPE_AXIS_DIM], cos_r.reshape(bshape), sin_r.reshape(bshape))
    xc = rotate_half(xf[..., ROPE_AXIS_DIM:], cos_c.reshape(bshape), sin_c.reshape(bshape))
    return jnp.concatenate([xr, xc], axis=-1).astype(x.dtype)


def attention_mixer(u_lat, u_ctx, w_in, w_out, sink, rope, need_ctx_out):
    bsz, n, _ = u_lat.shape
    n_ctx = u_ctx.shape[1]
    nb = n // BLOCK
    scale = HEAD_DIM ** -0.5
    sink_f = sink.astype(jnp.float32).reshape(ATT_KV_HEADS, ATT_GROUP)

    def project(u):
        length = u.shape[1]
        q, k, v = jnp.split(u @ w_in, [ATT_DIM, ATT_DIM + ATT_KV_HEADS * HEAD_DIM], axis=-1)
        return (q.reshape(bsz, length, ATT_KV_HEADS, ATT_GROUP, HEAD_DIM),
                k.reshape(bsz, length, ATT_KV_HEADS, HEAD_DIM),
                v.reshape(bsz, length, ATT_KV_HEADS, HEAD_DIM))

    def sink_softmax(logits):
        s = jnp.broadcast_to(sink_f[None, :, :, None, None], logits.shape[:-1] + (1,))
        return jax.nn.softmax(jnp.concatenate([logits, s], axis=-1), axis=-1)

    q_l, k_l, v_l = project(u_lat)
    q_c, k_c, v_c = project(u_ctx)
    qr_l = apply_axial_rope(q_l, rope)
    kr_l = apply_axial_rope(k_l, rope)
    pad = ((0, 0), (WINDOW, WINDOW), (0, 0), (0, 0))
    k_pad = jnp.pad(kr_l, pad)
    v_pad = jnp.pad(v_l, pad)
    q_blocks = jnp.moveaxis(qr_l.reshape(bsz, nb, BLOCK, ATT_KV_HEADS, ATT_GROUP, HEAD_DIM), 1, 0)
    qp_blocks = jnp.moveaxis(q_l.reshape(bsz, nb, BLOCK, ATT_KV_HEADS, ATT_GROUP, HEAD_DIM), 1, 0)
    offs_q = jnp.arange(BLOCK)
    offs_k = jnp.arange(SPAN) - WINDOW

    def block_attend(args):
        blk, qb, qpb = args
        start = blk * BLOCK
        kw = lax.dynamic_slice_in_dim(k_pad, start, SPAN, axis=1)
        vw = lax.dynamic_slice_in_dim(v_pad, start, SPAN, axis=1)
        qi = start + offs_q
        kj = start + offs_k
        ok = (kj[None, :] >= 0) & (kj[None, :] < n) & (jnp.abs(qi[:, None] - kj[None, :]) <= WINDOW)
        s_win = jnp.einsum('blgrd,bsgd->bgrls', qb, kw).astype(jnp.float32) * scale
        s_win = jnp.where(ok, s_win, NEG_INF)
        s_ctx = jnp.einsum('blgrd,bsgd->bgrls', qpb, k_c).astype(jnp.float32) * scale
        p = sink_softmax(jnp.concatenate([s_win, s_ctx], axis=-1)).astype(vw.dtype)
        return (jnp.einsum('bgrls,bsgd->blgrd', p[..., :SPAN], vw)
                + jnp.einsum('bgrls,bsgd->blgrd', p[..., SPAN:SPAN + n_ctx], v_c))

    o = lax.map(block_attend, (jnp.arange(nb), q_blocks, qp_blocks))
    y_lat = jnp.moveaxis(o, 0, 1).reshape(bsz, n, ATT_DIM) @ w_out
    y_ctx = None
    if need_ctx_out:
        s = jnp.einsum('blgrd,bsgd->bgrls', q_c, k_c).astype(jnp.float32) * scale
        p = sink_softmax(s).astype(v_c.dtype)
        o_c = jnp.einsum('bgrls,bsgd->blgrd', p[..., :n_ctx], v_c)
        y_ctx = o_c.reshape(bsz, n_ctx, ATT_DIM) @ w_out
    return y_lat, y_ctx


def depthwise_conv(x, w, bias):
    ch = x.shape[-1]
    y = lax.conv_general_dilated(x, w[:, None, :].astype(x.dtype), window_strides=(1,),
                                 padding=[(SSM_CONV_W // 2, SSM_CONV_W // 2)],
                                 dimension_numbers=('NWC', 'WIO', 'NWC'), feature_group_count=ch)
    return y + bias


def segsum(a):
    t = a.shape[-1]
    a_rep = jnp.broadcast_to(a[..., None], a.shape + (t,))
    a_rep = jnp.where(jnp.tril(jnp.ones((t, t), dtype=bool), -1), a_rep, 0.0)
    cs = jnp.cumsum(a_rep, axis=-2)
    return jnp.where(jnp.tril(jnp.ones((t, t), dtype=bool)), cs, -jnp.inf)


def ssd_scan(xs, dt, a_neg, bm, cm, init_state, with_outputs):
    b, length, g, r, p = xs.shape
    n = bm.shape[-1]
    nc = length // SSM_CHUNK
    q = SSM_CHUNK
    dt4 = dt.reshape(b, length, g, r)
    x_dt = (xs.astype(jnp.float32) * dt4[..., None]).reshape(b, nc, q, g, r, p)
    a = jnp.moveaxis((dt4 * a_neg.reshape(g, r)).reshape(b, nc, q, g, r), 2, -1)
    a_cs = jnp.cumsum(a, axis=-1)
    bc = bm.astype(jnp.float32).reshape(b, nc, q, g, n)
    cc = cm.astype(jnp.float32).reshape(b, nc, q, g, n)
    decay_states = jnp.exp(a_cs[..., -1:] - a_cs)
    states = jnp.einsum('bcsgn,bcgrs,bcsgrp->bcgrpn', bc, decay_states, x_dt)
    states = jnp.concatenate([init_state[:, None], states], axis=1)
    chunk_a = jnp.pad(a_cs[..., -1], ((0, 0), (1, 0), (0, 0), (0, 0)))
    decay_chunk = jnp.exp(segsum(jnp.moveaxis(chunk_a, 1, -1)))
    new_states = jnp.einsum('bgrzc,bcgrpn->bzgrpn', decay_chunk, states)
    final_state = new_states[:, -1]
    if not with_outputs:
        return None, final_state
    prev_states = new_states[:, :-1]
    lmat = jnp.exp(segsum(a))
    cb = jnp.einsum('bclgn,bcsgn->bcgls', cc, bc)
    y_diag = jnp.einsum('bcgrls,bcsgrp->bclgrp', cb[:, :, :, None] * lmat, x_dt)
    y_off = jnp.einsum('bclgn,bcgrpn,bcgrl->bclgrp', cc, prev_states, jnp.exp(a_cs))
    return (y_diag + y_off).reshape(b, length, g, r, p), final_state


def ssm_mixer(u_lat, u_ctx, w_in, conv_w, conv_b, dt_bias, a_log, d_skip, norm_g, w_out, need_ctx_out):
    a_neg = -jnp.exp(a_log.astype(jnp.float32))
    bsz = u_lat.shape[0]

    def flip(t):
        return jnp.flip(t, axis=1)

    def prep(u):
        length = u.shape[1]
        z, xbc, dt = jnp.split(u @ w_in, [SSM_D_INNER, SSM_D_INNER + SSM_CONV_DIM], axis=-1)
        xbc = jax.nn.silu(depthwise_conv(xbc, conv_w, conv_b))
        xs, bm, cm = jnp.split(xbc, [SSM_D_INNER, SSM_D_INNER + SSM_GROUPS * SSM_STATE], axis=-1)
        xs = xs.reshape(bsz, length, SSM_GROUPS, SSM_GROUP_HEADS, SSM_HEAD_DIM)
        bm = bm.reshape(bsz, length, SSM_GROUPS, SSM_STATE)
        cm = cm.reshape(bsz, length, SSM_GROUPS, SSM_STATE)
        dt = jax.nn.softplus(dt.astype(jnp.float32).reshape(bsz, length, 2, SSM_HEADS)
                             + dt_bias.astype(jnp.float32))
        return z, xs, bm, cm, dt

    def finish(z, xs, y_f, y_b):
        length = xs.shape[1]
        y = y_f + y_b + xs.astype(jnp.float32) * d_skip.astype(jnp.float32).reshape(SSM_GROUPS, SSM_GROUP_HEADS, 1)
        y = y.reshape(bsz, length, SSM_D_INNER) * jax.nn.silu(z.astype(jnp.float32))
        y = y.reshape(bsz, length, SSM_NORM_GROUPS, -1)
        y = y * lax.rsqrt(jnp.mean(jnp.square(y), axis=-1, keepdims=True) + RMS_EPS)
        y = (y.reshape(bsz, length, SSM_D_INNER) * norm_g.astype(jnp.float32)).astype(z.dtype)
        return y @ w_out

    z_c, x_c, b_c, c_c, dt_c = prep(u_ctx)
    z_l, x_l, b_l, c_l, dt_l = prep(u_lat)
    init = jnp.zeros((bsz, SSM_GROUPS, SSM_GROUP_HEADS, SSM_HEAD_DIM, SSM_STATE), jnp.float32)
    yc_f, st_f = ssd_scan(x_c, dt_c[:, :, 0], a_neg[0], b_c, c_c, init, need_ctx_out)
    yc_b, st_b = ssd_scan(flip(x_c), flip(dt_c[:, :, 1]), a_neg[1], flip(b_c), flip(c_c), init, need_ctx_out)
    yl_f, _ = ssd_scan(x_l, dt_l[:, :, 0], a_neg[0], b_l, c_l, st_f, True)
    yl_b, _ = ssd_scan(flip(x_l), flip(dt_l[:, :, 1]), a_neg[1], flip(b_l), flip(c_l), st_b, True)
    y_lat = finish(z_l, x_l, yl_f, flip(yl_b))
    y_ctx = finish(z_c, x_c, yc_f, flip(yc_b)) if need_ctx_out else None
    return y_lat, y_ctx


def setup_inputs(seed: int = 0) -> dict:
    key = jax.random.key(seed)
    ks = jax.random.split(key, 24)
    f = jnp.float32
    d = D_MODEL

    def nrm(k, shape, scale):
        return jax.random.normal(k, shape, f) * scale

    dt0 = jnp.exp(jax.random.uniform(ks[15], (N_SSM_LAYERS, 2, SSM_HEADS), f, math.log(1e-3), math.log(1e-1)))
    return {
        'x': nrm(ks[0], (BATCH, SEQ, d), 1.0),
        'c': nrm(ks[1], (BATCH, d), 1.0),
        'ctx': nrm(ks[2], (BATCH, CTX_LEN, d), 1.0),
        'c_ctx': nrm(ks[3], (d,), 1.0),
        'w_mod': nrm(ks[4], (DEPTH, d, N_MOD * d), 0.5 * d ** -0.5),
        'b_mod': nrm(ks[5], (DEPTH, N_MOD * d), 0.02),
        'ln_mix_g': 1.0 + nrm(ks[6], (DEPTH, d), 0.02),
        'ln_mix_b': nrm(ks[7], (DEPTH, d), 0.02),
        'ln_ff_g': 1.0 + nrm(ks[8], (DEPTH, d), 0.02),
        'ln_ff_b': nrm(ks[9], (DEPTH, d), 0.02),
        'att_w_in': nrm(ks[10], (N_ATT_LAYERS, d, ATT_IN_DIM), d ** -0.5),
        'att_w_out': nrm(ks[11], (N_ATT_LAYERS, ATT_DIM, d), BETA * ATT_DIM ** -0.5),
        'att_sink': nrm(ks[12], (N_ATT_LAYERS, ATT_HEADS), 0.5),
        'ssm_w_in': nrm(ks[13], (N_SSM_LAYERS, d, SSM_IN_DIM), d ** -0.5),
        'ssm_conv_w': nrm(ks[14], (N_SSM_LAYERS, SSM_CONV_W, SSM_CONV_DIM), SSM_CONV_W ** -0.5),
        'ssm_conv_b': nrm(ks[16], (N_SSM_LAYERS, SSM_CONV_DIM), 0.02),
        'ssm_dt_bias': dt0 + jnp.log(-jnp.expm1(-dt0)),
        'ssm_a_log': jnp.log(jax.random.uniform(ks[17], (N_SSM_LAYERS, 2, SSM_HEADS), f, 1.0, 16.0)),
        'ssm_d': 1.0 + nrm(ks[18], (N_SSM_LAYERS, SSM_HEADS), 0.1),
        'ssm_norm_g': 1.0 + nrm(ks[19], (N_SSM_LAYERS, SSM_D_INNER), 0.02),
        'ssm_w_out': nrm(ks[20], (N_SSM_LAYERS, SSM_D_INNER, d), BETA * SSM_D_INNER ** -0.5),
        'ff_w1': nrm(ks[21], (DEPTH, d, FF_DIM), d ** -0.5),
        'ff_w2': nrm(ks[22], (DEPTH, FF_DIM, d), BETA * FF_DIM ** -0.5),
    }


def reference(x, c, ctx, c_ctx, w_mod, b_mod, ln_mix_g, ln_mix_b, ln_ff_g, ln_ff_b,
              att_w_in, att_w_out, att_sink, ssm_w_in, ssm_conv_w, ssm_conv_b, ssm_dt_bias,
              ssm_a_log, ssm_d, ssm_norm_g, ssm_w_out, ff_w1, ff_w2):
    rows = x.shape[1] // GRID_W
    rope = axial_rope_tables(rows)
    h_lat, h_ctx = x, ctx
    for i in range(DEPTH):
        last = i == DEPTH - 1
        j = i // N_MIXERS
        m_lat = (jax.nn.silu(c) @ w_mod[i] + b_mod[i])[:, None, :]
        m_ctx = (jax.nn.silu(c_ctx) @ w_mod[i] + b_mod[i])[None, None, :]
        sh_m_l, sc_m_l, g_m_l, sh_f_l, sc_f_l, g_f_l = jnp.split(m_lat, N_MOD, axis=-1)
        sh_m_c, sc_m_c, g_m_c, sh_f_c, sc_f_c, g_f_c = jnp.split(m_ctx, N_MOD, axis=-1)
        u_lat = modulate(h_lat, sh_m_l, sc_m_l)
        u_ctx = modulate(h_ctx, sh_m_c, sc_m_c)
        if i % N_MIXERS == 0:
            y_lat, y_ctx = attention_mixer(u_lat, u_ctx, att_w_in[j], att_w_out[j], att_sink[j], rope, not last)
        else:
            y_lat, y_ctx = ssm_mixer(u_lat, u_ctx, ssm_w_in[j], ssm_conv_w[j], ssm_conv_b[j], ssm_dt_bias[j],
                                     ssm_a_log[j], ssm_d[j], ssm_norm_g[j], ssm_w_out[j], not last)
        h_lat = layer_norm(ALPHA * h_lat + g_m_l * y_lat, ln_mix_g[i], ln_mix_b[i])
        f_lat = squared_relu_mlp(modulate(h_lat, sh_f_l, sc_f_l), ff_w1[i], ff_w2[i])
        h_lat = layer_norm(ALPHA * h_lat + g_f_l * f_lat, ln_ff_g[i], ln_ff_b[i])
        if not last:
            h_ctx = layer_norm(ALPHA * h_ctx + g_m_c * y_ctx, ln_mix_g[i], ln_mix_b[i])
            f_ctx = squared_relu_mlp(modulate(h_ctx, sh_f_c, sc_f_c), ff_w1[i], ff_w2[i])
            h_ctx = layer_norm(ALPHA * h_ctx + g_f_c * f_ctx, ln_ff_g[i], ln_ff_b[i])
    return h_lat
```

```python
from contextlib import ExitStack
import numpy as np
import ml_dtypes
import concourse.bass as bass
import concourse.mybir as mybir
from concourse.bass_utils import run_bass_kernel_spmd

F32 = mybir.dt.float32
BF16 = mybir.dt.bfloat16
AF = mybir.ActivationFunctionType
ALU = mybir.AluOpType
AX = mybir.AxisListType

ENGS = ("pe", "act", "dve", "pool", "sp")

D = 1024
SEQ = 2048
CTX = 256
T = SEQ + CTX
NCORE = 8
BLOC = 2
DEPTH = 2
ALPHA = (2.0 * DEPTH) ** 0.25
LN_EPS = 1e-5
RMS_EPS = 1e-5
NEG = -30000.0
TGS = [(0, 256, True), (256, 512, False), (768, 512, False), (1280, 512, False), (1792, 512, False)]


class Buf:
    __slots__ = ("name", "writers", "readers", "prev_readers", "sem", "dcount", "excl")

    def __init__(self, name, excl=False):
        self.name = name
        self.excl = excl
        self.writers = {}
        self.readers = {}
        self.prev_readers = {}
        self.sem = None
        self.dcount = 0


class Op:
    __slots__ = ("emit", "deps", "signal", "dma", "isnop")

    def __init__(self, emit, deps, dma, isnop=False):
        self.emit = emit
        self.deps = deps
        self.signal = False
        self.dma = dma
        self.isnop = isnop


class Prog:
    def __init__(self, nc):
        self.nc = nc
        self.ops = {e: [] for e in ENGS}
        self.seen = {e: {} for e in ENGS}
        self.dma_bufs = []
        self.nbuf = 0
        self.ntens = 0

    def sb(self, name, shape, dt, off):
        self.ntens += 1
        return self.nc.alloc_sbuf_tensor_at(f"{name}_{self.ntens}", list(shape), dt, offset=off + 16512)

    def buf(self, name=None):
        self.nbuf += 1
        return Buf(name or f"b{self.nbuf}")

    def add(self, eng, emit, r=(), w=(), pw=(), dma_w=None, dma_r=None):
        ops = self.ops[eng]
        idx = len(ops)
        deps = {}

        def need(k, v):
            if deps.get(k, -1) < v:
                deps[k] = v

        mykey = ("E", eng)
        allr = list(r) + ([dma_r] if dma_r is not None else [])
        allw = list(w) + ([dma_w] if dma_w is not None else [])
        for b in allr:
            for k, v in b.writers.items():
                need(k, v)
            if b.excl:
                for k, v in b.readers.items():
                    if k != mykey:
                        need(k, v)
        for b in allw:
            for k, v in b.readers.items():
                need(k, v)
            for k, v in b.writers.items():
                if k == mykey and eng == "pe":
                    continue
                if dma_w is not None and k == ("D", dma_w):
                    continue
                need(k, v)
            if not b.readers:
                for k, v in b.prev_readers.items():
                    need(k, v)
        for b in pw:
            for k, v in b.readers.items():
                need(k, v)
            for k, v in b.prev_readers.items():
                need(k, v)
            for k, v in b.writers.items():
                if k[0] == "D":
                    need(k, v)
        seen = self.seen[eng]
        fdeps = {}
        for k, v in deps.items():
            if seen.get(k, -1) >= v:
                continue
            seen[k] = v
            fdeps[k] = v
            if k[0] == "E":
                self.ops[k[1]][v].signal = True
        is_dma = (dma_w is not None) or (dma_r is not None)
        dbuf = dma_w if dma_w is not None else dma_r
        op = Op(emit, fdeps, dbuf if is_dma else None)
        ops.append(op)
        if is_dma:
            if dbuf.sem is None:
                dbuf.sem = True
                self.dma_bufs.append(dbuf)
            dbuf.dcount += 16
            ev = (("D", dbuf), dbuf.dcount)
        else:
            ev = (mykey, idx)
        for b in allr:
            if b.readers.get(ev[0], -1) < ev[1]:
                b.readers[ev[0]] = ev[1]
        for b in allw:
            if b.readers:
                b.prev_readers = b.readers
            b.readers = {}
            b.writers = {ev[0]: ev[1]}
        for b in pw:
            if b.readers:
                b.prev_readers = b.readers
                b.readers = {}
                b.writers = {}
            if b.writers.get(ev[0], -1) < ev[1]:
                b.writers[ev[0]] = ev[1]
        return op

    def pe(self, emit, **kw): return self.add("pe", emit, **kw)
    def act(self, emit, **kw): return self.add("act", emit, **kw)
    def dve(self, emit, **kw): return self.add("dve", emit, **kw)
    def pool(self, emit, **kw): return self.add("pool", emit, **kw)
    def sp(self, emit, **kw): return self.add("sp", emit, **kw)

    def barrier(self):
        last = {}
        for e in ENGS:
            ops = self.ops[e]
            for i in range(len(ops) - 1, -1, -1):
                if ops[i].dma is None and not ops[i].isnop:
                    last[("E", e)] = i
                    break
        for b in self.dma_bufs:
            last[("D", b)] = b.dcount
        for e in ENGS:
            seen = self.seen[e]
            fdeps = {}
            for k, v in last.items():
                if k == ("E", e):
                    continue
                if seen.get(k, -1) >= v:
                    continue
                seen[k] = v
                fdeps[k] = v
                if k[0] == "E":
                    self.ops[k[1]][v].signal = True
            if fdeps:
                self.ops[e].append(Op(lambda h: h.nop(), fdeps, None, True))

    def finish(self, final_bufs):
        nc = self.nc
        with ExitStack() as es:
            sems = {e: es.enter_context(nc.semaphore(f"s_{e}")) for e in ENGS}
            for i, b in enumerate(self.dma_bufs):
                b.sem = es.enter_context(nc.semaphore(f"d{i}"))
            sigcnt = {}
            for e in ENGS:
                c = 0
                arr = []
                for op in self.ops[e]:
                    if op.signal:
                        c += 1
                    arr.append(c)
                sigcnt[e] = arr
            fin = [(b.sem, b.dcount) for b in final_bufs]

            def run(e, h):
                for op in self.ops[e]:
                    for k, v in op.deps.items():
                        if k[0] == "E":
                            h.wait_ge(sems[k[1]], sigcnt[k[1]][v])
                        else:
                            h.wait_ge(k[1].sem, v)
                    ins = op.emit(h)
                    if op.dma is not None:
                        ins.then_inc(op.dma.sem, 16)
                    elif op.signal:
                        ins.then_inc(sems[e], 1)
                if e == "sp":
                    for s, v in fin:
                        h.wait_ge(s, v)

            with nc.Block() as block:
                @block.tensor
                def _(h): run("pe", h)

                @block.scalar
                def _(h): run("act", h)

                @block.vector
                def _(h): run("dve", h)

                @block.gpsimd
                def _(h): run("pool", h)

                @block.sync
                def _(h): run("sp", h)
        return {e: len(self.ops[e]) for e in ENGS}


class K:
    pass


def build_program(nseq=BLOC, nlayers=DEPTH, dbg=None):
    nc = bass.Bass("TRN2", target_bir_lowering=False)
    P = Prog(nc)

    def din(name, shape, dt=F32):
        return nc.dram_tensor(name, list(shape), dt, kind="ExternalInput").ap()

    xT = din("xT", [BLOC, 8, 128, SEQ])
    cxT = din("cxT", [BLOC, 8, 128, CTX])
    cT = din("cT", [128, 8, 3])
    wmod = din("wmod", [DEPTH, 12, 128, 8, 512])
    bmod = din("bmod", [DEPTH, 128, 48])
    lnv_d = din("lnv", [128, DEPTH * 4 * 8])
    sink_d = din("sinkb", [128, 16])
    ident_d = din("ident", [128, 128], BF16)
    mprev_d = din("mprev", [128, 512], BF16)
    mnext_d = din("mnext", [128, 512], BF16)
    cs_d = din("cossin", [128, SEQ])
    wqq_d = din("wqq", [4, 128, 8, 512])
    wkv_d = din("wkv", [4, 128, 8, 192])
    wo_d = din("wo", [4, 128, 2, 1024])
    wxbc_d = din("wxbc", [4, 128, 8, 768])
    wz_d = din("wz", [4, 128, 8, 512])
    wdt_d = din("wdt", [4, 128, 8, 16])
    wso_d = din("wso", [4, 128, 4, 1024])
    cw_d = din("convw", [4, 128, 36])
    sb_d = din("ssmb", [4, 128, 40])
    ng_d = din("normg", [4, 128, 4])
    tri_d = din("tri", [128, 256])
    w1_d = din("w1", [DEPTH, 8, 128, 8, 512])
    w2_d = din("w2", [DEPTH, 8, 128, 32, 128])
    outT = nc.dram_tensor("outT", [BLOC, 8, 128, SEQ], F32, kind="ExternalOutput").ap()
    if dbg is not None:
        dbgT = nc.dram_tensor("dbgT", [BLOC, 8, 128, T], F32, kind="ExternalOutput").ap()

    HS_OFF = 0
    CONST_OFF = 73728
    ARENA = 78848
    hs = P.sb("hs", [128, 8, T], F32, HS_OFF)
    b_hs = [[P.buf(f"hs{g}") for g in range(len(TGS))]]

    co = [CONST_OFF]

    def calloc(name, shape, dt):
        n = int(np.prod(shape[1:])) * (4 if dt == F32 else 2)
        t = P.sb(name, shape, dt, co[0])
        co[0] += (n + 31) // 32 * 32
        assert co[0] <= ARENA
        return t

    ident = calloc("ident", [128, 128], BF16)
    ones_f = calloc("ones_f", [128, 128], F32)
    ones_bf = calloc("ones_bf", [128, 128], BF16)
    mprev = calloc("mprev", [128, 512], BF16)
    mnext = calloc("mnext", [128, 512], BF16)
    modv = [calloc(f"modv{i}", [128, 48, 3], F32) for i in range(DEPTH)]
    lnv = calloc("lnv", [128, DEPTH * 4 * 8], F32)
    lnva = calloc("lnva", [128, DEPTH * 4 * 8], F32)
    sinkv = calloc("sinkv", [128, 16], F32)
    epsv = calloc("epsv", [128, 2], F32)
    kmaxp = calloc("kmaxp", [128, 8], F32)
    kmax2 = calloc("kmax2", [128, 1], F32)
    b_const = P.buf("const")
    b_modv = [P.buf(f"modv{i}") for i in range(DEPTH)]
    b_kmax = P.buf("kmax")

    pbank = [nc.alloc_psum_tensor(f"pb{i}", [128, 512], F32) for i in range(8)]
    b_bank = [Buf(f"pb{i}", excl=True) for i in range(8)]
    bank_rr = [0]

    def bank():
        i = bank_rr[0]
        bank_rr[0] = (i + 1) % 8
        return pbank[i], b_bank[i]

    P.sp(lambda h: h.dma_start(out=ident[:, :], in_=ident_d[:, :]), dma_w=b_const)
    P.sp(lambda h: h.dma_start(out=mprev[:, :], in_=mprev_d[:, :]), dma_w=b_const)
    P.sp(lambda h: h.dma_start(out=mnext[:, :], in_=mnext_d[:, :]), dma_w=b_const)
    P.sp(lambda h: h.dma_start(out=lnv[:, :], in_=lnv_d[:, :]), dma_w=b_const)
    P.sp(lambda h: h.dma_start(out=sinkv[:, :], in_=sink_d[:, :]), dma_w=b_const)
    b_c2 = P.buf("const2")
    P.pool(lambda h: h.memset(ones_f[:, :], 1.0 / D), pw=[b_c2])
    P.pool(lambda h: h.memset(ones_bf[:, :], 1.0), pw=[b_c2])
    P.pool(lambda h: h.memset(epsv[:, 0:1], LN_EPS), pw=[b_c2])
    P.pool(lambda h: h.memset(epsv[:, 1:2], 1e-20), pw=[b_c2])
    P.dve(lambda h: h.tensor_scalar(out=lnva[:, :], in0=lnv[:, :], scalar1=ALPHA, scalar2=None, op0=ALU.mult),
          r=[b_const], w=[b_c2])

    def lnvec(layer, which, k, alpha):
        t = lnva if alpha else lnv
        c = (layer * 4 + which) * 8 + k
        return t[:, c:c + 1]

    a_cT = P.sb("cTs", [128, 8, 3], F32, ARENA)
    a_e = P.sb("cTe", [128, 8, 3], F32, ARENA + 128)
    a_sc = P.sb("scT", [128, 8, 3], F32, ARENA + 256)
    a_bm = P.sb("bm", [128, 48], F32, ARENA + 384)
    a_w = [P.sb(f"wm{i}", [128, 8, 512], F32, ARENA + 1024 + i * 16384) for i in range(2)]
    b_cT = P.buf("cT")
    b_sc = P.buf("scT")
    b_bm = P.buf("bm")
    b_w = [P.buf("wm0"), P.buf("wm1")]
    P.sp(lambda h: h.dma_start(out=a_cT[:, :, :], in_=cT[:, :, :]), dma_w=b_cT)
    P.act(lambda h: h.activation(out=a_e[:, :, :], in_=a_cT[:, :, :], func=AF.Exp, scale=-1.0), r=[b_cT], w=[b_sc])
    P.dve(lambda h: h.tensor_scalar(out=a_e[:, :, :], in0=a_e[:, :, :], scalar1=1.0, scalar2=None, op0=ALU.add), r=[b_sc], w=[b_sc])
    P.dve(lambda h: h.reciprocal(out=a_e[:, :, :], in_=a_e[:, :, :]), r=[b_sc], w=[b_sc])
    P.dve(lambda h: h.tensor_tensor(out=a_sc[:, :, :], in0=a_e[:, :, :], in1=a_cT[:, :, :], op=ALU.mult), r=[b_sc, b_cT], w=[b_sc])
    wi = 0
    for i in range(nlayers):
        pm, b_pm = bank()
        P.sp(lambda h, i=i: h.dma_start(out=a_bm[:, :], in_=bmod[i]), dma_w=b_bm)
        for ft in range(12):
            wt, bw = a_w[wi % 2], b_w[wi % 2]
            wi += 1
            P.sp(lambda h, wt=wt, i=i, ft=ft: h.dma_start(out=wt[:, :, :], in_=wmod[i, ft]), dma_w=bw)
            for fc in range(4):
                c = ft * 4 + fc
                for k in range(8):
                    P.pe(lambda h, pm=pm, wt=wt, c=c, fc=fc, k=k: h.matmul(
                        pm[:, c * 3:c * 3 + 3], lhsT=wt[:, k, fc * 128:(fc + 1) * 128], rhs=a_sc[:, k, :],
                        start=(k == 0), stop=(k == 7)),
                        r=[bw, b_sc], **({"w": [b_pm]} if (c == 0 and k == 0) else {"pw": [b_pm]}))
        mv = modv[i]
        P.dve(lambda h, mv=mv, pm=pm: h.tensor_tensor(
            out=mv[:, :, :], in0=pm[:, 0:144].rearrange("p (c r) -> p c r", r=3),
            in1=a_bm[:, :].unsqueeze(2).to_broadcast([128, 48, 3]), op=ALU.add),
            r=[b_pm, b_bm], w=[b_modv[i]])
        for which in (1, 4):
            P.dve(lambda h, mv=mv, which=which: h.tensor_scalar(
                out=mv[:, which * 8:(which + 1) * 8, :], in0=mv[:, which * 8:(which + 1) * 8, :],
                scalar1=1.0, scalar2=1.0 / ALPHA, op0=ALU.add, op1=ALU.mult),
                r=[b_modv[i]], w=[b_modv[i]])

    def mod(layer, which, k, rr):
        return modv[layer][:, which * 8 + k, rr:rr + 1]

    P.barrier()

    LN_OFF = ARENA + 110080

    def layer_norm(tgi, layer, which_g, final=False):
        t0, n, _ = TGS[tgi]
        sq = [P.sb("lnsq", [128, 512], F32, LN_OFF + i * 2048) for i in range(2)]
        b_sq = [K.b_lnsq0, K.b_lnsq1]
        st = [P.sb("lnst", [128, 512], F32, LN_OFF + 4096 + i * 2048) for i in range(4)]
        b_st = K.b_lnst
        bh = b_hs[0][tgi]
        p1, b_p1 = bank()
        p2, b_p2 = bank()
        for k in range(8):
            s, bs = sq[k % 2], b_sq[k % 2]
            P.act(lambda h, s=s, k=k: h.activation(out=s[:, 0:n], in_=hs[:, k, t0:t0 + n], func=AF.Square),
                  r=[bh], w=[bs])
            P.pe(lambda h, k=k: h.matmul(p1[:, 0:n], lhsT=ones_f[:, :], rhs=hs[:, k, t0:t0 + n],
                                         start=(k == 0), stop=(k == 7)),
                 r=[bh, b_c2], **({"w": [b_p1]} if k == 0 else {"pw": [b_p1]}))
            P.pe(lambda h, s=s, k=k: h.matmul(p2[:, 0:n], lhsT=ones_f[:, :], rhs=s[:, 0:n],
                                              start=(k == 0), stop=(k == 7)),
                 r=[bs, b_c2], **({"w": [b_p2]} if k == 0 else {"pw": [b_p2]}))
        mean, var, rstd, nmr = st
        P.act(lambda h: h.activation(out=mean[:, 0:n], in_=p1[:, 0:n], func=AF.Copy), r=[b_p1], w=[b_st])
        P.dve(lambda h: h.tensor_tensor(out=var[:, 0:n], in0=mean[:, 0:n], in1=mean[:, 0:n], op=ALU.mult), r=[b_st], w=[b_st])
        P.dve(lambda h: h.tensor_tensor(out=var[:, 0:n], in0=p2[:, 0:n], in1=var[:, 0:n], op=ALU.subtract), r=[b_st, b_p2], w=[b_st])
        P.act(lambda h: h.activation(out=rstd[:, 0:n], in_=var[:, 0:n], func=AF.Ln, bias=epsv[:, 0:1], scale=1.0), r=[b_st, b_c2], w=[b_st])
        P.act(lambda h: h.activation(out=rstd[:, 0:n], in_=rstd[:, 0:n], func=AF.Exp, scale=-0.5), r=[b_st], w=[b_st])
        P.dve(lambda h: h.scalar_tensor_tensor(out=nmr[:, 0:n], in0=mean[:, 0:n], scalar=-1.0, in1=rstd[:, 0:n],
                                               op0=ALU.mult, op1=ALU.mult), r=[b_st], w=[b_st])
        for k in range(8):
            P.dve(lambda h, k=k: h.tensor_tensor(out=hs[:, k, t0:t0 + n], in0=hs[:, k, t0:t0 + n], in1=rstd[:, 0:n], op=ALU.mult),
                  r=[bh, b_st], w=[bh])
            P.dve(lambda h, k=k: h.tensor_tensor(out=hs[:, k, t0:t0 + n], in0=hs[:, k, t0:t0 + n], in1=nmr[:, 0:n], op=ALU.add),
                  r=[bh, b_st], w=[bh])
            P.act(lambda h, k=k: h.activation(out=hs[:, k, t0:t0 + n], in_=hs[:, k, t0:t0 + n], func=AF.Identity,
                                              scale=lnvec(layer, which_g, k, not final), bias=lnvec(layer, which_g + 1, k, not final)),
                  r=[bh, b_c2], w=[bh])

    K.att_stage = dbg.get("stage", 4) if isinstance(dbg, dict) else 4
    K.b_lnsq0 = P.buf("lnsq0")
    K.b_lnsq1 = P.buf("lnsq1")
    K.b_lnst = P.buf("lnst")

    M_UT = ARENA
    M_HID = ARENA + 16384
    M_W1 = M_HID + 32768
    M_W2 = M_W1 + 16384
    M_RT = M_W2 + 16384
    m_uT = [P.sb("m_uT", [128, 8, 512], BF16, M_UT + i * 8192) for i in range(2)]
    m_hid = P.sb("m_hid", [128, 32, 512], BF16, M_HID)
    m_w1 = [P.sb("m_w1", [128, 8, 512], BF16, M_W1 + i * 8192) for i in range(2)]
    m_w2 = [P.sb("m_w2", [128, 32, 128], BF16, M_W2 + i * 8192) for i in range(2)]
    m_rt = [P.sb("m_rt", [128, 512], F32, M_RT + i * 2048) for i in range(2)]
    assert M_RT + 4096 <= LN_OFF
    bm_uT = [P.buf("m_uT0"), P.buf("m_uT1")]
    bm_hid = P.buf("m_hid")
    bm_w1 = [P.buf("m_w10"), P.buf("m_w11")]
    bm_w2 = [P.buf("m_w20"), P.buf("m_w21")]
    bm_rt = [P.buf("m_rt0"), P.buf("m_rt1")]
    cnt = {"ut": 0, "w1": 0, "w2": 0, "rt": 0}

    def mlp(tgi, layer, rr):
        t0, n, _ = TGS[tgi]
        bh = b_hs[0][tgi]
        ui = cnt["ut"] % 2
        cnt["ut"] += 1
        uT, buT = m_uT[ui], bm_uT[ui]
        for k in range(8):
            P.act(lambda h, k=k: h.activation(out=uT[:, k, 0:n], in_=hs[:, k, t0:t0 + n], func=AF.Identity,
                                              scale=mod(layer, 4, k, rr), bias=mod(layer, 3, k, rr)),
                  r=[bh, b_modv[layer]], **({"w": [buT]} if k == 0 else {"pw": [buT]}))
        for fb in range(8):
            wi_ = cnt["w1"] % 2
            cnt["w1"] += 1
            w1t, bw1 = m_w1[wi_], bm_w1[wi_]
            P.pool(lambda h, w1t=w1t, fb=fb: h.dma_start(out=w1t[:, :, :], in_=w1_d[layer, fb]), dma_w=bw1)
            for fc in range(4):
                pb, b_pb = bank()
                for k in range(8):
                    P.pe(lambda h, pb=pb, w1t=w1t, fc=fc, k=k: h.matmul(
                        pb[:, 0:n], lhsT=w1t[:, k, fc * 128:(fc + 1) * 128], rhs=uT[:, k, 0:n],
                        start=(k == 0), stop=(k == 7)),
                        r=[bw1, buT], **({"w": [b_pb]} if k == 0 else {"pw": [b_pb]}))
                ri = cnt["rt"] % 2
                cnt["rt"] += 1
                rt, brt = m_rt[ri], bm_rt[ri]
                P.dve(lambda h, pb=pb, rt=rt: h.tensor_scalar(out=rt[:, 0:n], in0=pb[:, 0:n], scalar1=0.0, scalar2=None, op0=ALU.max),
                      r=[b_pb], w=[brt])
                f = fb * 4 + fc
                P.act(lambda h, rt=rt, f=f: h.activation(out=m_hid[:, f, 0:n], in_=rt[:, 0:n], func=AF.Square),
                      r=[brt], **({"w": [bm_hid]} if f == 0 else {"pw": [bm_hid]}))
        for dc in range(8):
            wi_ = cnt["w2"] % 2
            cnt["w2"] += 1
            w2t, bw2 = m_w2[wi_], bm_w2[wi_]
            P.pool(lambda h, w2t=w2t, dc=dc: h.dma_start(out=w2t[:, :, :], in_=w2_d[layer, dc]), dma_w=bw2)
            pb, b_pb = bank()
            for kf in range(32):
                P.pe(lambda h, pb=pb, w2t=w2t, kf=kf: h.matmul(
                    pb[:, 0:n], lhsT=w2t[:, kf, :], rhs=m_hid[:, kf, 0:n], start=(kf == 0), stop=(kf == 31)),
                    r=[bw2, bm_hid], **({"w": [b_pb]} if kf == 0 else {"pw": [b_pb]}))
            P.dve(lambda h, pb=pb, dc=dc: h.scalar_tensor_tensor(
                out=hs[:, dc, t0:t0 + n], in0=pb[:, 0:n], scalar=mod(layer, 5, dc, rr), in1=hs[:, dc, t0:t0 + n],
                op0=ALU.mult, op1=ALU.add),
                r=[b_pb, bh, b_modv[layer]], w=[bh])

    A_UT = ARENA
    A_COS = A_UT + 36864
    A_W = A_COS + 16384
    A_KT = A_W + 19456
    A_V = A_KT + 4608
    A_Q = A_V + 4608
    A_OT = A_Q + 8192
    A_PT = A_OT + 4096
    A_TMP = A_PT + 3072
    A_MN = A_TMP + 6144
    A_SK = A_MN + 4096
    A_RD = A_SK + 8192
    A_SQ = A_RD + 2048
    A_OUN = A_SQ + 1024
    A_END = A_OUN + 2048
    assert A_END <= 212832, A_END
    uTa = P.sb("uTa", [128, 8, T], BF16, A_UT)
    cstab = P.sb("cstab", [128, SEQ], F32, A_COS)
    wqq = P.sb("wqq", [128, 8, 512], BF16, A_W)
    wkv = P.sb("wkv", [128, 8, 192], BF16, A_W + 8192)
    wo = P.sb("wo", [128, 2, 1024], BF16, A_W + 11264)
    kTa = P.sb("kTa", [65, T], BF16, A_KT)
    Va = P.sb("Va", [128, 18, 128], BF16, A_V)
    qa = P.sb("qa", [65, 4, 512], BF16, A_Q)
    qpa = P.sb("qpa", [65, 4, 512], BF16, A_Q + 4096)
    OT = P.sb("OT", [128, 2, 512], BF16, A_OT)
    PT = [P.sb("PT", [128, 512], BF16, A_PT + i * 1024) for i in range(3)]
    tmp = [P.sb("atmp", [128, 512], F32, A_TMP + i * 2048) for i in range(3)]
    mneg = P.sb("mneg", [128, 4, 512], BF16, A_MN)
    skt = P.sb("skt", [128, 4, 512], F32, A_SK)
    rden = P.sb("rden", [64, 512], F32, A_RD)
    sqt = P.sb("sqt", [64, 512], BF16, A_SQ)
    oun = P.sb("oun", [64, 512], F32, A_OUN)
    b_oun = P.buf("oun")
    qbc = [0]
    psc = [0]
    b_uTa = [P.buf(f"uTa{g}") for g in range(len(TGS))]
    b_rope = P.buf("rope")
    b_wq, b_wkv, b_wo = P.buf("wq"), P.buf("wkv"), P.buf("wo")
    b_kT, b_V = P.buf("kT"), P.buf("V")
    b_qa, b_qpa, b_OT = P.buf("qa"), P.buf("qpa"), P.buf("OT")
    b_PT = [P.buf(f"PT{i}") for i in range(3)]
    b_tmp = [P.buf(f"atmp{i}") for i in range(3)]
    b_mneg, b_skt, b_rden, b_sqt = P.buf("mneg"), P.buf("skt"), P.buf("rden"), P.buf("sqt")
    ptc = [0]

    def rope(dst, bdst, pa, b_pa, lt0, n, pwflag):
        t1, t2 = tmp[0], tmp[1]
        P.dve(lambda h: h.tensor_tensor(out=t1[0:64, 0:n], in0=pa[0:64, 0:n], in1=cstab[0:64, lt0:lt0 + n], op=ALU.mult),
              r=[b_pa, b_rope], w=[b_tmp[0]])
        P.dve(lambda h: h.tensor_tensor(out=t2[0:64, 0:n], in0=pa[64:128, 0:n], in1=cstab[64:128, lt0:lt0 + n], op=ALU.mult),
              r=[b_pa, b_rope], w=[b_tmp[1]])
        P.dve(lambda h: h.tensor_tensor(out=dst, in0=t1[0:64, 0:n], in1=t2[0:64, 0:n], op=ALU.add),
              r=[b_tmp[0], b_tmp[1]], **({"pw": [bdst]} if pwflag else {"w": [bdst]}))

    def attention(layer, b):
        rr_l = b
        for tgi, (t0, n, isctx) in enumerate(TGS):
            rr = 2 if isctx else rr_l
            for k in range(8):
                P.act(lambda h, k=k, t0=t0, n=n, rr=rr: h.activation(
                    out=uTa[:, k, t0:t0 + n], in_=hs[:, k, t0:t0 + n], func=AF.Identity,
                    scale=mod(layer, 1, k, rr), bias=mod(layer, 0, k, rr)),
                    r=[b_hs[0][tgi], b_modv[layer]], **({"w": [b_uTa[tgi]]} if k == 0 else {"pw": [b_uTa[tgi]]}))
        P.sp(lambda h: h.dma_start(out=cstab[:, :], in_=cs_d[:, :]), dma_w=b_rope)
        for g in range(4):
            P.pool(lambda h, g=g: h.dma_start(out=wqq[:, :, :], in_=wqq_d[g]), dma_w=b_wq)
            P.pool(lambda h, g=g: h.dma_start(out=wkv[:, :, :], in_=wkv_d[g]), dma_w=b_wkv)
            P.pool(lambda h, g=g: h.dma_start(out=wo[:, :, :], in_=wo_d[g]), dma_w=b_wo)
            P.pool(lambda h: h.memset(kTa[64:65, :], 1.0), w=[b_kT])
            P.pool(lambda h: h.memset(Va[:, :, 64:128], 1.0), w=[b_V])
            for tgi, (t0, n, isctx) in enumerate(TGS):
                pk, b_pk = bank()
                for k in range(8):
                    P.pe(lambda h, pk=pk, k=k, t0=t0, n=n: h.matmul(
                        pk[:, 0:n], lhsT=wkv[:, k, 0:128], rhs=uTa[:, k, t0:t0 + n], start=(k == 0), stop=(k == 7)),
                        r=[b_wkv, b_uTa[tgi]], **({"w": [b_pk]} if k == 0 else {"pw": [b_pk]}))
                if isctx:
                    P.act(lambda h, pk=pk, t0=t0, n=n: h.activation(out=kTa[0:64, t0:t0 + n], in_=pk[0:64, 0:n], func=AF.Copy),
                          r=[b_pk], pw=[b_kT])
                else:
                    rope(kTa[0:64, t0:t0 + n], b_kT, pk, b_pk, t0 - CTX, n, True)
                P.act(lambda h, t0=t0, n=n: h.activation(out=sqt[:, 0:n], in_=kTa[0:64, t0:t0 + n], func=AF.Square),
                      r=[b_kT], w=[b_sqt])
                pn, b_pn = bank()
                P.pe(lambda h, pn=pn, n=n: h.matmul(pn[:, 0:n], lhsT=ones_bf[0:64, :], rhs=sqt[:, 0:n], start=True, stop=True),
                     r=[b_sqt, b_c2], w=[b_pn])
                P.dve(lambda h, pn=pn, n=n, tgi=tgi: h.tensor_reduce(out=kmaxp[:, tgi:tgi + 1], in_=pn[:, 0:n], axis=AX.X, op=ALU.max),
                      r=[b_pn], **({"w": [b_kmax]} if tgi == 0 else {"pw": [b_kmax]}))
                for bi in range(n // 128):
                    blk = t0 // 128 + bi
                    pv, b_pv = bank()
                    for k in range(8):
                        P.pe(lambda h, pv=pv, k=k, blk=blk: h.matmul(
                            pv[:, 0:64], lhsT=uTa[:, k, blk * 128:(blk + 1) * 128], rhs=wkv[:, k, 128:192],
                            start=(k == 0), stop=(k == 7)),
                            r=[b_wkv, b_uTa[tgi]], **({"w": [b_pv]} if k == 0 else {"pw": [b_pv]}))
                    P.act(lambda h, pv=pv, blk=blk: h.activation(out=Va[:, blk, 0:64], in_=pv[:, 0:64], func=AF.Copy),
                          r=[b_pv], pw=[b_V])
            P.dve(lambda h: h.tensor_reduce(out=kmax2[:, :], in_=kmaxp[:, 0:5], axis=AX.X, op=ALU.max), r=[b_kmax], w=[b_kmax])
            P.dve(lambda h: h.tensor_scalar(out=kmax2[:, :], in0=kmax2[:, :], scalar1=1.05, scalar2=None, op0=ALU.mult), r=[b_kmax], w=[b_kmax])
            for tgi, (t0, n, isctx) in enumerate(TGS):
                if K.att_stage < 2:
                    break
                pns = {}

                def q_chain(r):
                    pn, b_pn = pns[r]
                    t3 = tmp[2]
                    P.act(lambda h, pn=pn, n=n: h.activation(out=t3[:, 0:n], in_=pn[:, 0:n], func=AF.Ln, scale=kmax2[:, 0:1], bias=epsv[:, 1:2]),
                          r=[b_pn, b_kmax, b_c2], w=[b_tmp[2]])
                    P.act(lambda h, n=n: h.activation(out=t3[:, 0:n], in_=t3[:, 0:n], func=AF.Exp, scale=0.5), r=[b_tmp[2]], w=[b_tmp[2]])
                    P.dve(lambda h, r=r, n=n: h.tensor_scalar(out=mneg[:, r, 0:n], in0=t3[:, 0:n], scalar1=-1.0, scalar2=None, op0=ALU.mult),
                          r=[b_tmp[2]], **({"w": [b_mneg]} if r == 0 else {"pw": [b_mneg]}))
                    hd = g * 4 + r
                    P.act(lambda h, r=r, n=n, hd=hd: h.activation(out=skt[64:128, r, 0:n], in_=mneg[64:128, r, 0:n], func=AF.Exp,
                                                                   scale=0.125, bias=sinkv[64:128, hd:hd + 1]),
                          r=[b_mneg, b_const], **({"w": [b_skt]} if r == 0 else {"pw": [b_skt]}))

                for r in range(4):
                    pq, b_pq = bank()
                    for k in range(8):
                        P.pe(lambda h, pq=pq, k=k, r=r, t0=t0, n=n: h.matmul(
                            pq[:, 0:n], lhsT=wqq[:, k, r * 128:(r + 1) * 128], rhs=uTa[:, k, t0:t0 + n], start=(k == 0), stop=(k == 7)),
                            r=[b_wq, b_uTa[tgi]], **({"w": [b_pq]} if k == 0 else {"pw": [b_pq]}))
                    P.act(lambda h, pq=pq, r=r, n=n: h.activation(out=qpa[0:64, r, 0:n], in_=pq[0:64, 0:n], func=AF.Copy),
                          r=[b_pq], **({"w": [b_qpa]} if r == 0 else {"pw": [b_qpa]}))
                    if not isctx:
                        rope(qa[0:64, r, 0:n], b_qa, pq, b_pq, t0 - CTX, n, r != 0)
                    P.act(lambda h, r=r, n=n: h.activation(out=sqt[:, 0:n], in_=qpa[0:64, r, 0:n], func=AF.Square),
                          r=[b_qpa], w=[b_sqt])
                    pn, b_pn = bank()
                    pns[r] = (pn, b_pn)
                    P.pe(lambda h, pn=pn, n=n: h.matmul(pn[:, 0:n], lhsT=ones_bf[0:64, :], rhs=sqt[:, 0:n], start=True, stop=True),
                         r=[b_sqt, b_c2], w=[b_pn])
                    if r >= 1:
                        q_chain(r - 1)
                q_chain(3)
                P.dve(lambda h, n=n: h.tensor_copy(out=qpa[64:65, :, 0:n], in_=mneg[64:65, :, 0:n]), r=[b_mneg], pw=[b_qpa])
                if not isctx:
                    P.dve(lambda h, n=n: h.tensor_copy(out=qa[64:65, :, 0:n], in_=mneg[64:65, :, 0:n]), r=[b_mneg], pw=[b_qa])
                if K.att_stage < 3:
                    continue
                for qb in range(n // 128):
                    qs = slice(qb * 128, (qb + 1) * 128)
                    if isctx:
                        chunks = [(0, qpa, b_qpa, None), (1, qpa, b_qpa, None)]
                    else:
                        j = (t0 - CTX) // 128 + qb
                        chunks = []
                        if j > 0:
                            chunks.append((2 + j - 1, qa, b_qa, mprev))
                        chunks.append((2 + j, qa, b_qa, None))
                        if j < 15:
                            chunks.append((2 + j + 1, qa, b_qa, mnext))
                        chunks += [(0, qpa, b_qpa, None), (1, qpa, b_qpa, None)]
                    po, b_po = pbank[6 + qbc[0] % 2], b_bank[6 + qbc[0] % 2]
                    qbc[0] += 1
                    nch = len(chunks)
                    pss = [None] * nch

                    def emit_S(ci):
                        kb, qt, bqt, msk = chunks[ci]
                        bi_ = psc[0] % 6
                        psc[0] += 1
                        ps_, b_ps = pbank[bi_], b_bank[bi_]
                        pss[ci] = (ps_, b_ps)
                        P.pe(lambda h, ps_=ps_, kb=kb, qt=qt, qs=qs, msk=msk: h.matmul(
                            ps_[:, :], lhsT=kTa[0:65, kb * 128:(kb + 1) * 128], rhs=qt[0:65, :, qs],
                            start=True, stop=(msk is None)),
                            r=[b_kT, bqt], w=[b_ps])
                        if msk is not None:
                            P.pe(lambda h, ps_=ps_, msk=msk: h.matmul(ps_[:, :], lhsT=ident[:, :], rhs=msk[:, :], start=False, stop=True),
                                 r=[b_const], pw=[b_ps])

                    emit_S(0)
                    for ci in range(nch):
                        if ci + 1 < nch:
                            emit_S(ci + 1)
                        kb = chunks[ci][0]
                        ps_, b_ps = pss[ci]
                        pi = ptc[0] % 3
                        ptc[0] += 1
                        pt_, bpt = PT[pi], b_PT[pi]
                        P.act(lambda h, ps_=ps_, pt_=pt_: h.activation(out=pt_[:, :], in_=ps_[:, :], func=AF.Exp, scale=0.125),
                              r=[b_ps], w=[bpt])
                        P.pe(lambda h, po=po, kb=kb, pt_=pt_, ci=ci, nch=nch: h.matmul(
                            po[:, :], lhsT=Va[:, kb, :], rhs=pt_[:, :], start=(ci == 0), stop=(ci == nch - 1)),
                            r=[b_V, bpt], **({"w": [b_po]} if ci == 0 else {"pw": [b_po]}))
                    P.dve(lambda h, po=po, qs=qs: h.tensor_tensor(out=rden[0:64, :].rearrange("p (r q) -> p r q", r=4),
                                                                  in0=po[64:128, :].rearrange("p (r q) -> p r q", r=4),
                                                                  in1=skt[64:128, :, qs], op=ALU.add),
                          r=[b_po, b_skt], w=[b_rden])
                    P.dve(lambda h, po=po: h.tensor_copy(out=oun[0:64, :], in_=po[0:64, :]), r=[b_po], w=[b_oun])
                    P.act(lambda h: h.activation(out=rden[0:64, :], in_=rden[0:64, :], func=AF.Ln), r=[b_rden], w=[b_rden])
                    P.act(lambda h: h.activation(out=rden[0:64, :], in_=rden[0:64, :], func=AF.Exp, scale=-1.0), r=[b_rden], w=[b_rden])
                    for par in range(2):
                        P.dve(lambda h, qs=qs, par=par: h.tensor_tensor(
                            out=OT[par * 64:(par + 1) * 64, :, qs],
                            in0=oun[0:64, :].rearrange("p (a b q) -> p a b q", a=2, b=2)[:, :, par, :],
                            in1=rden[0:64, :].rearrange("p (a b q) -> p a b q", a=2, b=2)[:, :, par, :], op=ALU.mult),
                            r=[b_oun, b_rden], **({"w": [b_OT]} if (qb == 0 and par == 0) else {"pw": [b_OT]}))
                if K.att_stage < 4:
                    continue
                rr = 2 if isctx else rr_l
                for kd in range(8):
                    py, b_py = bank()
                    for r in range(2):
                        P.pe(lambda h, py=py, r=r, kd=kd, n=n: h.matmul(
                            py[:, 0:n], lhsT=wo[:, r, kd * 128:(kd + 1) * 128], rhs=OT[:, r, 0:n],
                            start=(r == 0), stop=(r == 1)),
                            r=[b_wo, b_OT], **({"w": [b_py]} if r == 0 else {"pw": [b_py]}))
                    P.dve(lambda h, py=py, kd=kd, t0=t0, n=n, rr=rr: h.scalar_tensor_tensor(
                        out=hs[:, kd, t0:t0 + n], in0=py[:, 0:n], scalar=mod(layer, 2, kd, rr), in1=hs[:, kd, t0:t0 + n],
                        op0=ALU.mult, op1=ALU.add),
                        r=[b_py, b_hs[0][tgi], b_modv[layer]], w=[b_hs[0][tgi]])
        P.barrier()
        for tgi in range(len(TGS)):
            layer_norm(tgi, layer, 0)
        P.barrier()


    so = [ARENA + 36864]

    def salloc(name, shape, dt, n=1):
        nb = int(np.prod(shape[1:])) * (4 if dt == F32 else 2)
        nb = (nb + 31) // 32 * 32
        ts = [P.sb(name, shape, dt, so[0] + i * nb) for i in range(n)]
        so[0] += nb * n
        return ts if n > 1 else ts[0]

    s_wxbc = salloc("s_wxbc", [128, 8, 768], BF16)
    s_wdt = salloc("s_wdt", [128, 8, 16], BF16)
    s_raw = salloc("s_raw", [128, 520], F32, 3)
    s_acc = salloc("s_acc", [128, 512], F32, 3)
    s_th = salloc("s_th", [128, 512], F32, 3)
    s_xo = salloc("s_xo", [128, 512], BF16, 3)
    s_t1 = salloc("s_t1", [128, 18, 16], F32)
    S1_END = so[0]
    so[0] = ARENA + 36864
    s_wso = salloc("s_wso", [128, 4, 1024], BF16)
    s_Sbin = salloc("s_Sbin", [128, 16, 512], BF16)
    s_Sf = salloc("s_Sf", [128, 512], F32)
    s_Sfb = salloc("s_Sfb", [128, 512], BF16)
    s_xw = [salloc("s_xw", [128, 512], BF16)] * 2
    s_cbm = salloc("s_cbm", [128, 128], BF16, 2)
    s_arg = salloc("s_arg", [128, 4, 128], BF16, 2)
    s_M = salloc("s_M", [128, 4, 128], BF16, 4)
    s_ya = salloc("s_ya", [128, 512], F32)
    s_yb = salloc("s_yb", [128, 512], F32)
    s_Sb = s_yb
    s_yg = s_ya
    s_yn = salloc("s_yn", [128, 512], BF16, 2)
    s_gz = salloc("s_gz", [128, 512], F32)
    s_ynT = salloc("s_ynT", [128, 4, 512], BF16)
    s_ss = salloc("s_ss", [128, 4], F32)
    S2_END = so[0]
    so[0] = max(S1_END, S2_END)
    s_wz = salloc("s_wz", [128, 8, 512], BF16)
    s_xtok = salloc("s_xtok", [128, 18, 512], BF16)
    s_btok = salloc("s_btok", [128, 18, 128], BF16)
    s_BT = salloc("s_BT", [128, T], BF16)
    s_CT = salloc("s_CT", [128, T], BF16)
    s_dt = salloc("s_dt", [128, 18, 16], F32)
    s_a = salloc("s_a", [128, 18, 16], F32)
    s_cs = salloc("s_cs", [128, 18, 16], F32)
    s_tot = salloc("s_tot", [128, 18, 16], F32)
    s_E = salloc("s_E", [128, 18, 16], F32)
    s_dec = salloc("s_dec", [128, 18, 16], F32)
    s_cw = salloc("s_cw", [128, 36], F32)
    s_sb = salloc("s_sb", [128, 40], F32)
    s_ng = salloc("s_ng", [128, 4], F32)
    s_tri = salloc("s_tri", [128, 256], F32)
    s_one = salloc("s_one", [128, 2], F32)
    assert so[0] <= 212832, so[0]
    bs = {nm: P.buf(nm) for nm in ("xtok", "btok", "BT", "CT", "gz", "dt", "a", "cs", "tot", "E", "dec", "t1", "t2", "par",
                                   "wxbc", "wz", "wdt", "raw0", "raw1", "raw2", "acc0", "acc1", "acc2", "th0", "th1", "th2", "xo0", "xo1", "xo2", "arg0", "arg1", "xdt0", "xdt1", "wso", "Sbin", "Sf", "Sb",
                                   "Sfb", "xw0", "xw1", "cbm0", "cbm1", "arg", "L", "M0", "M1", "M2", "M3", "ya", "yb", "yg",
                                   "yn0", "yn1", "ynT", "ss")}
    sc_ = {"raw": 0, "xo": 0, "xw": 0, "M": 0, "acc": 0, "arg": 0}
    tri = s_tri[:, 0:128]
    trirev = s_tri[:, 128:256]

    def halo_ap(base, stride):
        a = base.ap
        return bass.AP(base.tensor, base.offset, [list(a[0]), [stride, 2], [1, 2]])

    def bc8(ap2):
        return ap2.unsqueeze(2).to_broadcast([128, 8, 64])

    def ssm(layer, b):
        rr_l = b
        for tgi, (t0, n, isctx) in enumerate(TGS):
            rr = 2 if isctx else rr_l
            for k in range(8):
                P.act(lambda h, k=k, t0=t0, n=n, rr=rr: h.activation(
                    out=uTa[:, k, t0:t0 + n], in_=hs[:, k, t0:t0 + n], func=AF.Identity,
                    scale=mod(layer, 1, k, rr), bias=mod(layer, 0, k, rr)),
                    r=[b_hs[0][tgi], b_modv[layer]], **({"w": [b_uTa[tgi]]} if k == 0 else {"pw": [b_uTa[tgi]]}))
        P.sp(lambda h: h.dma_start(out=s_tri[:, :], in_=tri_d[:, :]), dma_w=bs["par"])
        P.pool(lambda h: h.memset(s_one[:, :], 1.0), w=[bs["t2"]])
        t2v = None
        for g in range(4):
            P.sp(lambda h, g=g: h.dma_start(out=s_cw[:, :], in_=cw_d[g]), dma_w=bs["par"])
            P.sp(lambda h, g=g: h.dma_start(out=s_sb[:, :], in_=sb_d[g]), dma_w=bs["par"])
            P.sp(lambda h, g=g: h.dma_start(out=s_ng[:, :], in_=ng_d[g]), dma_w=bs["par"])
            P.pool(lambda h, g=g: h.dma_start(out=s_wxbc[:, :, :], in_=wxbc_d[g]), dma_w=bs["wxbc"])
            P.pool(lambda h, g=g: h.dma_start(out=s_wz[:, :, :], in_=wz_d[g]), dma_w=bs["wz"])
            P.pool(lambda h, g=g: h.dma_start(out=s_wdt[:, :, :], in_=wdt_d[g]), dma_w=bs["wdt"])
            P.dve(lambda h: h.tensor_scalar(out=s_cw[:, :], in0=s_cw[:, :], scalar1=0.5, scalar2=None, op0=ALU.mult), r=[bs["par"]], w=[bs["par"]])
            P.act(lambda h: h.activation(out=s_sb[:, 16:32], in_=s_sb[:, 16:32], func=AF.Exp), r=[bs["par"]], w=[bs["par"]])
            P.dve(lambda h: h.tensor_scalar(out=s_sb[:, 16:32], in0=s_sb[:, 16:32], scalar1=-1.0, scalar2=None, op0=ALU.mult), r=[bs["par"]], w=[bs["par"]])
            def A1(tile):
                tgi, c = tile["tgi"], tile["c"]
                t0, n, isctx = TGS[tgi]
                seg0, seg1 = (0, CTX) if isctx else (CTX, T)
                ri = sc_["raw"] % 3
                sc_["raw"] += 1
                raw, braw = s_raw[ri], bs[f"raw{ri}"]
                ai = sc_["acc"] % 3
                sc_["acc"] += 1
                acc_, bacc = s_acc[ai], bs[f"acc{ai}"]
                th_, bth = s_th[ai], bs[f"th{ai}"]
                tile.update(raw=raw, braw=braw, acc=acc_, bacc=bacc, th=th_, bth=bth)
                pm_, b_pm_ = bank()
                for k in range(8):
                    P.pe(lambda h, k=k: h.matmul(pm_[:, 0:n], lhsT=s_wxbc[:, k, c * 128:(c + 1) * 128], rhs=uTa[:, k, t0:t0 + n],
                                                 start=(k == 0), stop=(k == 7)),
                         r=[bs["wxbc"], b_uTa[tgi]], **({"w": [b_pm_]} if k == 0 else {"pw": [b_pm_]}))
                P.act(lambda h: h.activation(out=raw[:, 2:2 + n], in_=pm_[:, 0:n], func=AF.Copy), r=[b_pm_], w=[braw])
                hasl = (t0 - 2) >= seg0
                hasr = (t0 + n + 2) <= seg1
                if not hasl:
                    P.pool(lambda h: h.memset(raw[:, 0:2], 0.0), pw=[braw])
                if not hasr:
                    P.pool(lambda h: h.memset(raw[:, 2 + n:4 + n], 0.0), pw=[braw])
                if hasl or hasr:
                    ph, b_ph = bank()
                    hbufs = []
                    if hasl:
                        hbufs.append(b_uTa[[i for i, (a0, an, _) in enumerate(TGS) if a0 <= t0 - 2 < a0 + an][0]])
                    if hasr:
                        hbufs.append(b_uTa[[i for i, (a0, an, _) in enumerate(TGS) if a0 <= t0 + n < a0 + an][0]])
                    for k in range(8):
                        if hasl and hasr:
                            rhs_fn = lambda k=k: halo_ap(uTa[:, k, t0 - 2:t0], n + 2)
                            ncol = 4
                        elif hasl:
                            rhs_fn = lambda k=k: uTa[:, k, t0 - 2:t0]
                            ncol = 2
                        else:
                            rhs_fn = lambda k=k: uTa[:, k, t0 + n:t0 + n + 2]
                            ncol = 2
                        P.pe(lambda h, k=k, rhs_fn=rhs_fn, ncol=ncol: h.matmul(
                            ph[:, 0:ncol], lhsT=s_wxbc[:, k, c * 128:(c + 1) * 128], rhs=rhs_fn(), start=(k == 0), stop=(k == 7)),
                            r=[bs["wxbc"]] + hbufs, **({"w": [b_ph]} if k == 0 else {"pw": [b_ph]}))
                    if hasl:
                        P.act(lambda h: h.activation(out=raw[:, 0:2], in_=ph[:, 0:2], func=AF.Copy), r=[b_ph], pw=[braw])
                    if hasr:
                        o_ = 2 if hasl else 0
                        P.act(lambda h: h.activation(out=raw[:, 2 + n:4 + n], in_=ph[:, o_:o_ + 2], func=AF.Copy), r=[b_ph], pw=[braw])
                P.act(lambda h: h.activation(out=acc_[:, 0:n], in_=raw[:, 0:n], func=AF.Identity,
                                             scale=s_cw[:, c * 6:c * 6 + 1], bias=s_cw[:, c * 6 + 5:c * 6 + 6]),
                      r=[braw, bs["par"]], w=[bacc])

            def A2(tile):
                c = tile["c"]
                t0, n, isctx = TGS[tile["tgi"]]
                raw, braw, acc_, bacc = tile["raw"], tile["braw"], tile["acc"], tile["bacc"]
                for j in range(1, 5):
                    P.dve(lambda h, j=j: h.scalar_tensor_tensor(
                        out=acc_[:, 0:n], in0=raw[:, j:j + n], scalar=s_cw[:, c * 6 + j:c * 6 + j + 1], in1=acc_[:, 0:n],
                        op0=ALU.mult, op1=ALU.add), r=[braw, bs["par"], bacc], w=[bacc])

            def A3(tile):
                t0, n, isctx = TGS[tile["tgi"]]
                acc_, bacc, th_, bth = tile["acc"], tile["bacc"], tile["th"], tile["bth"]
                P.act(lambda h: h.activation(out=th_[:, 0:n], in_=acc_[:, 0:n], func=AF.Tanh), r=[bacc], w=[bth])

            def A4(tile):
                tgi, c = tile["tgi"], tile["c"]
                t0, n, isctx = TGS[tgi]
                acc_, bacc, th_, bth = tile["acc"], tile["bacc"], tile["th"], tile["bth"]
                if c <= 4:
                    xi = sc_["xo"] % 3
                    sc_["xo"] += 1
                    xo, bxo = s_xo[xi], bs[f"xo{xi}"]
                    P.dve(lambda h: h.scalar_tensor_tensor(out=xo[:, 0:n], in0=th_[:, 0:n], scalar=1.0, in1=acc_[:, 0:n],
                                                           op0=ALU.add, op1=ALU.mult), r=[bth, bacc], w=[bxo])
                    if c == 4:
                        P.act(lambda h: h.activation(out=s_BT[:, t0:t0 + n], in_=xo[:, 0:n], func=AF.Copy), r=[bxo], pw=[bs["BT"]])
                    for bi in range(n // 128):
                        blk = t0 // 128 + bi
                        ptr, b_ptr = bank()
                        P.pe(lambda h, ptr=ptr, bi=bi: h.matmul(ptr[:, 0:128], lhsT=xo[:, bi * 128:(bi + 1) * 128], rhs=ident[:, :], start=True, stop=True),
                             r=[bxo, b_const], w=[b_ptr])
                        if c < 4:
                            P.act(lambda h, ptr=ptr, blk=blk: h.activation(out=s_xtok[:, blk, c * 128:(c + 1) * 128], in_=ptr[:, 0:128], func=AF.Copy),
                                  r=[b_ptr], pw=[bs["xtok"]])
                        else:
                            P.act(lambda h, ptr=ptr, blk=blk: h.activation(out=s_btok[:, blk, :], in_=ptr[:, 0:128], func=AF.Copy),
                                  r=[b_ptr], pw=[bs["btok"]])
                else:
                    P.dve(lambda h: h.scalar_tensor_tensor(out=s_CT[:, t0:t0 + n], in0=th_[:, 0:n], scalar=1.0, in1=acc_[:, 0:n],
                                                           op0=ALU.add, op1=ALU.mult), r=[bth, bacc], pw=[bs["CT"]])
                if c == 5:
                    for bi in range(n // 128):
                        blk = t0 // 128 + bi
                        pd, b_pd = bank()
                        for k in range(8):
                            P.pe(lambda h, pd=pd, k=k, blk=blk: h.matmul(pd[:, 0:16], lhsT=uTa[:, k, blk * 128:(blk + 1) * 128], rhs=s_wdt[:, k, :],
                                                                         start=(k == 0), stop=(k == 7)),
                                 r=[bs["wdt"], b_uTa[tgi]], **({"w": [b_pd]} if k == 0 else {"pw": [b_pd]}))
                        P.dve(lambda h, pd=pd, blk=blk: h.tensor_tensor(out=s_dt[:, blk, :], in0=pd[:, 0:16], in1=s_sb[:, 0:16], op=ALU.add),
                              r=[b_pd, bs["par"]], pw=[bs["dt"]])

            tiles = [dict(tgi=tgi, c=c) for tgi in range(len(TGS)) for c in range(6)]
            A1(tiles[0])
            for i, tile in enumerate(tiles):
                A2(tile)
                if i + 1 < len(tiles):
                    A1(tiles[i + 1])
                A3(tile)
                if i >= 1:
                    A4(tiles[i - 1])
            A4(tiles[-1])
            dtv, t1v = s_dt[:, :, :], s_t1[:, :, :]
            P.dve(lambda h: h.tensor_scalar(out=t1v, in0=dtv, scalar1=-1.0, scalar2=None, op0=ALU.mult), r=[bs["dt"]], w=[bs["t1"]])
            P.dve(lambda h: h.tensor_tensor(out=t1v, in0=t1v, in1=dtv, op=ALU.max), r=[bs["dt"], bs["t1"]], w=[bs["t1"]])
            P.act(lambda h: h.activation(out=t1v, in_=t1v, func=AF.Exp, scale=-1.0), r=[bs["t1"]], w=[bs["t1"]])
            P.act(lambda h: h.activation(out=t1v, in_=t1v, func=AF.Ln, bias=s_one[:, 0:1], scale=1.0), r=[bs["t1"], bs["t2"]], w=[bs["t1"]])
            P.dve(lambda h: h.scalar_tensor_tensor(out=dtv, in0=dtv, scalar=0.0, in1=t1v, op0=ALU.max, op1=ALU.add), r=[bs["dt"], bs["t1"]], w=[bs["dt"]])
            P.dve(lambda h: h.tensor_tensor(out=s_a[:, :, :], in0=dtv, in1=s_sb[:, 16:32].unsqueeze(1).to_broadcast([128, 18, 16]), op=ALU.mult),
                  r=[bs["dt"], bs["par"]], w=[bs["a"]])
            for blk in range(18):
                pc, b_pc = bank()
                P.pe(lambda h, pc=pc, blk=blk: h.matmul(pc[:, 0:8], lhsT=tri, rhs=s_a[:, blk, 0:8], start=True, stop=True), r=[bs["a"], bs["par"]], w=[b_pc])
                P.pe(lambda h, pc=pc, blk=blk: h.matmul(pc[:, 8:16], lhsT=trirev, rhs=s_a[:, blk, 8:16], start=True, stop=True), r=[bs["a"], bs["par"]], pw=[b_pc])
                P.pe(lambda h, pc=pc, blk=blk: h.matmul(pc[:, 16:32], lhsT=ones_f[:, :], rhs=s_a[:, blk, :], start=True, stop=True), r=[bs["a"], b_c2], pw=[b_pc])
                P.act(lambda h, pc=pc, blk=blk: h.activation(out=s_cs[:, blk, :], in_=pc[:, 0:16], func=AF.Copy), r=[b_pc], pw=[bs["cs"]])
                P.act(lambda h, pc=pc, blk=blk: h.activation(out=s_tot[:, blk, :], in_=pc[:, 16:32], func=AF.Copy, scale=float(D)), r=[b_pc], pw=[bs["tot"]])
            P.act(lambda h: h.activation(out=s_E[:, :, :], in_=s_cs[:, :, :], func=AF.Exp), r=[bs["cs"]], w=[bs["E"]])
            P.dve(lambda h: h.tensor_tensor(out=s_dec[:, :, :], in0=s_tot[:, :, :], in1=s_cs[:, :, :], op=ALU.subtract), r=[bs["tot"], bs["cs"]], w=[bs["dec"]])
            P.act(lambda h: h.activation(out=s_dec[:, :, :], in_=s_dec[:, :, :], func=AF.Exp), r=[bs["dec"]], w=[bs["dec"]])
            P.dve(lambda h: h.tensor_tensor(out=s_dec[:, :, :], in0=s_dec[:, :, :], in1=s_dt[:, :, :], op=ALU.mult), r=[bs["dec"], bs["dt"]], w=[bs["dec"]])
            P.act(lambda h: h.activation(out=s_tot[:, :, :], in_=s_tot[:, :, :], func=AF.Exp), r=[bs["tot"]], w=[bs["tot"]])
            P.act(lambda h: h.activation(out=s_dt[:, :, :], in_=s_dt[:, :, :], func=AF.Ln), r=[bs["dt"], bs["dec"]], w=[bs["dt"]])
            P.dve(lambda h: h.tensor_tensor(out=s_dt[:, :, :], in0=s_dt[:, :, :], in1=s_cs[:, :, :], op=ALU.subtract), r=[bs["dt"], bs["cs"]], w=[bs["dt"]])
            P.barrier()
            P.pool(lambda h, g=g: h.dma_start(out=s_wso[:, :, :], in_=wso_d[g]), dma_w=bs["wso"])
            for c in range(4):
                P.dve(lambda h, c=c: h.tensor_scalar(out=s_wso[:, c, :], in0=s_wso[:, c, :], scalar1=s_ng[:, c:c + 1], scalar2=None, op0=ALU.mult),
                      r=[bs["wso"], bs["par"]], w=[bs["wso"]])
            P.pool(lambda h: h.memset(s_Sf[:, :], 0.0), w=[bs["Sf"]])
            P.pool(lambda h: h.memset(s_Sb[:, :], 0.0), w=[bs["Sb"]])

            def state_update(S, bS, blk, d):
                xi = sc_["xw"] % 2
                sc_["xw"] += 1
                xw, bxw = s_xw[0], bs["xw0"]
                P.dve(lambda h: h.tensor_tensor(out=xw[:, :].rearrange("p (a b) -> p a b", a=8), in0=s_xtok[:, blk, :].rearrange("p (a b) -> p a b", a=8),
                                                in1=bc8(s_dec[:, blk, d * 8:d * 8 + 8]), op=ALU.mult), r=[bs["xtok"], bs["dec"]], w=[bxw])
                pst, b_pst = bank()
                P.pe(lambda h: h.matmul(pst[:, :], lhsT=s_btok[:, blk, :], rhs=xw[:, :], start=True, stop=True), r=[bs["btok"], bxw], w=[b_pst])
                P.dve(lambda h: h.tensor_tensor(out=S[:, :].rearrange("p (a b) -> p a b", a=8), in0=S[:, :].rearrange("p (a b) -> p a b", a=8),
                                                in1=bc8(s_tot[:, blk, d * 8:d * 8 + 8]), op=ALU.mult), r=[bS, bs["tot"]], w=[bS])
                P.dve(lambda h: h.tensor_tensor(out=S[:, :], in0=S[:, :], in1=pst[:, :], op=ALU.add), r=[bS, b_pst], w=[bS])

            for blk in [1, 0] + list(range(17, 1, -1)):
                if blk >= 2:
                    P.act(lambda h, blk=blk: h.activation(out=s_Sbin[:, blk - 2, :], in_=s_Sb[:, :], func=AF.Copy), r=[bs["Sb"]], pw=[bs["Sbin"]])
                state_update(s_Sb, bs["Sb"], blk, 1)
            groups = [(half, d) for half in range(2) for d in range(2)]
            st = {}

            def F1(blk):
                cols = slice(blk * 128, (blk + 1) * 128)
                P.act(lambda h: h.activation(out=s_Sfb[:, :], in_=s_Sf[:, :], func=AF.Copy), r=[bs["Sf"]], w=[bs["Sfb"]])
                state_update(s_Sf, bs["Sf"], blk, 0)
                pcb, b_pcb = bank()
                P.pe(lambda h: h.matmul(pcb[:, 0:128], lhsT=s_BT[:, cols], rhs=s_CT[:, cols], start=True, stop=True), r=[bs["BT"], bs["CT"]], w=[b_pcb])
                P.dve(lambda h: h.tensor_tensor(out=s_cbm[0][:, :], in0=pcb[:, 0:128], in1=tri, op=ALU.mult), r=[b_pcb, bs["par"]], w=[bs["cbm0"]])
                P.dve(lambda h: h.tensor_tensor(out=s_cbm[1][:, :], in0=pcb[:, 0:128], in1=trirev, op=ALU.mult), r=[b_pcb, bs["par"]], w=[bs["cbm1"]])
                pabs = []
                for (half, d) in groups:
                    pab, b_pab = bank()
                    pabs.append((pab, b_pab))
                    P.pe(lambda h, pab=pab, d=d: h.matmul(pab[:, :], lhsT=ident[:, :], rhs=(mnext if d == 0 else mprev)[:, :], start=True, stop=False),
                         r=[b_const], w=[b_pab])
                    for ci in range(4):
                        col = d * 8 + half * 4 + ci
                        P.pe(lambda h, pab=pab, ci=ci, col=col, d=d: h.matmul(
                            pab[:, ci * 128:(ci + 1) * 128], lhsT=s_a[:, blk, col:col + 1].to_broadcast([128, 128]),
                            rhs=(tri if d == 0 else trirev), start=False, stop=(ci == 3)),
                            r=[bs["a"], bs["par"]], pw=[b_pab])
                pz, b_pz = bank()
                tgz = 1 + (blk - 2) // 4
                for k in range(8):
                    P.pe(lambda h, k=k: h.matmul(pz[:, :], lhsT=uTa[:, k, cols], rhs=s_wz[:, k, :], start=(k == 0), stop=(k == 7)),
                         r=[bs["wz"], b_uTa[tgz]], **({"w": [b_pz]} if k == 0 else {"pw": [b_pz]}))
                args = [None] * 4
                Ms = [None] * 4

                def emit_exp(gi):
                    half, d = groups[gi]
                    pab, b_pab = pabs[gi]
                    ai = sc_["arg"] % 2
                    sc_["arg"] += 1
                    arg, barg = s_arg[ai], bs[f"arg{ai}"]
                    args[gi] = (arg, barg)
                    for ci in range(4):
                        col = d * 8 + half * 4 + ci
                        P.act(lambda h, ci=ci, col=col: h.activation(
                            out=arg[:, ci, :], in_=pab[:, ci * 128:(ci + 1) * 128], func=AF.Exp, bias=s_dt[:, blk, col:col + 1], scale=1.0),
                            r=[b_pab, bs["dt"]], **({"w": [barg]} if ci == 0 else {"pw": [barg]}))

                def emit_mul(gi):
                    half, d = groups[gi]
                    arg, barg = args[gi]
                    mi = sc_["M"] % 4
                    sc_["M"] += 1
                    Mt, bM = s_M[mi], bs[f"M{mi}"]
                    Ms[gi] = (Mt, bM)
                    P.dve(lambda h: h.tensor_tensor(
                        out=Mt[:, :, :], in0=arg[:, :, :], in1=s_cbm[d][:, :].unsqueeze(1).to_broadcast([128, 4, 128]), op=ALU.mult),
                        r=[barg, bs[f"cbm{d}"]], w=[bM])

                emit_exp(0)
                emit_exp(1)
                emit_mul(0)
                emit_exp(2)
                emit_mul(1)
                emit_exp(3)
                emit_mul(2)
                emit_mul(3)
                P.act(lambda h: h.activation(out=s_gz[:, :], in_=pz[:, :], func=AF.Exp, scale=-1.0), r=[b_pz], w=[bs["gz"]])
                P.act(lambda h: h.activation(out=s_gz[:, :], in_=s_gz[:, :], func=AF.Ln, bias=s_one[:, 0:1], scale=1.0), r=[bs["gz"], bs["t2"]], w=[bs["gz"]])
                P.act(lambda h: h.activation(out=s_gz[:, :], in_=s_gz[:, :], func=AF.Exp, scale=-1.0), r=[bs["gz"]], w=[bs["gz"]])
                P.dve(lambda h: h.tensor_tensor(out=s_gz[:, :], in0=s_gz[:, :], in1=pz[:, :], op=ALU.mult), r=[bs["gz"], b_pz], w=[bs["gz"]])
                pof, b_pof = bank()
                P.pe(lambda h: h.matmul(pof[:, :], lhsT=s_CT[:, cols], rhs=s_Sfb[:, :], start=True, stop=True), r=[bs["CT"], bs["Sfb"]], w=[b_pof])
                pob, b_pob = bank()
                P.pe(lambda h: h.matmul(pob[:, :], lhsT=s_CT[:, cols], rhs=s_Sbin[:, blk - 2, :], start=True, stop=True), r=[bs["CT"], bs["Sbin"]], w=[b_pob])
                P.dve(lambda h: h.tensor_tensor(out=s_ya[:, :].rearrange("p (a b) -> p a b", a=8), in0=pof[:, :].rearrange("p (a b) -> p a b", a=8),
                                                in1=bc8(s_E[:, blk, 0:8]), op=ALU.mult), r=[b_pof, bs["E"]], w=[bs["ya"]])
                P.dve(lambda h: h.tensor_tensor(out=s_yb[:, :].rearrange("p (a b) -> p a b", a=8), in0=pob[:, :].rearrange("p (a b) -> p a b", a=8),
                                                in1=bc8(s_E[:, blk, 8:16]), op=ALU.mult), r=[b_pob, bs["E"]], w=[bs["yb"]])
                P.dve(lambda h: h.tensor_tensor(out=s_ya[:, :], in0=s_ya[:, :], in1=s_yb[:, :], op=ALU.add), r=[bs["ya"], bs["yb"]], w=[bs["ya"]])
                P.dve(lambda h: h.tensor_tensor(out=s_yb[:, :].rearrange("p (a b) -> p a b", a=8), in0=s_xtok[:, blk, :].rearrange("p (a b) -> p a b", a=8),
                                                in1=bc8(s_sb[:, 32:40]), op=ALU.mult), r=[bs["xtok"], bs["par"]], w=[bs["yb"]])
                P.dve(lambda h: h.tensor_tensor(out=s_ya[:, :], in0=s_ya[:, :], in1=s_yb[:, :], op=ALU.add), r=[bs["ya"], bs["yb"]], w=[bs["ya"]])
                st[blk] = Ms

            def F2(blk):
                Ms = st.pop(blk)
                pyd, b_pyd = bank()
                for half in range(2):
                    for ci in range(4):
                        hh = half * 4 + ci
                        for d in range(2):
                            Mt, bM = Ms[half * 2 + d]
                            P.pe(lambda h, Mt=Mt, ci=ci, hh=hh, d=d: h.matmul(
                                pyd[:, hh * 64:(hh + 1) * 64], lhsT=Mt[:, ci, :], rhs=s_xtok[:, blk, hh * 64:(hh + 1) * 64], start=(d == 0), stop=(d == 1)),
                                r=[bM, bs["xtok"]], **({"w": [b_pyd]} if (half == 0 and ci == 0 and d == 0) else {"pw": [b_pyd]}))
                P.dve(lambda h: h.tensor_tensor(out=s_ya[:, :], in0=s_ya[:, :], in1=pyd[:, :], op=ALU.add), r=[bs["ya"], b_pyd], w=[bs["ya"]])
                P.dve(lambda h: h.tensor_tensor(out=s_ya[:, :], in0=s_ya[:, :], in1=s_gz[:, :], op=ALU.mult), r=[bs["ya"], bs["gz"]], w=[bs["ya"]])
                P.act(lambda h: h.activation(out=s_yb[:, :], in_=s_ya[:, :], func=AF.Square, accum_out=s_ss[:, 0:1]), r=[bs["ya"]], w=[bs["yb"], bs["ss"]])
                P.act(lambda h: h.activation(out=s_ss[:, 1:2], in_=s_ss[:, 0:1], func=AF.Ln, scale=1.0 / 512.0, bias=epsv[:, 0:1]), r=[bs["ss"], b_c2], w=[bs["ss"]])
                P.act(lambda h: h.activation(out=s_ss[:, 2:3], in_=s_ss[:, 1:2], func=AF.Exp, scale=-0.5), r=[bs["ss"]], w=[bs["ss"]])
                yn, byn = s_yn[blk % 2], bs[f"yn{blk % 2}"]
                P.dve(lambda h: h.tensor_scalar(out=yn[:, :], in0=s_ya[:, :], scalar1=s_ss[:, 2:3], scalar2=None, op0=ALU.mult),
                      r=[bs["ya"], bs["ss"]], w=[byn])

            def T_(blk):
                j = (blk - 2) % 4
                yn, byn = s_yn[blk % 2], bs[f"yn{blk % 2}"]
                for c in range(4):
                    ptr, b_ptr = bank()
                    P.pe(lambda h, ptr=ptr, c=c: h.matmul(ptr[:, 0:128], lhsT=yn[:, c * 128:(c + 1) * 128], rhs=ident[:, :], start=True, stop=True), r=[byn, b_const], w=[b_ptr])
                    P.act(lambda h, ptr=ptr, c=c: h.activation(out=s_ynT[:, c, j * 128:(j + 1) * 128], in_=ptr[:, 0:128], func=AF.Copy), r=[b_ptr],
                          **({"w": [bs["ynT"]]} if (c == 0 and j == 0) else {"pw": [bs["ynT"]]}))
                if j != 3:
                    return
                tgi = 1 + (blk - 2) // 4
                q0 = TGS[tgi][0]
                for kd in range(8):
                    py, b_py = bank()
                    for c in range(4):
                        P.pe(lambda h, py=py, c=c, kd=kd: h.matmul(py[:, :], lhsT=s_wso[:, c, kd * 128:(kd + 1) * 128], rhs=s_ynT[:, c, :], start=(c == 0), stop=(c == 3)),
                             r=[bs["wso"], bs["ynT"]], **({"w": [b_py]} if c == 0 else {"pw": [b_py]}))
                    P.dve(lambda h, py=py, kd=kd: h.scalar_tensor_tensor(
                        out=hs[:, kd, q0:q0 + 512], in0=py[:, :], scalar=mod(layer, 2, kd, rr_l), in1=hs[:, kd, q0:q0 + 512], op0=ALU.mult, op1=ALU.add),
                        r=[b_py, b_hs[0][tgi], b_modv[layer]], w=[b_hs[0][tgi]])

            state_update(s_Sf, bs["Sf"], 0, 0)
            state_update(s_Sf, bs["Sf"], 1, 0)
            F1(2)
            F2(2)
            for blk in range(3, 18):
                F1(blk)
                T_(blk - 1)
                F2(blk)
            T_(17)
            P.barrier()
        for tgi in range(1, len(TGS)):
            layer_norm(tgi, layer, 0)
        P.barrier()

    b_out = P.buf("outst")
    for b in range(nseq):
        for k in range(8):
            P.sp(lambda h, k=k, b=b: h.dma_start(out=hs[:, k, 0:CTX], in_=cxT[b, k]), dma_w=b_hs[0][0])
            for tgi in range(1, 5):
                t0, n, _ = TGS[tgi]
                P.sp(lambda h, k=k, b=b, t0=t0, n=n: h.dma_start(out=hs[:, k, t0:t0 + n], in_=xT[b, k, :, t0 - CTX:t0 - CTX + n]),
                     dma_w=b_hs[0][tgi])
        for tgi, (t0, n, _) in enumerate(TGS):
            P.dve(lambda h, t0=t0, n=n: h.tensor_scalar(out=hs[:, :, t0:t0 + n], in0=hs[:, :, t0:t0 + n], scalar1=ALPHA, scalar2=None, op0=ALU.mult),
                  r=[b_hs[0][tgi]], w=[b_hs[0][tgi]])
        for layer in range(nlayers):
            last = layer == DEPTH - 1
            flags = dbg if isinstance(dbg, dict) else {}
            if layer % 2 == 0:
                if flags.get("att", True):
                    attention(layer, b)
            else:
                ssm(layer, b)
            for tgi, (t0, n, isctx) in enumerate(TGS):
                if last and isctx:
                    continue
                if flags.get("mlp", True):
                    mlp(tgi, layer, 2 if isctx else b)
                if flags.get("ln", True):
                    layer_norm(tgi, layer, 2, final=last)
            P.barrier()
        if dbg is not None:
            for k in range(8):
                P.sp(lambda h, k=k, b=b: h.dma_start(out=dbgT[b, k], in_=hs[:, k, :]), r=b_hs[0], dma_r=b_out)
        for k in range(8):
            P.sp(lambda h, k=k, b=b: h.dma_start(out=outT[b, k], in_=hs[:, k, CTX:T]), r=b_hs[0], dma_r=b_out)
        P.barrier()
    counts = P.finish([b_out])
    return nc, counts


def _rope_perm():
    idx = np.arange(64)
    a = idx // 32
    half = (idx % 32) // 16
    j = idx % 16
    return a * 32 + (1 - half) * 16 + j


def host_constants():
    bf = ml_dtypes.bfloat16
    c = {}
    c["ident"] = np.eye(128, dtype=np.float32).astype(bf)
    jj = np.arange(128)[:, None]
    ii = np.arange(128)[None, :]
    mp = np.where(jj >= ii, 0.0, NEG).astype(np.float32)
    mn = np.where(jj <= ii, 0.0, NEG).astype(np.float32)
    c["mprev"] = np.tile(mp, (1, 4)).astype(bf)
    c["mnext"] = np.tile(mn, (1, 4)).astype(bf)
    t = np.arange(SEQ)
    row = (t // 64).astype(np.float32)
    col = (t % 64).astype(np.float32)
    inv = (10000.0 ** (-np.arange(0, 32, 2, dtype=np.float32) / 32)).astype(np.float32)
    cosT = np.zeros((64, SEQ), np.float32)
    sinT = np.zeros((64, SEQ), np.float32)
    for a, pos in enumerate((row, col)):
        ang = (pos[None, :] * inv[:, None]).astype(np.float32)
        for half in range(2):
            sl = slice(a * 32 + half * 16, a * 32 + half * 16 + 16)
            cosT[sl] = np.cos(ang)
            sinT[sl] = np.sin(ang) * (-1.0 if half == 0 else 1.0)
    kk = np.arange(128)[:, None]
    ll = np.arange(128)[None, :]
    c["tri"] = np.concatenate([(kk <= ll), (kk >= ll)], axis=1).astype(np.float32)
    c["cossin"] = np.concatenate([cosT, sinT], axis=0)
    return c


def host_weights(inp):
    w = {}
    f = np.float32
    wm = np.asarray(inp["w_mod"], f)
    w["wmod"] = np.ascontiguousarray(wm.reshape(DEPTH, 8, 128, 12, 512).transpose(0, 3, 2, 1, 4))
    w["bmod"] = np.ascontiguousarray(np.asarray(inp["b_mod"], f).reshape(DEPTH, 48, 128).transpose(0, 2, 1))
    lnv = np.stack([np.asarray(inp[k], f) for k in ("ln_mix_g", "ln_mix_b", "ln_ff_g", "ln_ff_b")], axis=1)
    w["lnv"] = np.ascontiguousarray(lnv.reshape(DEPTH, 4, 8, 128).transpose(3, 0, 1, 2).reshape(128, DEPTH * 4 * 8))
    w["sinkb"] = np.ascontiguousarray(np.broadcast_to(np.asarray(inp["att_sink"], f).reshape(1, 16), (128, 16)))
    win = np.asarray(inp["att_w_in"], f)[0]
    perm = _rope_perm()
    wq = win[:, :1024].reshape(8, 128, 4, 4, 64)
    wqq = np.concatenate([wq, wq[..., perm]], axis=-1)
    w["wqq"] = np.ascontiguousarray(wqq.transpose(2, 1, 0, 3, 4).reshape(4, 128, 8, 512))
    wk = win[:, 1024:1280].reshape(8, 128, 4, 64)
    wv = win[:, 1280:1536].reshape(8, 128, 4, 64)
    wkv = np.concatenate([wk, wk[..., perm], wv], axis=-1)
    w["wkv"] = np.ascontiguousarray(wkv.transpose(2, 1, 0, 3))
    wo = np.asarray(inp["att_w_out"], f)[0].reshape(4, 2, 2, 64, 1024)
    w["wo"] = np.ascontiguousarray(wo.transpose(0, 2, 3, 1, 4).reshape(4, 128, 2, 1024))
    sw = np.asarray(inp["ssm_w_in"], f)[0]
    wx = sw[:, 2048:4096].reshape(8, 128, 4, 512)
    wB = sw[:, 4096:4608].reshape(8, 128, 4, 128)
    wC = sw[:, 4608:5120].reshape(8, 128, 4, 128)
    w["wxbc"] = np.ascontiguousarray(np.concatenate([wx, wB, wC], axis=-1).transpose(2, 1, 0, 3))
    w["wz"] = np.ascontiguousarray(sw[:, 0:2048].reshape(8, 128, 4, 512).transpose(2, 1, 0, 3))
    wd = sw[:, 5120:5184].reshape(8, 128, 2, 4, 8)
    w["wdt"] = np.ascontiguousarray(wd.transpose(3, 1, 0, 2, 4).reshape(4, 128, 8, 16))
    w["wso"] = np.ascontiguousarray(np.asarray(inp["ssm_w_out"], f)[0].reshape(4, 4, 128, 1024).transpose(0, 2, 1, 3))
    cw = np.asarray(inp["ssm_conv_w"], f)[0]
    cb = np.asarray(inp["ssm_conv_b"], f)[0]
    convw = np.zeros((4, 128, 6, 6), f)
    for g in range(4):
        chans = [np.arange(g * 512 + c * 128, g * 512 + (c + 1) * 128) for c in range(4)]
        chans.append(np.arange(2048 + g * 128, 2048 + (g + 1) * 128))
        chans.append(np.arange(2560 + g * 128, 2560 + (g + 1) * 128))
        for c, ch in enumerate(chans):
            convw[g, :, c, 0:5] = cw[:, ch].T
            convw[g, :, c, 5] = cb[ch]
    w["convw"] = convw.reshape(4, 128, 36)
    dtb = np.asarray(inp["ssm_dt_bias"], f)[0].reshape(2, 4, 8)
    alog = np.asarray(inp["ssm_a_log"], f)[0].reshape(2, 4, 8)
    dsk = np.asarray(inp["ssm_d"], f)[0].reshape(4, 8)
    ssmb = np.zeros((4, 128, 40), f)
    for g in range(4):
        ssmb[g, :, 0:16] = dtb[:, g, :].reshape(1, 16)
        ssmb[g, :, 16:32] = alog[:, g, :].reshape(1, 16)
        ssmb[g, :, 32:40] = dsk[g].reshape(1, 8)
    w["ssmb"] = ssmb
    ng = np.asarray(inp["ssm_norm_g"], f)[0].reshape(4, 4, 128)
    w["normg"] = np.ascontiguousarray(ng.transpose(0, 2, 1))
    w1 = np.asarray(inp["ff_w1"], f).reshape(DEPTH, 8, 128, 8, 512)
    w["w1"] = np.ascontiguousarray(w1.transpose(0, 3, 2, 1, 4))
    w2 = np.asarray(inp["ff_w2"], f).reshape(DEPTH, 32, 128, 8, 128)
    w["w2"] = np.ascontiguousarray(w2.transpose(0, 3, 2, 1, 4))
    return w


def host_core_inputs(inp, core):
    f = np.float32
    b0 = core * BLOC
    x = np.asarray(inp["x"], f)[b0:b0 + BLOC]
    ctx = np.asarray(inp["ctx"], f)[b0:b0 + BLOC]
    d = {}
    d["xT"] = np.ascontiguousarray(x.transpose(0, 2, 1).reshape(BLOC, 8, 128, SEQ))
    d["cxT"] = np.ascontiguousarray(ctx.transpose(0, 2, 1).reshape(BLOC, 8, 128, CTX))
    cc = np.concatenate([np.asarray(inp["c"], f)[b0:b0 + BLOC], np.asarray(inp["c_ctx"], f)[None]], axis=0)
    d["cT"] = np.ascontiguousarray(cc.reshape(3, 8, 128).transpose(2, 1, 0))
    return d


_CACHE = {}


def kernel(**inputs):
    if "nc" not in _CACHE:
        _CACHE["nc"] = build_program()[0]
    nc = _CACHE["nc"]
    shared = {}
    shared.update(host_constants())
    shared.update(host_weights(inputs))
    in_maps = []
    for core in range(NCORE):
        m = dict(shared)
        m.update(host_core_inputs(inputs, core))
        in_maps.append(m)
    res = run_bass_kernel_spmd(nc, in_maps, core_ids=list(range(NCORE)))
    outs = []
    for core in range(NCORE):
        oT = np.asarray(res.results[core]["outT"]).reshape(BLOC, D, SEQ)
        outs.append(oT.transpose(0, 2, 1))
    return np.ascontiguousarray(np.concatenate(outs, axis=0)).astype(np.float32)
```

```python
from contextlib import ExitStack
import numpy as np
import ml_dtypes
import concourse.bass as bass
import concourse.mybir as mybir
from concourse.bass_utils import run_bass_kernel_spmd

F32 = mybir.dt.float32
BF16 = mybir.dt.bfloat16
AF = mybir.ActivationFunctionType
ALU = mybir.AluOpType
AX = mybir.AxisListType

ENGS = ("pe", "act", "dve", "pool", "sp")

D = 1024
SEQ = 2048
CTX = 256
T = SEQ + CTX
NCORE = 8
BLOC = 2
DEPTH = 2
ALPHA = (2.0 * DEPTH) ** 0.25
LN_EPS = 1e-5
RMS_EPS = 1e-5
NEG = -30000.0
TGS = [(0, 256, True), (256, 512, False), (768, 512, False), (1280, 512, False), (1792, 512, False)]


class Buf:
    __slots__ = ("name", "writers", "readers", "prev_readers", "sem", "dcount", "excl")

    def __init__(self, name, excl=False):
        self.name = name
        self.excl = excl
        self.writers = {}
        self.readers = {}
        self.prev_readers = {}
        self.sem = None
        self.dcount = 0


class Op:
    __slots__ = ("emit", "deps", "signal", "dma", "isnop")

    def __init__(self, emit, deps, dma, isnop=False):
        self.emit = emit
        self.deps = deps
        self.signal = False
        self.dma = dma
        self.isnop = isnop


class Prog:
    def __init__(self, nc):
        self.nc = nc
        self.ops = {e: [] for e in ENGS}
        self.seen = {e: {} for e in ENGS}
        self.dma_bufs = []
        self.nbuf = 0
        self.ntens = 0

    def sb(self, name, shape, dt, off):
        self.ntens += 1
        return self.nc.alloc_sbuf_tensor_at(f"{name}_{self.ntens}", list(shape), dt, offset=off + 16512)

    def buf(self, name=None):
        self.nbuf += 1
        return Buf(name or f"b{self.nbuf}")

    def add(self, eng, emit, r=(), w=(), pw=(), dma_w=None, dma_r=None):
        ops = self.ops[eng]
        idx = len(ops)
        deps = {}

        def need(k, v):
            if deps.get(k, -1) < v:
                deps[k] = v

        mykey = ("E", eng)
        allr = list(r) + ([dma_r] if dma_r is not None else [])
        allw = list(w) + ([dma_w] if dma_w is not None else [])
        for b in allr:
            for k, v in b.writers.items():
                need(k, v)
            if b.excl:
                for k, v in b.readers.items():
                    if k != mykey:
                        need(k, v)
        for b in allw:
            for k, v in b.readers.items():
                need(k, v)
            for k, v in b.writers.items():
                if k == mykey and eng == "pe":
                    continue
                if dma_w is not None and k == ("D", dma_w):
                    continue
                need(k, v)
            if not b.readers:
                for k, v in b.prev_readers.items():
                    need(k, v)
        for b in pw:
            for k, v in b.readers.items():
                need(k, v)
            for k, v in b.prev_readers.items():
                need(k, v)
            for k, v in b.writers.items():
                if k[0] == "D":
                    need(k, v)
        seen = self.seen[eng]
        fdeps = {}
        for k, v in deps.items():
            if seen.get(k, -1) >= v:
                continue
            seen[k] = v
            fdeps[k] = v
            if k[0] == "E":
                self.ops[k[1]][v].signal = True
        is_dma = (dma_w is not None) or (dma_r is not None)
        dbuf = dma_w if dma_w is not None else dma_r
        op = Op(emit, fdeps, dbuf if is_dma else None)
        ops.append(op)
        if is_dma:
            if dbuf.sem is None:
                dbuf.sem = True
                self.dma_bufs.append(dbuf)
            dbuf.dcount += 16
            ev = (("D", dbuf), dbuf.dcount)
        else:
            ev = (mykey, idx)
        for b in allr:
            if b.readers.get(ev[0], -1) < ev[1]:
                b.readers[ev[0]] = ev[1]
        for b in allw:
            if b.readers:
                b.prev_readers = b.readers
            b.readers = {}
            b.writers = {ev[0]: ev[1]}
        for b in pw:
            if b.readers:
                b.prev_readers = b.readers
                b.readers = {}
                b.writers = {}
            if b.writers.get(ev[0], -1) < ev[1]:
                b.writers[ev[0]] = ev[1]
        return op

    def pe(self, emit, **kw): return self.add("pe", emit, **kw)
    def act(self, emit, **kw): return self.add("act", emit, **kw)
    def dve(self, emit, **kw): return self.add("dve", emit, **kw)
    def pool(self, emit, **kw): return self.add("pool", emit, **kw)
    def sp(self, emit, **kw): return self.add("sp", emit, **kw)

    def barrier(self):
        last = {}
        for e in ENGS:
            ops = self.ops[e]
            for i in range(len(ops) - 1, -1, -1):
                if ops[i].dma is None and not ops[i].isnop:
                    last[("E", e)] = i
                    break
        for b in self.dma_bufs:
            last[("D", b)] = b.dcount
        for e in ENGS:
            seen = self.seen[e]
            fdeps = {}
            for k, v in last.items():
                if k == ("E", e):
                    continue
                if seen.get(k, -1) >= v:
                    continue
                seen[k] = v
                fdeps[k] = v
                if k[0] == "E":
                    self.ops[k[1]][v].signal = True
            if fdeps:
                self.ops[e].append(Op(lambda h: h.nop(), fdeps, None, True))

    def finish(self, final_bufs):
        nc = self.nc
        with ExitStack() as es:
            sems = {e: es.enter_context(nc.semaphore(f"s_{e}")) for e in ENGS}
            for i, b in enumerate(self.dma_bufs):
                b.sem = es.enter_context(nc.semaphore(f"d{i}"))
            sigcnt = {}
            for e in ENGS:
                c = 0
                arr = []
                for op in self.ops[e]:
                    if op.signal:
                        c += 1
                    arr.append(c)
                sigcnt[e] = arr
            fin = [(b.sem, b.dcount) for b in final_bufs]

            def run(e, h):
                for op in self.ops[e]:
                    for k, v in op.deps.items():
                        if k[0] == "E":
                            h.wait_ge(sems[k[1]], sigcnt[k[1]][v])
                        else:
                            h.wait_ge(k[1].sem, v)
                    ins = op.emit(h)
                    if op.dma is not None:
                        ins.then_inc(op.dma.sem, 16)
                    elif op.signal:
                        ins.then_inc(sems[e], 1)
                if e == "sp":
                    for s, v in fin:
                        h.wait_ge(s, v)

            with nc.Block() as block:
                @block.tensor
                def _(h): run("pe", h)

                @block.scalar
                def _(h): run("act", h)

                @block.vector
                def _(h): run("dve", h)

                @block.gpsimd
                def _(h): run("pool", h)

                @block.sync
                def _(h): run("sp", h)
        return {e: len(self.ops[e]) for e in ENGS}


class K:
    pass


def build_program(nseq=BLOC, nlayers=DEPTH, dbg=None):
    nc = bass.Bass("TRN2", target_bir_lowering=False)
    P = Prog(nc)

    def din(name, shape, dt=F32):
        return nc.dram_tensor(name, list(shape), dt, kind="ExternalInput").ap()

    xT = din("xT", [BLOC, 8, 128, SEQ])
    cxT = din("cxT", [BLOC, 8, 128, CTX])
    cT = din("cT", [128, 8, 3])
    wmod = din("wmod", [DEPTH, 12, 128, 8, 512])
    bmod = din("bmod", [DEPTH, 128, 48])
    lnv_d = din("lnv", [128, DEPTH * 4 * 8])
    sink_d = din("sinkb", [128, 16])
    ident_d = din("ident", [128, 128], BF16)
    mprev_d = din("mprev", [128, 512], BF16)
    mnext_d = din("mnext", [128, 512], BF16)
    cs_d = din("cossin", [128, SEQ])
    wqq_d = din("wqq", [4, 128, 8, 512])
    wkv_d = din("wkv", [4, 128, 8, 192])
    wo_d = din("wo", [4, 128, 2, 1024])
    wxbc_d = din("wxbc", [4, 128, 8, 768])
    wz_d = din("wz", [4, 128, 8, 512])
    wdt_d = din("wdt", [4, 128, 8, 16])
    wso_d = din("wso", [4, 128, 4, 1024])
    cw_d = din("convw", [4, 128, 36])
    sb_d = din("ssmb", [4, 128, 40])
    ng_d = din("normg", [4, 128, 4])
    tri_d = din("tri", [128, 256])
    w1_d = din("w1", [DEPTH, 8, 128, 8, 512])
    w2_d = din("w2", [DEPTH, 8, 128, 32, 128])
    outT = nc.dram_tensor("outT", [BLOC, 8, 128, SEQ], F32, kind="ExternalOutput").ap()
    if dbg is not None:
        dbgT = nc.dram_tensor("dbgT", [BLOC, 8, 128, T], F32, kind="ExternalOutput").ap()

    HS_OFF = 0
    CONST_OFF = 73728
    ARENA = 78848
    hs = P.sb("hs", [128, 8, T], F32, HS_OFF)
    b_hs = [[P.buf(f"hs{g}") for g in range(len(TGS))]]

    co = [CONST_OFF]

    def calloc(name, shape, dt):
        n = int(np.prod(shape[1:])) * (4 if dt == F32 else 2)
        t = P.sb(name, shape, dt, co[0])
        co[0] += (n + 31) // 32 * 32
        assert co[0] <= ARENA
        return t

    ident = calloc("ident", [128, 128], BF16)
    ones_f = calloc("ones_f", [128, 128], F32)
    ones_bf = calloc("ones_bf", [128, 128], BF16)
    mprev = calloc("mprev", [128, 512], BF16)
    mnext = calloc("mnext", [128, 512], BF16)
    modv = [calloc(f"modv{i}", [128, 48, 3], F32) for i in range(DEPTH)]
    lnv = calloc("lnv", [128, DEPTH * 4 * 8], F32)
    lnva = calloc("lnva", [128, DEPTH * 4 * 8], F32)
    sinkv = calloc("sinkv", [128, 16], F32)
    epsv = calloc("epsv", [128, 2], F32)
    kmaxp = calloc("kmaxp", [128, 8], F32)
    kmax2 = calloc("kmax2", [128, 1], F32)
    b_const = P.buf("const")
    b_modv = [P.buf(f"modv{i}") for i in range(DEPTH)]
    b_kmax = P.buf("kmax")

    pbank = [nc.alloc_psum_tensor(f"pb{i}", [128, 512], F32) for i in range(8)]
    b_bank = [Buf(f"pb{i}", excl=True) for i in range(8)]
    bank_rr = [0]

    def bank():
        i = bank_rr[0]
        bank_rr[0] = (i + 1) % 8
        return pbank[i], b_bank[i]

    P.sp(lambda h: h.dma_start(out=ident[:, :], in_=ident_d[:, :]), dma_w=b_const)
    P.sp(lambda h: h.dma_start(out=mprev[:, :], in_=mprev_d[:, :]), dma_w=b_const)
    P.sp(lambda h: h.dma_start(out=mnext[:, :], in_=mnext_d[:, :]), dma_w=b_const)
    P.sp(lambda h: h.dma_start(out=lnv[:, :], in_=lnv_d[:, :]), dma_w=b_const)
    P.sp(lambda h: h.dma_start(out=sinkv[:, :], in_=sink_d[:, :]), dma_w=b_const)
    b_c2 = P.buf("const2")
    P.pool(lambda h: h.memset(ones_f[:, :], 1.0 / D), pw=[b_c2])
    P.pool(lambda h: h.memset(ones_bf[:, :], 1.0), pw=[b_c2])
    P.pool(lambda h: h.memset(epsv[:, 0:1], LN_EPS), pw=[b_c2])
    P.pool(lambda h: h.memset(epsv[:, 1:2], 1e-20), pw=[b_c2])
    P.dve(lambda h: h.tensor_scalar(out=lnva[:, :], in0=lnv[:, :], scalar1=ALPHA, scalar2=None, op0=ALU.mult),
          r=[b_const], w=[b_c2])

    def lnvec(layer, which, k, alpha):
        t = lnva if alpha else lnv
        c = (layer * 4 + which) * 8 + k
        return t[:, c:c + 1]

    a_cT = P.sb("cTs", [128, 8, 3], F32, ARENA)
    a_e = P.sb("cTe", [128, 8, 3], F32, ARENA + 128)
    a_sc = P.sb("scT", [128, 8, 3], F32, ARENA + 256)
    a_bm = P.sb("bm", [128, 48], F32, ARENA + 384)
    a_w = [P.sb(f"wm{i}", [128, 8, 512], F32, ARENA + 1024 + i * 16384) for i in range(2)]
    b_cT = P.buf("cT")
    b_sc = P.buf("scT")
    b_bm = P.buf("bm")
    b_w = [P.buf("wm0"), P.buf("wm1")]
    P.sp(lambda h: h.dma_start(out=a_cT[:, :, :], in_=cT[:, :, :]), dma_w=b_cT)
    P.act(lambda h: h.activation(out=a_e[:, :, :], in_=a_cT[:, :, :], func=AF.Exp, scale=-1.0), r=[b_cT], w=[b_sc])
    P.dve(lambda h: h.tensor_scalar(out=a_e[:, :, :], in0=a_e[:, :, :], scalar1=1.0, scalar2=None, op0=ALU.add), r=[b_sc], w=[b_sc])
    P.dve(lambda h: h.reciprocal(out=a_e[:, :, :], in_=a_e[:, :, :]), r=[b_sc], w=[b_sc])
    P.dve(lambda h: h.tensor_tensor(out=a_sc[:, :, :], in0=a_e[:, :, :], in1=a_cT[:, :, :], op=ALU.mult), r=[b_sc, b_cT], w=[b_sc])
    wi = 0
    for i in range(nlayers):
        pm, b_pm = bank()
        P.sp(lambda h, i=i: h.dma_start(out=a_bm[:, :], in_=bmod[i]), dma_w=b_bm)
        for ft in range(12):
            wt, bw = a_w[wi % 2], b_w[wi % 2]
            wi += 1
            P.sp(lambda h, wt=wt, i=i, ft=ft: h.dma_start(out=wt[:, :, :], in_=wmod[i, ft]), dma_w=bw)
            for fc in range(4):
                c = ft * 4 + fc
                for k in range(8):
                    P.pe(lambda h, pm=pm, wt=wt, c=c, fc=fc, k=k: h.matmul(
                        pm[:, c * 3:c * 3 + 3], lhsT=wt[:, k, fc * 128:(fc + 1) * 128], rhs=a_sc[:, k, :],
                        start=(k == 0), stop=(k == 7)),
                        r=[bw, b_sc], **({"w": [b_pm]} if (c == 0 and k == 0) else {"pw": [b_pm]}))
        mv = modv[i]
        P.dve(lambda h, mv=mv, pm=pm: h.tensor_tensor(
            out=mv[:, :, :], in0=pm[:, 0:144].rearrange("p (c r) -> p c r", r=3),
            in1=a_bm[:, :].unsqueeze(2).to_broadcast([128, 48, 3]), op=ALU.add),
            r=[b_pm, b_bm], w=[b_modv[i]])
        for which in (1, 4):
            P.dve(lambda h, mv=mv, which=which: h.tensor_scalar(
                out=mv[:, which * 8:(which + 1) * 8, :], in0=mv[:, which * 8:(which + 1) * 8, :],
                scalar1=1.0, scalar2=1.0 / ALPHA, op0=ALU.add, op1=ALU.mult),
                r=[b_modv[i]], w=[b_modv[i]])

    def mod(layer, which, k, rr):
        return modv[layer][:, which * 8 + k, rr:rr + 1]

    P.barrier()

    LN_OFF = ARENA + 110080

    def layer_norm(tgi, layer, which_g, final=False):
        t0, n, _ = TGS[tgi]
        sq = [P.sb("lnsq", [128, 512], F32, LN_OFF + i * 2048) for i in range(2)]
        b_sq = [K.b_lnsq0, K.b_lnsq1]
        st = [P.sb("lnst", [128, 512], F32, LN_OFF + 4096 + i * 2048) for i in range(4)]
        b_st = K.b_lnst
        bh = b_hs[0][tgi]
        p1, b_p1 = bank()
        p2, b_p2 = bank()
        for k in range(8):
            s, bs = sq[k % 2], b_sq[k % 2]
            P.act(lambda h, s=s, k=k: h.activation(out=s[:, 0:n], in_=hs[:, k, t0:t0 + n], func=AF.Square),
                  r=[bh], w=[bs])
            P.pe(lambda h, k=k: h.matmul(p1[:, 0:n], lhsT=ones_f[:, :], rhs=hs[:, k, t0:t0 + n],
                                         start=(k == 0), stop=(k == 7)),
                 r=[bh, b_c2], **({"w": [b_p1]} if k == 0 else {"pw": [b_p1]}))
            P.pe(lambda h, s=s, k=k: h.matmul(p2[:, 0:n], lhsT=ones_f[:, :], rhs=s[:, 0:n],
                                              start=(k == 0), stop=(k == 7)),
                 r=[bs, b_c2], **({"w": [b_p2]} if k == 0 else {"pw": [b_p2]}))
        mean, var, rstd, nmr = st
        P.act(lambda h: h.activation(out=mean[:, 0:n], in_=p1[:, 0:n], func=AF.Copy), r=[b_p1], w=[b_st])
        P.dve(lambda h: h.tensor_tensor(out=var[:, 0:n], in0=mean[:, 0:n], in1=mean[:, 0:n], op=ALU.mult), r=[b_st], w=[b_st])
        P.dve(lambda h: h.tensor_tensor(out=var[:, 0:n], in0=p2[:, 0:n], in1=var[:, 0:n], op=ALU.subtract), r=[b_st, b_p2], w=[b_st])
        P.act(lambda h: h.activation(out=rstd[:, 0:n], in_=var[:, 0:n], func=AF.Ln, bias=epsv[:, 0:1], scale=1.0), r=[b_st, b_c2], w=[b_st])
        P.act(lambda h: h.activation(out=rstd[:, 0:n], in_=rstd[:, 0:n], func=AF.Exp, scale=-0.5), r=[b_st], w=[b_st])
        P.dve(lambda h: h.scalar_tensor_tensor(out=nmr[:, 0:n], in0=mean[:, 0:n], scalar=-1.0, in1=rstd[:, 0:n],
                                               op0=ALU.mult, op1=ALU.mult), r=[b_st], w=[b_st])
        for k in range(8):
            P.dve(lambda h, k=k: h.tensor_tensor(out=hs[:, k, t0:t0 + n], in0=hs[:, k, t0:t0 + n], in1=rstd[:, 0:n], op=ALU.mult),
                  r=[bh, b_st], w=[bh])
            P.dve(lambda h, k=k: h.tensor_tensor(out=hs[:, k, t0:t0 + n], in0=hs[:, k, t0:t0 + n], in1=nmr[:, 0:n], op=ALU.add),
                  r=[bh, b_st], w=[bh])
            P.act(lambda h, k=k: h.activation(out=hs[:, k, t0:t0 + n], in_=hs[:, k, t0:t0 + n], func=AF.Identity,
                                              scale=lnvec(layer, which_g, k, not final), bias=lnvec(layer, which_g + 1, k, not final)),
                  r=[bh, b_c2], w=[bh])

    K.att_stage = dbg.get("stage", 4) if isinstance(dbg, dict) else 4
    K.b_lnsq0 = P.buf("lnsq0")
    K.b_lnsq1 = P.buf("lnsq1")
    K.b_lnst = P.buf("lnst")

    M_UT = ARENA
    M_HID = ARENA + 16384
    M_W1 = M_HID + 32768
    M_W2 = M_W1 + 16384
    M_RT = M_W2 + 16384
    m_uT = [P.sb("m_uT", [128, 8, 512], BF16, M_UT + i * 8192) for i in range(2)]
    m_hid = P.sb("m_hid", [128, 32, 512], BF16, M_HID)
    m_w1 = [P.sb("m_w1", [128, 8, 512], BF16, M_W1 + i * 8192) for i in range(2)]
    m_w2 = [P.sb("m_w2", [128, 32, 128], BF16, M_W2 + i * 8192) for i in range(2)]
    m_rt = [P.sb("m_rt", [128, 512], F32, M_RT + i * 2048) for i in range(2)]
    assert M_RT + 4096 <= LN_OFF
    bm_uT = [P.buf("m_uT0"), P.buf("m_uT1")]
    bm_hid = P.buf("m_hid")
    bm_w1 = [P.buf("m_w10"), P.buf("m_w11")]
    bm_w2 = [P.buf("m_w20"), P.buf("m_w21")]
    bm_rt = [P.buf("m_rt0"), P.buf("m_rt1")]
    cnt = {"ut": 0, "w1": 0, "w2": 0, "rt": 0}

    def mlp(tgi, layer, rr):
        t0, n, _ = TGS[tgi]
        bh = b_hs[0][tgi]
        ui = cnt["ut"] % 2
        cnt["ut"] += 1
        uT, buT = m_uT[ui], bm_uT[ui]
        for k in range(8):
            P.act(lambda h, k=k: h.activation(out=uT[:, k, 0:n], in_=hs[:, k, t0:t0 + n], func=AF.Identity,
                                              scale=mod(layer, 4, k, rr), bias=mod(layer, 3, k, rr)),
                  r=[bh, b_modv[layer]], **({"w": [buT]} if k == 0 else {"pw": [buT]}))
        for fb in range(8):
            wi_ = cnt["w1"] % 2
            cnt["w1"] += 1
            w1t, bw1 = m_w1[wi_], bm_w1[wi_]
            P.pool(lambda h, w1t=w1t, fb=fb: h.dma_start(out=w1t[:, :, :], in_=w1_d[layer, fb]), dma_w=bw1)
            for fc in range(4):
                pb, b_pb = bank()
                for k in range(8):
                    P.pe(lambda h, pb=pb, w1t=w1t, fc=fc, k=k: h.matmul(
                        pb[:, 0:n], lhsT=w1t[:, k, fc * 128:(fc + 1) * 128], rhs=uT[:, k, 0:n],
                        start=(k == 0), stop=(k == 7)),
                        r=[bw1, buT], **({"w": [b_pb]} if k == 0 else {"pw": [b_pb]}))
                ri = cnt["rt"] % 2
                cnt["rt"] += 1
                rt, brt = m_rt[ri], bm_rt[ri]
                P.dve(lambda h, pb=pb, rt=rt: h.tensor_scalar(out=rt[:, 0:n], in0=pb[:, 0:n], scalar1=0.0, scalar2=None, op0=ALU.max),
                      r=[b_pb], w=[brt])
                f = fb * 4 + fc
                P.act(lambda h, rt=rt, f=f: h.activation(out=m_hid[:, f, 0:n], in_=rt[:, 0:n], func=AF.Square),
                      r=[brt], **({"w": [bm_hid]} if f == 0 else {"pw": [bm_hid]}))
        for dc in range(8):
            wi_ = cnt["w2"] % 2
            cnt["w2"] += 1
            w2t, bw2 = m_w2[wi_], bm_w2[wi_]
            P.pool(lambda h, w2t=w2t, dc=dc: h.dma_start(out=w2t[:, :, :], in_=w2_d[layer, dc]), dma_w=bw2)
            pb, b_pb = bank()
            for kf in range(32):
                P.pe(lambda h, pb=pb, w2t=w2t, kf=kf: h.matmul(
                    pb[:, 0:n], lhsT=w2t[:, kf, :], rhs=m_hid[:, kf, 0:n], start=(kf == 0), stop=(kf == 31)),
                    r=[bw2, bm_hid], **({"w": [b_pb]} if kf == 0 else {"pw": [b_pb]}))
            P.dve(lambda h, pb=pb, dc=dc: h.scalar_tensor_tensor(
                out=hs[:, dc, t0:t0 + n], in0=pb[:, 0:n], scalar=mod(layer, 5, dc, rr), in1=hs[:, dc, t0:t0 + n],
                op0=ALU.mult, op1=ALU.add),
                r=[b_pb, bh, b_modv[layer]], w=[bh])

    A_UT = ARENA
    A_COS = A_UT + 36864
    A_W = A_COS + 16384
    A_KT = A_W + 19456
    A_V = A_KT + 4608
    A_Q = A_V + 4608
    A_OT = A_Q + 8192
    A_PT = A_OT + 4096
    A_TMP = A_PT + 3072
    A_MN = A_TMP + 6144
    A_SK = A_MN + 4096
    A_RD = A_SK + 8192
    A_SQ = A_RD + 2048
    A_OUN = A_SQ + 1024
    A_END = A_OUN + 2048
    assert A_END <= 212832, A_END
    uTa = P.sb("uTa", [128, 8, T], BF16, A_UT)
    cstab = P.sb("cstab", [128, SEQ], F32, A_COS)
    wqq = P.sb("wqq", [128, 8, 512], BF16, A_W)
    wkv = P.sb("wkv", [128, 8, 192], BF16, A_W + 8192)
    wo = P.sb("wo", [128, 2, 1024], BF16, A_W + 11264)
    kTa = P.sb("kTa", [65, T], BF16, A_KT)
    Va = P.sb("Va", [128, 18, 128], BF16, A_V)
    qa = P.sb("qa", [65, 4, 512], BF16, A_Q)
    qpa = P.sb("qpa", [65, 4, 512], BF16, A_Q + 4096)
    OT = P.sb("OT", [128, 2, 512], BF16, A_OT)
    PT = [P.sb("PT", [128, 512], BF16, A_PT + i * 1024) for i in range(3)]
    tmp = [P.sb("atmp", [128, 512], F32, A_TMP + i * 2048) for i in range(3)]
    mneg = P.sb("mneg", [128, 4, 512], BF16, A_MN)
    skt = P.sb("skt", [128, 4, 512], F32, A_SK)
    rden = P.sb("rden", [64, 512], F32, A_RD)
    sqt = P.sb("sqt", [64, 512], BF16, A_SQ)
    oun = P.sb("oun", [64, 512], F32, A_OUN)
    b_oun = P.buf("oun")
    qbc = [0]
    psc = [0]
    pending_norm = []
    b_uTa = [P.buf(f"uTa{g}") for g in range(len(TGS))]
    b_rope = P.buf("rope")
    b_wq, b_wkv, b_wo = P.buf("wq"), P.buf("wkv"), P.buf("wo")
    b_kT, b_V = P.buf("kT"), P.buf("V")
    b_qa, b_qpa, b_OT = P.buf("qa"), P.buf("qpa"), P.buf("OT")
    b_PT = [P.buf(f"PT{i}") for i in range(3)]
    b_tmp = [P.buf(f"atmp{i}") for i in range(3)]
    b_mneg, b_skt, b_rden, b_sqt = P.buf("mneg"), P.buf("skt"), P.buf("rden"), P.buf("sqt")
    ptc = [0]

    def rope(dst, bdst, pa, b_pa, lt0, n, pwflag):
        t1, t2 = tmp[0], tmp[1]
        P.dve(lambda h: h.tensor_tensor(out=t1[0:64, 0:n], in0=pa[0:64, 0:n], in1=cstab[0:64, lt0:lt0 + n], op=ALU.mult),
              r=[b_pa, b_rope], w=[b_tmp[0]])
        P.dve(lambda h: h.tensor_tensor(out=t2[0:64, 0:n], in0=pa[64:128, 0:n], in1=cstab[64:128, lt0:lt0 + n], op=ALU.mult),
              r=[b_pa, b_rope], w=[b_tmp[1]])
        P.dve(lambda h: h.tensor_tensor(out=dst, in0=t1[0:64, 0:n], in1=t2[0:64, 0:n], op=ALU.add),
              r=[b_tmp[0], b_tmp[1]], **({"pw": [bdst]} if pwflag else {"w": [bdst]}))

    def attention(layer, b):
        rr_l = b
        for tgi, (t0, n, isctx) in enumerate(TGS):
            rr = 2 if isctx else rr_l
            for k in range(8):
                P.act(lambda h, k=k, t0=t0, n=n, rr=rr: h.activation(
                    out=uTa[:, k, t0:t0 + n], in_=hs[:, k, t0:t0 + n], func=AF.Identity,
                    scale=mod(layer, 1, k, rr), bias=mod(layer, 0, k, rr)),
                    r=[b_hs[0][tgi], b_modv[layer]], **({"w": [b_uTa[tgi]]} if k == 0 else {"pw": [b_uTa[tgi]]}))
        P.sp(lambda h: h.dma_start(out=cstab[:, :], in_=cs_d[:, :]), dma_w=b_rope)
        for g in range(4):
            P.pool(lambda h, g=g: h.dma_start(out=wqq[:, :, :], in_=wqq_d[g]), dma_w=b_wq)
            P.pool(lambda h, g=g: h.dma_start(out=wkv[:, :, :], in_=wkv_d[g]), dma_w=b_wkv)
            P.pool(lambda h, g=g: h.dma_start(out=wo[:, :, :], in_=wo_d[g]), dma_w=b_wo)
            P.pool(lambda h: h.memset(kTa[64:65, :], 1.0), w=[b_kT])
            P.pool(lambda h: h.memset(Va[:, :, 64:128], 1.0), w=[b_V])
            for tgi, (t0, n, isctx) in enumerate(TGS):
                pk, b_pk = bank()
                for k in range(8):
                    P.pe(lambda h, pk=pk, k=k, t0=t0, n=n: h.matmul(
                        pk[:, 0:n], lhsT=wkv[:, k, 0:128], rhs=uTa[:, k, t0:t0 + n], start=(k == 0), stop=(k == 7)),
                        r=[b_wkv, b_uTa[tgi]], **({"w": [b_pk]} if k == 0 else {"pw": [b_pk]}))
                if isctx:
                    P.act(lambda h, pk=pk, t0=t0, n=n: h.activation(out=kTa[0:64, t0:t0 + n], in_=pk[0:64, 0:n], func=AF.Copy),
                          r=[b_pk], pw=[b_kT])
                else:
                    rope(kTa[0:64, t0:t0 + n], b_kT, pk, b_pk, t0 - CTX, n, True)
                P.act(lambda h, t0=t0, n=n: h.activation(out=sqt[:, 0:n], in_=kTa[0:64, t0:t0 + n], func=AF.Square),
                      r=[b_kT], w=[b_sqt])
                pn, b_pn = bank()
                P.pe(lambda h, pn=pn, n=n: h.matmul(pn[:, 0:n], lhsT=ones_bf[0:64, :], rhs=sqt[:, 0:n], start=True, stop=True),
                     r=[b_sqt, b_c2], w=[b_pn])
                P.dve(lambda h, pn=pn, n=n, tgi=tgi: h.tensor_reduce(out=kmaxp[:, tgi:tgi + 1], in_=pn[:, 0:n], axis=AX.X, op=ALU.max),
                      r=[b_pn], **({"w": [b_kmax]} if tgi == 0 else {"pw": [b_kmax]}))
                for bi in range(n // 128):
                    blk = t0 // 128 + bi
                    pv, b_pv = bank()
                    for k in range(8):
                        P.pe(lambda h, pv=pv, k=k, blk=blk: h.matmul(
                            pv[:, 0:64], lhsT=uTa[:, k, blk * 128:(blk + 1) * 128], rhs=wkv[:, k, 128:192],
                            start=(k == 0), stop=(k == 7)),
                            r=[b_wkv, b_uTa[tgi]], **({"w": [b_pv]} if k == 0 else {"pw": [b_pv]}))
                    P.act(lambda h, pv=pv, blk=blk: h.activation(out=Va[:, blk, 0:64], in_=pv[:, 0:64], func=AF.Copy),
                          r=[b_pv], pw=[b_V])
            P.dve(lambda h: h.tensor_reduce(out=kmax2[:, :], in_=kmaxp[:, 0:5], axis=AX.X, op=ALU.max), r=[b_kmax], w=[b_kmax])
            P.dve(lambda h: h.tensor_scalar(out=kmax2[:, :], in0=kmax2[:, :], scalar1=1.05, scalar2=None, op0=ALU.mult), r=[b_kmax], w=[b_kmax])
            for tgi, (t0, n, isctx) in enumerate(TGS):
                if K.att_stage < 2:
                    break
                pns = {}

                def q_chain(r):
                    pn, b_pn = pns[r]
                    t3 = tmp[2]
                    P.act(lambda h, pn=pn, n=n: h.activation(out=t3[:, 0:n], in_=pn[:, 0:n], func=AF.Ln, scale=kmax2[:, 0:1], bias=epsv[:, 1:2]),
                          r=[b_pn, b_kmax, b_c2], w=[b_tmp[2]])
                    P.act(lambda h, n=n: h.activation(out=t3[:, 0:n], in_=t3[:, 0:n], func=AF.Exp, scale=0.5), r=[b_tmp[2]], w=[b_tmp[2]])
                    P.dve(lambda h, r=r, n=n: h.tensor_scalar(out=mneg[:, r, 0:n], in0=t3[:, 0:n], scalar1=-1.0, scalar2=None, op0=ALU.mult),
                          r=[b_tmp[2]], **({"w": [b_mneg]} if r == 0 else {"pw": [b_mneg]}))
                    hd = g * 4 + r
                    P.act(lambda h, r=r, n=n, hd=hd: h.activation(out=skt[64:128, r, 0:n], in_=mneg[64:128, r, 0:n], func=AF.Exp,
                                                                   scale=0.125, bias=sinkv[64:128, hd:hd + 1]),
                          r=[b_mneg, b_const], **({"w": [b_skt]} if r == 0 else {"pw": [b_skt]}))

                for r in range(4):
                    pq, b_pq = bank()
                    for k in range(8):
                        P.pe(lambda h, pq=pq, k=k, r=r, t0=t0, n=n: h.matmul(
                            pq[:, 0:n], lhsT=wqq[:, k, r * 128:(r + 1) * 128], rhs=uTa[:, k, t0:t0 + n], start=(k == 0), stop=(k == 7)),
                            r=[b_wq, b_uTa[tgi]], **({"w": [b_pq]} if k == 0 else {"pw": [b_pq]}))
                    P.act(lambda h, pq=pq, r=r, n=n: h.activation(out=qpa[0:64, r, 0:n], in_=pq[0:64, 0:n], func=AF.Copy),
                          r=[b_pq], **({"w": [b_qpa]} if r == 0 else {"pw": [b_qpa]}))
                    if not isctx:
                        rope(qa[0:64, r, 0:n], b_qa, pq, b_pq, t0 - CTX, n, r != 0)
                    P.act(lambda h, r=r, n=n: h.activation(out=sqt[:, 0:n], in_=qpa[0:64, r, 0:n], func=AF.Square),
                          r=[b_qpa], w=[b_sqt])
                    pn, b_pn = bank()
                    pns[r] = (pn, b_pn)
                    P.pe(lambda h, pn=pn, n=n: h.matmul(pn[:, 0:n], lhsT=ones_bf[0:64, :], rhs=sqt[:, 0:n], start=True, stop=True),
                         r=[b_sqt, b_c2], w=[b_pn])
                    if r >= 1:
                        q_chain(r - 1)
                q_chain(3)
                P.dve(lambda h, n=n: h.tensor_copy(out=qpa[64:65, :, 0:n], in_=mneg[64:65, :, 0:n]), r=[b_mneg], pw=[b_qpa])
                if not isctx:
                    P.dve(lambda h, n=n: h.tensor_copy(out=qa[64:65, :, 0:n], in_=mneg[64:65, :, 0:n]), r=[b_mneg], pw=[b_qa])
                if K.att_stage < 3:
                    continue
                for qb in range(n // 128):
                    qs = slice(qb * 128, (qb + 1) * 128)
                    if isctx:
                        chunks = [(0, qpa, b_qpa, None), (1, qpa, b_qpa, None)]
                    else:
                        j = (t0 - CTX) // 128 + qb
                        chunks = []
                        if j > 0:
                            chunks.append((2 + j - 1, qa, b_qa, mprev))
                        chunks.append((2 + j, qa, b_qa, None))
                        if j < 15:
                            chunks.append((2 + j + 1, qa, b_qa, mnext))
                        chunks += [(0, qpa, b_qpa, None), (1, qpa, b_qpa, None)]
                    po, b_po = pbank[6 + qbc[0] % 2], b_bank[6 + qbc[0] % 2]
                    qbc[0] += 1
                    nch = len(chunks)
                    pss = [None] * nch

                    def emit_norm(po=po, b_po=b_po, qs=qs, qb=qb):
                        P.dve(lambda h, po=po, qs=qs: h.tensor_tensor(out=rden[0:64, :].rearrange("p (r q) -> p r q", r=4),
                                                                      in0=po[64:128, :].rearrange("p (r q) -> p r q", r=4),
                                                                      in1=skt[64:128, :, qs], op=ALU.add),
                              r=[b_po, b_skt], w=[b_rden])
                        P.dve(lambda h, po=po: h.tensor_copy(out=oun[0:64, :], in_=po[0:64, :]), r=[b_po], w=[b_oun])
                        P.act(lambda h: h.activation(out=rden[0:64, :], in_=rden[0:64, :], func=AF.Ln), r=[b_rden], w=[b_rden])
                        P.act(lambda h: h.activation(out=rden[0:64, :], in_=rden[0:64, :], func=AF.Exp, scale=-1.0), r=[b_rden], w=[b_rden])
                        for par in range(2):
                            P.dve(lambda h, qs=qs, par=par: h.tensor_tensor(
                                out=OT[par * 64:(par + 1) * 64, :, qs],
                                in0=oun[0:64, :].rearrange("p (a b q) -> p a b q", a=2, b=2)[:, :, par, :],
                                in1=rden[0:64, :].rearrange("p (a b q) -> p a b q", a=2, b=2)[:, :, par, :], op=ALU.mult),
                                r=[b_oun, b_rden], **({"w": [b_OT]} if (qb == 0 and par == 0) else {"pw": [b_OT]}))


                    def emit_S(ci):
                        kb, qt, bqt, msk = chunks[ci]
                        bi_ = psc[0] % 6
                        psc[0] += 1
                        ps_, b_ps = pbank[bi_], b_bank[bi_]
                        pss[ci] = (ps_, b_ps)
                        P.pe(lambda h, ps_=ps_, kb=kb, qt=qt, qs=qs, msk=msk: h.matmul(
                            ps_[:, :], lhsT=kTa[0:65, kb * 128:(kb + 1) * 128], rhs=qt[0:65, :, qs],
                            start=True, stop=(msk is None)),
                            r=[b_kT, bqt], w=[b_ps])
                        if msk is not None:
                            P.pe(lambda h, ps_=ps_, msk=msk: h.matmul(ps_[:, :], lhsT=ident[:, :], rhs=msk[:, :], start=False, stop=True),
                                 r=[b_const], pw=[b_ps])

                    emit_S(0)
                    for ci in range(nch):
                        if ci + 1 < nch:
                            emit_S(ci + 1)
                        kb = chunks[ci][0]
                        ps_, b_ps = pss[ci]
                        pi = ptc[0] % 3
                        ptc[0] += 1
                        pt_, bpt = PT[pi], b_PT[pi]
                        P.act(lambda h, ps_=ps_, pt_=pt_: h.activation(out=pt_[:, :], in_=ps_[:, :], func=AF.Exp, scale=0.125),
                              r=[b_ps], w=[bpt])
                        P.pe(lambda h, po=po, kb=kb, pt_=pt_, ci=ci, nch=nch: h.matmul(
                            po[:, :], lhsT=Va[:, kb, :], rhs=pt_[:, :], start=(ci == 0), stop=(ci == nch - 1)),
                            r=[b_V, bpt], **({"w": [b_po]} if ci == 0 else {"pw": [b_po]}))
                        if ci == 1 and len(pending_norm) > 0:
                            pending_norm.pop(0)()
                    pending_norm.append(emit_norm)
                while pending_norm:
                    pending_norm.pop(0)()
                if K.att_stage < 4:
                    continue
                rr = 2 if isctx else rr_l
                for kd in range(8):
                    py, b_py = bank()
                    for r in range(2):
                        P.pe(lambda h, py=py, r=r, kd=kd, n=n: h.matmul(
                            py[:, 0:n], lhsT=wo[:, r, kd * 128:(kd + 1) * 128], rhs=OT[:, r, 0:n],
                            start=(r == 0), stop=(r == 1)),
                            r=[b_wo, b_OT], **({"w": [b_py]} if r == 0 else {"pw": [b_py]}))
                    P.dve(lambda h, py=py, kd=kd, t0=t0, n=n, rr=rr: h.scalar_tensor_tensor(
                        out=hs[:, kd, t0:t0 + n], in0=py[:, 0:n], scalar=mod(layer, 2, kd, rr), in1=hs[:, kd, t0:t0 + n],
                        op0=ALU.mult, op1=ALU.add),
                        r=[b_py, b_hs[0][tgi], b_modv[layer]], w=[b_hs[0][tgi]])
        P.barrier()
        for tgi in range(len(TGS)):
            layer_norm(tgi, layer, 0)
        P.barrier()


    so = [ARENA + 36864]

    def salloc(name, shape, dt, n=1):
        nb = int(np.prod(shape[1:])) * (4 if dt == F32 else 2)
        nb = (nb + 31) // 32 * 32
        ts = [P.sb(name, shape, dt, so[0] + i * nb) for i in range(n)]
        so[0] += nb * n
        return ts if n > 1 else ts[0]

    s_wxbc = salloc("s_wxbc", [128, 8, 768], BF16)
    s_wdt = salloc("s_wdt", [128, 8, 16], BF16)
    s_raw = salloc("s_raw", [128, 520], F32, 3)
    s_acc = salloc("s_acc", [128, 512], F32, 3)
    s_th = salloc("s_th", [128, 512], F32, 3)
    s_xo = salloc("s_xo", [128, 512], BF16, 3)
    s_t1 = salloc("s_t1", [128, 18, 16], F32)
    S1_END = so[0]
    so[0] = ARENA + 36864
    s_wso = salloc("s_wso", [128, 4, 1024], BF16)
    s_Sbin = salloc("s_Sbin", [128, 16, 512], BF16)
    s_Sf = salloc("s_Sf", [128, 512], F32)
    s_Sfb = salloc("s_Sfb", [128, 512], BF16)
    s_xw = [salloc("s_xw", [128, 512], BF16)] * 2
    s_cbm = salloc("s_cbm", [128, 128], BF16, 2)
    s_arg = salloc("s_arg", [128, 4, 128], BF16, 2)
    s_M = salloc("s_M", [128, 4, 128], BF16, 4)
    s_ya = salloc("s_ya", [128, 512], F32)
    s_yb = salloc("s_yb", [128, 512], F32)
    s_Sb = s_yb
    s_yg = s_ya
    s_yn = salloc("s_yn", [128, 512], BF16, 2)
    s_gz = salloc("s_gz", [128, 512], F32)
    s_ynT = salloc("s_ynT", [128, 4, 512], BF16)
    s_ss = salloc("s_ss", [128, 4], F32)
    S2_END = so[0]
    so[0] = max(S1_END, S2_END)
    s_wz = salloc("s_wz", [128, 8, 512], BF16)
    s_xtok = salloc("s_xtok", [128, 18, 512], BF16)
    s_btok = salloc("s_btok", [128, 18, 128], BF16)
    s_BT = salloc("s_BT", [128, T], BF16)
    s_CT = salloc("s_CT", [128, T], BF16)
    s_dt = salloc("s_dt", [128, 18, 16], F32)
    s_a = salloc("s_a", [128, 18, 16], F32)
    s_cs = salloc("s_cs", [128, 18, 16], F32)
    s_tot = salloc("s_tot", [128, 18, 16], F32)
    s_E = salloc("s_E", [128, 18, 16], F32)
    s_dec = salloc("s_dec", [128, 18, 16], F32)
    s_cw = salloc("s_cw", [128, 36], F32)
    s_sb = salloc("s_sb", [128, 40], F32)
    s_ng = salloc("s_ng", [128, 4], F32)
    s_tri = salloc("s_tri", [128, 256], F32)
    s_one = salloc("s_one", [128, 2], F32)
    assert so[0] <= 212832, so[0]
    bs = {nm: P.buf(nm) for nm in ("xtok", "btok", "BT", "CT", "gz", "dt", "a", "cs", "tot", "E", "dec", "t1", "t2", "par",
                                   "wxbc", "wz", "wdt", "raw0", "raw1", "raw2", "acc0", "acc1", "acc2", "th0", "th1", "th2", "xo0", "xo1", "xo2", "arg0", "arg1", "xdt0", "xdt1", "wso", "Sbin", "Sf", "Sb",
                                   "Sfb", "xw0", "xw1", "cbm0", "cbm1", "arg", "L", "M0", "M1", "M2", "M3", "ya", "yb", "yg",
                                   "yn0", "yn1", "ynT", "ss")}
    sc_ = {"raw": 0, "xo": 0, "xw": 0, "M": 0, "acc": 0, "arg": 0}
    tri = s_tri[:, 0:128]
    trirev = s_tri[:, 128:256]

    def halo_ap(base, stride):
        a = base.ap
        return bass.AP(base.tensor, base.offset, [list(a[0]), [stride, 2], [1, 2]])

    def bc8(ap2):
        return ap2.unsqueeze(2).to_broadcast([128, 8, 64])

    def ssm(layer, b):
        rr_l = b
        for tgi, (t0, n, isctx) in enumerate(TGS):
            rr = 2 if isctx else rr_l
            for k in range(8):
                P.act(lambda h, k=k, t0=t0, n=n, rr=rr: h.activation(
                    out=uTa[:, k, t0:t0 + n], in_=hs[:, k, t0:t0 + n], func=AF.Identity,
                    scale=mod(layer, 1, k, rr), bias=mod(layer, 0, k, rr)),
                    r=[b_hs[0][tgi], b_modv[layer]], **({"w": [b_uTa[tgi]]} if k == 0 else {"pw": [b_uTa[tgi]]}))
        P.sp(lambda h: h.dma_start(out=s_tri[:, :], in_=tri_d[:, :]), dma_w=bs["par"])
        P.pool(lambda h: h.memset(s_one[:, :], 1.0), w=[bs["t2"]])
        t2v = None
        for g in range(4):
            P.sp(lambda h, g=g: h.dma_start(out=s_cw[:, :], in_=cw_d[g]), dma_w=bs["par"])
            P.sp(lambda h, g=g: h.dma_start(out=s_sb[:, :], in_=sb_d[g]), dma_w=bs["par"])
            P.sp(lambda h, g=g: h.dma_start(out=s_ng[:, :], in_=ng_d[g]), dma_w=bs["par"])
            P.pool(lambda h, g=g: h.dma_start(out=s_wxbc[:, :, :], in_=wxbc_d[g]), dma_w=bs["wxbc"])
            P.pool(lambda h, g=g: h.dma_start(out=s_wz[:, :, :], in_=wz_d[g]), dma_w=bs["wz"])
            P.pool(lambda h, g=g: h.dma_start(out=s_wdt[:, :, :], in_=wdt_d[g]), dma_w=bs["wdt"])
            P.dve(lambda h: h.tensor_scalar(out=s_cw[:, :], in0=s_cw[:, :], scalar1=0.5, scalar2=None, op0=ALU.mult), r=[bs["par"]], w=[bs["par"]])
            P.act(lambda h: h.activation(out=s_sb[:, 16:32], in_=s_sb[:, 16:32], func=AF.Exp), r=[bs["par"]], w=[bs["par"]])
            P.dve(lambda h: h.tensor_scalar(out=s_sb[:, 16:32], in0=s_sb[:, 16:32], scalar1=-1.0, scalar2=None, op0=ALU.mult), r=[bs["par"]], w=[bs["par"]])
            def A1(tile):
                tgi, c = tile["tgi"], tile["c"]
                t0, n, isctx = TGS[tgi]
                seg0, seg1 = (0, CTX) if isctx else (CTX, T)
                ri = sc_["raw"] % 3
                sc_["raw"] += 1
                raw, braw = s_raw[ri], bs[f"raw{ri}"]
                ai = sc_["acc"] % 3
                sc_["acc"] += 1
                acc_, bacc = s_acc[ai], bs[f"acc{ai}"]
                th_, bth = s_th[ai], bs[f"th{ai}"]
                tile.update(raw=raw, braw=braw, acc=acc_, bacc=bacc, th=th_, bth=bth)
                pm_, b_pm_ = bank()
                for k in range(8):
                    P.pe(lambda h, k=k: h.matmul(pm_[:, 0:n], lhsT=s_wxbc[:, k, c * 128:(c + 1) * 128], rhs=uTa[:, k, t0:t0 + n],
                                                 start=(k == 0), stop=(k == 7)),
                         r=[bs["wxbc"], b_uTa[tgi]], **({"w": [b_pm_]} if k == 0 else {"pw": [b_pm_]}))
                P.act(lambda h: h.activation(out=raw[:, 2:2 + n], in_=pm_[:, 0:n], func=AF.Copy), r=[b_pm_], w=[braw])
                hasl = (t0 - 2) >= seg0
                hasr = (t0 + n + 2) <= seg1
                if not hasl:
                    P.pool(lambda h: h.memset(raw[:, 0:2], 0.0), pw=[braw])
                if not hasr:
                    P.pool(lambda h: h.memset(raw[:, 2 + n:4 + n], 0.0), pw=[braw])
                if hasl or hasr:
                    ph, b_ph = bank()
                    hbufs = []
                    if hasl:
                        hbufs.append(b_uTa[[i for i, (a0, an, _) in enumerate(TGS) if a0 <= t0 - 2 < a0 + an][0]])
                    if hasr:
                        hbufs.append(b_uTa[[i for i, (a0, an, _) in enumerate(TGS) if a0 <= t0 + n < a0 + an][0]])
                    for k in range(8):
                        if hasl and hasr:
                            rhs_fn = lambda k=k: halo_ap(uTa[:, k, t0 - 2:t0], n + 2)
                            ncol = 4
                        elif hasl:
                            rhs_fn = lambda k=k: uTa[:, k, t0 - 2:t0]
                            ncol = 2
                        else:
                            rhs_fn = lambda k=k: uTa[:, k, t0 + n:t0 + n + 2]
                            ncol = 2
                        P.pe(lambda h, k=k, rhs_fn=rhs_fn, ncol=ncol: h.matmul(
                            ph[:, 0:ncol], lhsT=s_wxbc[:, k, c * 128:(c + 1) * 128], rhs=rhs_fn(), start=(k == 0), stop=(k == 7)),
                            r=[bs["wxbc"]] + hbufs, **({"w": [b_ph]} if k == 0 else {"pw": [b_ph]}))
                    if hasl:
                        P.act(lambda h: h.activation(out=raw[:, 0:2], in_=ph[:, 0:2], func=AF.Copy), r=[b_ph], pw=[braw])
                    if hasr:
                        o_ = 2 if hasl else 0
                        P.act(lambda h: h.activation(out=raw[:, 2 + n:4 + n], in_=ph[:, o_:o_ + 2], func=AF.Copy), r=[b_ph], pw=[braw])
                P.act(lambda h: h.activation(out=acc_[:, 0:n], in_=raw[:, 0:n], func=AF.Identity,
                                             scale=s_cw[:, c * 6:c * 6 + 1], bias=s_cw[:, c * 6 + 5:c * 6 + 6]),
                      r=[braw, bs["par"]], w=[bacc])

            def A2(tile):
                c = tile["c"]
                t0, n, isctx = TGS[tile["tgi"]]
                raw, braw, acc_, bacc = tile["raw"], tile["braw"], tile["acc"], tile["bacc"]
                for j in range(1, 5):
                    P.dve(lambda h, j=j: h.scalar_tensor_tensor(
                        out=acc_[:, 0:n], in0=raw[:, j:j + n], scalar=s_cw[:, c * 6 + j:c * 6 + j + 1], in1=acc_[:, 0:n],
                        op0=ALU.mult, op1=ALU.add), r=[braw, bs["par"], bacc], w=[bacc])

            def A3(tile):
                t0, n, isctx = TGS[tile["tgi"]]
                acc_, bacc, th_, bth = tile["acc"], tile["bacc"], tile["th"], tile["bth"]
                P.act(lambda h: h.activation(out=th_[:, 0:n], in_=acc_[:, 0:n], func=AF.Tanh), r=[bacc], w=[bth])

            def A4(tile):
                tgi, c = tile["tgi"], tile["c"]
                t0, n, isctx = TGS[tgi]
                acc_, bacc, th_, bth = tile["acc"], tile["bacc"], tile["th"], tile["bth"]
                if c <= 4:
                    xi = sc_["xo"] % 3
                    sc_["xo"] += 1
                    xo, bxo = s_xo[xi], bs[f"xo{xi}"]
                    P.dve(lambda h: h.scalar_tensor_tensor(out=xo[:, 0:n], in0=th_[:, 0:n], scalar=1.0, in1=acc_[:, 0:n],
                                                           op0=ALU.add, op1=ALU.mult), r=[bth, bacc], w=[bxo])
                    if c == 4:
                        P.act(lambda h: h.activation(out=s_BT[:, t0:t0 + n], in_=xo[:, 0:n], func=AF.Copy), r=[bxo], pw=[bs["BT"]])
                    for bi in range(n // 128):
                        blk = t0 // 128 + bi
                        ptr, b_ptr = bank()
                        P.pe(lambda h, ptr=ptr, bi=bi: h.matmul(ptr[:, 0:128], lhsT=xo[:, bi * 128:(bi + 1) * 128], rhs=ident[:, :], start=True, stop=True),
                             r=[bxo, b_const], w=[b_ptr])
                        if c < 4:
                            P.act(lambda h, ptr=ptr, blk=blk: h.activation(out=s_xtok[:, blk, c * 128:(c + 1) * 128], in_=ptr[:, 0:128], func=AF.Copy),
                                  r=[b_ptr], pw=[bs["xtok"]])
                        else:
                            P.act(lambda h, ptr=ptr, blk=blk: h.activation(out=s_btok[:, blk, :], in_=ptr[:, 0:128], func=AF.Copy),
                                  r=[b_ptr], pw=[bs["btok"]])
                else:
                    P.dve(lambda h: h.scalar_tensor_tensor(out=s_CT[:, t0:t0 + n], in0=th_[:, 0:n], scalar=1.0, in1=acc_[:, 0:n],
                                                           op0=ALU.add, op1=ALU.mult), r=[bth, bacc], pw=[bs["CT"]])
                if c == 5:
                    for bi in range(n // 128):
                        blk = t0 // 128 + bi
                        pd, b_pd = bank()
                        for k in range(8):
                            P.pe(lambda h, pd=pd, k=k, blk=blk: h.matmul(pd[:, 0:16], lhsT=uTa[:, k, blk * 128:(blk + 1) * 128], rhs=s_wdt[:, k, :],
                                                                         start=(k == 0), stop=(k == 7)),
                                 r=[bs["wdt"], b_uTa[tgi]], **({"w": [b_pd]} if k == 0 else {"pw": [b_pd]}))
                        P.dve(lambda h, pd=pd, blk=blk: h.tensor_tensor(out=s_dt[:, blk, :], in0=pd[:, 0:16], in1=s_sb[:, 0:16], op=ALU.add),
                              r=[b_pd, bs["par"]], pw=[bs["dt"]])

            tiles = [dict(tgi=tgi, c=c) for tgi in range(len(TGS)) for c in range(6)]
            A1(tiles[0])
            for i, tile in enumerate(tiles):
                A2(tile)
                if i + 1 < len(tiles):
                    A1(tiles[i + 1])
                A3(tile)
                if i >= 1:
                    A4(tiles[i - 1])
            A4(tiles[-1])
            dtv, t1v = s_dt[:, :, :], s_t1[:, :, :]
            P.dve(lambda h: h.tensor_scalar(out=t1v, in0=dtv, scalar1=-1.0, scalar2=None, op0=ALU.mult), r=[bs["dt"]], w=[bs["t1"]])
            P.dve(lambda h: h.tensor_tensor(out=t1v, in0=t1v, in1=dtv, op=ALU.max), r=[bs["dt"], bs["t1"]], w=[bs["t1"]])
            P.act(lambda h: h.activation(out=t1v, in_=t1v, func=AF.Exp, scale=-1.0), r=[bs["t1"]], w=[bs["t1"]])
            P.act(lambda h: h.activation(out=t1v, in_=t1v, func=AF.Ln, bias=s_one[:, 0:1], scale=1.0), r=[bs["t1"], bs["t2"]], w=[bs["t1"]])
            P.dve(lambda h: h.scalar_tensor_tensor(out=dtv, in0=dtv, scalar=0.0, in1=t1v, op0=ALU.max, op1=ALU.add), r=[bs["dt"], bs["t1"]], w=[bs["dt"]])
            P.dve(lambda h: h.tensor_tensor(out=s_a[:, :, :], in0=dtv, in1=s_sb[:, 16:32].unsqueeze(1).to_broadcast([128, 18, 16]), op=ALU.mult),
                  r=[bs["dt"], bs["par"]], w=[bs["a"]])
            for blk in range(18):
                pc, b_pc = bank()
                P.pe(lambda h, pc=pc, blk=blk: h.matmul(pc[:, 0:8], lhsT=tri, rhs=s_a[:, blk, 0:8], start=True, stop=True), r=[bs["a"], bs["par"]], w=[b_pc])
                P.pe(lambda h, pc=pc, blk=blk: h.matmul(pc[:, 8:16], lhsT=trirev, rhs=s_a[:, blk, 8:16], start=True, stop=True), r=[bs["a"], bs["par"]], pw=[b_pc])
                P.pe(lambda h, pc=pc, blk=blk: h.matmul(pc[:, 16:32], lhsT=ones_f[:, :], rhs=s_a[:, blk, :], start=True, stop=True), r=[bs["a"], b_c2], pw=[b_pc])
                P.act(lambda h, pc=pc, blk=blk: h.activation(out=s_cs[:, blk, :], in_=pc[:, 0:16], func=AF.Copy), r=[b_pc], pw=[bs["cs"]])
                P.act(lambda h, pc=pc, blk=blk: h.activation(out=s_tot[:, blk, :], in_=pc[:, 16:32], func=AF.Copy, scale=float(D)), r=[b_pc], pw=[bs["tot"]])
            P.act(lambda h: h.activation(out=s_E[:, :, :], in_=s_cs[:, :, :], func=AF.Exp), r=[bs["cs"]], w=[bs["E"]])
            P.dve(lambda h: h.tensor_tensor(out=s_dec[:, :, :], in0=s_tot[:, :, :], in1=s_cs[:, :, :], op=ALU.subtract), r=[bs["tot"], bs["cs"]], w=[bs["dec"]])
            P.act(lambda h: h.activation(out=s_dec[:, :, :], in_=s_dec[:, :, :], func=AF.Exp), r=[bs["dec"]], w=[bs["dec"]])
            P.dve(lambda h: h.tensor_tensor(out=s_dec[:, :, :], in0=s_dec[:, :, :], in1=s_dt[:, :, :], op=ALU.mult), r=[bs["dec"], bs["dt"]], w=[bs["dec"]])
            P.act(lambda h: h.activation(out=s_tot[:, :, :], in_=s_tot[:, :, :], func=AF.Exp), r=[bs["tot"]], w=[bs["tot"]])
            P.act(lambda h: h.activation(out=s_dt[:, :, :], in_=s_dt[:, :, :], func=AF.Ln), r=[bs["dt"], bs["dec"]], w=[bs["dt"]])
            P.dve(lambda h: h.tensor_tensor(out=s_dt[:, :, :], in0=s_dt[:, :, :], in1=s_cs[:, :, :], op=ALU.subtract), r=[bs["dt"], bs["cs"]], w=[bs["dt"]])
            P.barrier()
            P.pool(lambda h, g=g: h.dma_start(out=s_wso[:, :, :], in_=wso_d[g]), dma_w=bs["wso"])
            for c in range(4):
                P.dve(lambda h, c=c: h.tensor_scalar(out=s_wso[:, c, :], in0=s_wso[:, c, :], scalar1=s_ng[:, c:c + 1], scalar2=None, op0=ALU.mult),
                      r=[bs["wso"], bs["par"]], w=[bs["wso"]])
            P.pool(lambda h: h.memset(s_Sf[:, :], 0.0), w=[bs["Sf"]])
            P.pool(lambda h: h.memset(s_Sb[:, :], 0.0), w=[bs["Sb"]])

            def state_update(S, bS, blk, d):
                xi = sc_["xw"] % 2
                sc_["xw"] += 1
                xw, bxw = s_xw[0], bs["xw0"]
                P.dve(lambda h: h.tensor_tensor(out=xw[:, :].rearrange("p (a b) -> p a b", a=8), in0=s_xtok[:, blk, :].rearrange("p (a b) -> p a b", a=8),
                                                in1=bc8(s_dec[:, blk, d * 8:d * 8 + 8]), op=ALU.mult), r=[bs["xtok"], bs["dec"]], w=[bxw])
                pst, b_pst = bank()
                P.pe(lambda h: h.matmul(pst[:, :], lhsT=s_btok[:, blk, :], rhs=xw[:, :], start=True, stop=True), r=[bs["btok"], bxw], w=[b_pst])
                P.dve(lambda h: h.tensor_tensor(out=S[:, :].rearrange("p (a b) -> p a b", a=8), in0=S[:, :].rearrange("p (a b) -> p a b", a=8),
                                                in1=bc8(s_tot[:, blk, d * 8:d * 8 + 8]), op=ALU.mult), r=[bS, bs["tot"]], w=[bS])
                P.dve(lambda h: h.tensor_tensor(out=S[:, :], in0=S[:, :], in1=pst[:, :], op=ALU.add), r=[bS, b_pst], w=[bS])

            for blk in [1, 0] + list(range(17, 1, -1)):
                if blk >= 2:
                    P.act(lambda h, blk=blk: h.activation(out=s_Sbin[:, blk - 2, :], in_=s_Sb[:, :], func=AF.Copy), r=[bs["Sb"]], pw=[bs["Sbin"]])
                state_update(s_Sb, bs["Sb"], blk, 1)
            groups = [(half, d) for half in range(2) for d in range(2)]
            st = {}

            def F1(blk):
                cols = slice(blk * 128, (blk + 1) * 128)
                P.act(lambda h: h.activation(out=s_Sfb[:, :], in_=s_Sf[:, :], func=AF.Copy), r=[bs["Sf"]], w=[bs["Sfb"]])
                state_update(s_Sf, bs["Sf"], blk, 0)
                pcb, b_pcb = bank()
                P.pe(lambda h: h.matmul(pcb[:, 0:128], lhsT=s_BT[:, cols], rhs=s_CT[:, cols], start=True, stop=True), r=[bs["BT"], bs["CT"]], w=[b_pcb])
                P.dve(lambda h: h.tensor_tensor(out=s_cbm[0][:, :], in0=pcb[:, 0:128], in1=tri, op=ALU.mult), r=[b_pcb, bs["par"]], w=[bs["cbm0"]])
                P.dve(lambda h: h.tensor_tensor(out=s_cbm[1][:, :], in0=pcb[:, 0:128], in1=trirev, op=ALU.mult), r=[b_pcb, bs["par"]], w=[bs["cbm1"]])
                pabs = []
                for (half, d) in groups:
                    pab, b_pab = bank()
                    pabs.append((pab, b_pab))
                    P.pe(lambda h, pab=pab, d=d: h.matmul(pab[:, :], lhsT=ident[:, :], rhs=(mnext if d == 0 else mprev)[:, :], start=True, stop=False),
                         r=[b_const], w=[b_pab])
                    for ci in range(4):
                        col = d * 8 + half * 4 + ci
                        P.pe(lambda h, pab=pab, ci=ci, col=col, d=d: h.matmul(
                            pab[:, ci * 128:(ci + 1) * 128], lhsT=s_a[:, blk, col:col + 1].to_broadcast([128, 128]),
                            rhs=(tri if d == 0 else trirev), start=False, stop=(ci == 3)),
                            r=[bs["a"], bs["par"]], pw=[b_pab])
                pz, b_pz = bank()
                tgz = 1 + (blk - 2) // 4
                for k in range(8):
                    P.pe(lambda h, k=k: h.matmul(pz[:, :], lhsT=uTa[:, k, cols], rhs=s_wz[:, k, :], start=(k == 0), stop=(k == 7)),
                         r=[bs["wz"], b_uTa[tgz]], **({"w": [b_pz]} if k == 0 else {"pw": [b_pz]}))
                args = [None] * 4
                Ms = [None] * 4

                def emit_exp(gi):
                    half, d = groups[gi]
                    pab, b_pab = pabs[gi]
                    ai = sc_["arg"] % 2
                    sc_["arg"] += 1
                    arg, barg = s_arg[ai], bs[f"arg{ai}"]
                    args[gi] = (arg, barg)
                    for ci in range(4):
                        col = d * 8 + half * 4 + ci
                        P.act(lambda h, ci=ci, col=col: h.activation(
                            out=arg[:, ci, :], in_=pab[:, ci * 128:(ci + 1) * 128], func=AF.Exp, bias=s_dt[:, blk, col:col + 1], scale=1.0),
                            r=[b_pab, bs["dt"]], **({"w": [barg]} if ci == 0 else {"pw": [barg]}))

                def emit_mul(gi):
                    half, d = groups[gi]
                    arg, barg = args[gi]
                    mi = sc_["M"] % 4
                    sc_["M"] += 1
                    Mt, bM = s_M[mi], bs[f"M{mi}"]
                    Ms[gi] = (Mt, bM)
                    P.dve(lambda h: h.tensor_tensor(
                        out=Mt[:, :, :], in0=arg[:, :, :], in1=s_cbm[d][:, :].unsqueeze(1).to_broadcast([128, 4, 128]), op=ALU.mult),
                        r=[barg, bs[f"cbm{d}"]], w=[bM])

                emit_exp(0)
                emit_exp(1)
                emit_mul(0)
                emit_exp(2)
                emit_mul(1)
                emit_exp(3)
                emit_mul(2)
                emit_mul(3)
                P.act(lambda h: h.activation(out=s_gz[:, :], in_=pz[:, :], func=AF.Exp, scale=-1.0), r=[b_pz], w=[bs["gz"]])
                P.act(lambda h: h.activation(out=s_gz[:, :], in_=s_gz[:, :], func=AF.Ln, bias=s_one[:, 0:1], scale=1.0), r=[bs["gz"], bs["t2"]], w=[bs["gz"]])
                P.act(lambda h: h.activation(out=s_gz[:, :], in_=s_gz[:, :], func=AF.Exp, scale=-1.0), r=[bs["gz"]], w=[bs["gz"]])
                P.dve(lambda h: h.tensor_tensor(out=s_gz[:, :], in0=s_gz[:, :], in1=pz[:, :], op=ALU.mult), r=[bs["gz"], b_pz], w=[bs["gz"]])
                pof, b_pof = bank()
                P.pe(lambda h: h.matmul(pof[:, :], lhsT=s_CT[:, cols], rhs=s_Sfb[:, :], start=True, stop=True), r=[bs["CT"], bs["Sfb"]], w=[b_pof])
                pob, b_pob = bank()
                P.pe(lambda h: h.matmul(pob[:, :], lhsT=s_CT[:, cols], rhs=s_Sbin[:, blk - 2, :], start=True, stop=True), r=[bs["CT"], bs["Sbin"]], w=[b_pob])
                P.dve(lambda h: h.tensor_tensor(out=s_ya[:, :].rearrange("p (a b) -> p a b", a=8), in0=pof[:, :].rearrange("p (a b) -> p a b", a=8),
                                                in1=bc8(s_E[:, blk, 0:8]), op=ALU.mult), r=[b_pof, bs["E"]], w=[bs["ya"]])
                P.dve(lambda h: h.tensor_tensor(out=s_yb[:, :].rearrange("p (a b) -> p a b", a=8), in0=pob[:, :].rearrange("p (a b) -> p a b", a=8),
                                                in1=bc8(s_E[:, blk, 8:16]), op=ALU.mult), r=[b_pob, bs["E"]], w=[bs["yb"]])
                P.dve(lambda h: h.tensor_tensor(out=s_ya[:, :], in0=s_ya[:, :], in1=s_yb[:, :], op=ALU.add), r=[bs["ya"], bs["yb"]], w=[bs["ya"]])
                P.dve(lambda h: h.tensor_tensor(out=s_yb[:, :].rearrange("p (a b) -> p a b", a=8), in0=s_xtok[:, blk, :].rearrange("p (a b) -> p a b", a=8),
                                                in1=bc8(s_sb[:, 32:40]), op=ALU.mult), r=[bs["xtok"], bs["par"]], w=[bs["yb"]])
                P.dve(lambda h: h.tensor_tensor(out=s_ya[:, :], in0=s_ya[:, :], in1=s_yb[:, :], op=ALU.add), r=[bs["ya"], bs["yb"]], w=[bs["ya"]])
                st[blk] = Ms

            def F2(blk):
                Ms = st.pop(blk)
                pyd, b_pyd = bank()
                for half in range(2):
                    for ci in range(4):
                        hh = half * 4 + ci
                        for d in range(2):
                            Mt, bM = Ms[half * 2 + d]
                            P.pe(lambda h, Mt=Mt, ci=ci, hh=hh, d=d: h.matmul(
                                pyd[:, hh * 64:(hh + 1) * 64], lhsT=Mt[:, ci, :], rhs=s_xtok[:, blk, hh * 64:(hh + 1) * 64], start=(d == 0), stop=(d == 1)),
                                r=[bM, bs["xtok"]], **({"w": [b_pyd]} if (half == 0 and ci == 0 and d == 0) else {"pw": [b_pyd]}))
                P.dve(lambda h: h.tensor_tensor(out=s_ya[:, :], in0=s_ya[:, :], in1=pyd[:, :], op=ALU.add), r=[bs["ya"], b_pyd], w=[bs["ya"]])
                P.dve(lambda h: h.tensor_tensor(out=s_ya[:, :], in0=s_ya[:, :], in1=s_gz[:, :], op=ALU.mult), r=[bs["ya"], bs["gz"]], w=[bs["ya"]])
                P.act(lambda h: h.activation(out=s_yb[:, :], in_=s_ya[:, :], func=AF.Square, accum_out=s_ss[:, 0:1]), r=[bs["ya"]], w=[bs["yb"], bs["ss"]])
                P.act(lambda h: h.activation(out=s_ss[:, 1:2], in_=s_ss[:, 0:1], func=AF.Ln, scale=1.0 / 512.0, bias=epsv[:, 0:1]), r=[bs["ss"], b_c2], w=[bs["ss"]])
                P.act(lambda h: h.activation(out=s_ss[:, 2:3], in_=s_ss[:, 1:2], func=AF.Exp, scale=-0.5), r=[bs["ss"]], w=[bs["ss"]])
                yn, byn = s_yn[blk % 2], bs[f"yn{blk % 2}"]
                P.dve(lambda h: h.tensor_scalar(out=yn[:, :], in0=s_ya[:, :], scalar1=s_ss[:, 2:3], scalar2=None, op0=ALU.mult),
                      r=[bs["ya"], bs["ss"]], w=[byn])

            def T_(blk):
                j = (blk - 2) % 4
                yn, byn = s_yn[blk % 2], bs[f"yn{blk % 2}"]
                for c in range(4):
                    ptr, b_ptr = bank()
                    P.pe(lambda h, ptr=ptr, c=c: h.matmul(ptr[:, 0:128], lhsT=yn[:, c * 128:(c + 1) * 128], rhs=ident[:, :], start=True, stop=True), r=[byn, b_const], w=[b_ptr])
                    P.act(lambda h, ptr=ptr, c=c: h.activation(out=s_ynT[:, c, j * 128:(j + 1) * 128], in_=ptr[:, 0:128], func=AF.Copy), r=[b_ptr],
                          **({"w": [bs["ynT"]]} if (c == 0 and j == 0) else {"pw": [bs["ynT"]]}))
                if j != 3:
                    return
                tgi = 1 + (blk - 2) // 4
                q0 = TGS[tgi][0]
                for kd in range(8):
                    py, b_py = bank()
                    for c in range(4):
                        P.pe(lambda h, py=py, c=c, kd=kd: h.matmul(py[:, :], lhsT=s_wso[:, c, kd * 128:(kd + 1) * 128], rhs=s_ynT[:, c, :], start=(c == 0), stop=(c == 3)),
                             r=[bs["wso"], bs["ynT"]], **({"w": [b_py]} if c == 0 else {"pw": [b_py]}))
                    P.dve(lambda h, py=py, kd=kd: h.scalar_tensor_tensor(
                        out=hs[:, kd, q0:q0 + 512], in0=py[:, :], scalar=mod(layer, 2, kd, rr_l), in1=hs[:, kd, q0:q0 + 512], op0=ALU.mult, op1=ALU.add),
                        r=[b_py, b_hs[0][tgi], b_modv[layer]], w=[b_hs[0][tgi]])

            state_update(s_Sf, bs["Sf"], 0, 0)
            state_update(s_Sf, bs["Sf"], 1, 0)
            F1(2)
            F2(2)
            for blk in range(3, 18):
                F1(blk)
                T_(blk - 1)
                F2(blk)
            T_(17)
            P.barrier()
        for tgi in range(1, len(TGS)):
            layer_norm(tgi, layer, 0)
        P.barrier()

    b_out = P.buf("outst")
    for b in range(nseq):
        for k in range(8):
            P.sp(lambda h, k=k, b=b: h.dma_start(out=hs[:, k, 0:CTX], in_=cxT[b, k]), dma_w=b_hs[0][0])
            for tgi in range(1, 5):
                t0, n, _ = TGS[tgi]
                P.sp(lambda h, k=k, b=b, t0=t0, n=n: h.dma_start(out=hs[:, k, t0:t0 + n], in_=xT[b, k, :, t0 - CTX:t0 - CTX + n]),
                     dma_w=b_hs[0][tgi])
        for tgi, (t0, n, _) in enumerate(TGS):
            P.dve(lambda h, t0=t0, n=n: h.tensor_scalar(out=hs[:, :, t0:t0 + n], in0=hs[:, :, t0:t0 + n], scalar1=ALPHA, scalar2=None, op0=ALU.mult),
                  r=[b_hs[0][tgi]], w=[b_hs[0][tgi]])
        for layer in range(nlayers):
            last = layer == DEPTH - 1
            flags = dbg if isinstance(dbg, dict) else {}
            if layer % 2 == 0:
                if flags.get("att", True):
                    attention(layer, b)
            else:
                ssm(layer, b)
            for tgi, (t0, n, isctx) in enumerate(TGS):
                if last and isctx:
                    continue
                if flags.get("mlp", True):
                    mlp(tgi, layer, 2 if isctx else b)
                if flags.get("ln", True):
                    layer_norm(tgi, layer, 2, final=last)
            P.barrier()
        if dbg is not None:
            for k in range(8):
                P.sp(lambda h, k=k, b=b: h.dma_start(out=dbgT[b, k], in_=hs[:, k, :]), r=b_hs[0], dma_r=b_out)
        for k in range(8):
            P.sp(lambda h, k=k, b=b: h.dma_start(out=outT[b, k], in_=hs[:, k, CTX:T]), r=b_hs[0], dma_r=b_out)
        P.barrier()
    counts = P.finish([b_out])
    return nc, counts


def _rope_perm():
    idx = np.arange(64)
    a = idx // 32
    half = (idx % 32) // 16
    j = idx % 16
    return a * 32 + (1 - half) * 16 + j


def host_constants():
    bf = ml_dtypes.bfloat16
    c = {}
    c["ident"] = np.eye(128, dtype=np.float32).astype(bf)
    jj = np.arange(128)[:, None]
    ii = np.arange(128)[None, :]
    mp = np.where(jj >= ii, 0.0, NEG).astype(np.float32)
    mn = np.where(jj <= ii, 0.0, NEG).astype(np.float32)
    c["mprev"] = np.tile(mp, (1, 4)).astype(bf)
    c["mnext"] = np.tile(mn, (1, 4)).astype(bf)
    t = np.arange(SEQ)
    row = (t // 64).astype(np.float32)
    col = (t % 64).astype(np.float32)
    inv = (10000.0 ** (-np.arange(0, 32, 2, dtype=np.float32) / 32)).astype(np.float32)
    cosT = np.zeros((64, SEQ), np.float32)
    sinT = np.zeros((64, SEQ), np.float32)
    for a, pos in enumerate((row, col)):
        ang = (pos[None, :] * inv[:, None]).astype(np.float32)
        for half in range(2):
            sl = slice(a * 32 + half * 16, a * 32 + half * 16 + 16)
            cosT[sl] = np.cos(ang)
            sinT[sl] = np.sin(ang) * (-1.0 if half == 0 else 1.0)
    kk = np.arange(128)[:, None]
    ll = np.arange(128)[None, :]
    c["tri"] = np.concatenate([(kk <= ll), (kk >= ll)], axis=1).astype(np.float32)
    c["cossin"] = np.concatenate([cosT, sinT], axis=0)
    return c


def host_weights(inp):
    w = {}
    f = np.float32
    wm = np.asarray(inp["w_mod"], f)
    w["wmod"] = np.ascontiguousarray(wm.reshape(DEPTH, 8, 128, 12, 512).transpose(0, 3, 2, 1, 4))
    w["bmod"] = np.ascontiguousarray(np.asarray(inp["b_mod"], f).reshape(DEPTH, 48, 128).transpose(0, 2, 1))
    lnv = np.stack([np.asarray(inp[k], f) for k in ("ln_mix_g", "ln_mix_b", "ln_ff_g", "ln_ff_b")], axis=1)
    w["lnv"] = np.ascontiguousarray(lnv.reshape(DEPTH, 4, 8, 128).transpose(3, 0, 1, 2).reshape(128, DEPTH * 4 * 8))
    w["sinkb"] = np.ascontiguousarray(np.broadcast_to(np.asarray(inp["att_sink"], f).reshape(1, 16), (128, 16)))
    win = np.asarray(inp["att_w_in"], f)[0]
    perm = _rope_perm()
    wq = win[:, :1024].reshape(8, 128, 4, 4, 64)
    wqq = np.concatenate([wq, wq[..., perm]], axis=-1)
    w["wqq"] = np.ascontiguousarray(wqq.transpose(2, 1, 0, 3, 4).reshape(4, 128, 8, 512))
    wk = win[:, 1024:1280].reshape(8, 128, 4, 64)
    wv = win[:, 1280:1536].reshape(8, 128, 4, 64)
    wkv = np.concatenate([wk, wk[..., perm], wv], axis=-1)
    w["wkv"] = np.ascontiguousarray(wkv.transpose(2, 1, 0, 3))
    wo = np.asarray(inp["att_w_out"], f)[0].reshape(4, 2, 2, 64, 1024)
    w["wo"] = np.ascontiguousarray(wo.transpose(0, 2, 3, 1, 4).reshape(4, 128, 2, 1024))
    sw = np.asarray(inp["ssm_w_in"], f)[0]
    wx = sw[:, 2048:4096].reshape(8, 128, 4, 512)
    wB = sw[:, 4096:4608].reshape(8, 128, 4, 128)
    wC = sw[:, 4608:5120].reshape(8, 128, 4, 128)
    w["wxbc"] = np.ascontiguousarray(np.concatenate([wx, wB, wC], axis=-1).transpose(2, 1, 0, 3))
    w["wz"] = np.ascontiguousarray(sw[:, 0:2048].reshape(8, 128, 4, 512).transpose(2, 1, 0, 3))
    wd = sw[:, 5120:5184].reshape(8, 128, 2, 4, 8)
    w["wdt"] = np.ascontiguousarray(wd.transpose(3, 1, 0, 2, 4).reshape(4, 128, 8, 16))
    w["wso"] = np.ascontiguousarray(np.asarray(inp["ssm_w_out"], f)[0].reshape(4, 4, 128, 1024).transpose(0, 2, 1, 3))
    cw = np.asarray(inp["ssm_conv_w"], f)[0]
    cb = np.asarray(inp["ssm_conv_b"], f)[0]
    convw = np.zeros((4, 128, 6, 6), f)
    for g in range(4):
        chans = [np.arange(g * 512 + c * 128, g * 512 + (c + 1) * 128) for c in range(4)]
        chans.append(np.arange(2048 + g * 128, 2048 + (g + 1) * 128))
        chans.append(np.arange(2560 + g * 128, 2560 + (g + 1) * 128))
        for c, ch in enumerate(chans):
            convw[g, :, c, 0:5] = cw[:, ch].T
            convw[g, :, c, 5] = cb[ch]
    w["convw"] = convw.reshape(4, 128, 36)
    dtb = np.asarray(inp["ssm_dt_bias"], f)[0].reshape(2, 4, 8)
    alog = np.asarray(inp["ssm_a_log"], f)[0].reshape(2, 4, 8)
    dsk = np.asarray(inp["ssm_d"], f)[0].reshape(4, 8)
    ssmb = np.zeros((4, 128, 40), f)
    for g in range(4):
        ssmb[g, :, 0:16] = dtb[:, g, :].reshape(1, 16)
        ssmb[g, :, 16:32] = alog[:, g, :].reshape(1, 16)
        ssmb[g, :, 32:40] = dsk[g].reshape(1, 8)
    w["ssmb"] = ssmb
    ng = np.asarray(inp["ssm_norm_g"], f)[0].reshape(4, 4, 128)
    w["normg"] = np.ascontiguousarray(ng.transpose(0, 2, 1))
    w1 = np.asarray(inp["ff_w1"], f).reshape(DEPTH, 8, 128, 8, 512)
    w["w1"] = np.ascontiguousarray(w1.transpose(0, 3, 2, 1, 4))
    w2 = np.asarray(inp["ff_w2"], f).reshape(DEPTH, 32, 128, 8, 128)
    w["w2"] = np.ascontiguousarray(w2.transpose(0, 3, 2, 1, 4))
    return w


def host_core_inputs(inp, core):
    f = np.float32
    b0 = core * BLOC
    x = np.asarray(inp["x"], f)[b0:b0 + BLOC]
    ctx = np.asarray(inp["ctx"], f)[b0:b0 + BLOC]
    d = {}
    d["xT"] = np.ascontiguousarray(x.transpose(0, 2, 1).reshape(BLOC, 8, 128, SEQ))
    d["cxT"] = np.ascontiguousarray(ctx.transpose(0, 2, 1).reshape(BLOC, 8, 128, CTX))
    cc = np.concatenate([np.asarray(inp["c"], f)[b0:b0 + BLOC], np.asarray(inp["c_ctx"], f)[None]], axis=0)
    d["cT"] = np.ascontiguousarray(cc.reshape(3, 8, 128).transpose(2, 1, 0))
    return d


_CACHE = {}


def kernel(**inputs):
    if "nc" not in _CACHE:
        _CACHE["nc"] = build_program()[0]
    nc = _CACHE["nc"]
    shared = {}
    shared.update(host_constants())
    shared.update(host_weights(inputs))
    in_maps = []
    for core in range(NCORE):
        m = dict(shared)
        m.update(host_core_inputs(inputs, core))
        in_maps.append(m)
    res = run_bass_kernel_spmd(nc, in_maps, core_ids=list(range(NCORE)))
    outs = []
    for core in range(NCORE):
        oT = np.asarray(res.results[core]["outT"]).reshape(BLOC, D, SEQ)
        outs.append(oT.transpose(0, 2, 1))
    return np.ascontiguousarray(np.concatenate(outs, axis=0)).astype(np.float32)
```

```python
from contextlib import ExitStack
import numpy as np
import ml_dtypes
import concourse.bass as bass
import concourse.mybir as mybir
from concourse.bass_utils import run_bass_kernel_spmd

F32 = mybir.dt.float32
BF16 = mybir.dt.bfloat16
AF = mybir.ActivationFunctionType
ALU = mybir.AluOpType
AX = mybir.AxisListType

ENGS = ("pe", "act", "dve", "pool", "sp")

D = 1024
SEQ = 2048
CTX = 256
T = SEQ + CTX
NCORE = 8
BLOC = 2
DEPTH = 2
ALPHA = (2.0 * DEPTH) ** 0.25
LN_EPS = 1e-5
RMS_EPS = 1e-5
NEG = -30000.0
TGS = [(0, 256, True), (256, 512, False), (768, 512, False), (1280, 512, False), (1792, 512, False)]


class Buf:
    __slots__ = ("name", "writers", "readers", "prev_readers", "sem", "dcount", "excl")

    def __init__(self, name, excl=False):
        self.name = name
        self.excl = excl
        self.writers = {}
        self.readers = {}
        self.prev_readers = {}
        self.sem = None
        self.dcount = 0


class Op:
    __slots__ = ("emit", "deps", "signal", "dma", "isnop")

    def __init__(self, emit, deps, dma, isnop=False):
        self.emit = emit
        self.deps = deps
        self.signal = False
        self.dma = dma
        self.isnop = isnop


class Prog:
    def __init__(self, nc):
        self.nc = nc
        self.ops = {e: [] for e in ENGS}
        self.seen = {e: {} for e in ENGS}
        self.dma_bufs = []
        self.nbuf = 0
        self.ntens = 0

    def sb(self, name, shape, dt, off):
        self.ntens += 1
        return self.nc.alloc_sbuf_tensor_at(f"{name}_{self.ntens}", list(shape), dt, offset=off + 16512)

    def buf(self, name=None):
        self.nbuf += 1
        return Buf(name or f"b{self.nbuf}")

    def add(self, eng, emit, r=(), w=(), pw=(), dma_w=None, dma_r=None):
        ops = self.ops[eng]
        idx = len(ops)
        deps = {}

        def need(k, v):
            if deps.get(k, -1) < v:
                deps[k] = v

        mykey = ("E", eng)
        allr = list(r) + ([dma_r] if dma_r is not None else [])
        allw = list(w) + ([dma_w] if dma_w is not None else [])
        for b in allr:
            for k, v in b.writers.items():
                need(k, v)
            if b.excl:
                for k, v in b.readers.items():
                    if k != mykey:
                        need(k, v)
        for b in allw:
            for k, v in b.readers.items():
                need(k, v)
            for k, v in b.writers.items():
                if k == mykey and eng == "pe":
                    continue
                if dma_w is not None and k == ("D", dma_w):
                    continue
                need(k, v)
            if not b.readers:
                for k, v in b.prev_readers.items():
                    need(k, v)
        for b in pw:
            for k, v in b.readers.items():
                need(k, v)
            for k, v in b.prev_readers.items():
                need(k, v)
            for k, v in b.writers.items():
                if k[0] == "D":
                    need(k, v)
        seen = self.seen[eng]
        fdeps = {}
        for k, v in deps.items():
            if seen.get(k, -1) >= v:
                continue
            seen[k] = v
            fdeps[k] = v
            if k[0] == "E":
                self.ops[k[1]][v].signal = True
        is_dma = (dma_w is not None) or (dma_r is not None)
        dbuf = dma_w if dma_w is not None else dma_r
        op = Op(emit, fdeps, dbuf if is_dma else None)
        ops.append(op)
        if is_dma:
            if dbuf.sem is None:
                dbuf.sem = True
                self.dma_bufs.append(dbuf)
            dbuf.dcount += 16
            ev = (("D", dbuf), dbuf.dcount)
        else:
            ev = (mykey, idx)
        for b in allr:
            if b.readers.get(ev[0], -1) < ev[1]:
                b.readers[ev[0]] = ev[1]
        for b in allw:
            if b.readers:
                b.prev_readers = b.readers
            b.readers = {}
            b.writers = {ev[0]: ev[1]}
        for b in pw:
            if b.readers:
                b.prev_readers = b.readers
                b.readers = {}
                b.writers = {}
            if b.writers.get(ev[0], -1) < ev[1]:
                b.writers[ev[0]] = ev[1]
        return op

    def pe(self, emit, **kw): return self.add("pe", emit, **kw)
    def act(self, emit, **kw): return self.add("act", emit, **kw)
    def dve(self, emit, **kw): return self.add("dve", emit, **kw)
    def pool(self, emit, **kw): return self.add("pool", emit, **kw)
    def sp(self, emit, **kw): return self.add("sp", emit, **kw)

    def barrier(self):
        last = {}
        for e in ENGS:
            ops = self.ops[e]
            for i in range(len(ops) - 1, -1, -1):
                if ops[i].dma is None and not ops[i].isnop:
                    last[("E", e)] = i
                    break
        for b in self.dma_bufs:
            last[("D", b)] = b.dcount
        for e in ENGS:
            seen = self.seen[e]
            fdeps = {}
            for k, v in last.items():
                if k == ("E", e):
                    continue
                if seen.get(k, -1) >= v:
                    continue
                seen[k] = v
                fdeps[k] = v
                if k[0] == "E":
                    self.ops[k[1]][v].signal = True
            if fdeps:
                self.ops[e].append(Op(lambda h: h.nop(), fdeps, None, True))

    def finish(self, final_bufs):
        nc = self.nc
        with ExitStack() as es:
            sems = {e: es.enter_context(nc.semaphore(f"s_{e}")) for e in ENGS}
            for i, b in enumerate(self.dma_bufs):
                b.sem = es.enter_context(nc.semaphore(f"d{i}"))
            sigcnt = {}
            for e in ENGS:
                c = 0
                arr = []
                for op in self.ops[e]:
                    if op.signal:
                        c += 1
                    arr.append(c)
                sigcnt[e] = arr
            fin = [(b.sem, b.dcount) for b in final_bufs]

            def run(e, h):
                for op in self.ops[e]:
                    for k, v in op.deps.items():
                        if k[0] == "E":
                            h.wait_ge(sems[k[1]], sigcnt[k[1]][v])
                        else:
                            h.wait_ge(k[1].sem, v)
                    ins = op.emit(h)
                    if op.dma is not None:
                        ins.then_inc(op.dma.sem, 16)
                    elif op.signal:
                        ins.then_inc(sems[e], 1)
                if e == "sp":
                    for s, v in fin:
                        h.wait_ge(s, v)

            with nc.Block() as block:
                @block.tensor
                def _(h): run("pe", h)

                @block.scalar
                def _(h): run("act", h)

                @block.vector
                def _(h): run("dve", h)

                @block.gpsimd
                def _(h): run("pool", h)

                @block.sync
                def _(h): run("sp", h)
        return {e: len(self.ops[e]) for e in ENGS}


class K:
    pass


def build_program(nseq=BLOC, nlayers=DEPTH, dbg=None):
    nc = bass.Bass("TRN2", target_bir_lowering=False)
    P = Prog(nc)

    def din(name, shape, dt=F32):
        return nc.dram_tensor(name, list(shape), dt, kind="ExternalInput").ap()

    xT = din("xT", [BLOC, 8, 128, SEQ])
    cxT = din("cxT", [BLOC, 8, 128, CTX])
    cT = din("cT", [128, 8, 3])
    wmod = din("wmod", [DEPTH, 12, 128, 8, 512])
    bmod = din("bmod", [DEPTH, 128, 48])
    lnv_d = din("lnv", [128, DEPTH * 4 * 8])
    sink_d = din("sinkb", [128, 16])
    ident_d = din("ident", [128, 128], BF16)
    mprev_d = din("mprev", [128, 512], BF16)
    mnext_d = din("mnext", [128, 512], BF16)
    cs_d = din("cossin", [128, SEQ])
    wqq_d = din("wqq", [4, 128, 8, 512])
    wkv_d = din("wkv", [4, 128, 8, 192])
    wo_d = din("wo", [4, 128, 2, 1024])
    wxbc_d = din("wxbc", [4, 128, 8, 768])
    wz_d = din("wz", [4, 128, 8, 512])
    wdt_d = din("wdt", [4, 128, 8, 16])
    wso_d = din("wso", [4, 128, 4, 1024])
    cw_d = din("convw", [4, 128, 36])
    sb_d = din("ssmb", [4, 128, 40])
    ng_d = din("normg", [4, 128, 4])
    tri_d = din("tri", [128, 256])
    w1_d = din("w1", [DEPTH, 8, 128, 8, 512])
    w2_d = din("w2", [DEPTH, 8, 128, 32, 128])
    outT = nc.dram_tensor("outT", [BLOC, 8, 128, SEQ], F32, kind="ExternalOutput").ap()
    if dbg is not None:
        dbgT = nc.dram_tensor("dbgT", [BLOC, 8, 128, T], F32, kind="ExternalOutput").ap()

    HS_OFF = 0
    CONST_OFF = 73728
    ARENA = 78848
    hs = P.sb("hs", [128, 8, T], F32, HS_OFF)
    b_hs = [[P.buf(f"hs{g}") for g in range(len(TGS))]]

    co = [CONST_OFF]

    def calloc(name, shape, dt):
        n = int(np.prod(shape[1:])) * (4 if dt == F32 else 2)
        t = P.sb(name, shape, dt, co[0])
        co[0] += (n + 31) // 32 * 32
        assert co[0] <= ARENA
        return t

    ident = calloc("ident", [128, 128], BF16)
    ones_f = calloc("ones_f", [128, 128], F32)
    ones_bf = calloc("ones_bf", [128, 128], BF16)
    mprev = calloc("mprev", [128, 512], BF16)
    mnext = calloc("mnext", [128, 512], BF16)
    modv = [calloc(f"modv{i}", [128, 48, 3], F32) for i in range(DEPTH)]
    lnv = calloc("lnv", [128, DEPTH * 4 * 8], F32)
    lnva = calloc("lnva", [128, DEPTH * 4 * 8], F32)
    sinkv = calloc("sinkv", [128, 16], F32)
    epsv = calloc("epsv", [128, 2], F32)
    kmaxp = calloc("kmaxp", [128, 8], F32)
    kmax2 = calloc("kmax2", [128, 1], F32)
    b_const = P.buf("const")
    b_modv = [P.buf(f"modv{i}") for i in range(DEPTH)]
    b_kmax = P.buf("kmax")

    pbank = [nc.alloc_psum_tensor(f"pb{i}", [128, 512], F32) for i in range(8)]
    b_bank = [Buf(f"pb{i}", excl=True) for i in range(8)]
    bank_rr = [0]

    def bank():
        i = bank_rr[0]
        bank_rr[0] = (i + 1) % 8
        return pbank[i], b_bank[i]

    P.sp(lambda h: h.dma_start(out=ident[:, :], in_=ident_d[:, :]), dma_w=b_const)
    P.sp(lambda h: h.dma_start(out=mprev[:, :], in_=mprev_d[:, :]), dma_w=b_const)
    P.sp(lambda h: h.dma_start(out=mnext[:, :], in_=mnext_d[:, :]), dma_w=b_const)
    P.sp(lambda h: h.dma_start(out=lnv[:, :], in_=lnv_d[:, :]), dma_w=b_const)
    P.sp(lambda h: h.dma_start(out=sinkv[:, :], in_=sink_d[:, :]), dma_w=b_const)
    b_c2 = P.buf("const2")
    P.pool(lambda h: h.memset(ones_f[:, :], 1.0 / D), pw=[b_c2])
    P.pool(lambda h: h.memset(ones_bf[:, :], 1.0), pw=[b_c2])
    P.pool(lambda h: h.memset(epsv[:, 0:1], LN_EPS), pw=[b_c2])
    P.pool(lambda h: h.memset(epsv[:, 1:2], 1e-20), pw=[b_c2])
    P.dve(lambda h: h.tensor_scalar(out=lnva[:, :], in0=lnv[:, :], scalar1=ALPHA, scalar2=None, op0=ALU.mult),
          r=[b_const], w=[b_c2])

    def lnvec(layer, which, k, alpha):
        t = lnva if alpha else lnv
        c = (layer * 4 + which) * 8 + k
        return t[:, c:c + 1]

    a_cT = P.sb("cTs", [128, 8, 3], F32, ARENA)
    a_e = P.sb("cTe", [128, 8, 3], F32, ARENA + 128)
    a_sc = P.sb("scT", [128, 8, 3], F32, ARENA + 256)
    a_bm = P.sb("bm", [128, 48], F32, ARENA + 384)
    a_w = [P.sb(f"wm{i}", [128, 8, 512], F32, ARENA + 1024 + i * 16384) for i in range(2)]
    b_cT = P.buf("cT")
    b_sc = P.buf("scT")
    b_bm = P.buf("bm")
    b_w = [P.buf("wm0"), P.buf("wm1")]
    P.sp(lambda h: h.dma_start(out=a_cT[:, :, :], in_=cT[:, :, :]), dma_w=b_cT)
    P.act(lambda h: h.activation(out=a_e[:, :, :], in_=a_cT[:, :, :], func=AF.Exp, scale=-1.0), r=[b_cT], w=[b_sc])
    P.dve(lambda h: h.tensor_scalar(out=a_e[:, :, :], in0=a_e[:, :, :], scalar1=1.0, scalar2=None, op0=ALU.add), r=[b_sc], w=[b_sc])
    P.dve(lambda h: h.reciprocal(out=a_e[:, :, :], in_=a_e[:, :, :]), r=[b_sc], w=[b_sc])
    P.dve(lambda h: h.tensor_tensor(out=a_sc[:, :, :], in0=a_e[:, :, :], in1=a_cT[:, :, :], op=ALU.mult), r=[b_sc, b_cT], w=[b_sc])
    wi = 0
    for i in range(nlayers):
        pm, b_pm = bank()
        P.sp(lambda h, i=i: h.dma_start(out=a_bm[:, :], in_=bmod[i]), dma_w=b_bm)
        for ft in range(12):
            wt, bw = a_w[wi % 2], b_w[wi % 2]
            wi += 1
            P.sp(lambda h, wt=wt, i=i, ft=ft: h.dma_start(out=wt[:, :, :], in_=wmod[i, ft]), dma_w=bw)
            for fc in range(4):
                c = ft * 4 + fc
                for k in range(8):
                    P.pe(lambda h, pm=pm, wt=wt, c=c, fc=fc, k=k: h.matmul(
                        pm[:, c * 3:c * 3 + 3], lhsT=wt[:, k, fc * 128:(fc + 1) * 128], rhs=a_sc[:, k, :],
                        start=(k == 0), stop=(k == 7)),
                        r=[bw, b_sc], **({"w": [b_pm]} if (c == 0 and k == 0) else {"pw": [b_pm]}))
        mv = modv[i]
        P.dve(lambda h, mv=mv, pm=pm: h.tensor_tensor(
            out=mv[:, :, :], in0=pm[:, 0:144].rearrange("p (c r) -> p c r", r=3),
            in1=a_bm[:, :].unsqueeze(2).to_broadcast([128, 48, 3]), op=ALU.add),
            r=[b_pm, b_bm], w=[b_modv[i]])
        for which in (1, 4):
            P.dve(lambda h, mv=mv, which=which: h.tensor_scalar(
                out=mv[:, which * 8:(which + 1) * 8, :], in0=mv[:, which * 8:(which + 1) * 8, :],
                scalar1=1.0, scalar2=1.0 / ALPHA, op0=ALU.add, op1=ALU.mult),
                r=[b_modv[i]], w=[b_modv[i]])

    def mod(layer, which, k, rr):
        return modv[layer][:, which * 8 + k, rr:rr + 1]

    P.barrier()

    LN_OFF = ARENA + 110080

    def layer_norm(tgi, layer, which_g, final=False):
        t0, n, _ = TGS[tgi]
        sq = [P.sb("lnsq", [128, 512], F32, LN_OFF + i * 2048) for i in range(2)]
        b_sq = [K.b_lnsq0, K.b_lnsq1]
        st = [P.sb("lnst", [128, 512], F32, LN_OFF + 4096 + i * 2048) for i in range(4)]
        b_st = K.b_lnst
        bh = b_hs[0][tgi]
        p1, b_p1 = bank()
        p2, b_p2 = bank()
        for k in range(8):
            s, bs = sq[k % 2], b_sq[k % 2]
            P.act(lambda h, s=s, k=k: h.activation(out=s[:, 0:n], in_=hs[:, k, t0:t0 + n], func=AF.Square),
                  r=[bh], w=[bs])
            P.pe(lambda h, k=k: h.matmul(p1[:, 0:n], lhsT=ones_f[:, :], rhs=hs[:, k, t0:t0 + n],
                                         start=(k == 0), stop=(k == 7)),
                 r=[bh, b_c2], **({"w": [b_p1]} if k == 0 else {"pw": [b_p1]}))
            P.pe(lambda h, s=s, k=k: h.matmul(p2[:, 0:n], lhsT=ones_f[:, :], rhs=s[:, 0:n],
                                              start=(k == 0), stop=(k == 7)),
                 r=[bs, b_c2], **({"w": [b_p2]} if k == 0 else {"pw": [b_p2]}))
        mean, var, rstd, nmr = st
        P.act(lambda h: h.activation(out=mean[:, 0:n], in_=p1[:, 0:n], func=AF.Copy), r=[b_p1], w=[b_st])
        P.dve(lambda h: h.tensor_tensor(out=var[:, 0:n], in0=mean[:, 0:n], in1=mean[:, 0:n], op=ALU.mult), r=[b_st], w=[b_st])
        P.dve(lambda h: h.tensor_tensor(out=var[:, 0:n], in0=p2[:, 0:n], in1=var[:, 0:n], op=ALU.subtract), r=[b_st, b_p2], w=[b_st])
        P.act(lambda h: h.activation(out=rstd[:, 0:n], in_=var[:, 0:n], func=AF.Ln, bias=epsv[:, 0:1], scale=1.0), r=[b_st, b_c2], w=[b_st])
        P.act(lambda h: h.activation(out=rstd[:, 0:n], in_=rstd[:, 0:n], func=AF.Exp, scale=-0.5), r=[b_st], w=[b_st])
        P.dve(lambda h: h.scalar_tensor_tensor(out=nmr[:, 0:n], in0=mean[:, 0:n], scalar=-1.0, in1=rstd[:, 0:n],
                                               op0=ALU.mult, op1=ALU.mult), r=[b_st], w=[b_st])
        for k in range(8):
            P.dve(lambda h, k=k: h.tensor_tensor(out=hs[:, k, t0:t0 + n], in0=hs[:, k, t0:t0 + n], in1=rstd[:, 0:n], op=ALU.mult),
                  r=[bh, b_st], w=[bh])
            P.dve(lambda h, k=k: h.tensor_tensor(out=hs[:, k, t0:t0 + n], in0=hs[:, k, t0:t0 + n], in1=nmr[:, 0:n], op=ALU.add),
                  r=[bh, b_st], w=[bh])
            P.act(lambda h, k=k: h.activation(out=hs[:, k, t0:t0 + n], in_=hs[:, k, t0:t0 + n], func=AF.Identity,
                                              scale=lnvec(layer, which_g, k, not final), bias=lnvec(layer, which_g + 1, k, not final)),
                  r=[bh, b_c2], w=[bh])

    K.att_stage = dbg.get("stage", 4) if isinstance(dbg, dict) else 4
    K.b_lnsq0 = P.buf("lnsq0")
    K.b_lnsq1 = P.buf("lnsq1")
    K.b_lnst = P.buf("lnst")

    M_UT = ARENA
    M_HID = ARENA + 16384
    M_W1 = M_HID + 32768
    M_W2 = M_W1 + 16384
    M_RT = M_W2 + 16384
    m_uT = [P.sb("m_uT", [128, 8, 512], BF16, M_UT + i * 8192) for i in range(2)]
    m_hid = P.sb("m_hid", [128, 32, 512], BF16, M_HID)
    m_w1 = [P.sb("m_w1", [128, 8, 512], BF16, M_W1 + i * 8192) for i in range(2)]
    m_w2 = [P.sb("m_w2", [128, 32, 128], BF16, M_W2 + i * 8192) for i in range(2)]
    m_rt = [P.sb("m_rt", [128, 512], F32, M_RT + i * 2048) for i in range(2)]
    assert M_RT + 4096 <= LN_OFF
    bm_uT = [P.buf("m_uT0"), P.buf("m_uT1")]
    bm_hid = P.buf("m_hid")
    bm_w1 = [P.buf("m_w10"), P.buf("m_w11")]
    bm_w2 = [P.buf("m_w20"), P.buf("m_w21")]
    bm_rt = [P.buf("m_rt0"), P.buf("m_rt1")]
    cnt = {"ut": 0, "w1": 0, "w2": 0, "rt": 0}

    def mlp(tgi, layer, rr):
        t0, n, _ = TGS[tgi]
        bh = b_hs[0][tgi]
        ui = cnt["ut"] % 2
        cnt["ut"] += 1
        uT, buT = m_uT[ui], bm_uT[ui]
        for k in range(8):
            P.act(lambda h, k=k: h.activation(out=uT[:, k, 0:n], in_=hs[:, k, t0:t0 + n], func=AF.Identity,
                                              scale=mod(layer, 4, k, rr), bias=mod(layer, 3, k, rr)),
                  r=[bh, b_modv[layer]], **({"w": [buT]} if k == 0 else {"pw": [buT]}))
        for fb in range(8):
            wi_ = cnt["w1"] % 2
            cnt["w1"] += 1
            w1t, bw1 = m_w1[wi_], bm_w1[wi_]
            P.pool(lambda h, w1t=w1t, fb=fb: h.dma_start(out=w1t[:, :, :], in_=w1_d[layer, fb]), dma_w=bw1)
            for fc in range(4):
                pb, b_pb = bank()
                for k in range(8):
                    P.pe(lambda h, pb=pb, w1t=w1t, fc=fc, k=k: h.matmul(
                        pb[:, 0:n], lhsT=w1t[:, k, fc * 128:(fc + 1) * 128], rhs=uT[:, k, 0:n],
                        start=(k == 0), stop=(k == 7)),
                        r=[bw1, buT], **({"w": [b_pb]} if k == 0 else {"pw": [b_pb]}))
                ri = cnt["rt"] % 2
                cnt["rt"] += 1
                rt, brt = m_rt[ri], bm_rt[ri]
                P.dve(lambda h, pb=pb, rt=rt: h.tensor_scalar(out=rt[:, 0:n], in0=pb[:, 0:n], scalar1=0.0, scalar2=None, op0=ALU.max),
                      r=[b_pb], w=[brt])
                f = fb * 4 + fc
                P.act(lambda h, rt=rt, f=f: h.activation(out=m_hid[:, f, 0:n], in_=rt[:, 0:n], func=AF.Square),
                      r=[brt], **({"w": [bm_hid]} if f == 0 else {"pw": [bm_hid]}))
        for dc in range(8):
            wi_ = cnt["w2"] % 2
            cnt["w2"] += 1
            w2t, bw2 = m_w2[wi_], bm_w2[wi_]
            P.pool(lambda h, w2t=w2t, dc=dc: h.dma_start(out=w2t[:, :, :], in_=w2_d[layer, dc]), dma_w=bw2)
            pb, b_pb = bank()
            for kf in range(32):
                P.pe(lambda h, pb=pb, w2t=w2t, kf=kf: h.matmul(
                    pb[:, 0:n], lhsT=w2t[:, kf, :], rhs=m_hid[:, kf, 0:n], start=(kf == 0), stop=(kf == 31)),
                    r=[bw2, bm_hid], **({"w": [b_pb]} if kf == 0 else {"pw": [b_pb]}))
            P.dve(lambda h, pb=pb, dc=dc: h.scalar_tensor_tensor(
                out=hs[:, dc, t0:t0 + n], in0=pb[:, 0:n], scalar=mod(layer, 5, dc, rr), in1=hs[:, dc, t0:t0 + n],
                op0=ALU.mult, op1=ALU.add),
                r=[b_pb, bh, b_modv[layer]], w=[bh])

    A_UT = ARENA
    A_COS = A_UT + 36864
    A_W = A_COS + 16384
    A_KT = A_W + 19456
    A_V = A_KT + 4608
    A_Q = A_V + 4608
    A_OT = A_Q + 8192
    A_PT = A_OT + 4096
    A_TMP = A_PT + 3072
    A_MN = A_TMP + 6144
    A_SK = A_MN + 4096
    A_RD = A_SK + 8192
    A_SQ = A_RD + 2048
    A_OUN = A_SQ + 1024
    A_END = A_OUN + 2048
    assert A_END <= 212832, A_END
    uTa = P.sb("uTa", [128, 8, T], BF16, A_UT)
    cstab = P.sb("cstab", [128, SEQ], F32, A_COS)
    wqq = P.sb("wqq", [128, 8, 512], BF16, A_W)
    wkv = P.sb("wkv", [128, 8, 192], BF16, A_W + 8192)
    wo = P.sb("wo", [128, 2, 1024], BF16, A_W + 11264)
    kTa = P.sb("kTa", [65, T], BF16, A_KT)
    Va = P.sb("Va", [128, 18, 128], BF16, A_V)
    qa = P.sb("qa", [65, 4, 512], BF16, A_Q)
    qpa = P.sb("qpa", [65, 4, 512], BF16, A_Q + 4096)
    OT = P.sb("OT", [128, 2, 512], BF16, A_OT)
    PT = [P.sb("PT", [128, 512], BF16, A_PT + i * 1024) for i in range(3)]
    tmp = [P.sb("atmp", [128, 512], F32, A_TMP + i * 2048) for i in range(3)]
    mneg = P.sb("mneg", [128, 4, 512], BF16, A_MN)
    skt = P.sb("skt", [128, 4, 512], F32, A_SK)
    rden = P.sb("rden", [64, 512], F32, A_RD)
    sqt = P.sb("sqt", [64, 512], BF16, A_SQ)
    oun = P.sb("oun", [64, 512], F32, A_OUN)
    b_oun = P.buf("oun")
    qbc = [0]
    psc = [0]
    pending_norm = []
    b_uTa = [P.buf(f"uTa{g}") for g in range(len(TGS))]
    b_rope = P.buf("rope")
    b_wq, b_wkv, b_wo = P.buf("wq"), P.buf("wkv"), P.buf("wo")
    b_kT, b_V = P.buf("kT"), P.buf("V")
    b_qa, b_qpa, b_OT = P.buf("qa"), P.buf("qpa"), P.buf("OT")
    b_PT = [P.buf(f"PT{i}") for i in range(3)]
    b_tmp = [P.buf(f"atmp{i}") for i in range(3)]
    b_mneg, b_skt, b_rden, b_sqt = P.buf("mneg"), P.buf("skt"), P.buf("rden"), P.buf("sqt")
    ptc = [0]

    def rope(dst, bdst, pa, b_pa, lt0, n, pwflag):
        t1, t2 = tmp[0], tmp[1]
        P.dve(lambda h: h.tensor_tensor(out=t1[0:64, 0:n], in0=pa[0:64, 0:n], in1=cstab[0:64, lt0:lt0 + n], op=ALU.mult),
              r=[b_pa, b_rope], w=[b_tmp[0]])
        P.dve(lambda h: h.tensor_tensor(out=t2[0:64, 0:n], in0=pa[64:128, 0:n], in1=cstab[64:128, lt0:lt0 + n], op=ALU.mult),
              r=[b_pa, b_rope], w=[b_tmp[1]])
        P.dve(lambda h: h.tensor_tensor(out=dst, in0=t1[0:64, 0:n], in1=t2[0:64, 0:n], op=ALU.add),
              r=[b_tmp[0], b_tmp[1]], **({"pw": [bdst]} if pwflag else {"w": [bdst]}))

    def attention(layer, b):
        rr_l = b
        for tgi, (t0, n, isctx) in enumerate(TGS):
            rr = 2 if isctx else rr_l
            for k in range(8):
                P.act(lambda h, k=k, t0=t0, n=n, rr=rr: h.activation(
                    out=uTa[:, k, t0:t0 + n], in_=hs[:, k, t0:t0 + n], func=AF.Identity,
                    scale=mod(layer, 1, k, rr), bias=mod(layer, 0, k, rr)),
                    r=[b_hs[0][tgi], b_modv[layer]], **({"w": [b_uTa[tgi]]} if k == 0 else {"pw": [b_uTa[tgi]]}))
        P.sp(lambda h: h.dma_start(out=cstab[:, :], in_=cs_d[:, :]), dma_w=b_rope)
        for g in range(4):
            P.pool(lambda h, g=g: h.dma_start(out=wqq[:, :, :], in_=wqq_d[g]), dma_w=b_wq)
            P.pool(lambda h, g=g: h.dma_start(out=wkv[:, :, :], in_=wkv_d[g]), dma_w=b_wkv)
            P.pool(lambda h, g=g: h.dma_start(out=wo[:, :, :], in_=wo_d[g]), dma_w=b_wo)
            P.pool(lambda h: h.memset(kTa[64:65, :], 1.0), w=[b_kT])
            P.pool(lambda h: h.memset(Va[:, :, 64:128], 1.0), w=[b_V])
            for tgi, (t0, n, isctx) in enumerate(TGS):
                pk, b_pk = bank()
                for k in range(8):
                    P.pe(lambda h, pk=pk, k=k, t0=t0, n=n: h.matmul(
                        pk[:, 0:n], lhsT=wkv[:, k, 0:128], rhs=uTa[:, k, t0:t0 + n], start=(k == 0), stop=(k == 7)),
                        r=[b_wkv, b_uTa[tgi]], **({"w": [b_pk]} if k == 0 else {"pw": [b_pk]}))
                if isctx:
                    P.act(lambda h, pk=pk, t0=t0, n=n: h.activation(out=kTa[0:64, t0:t0 + n], in_=pk[0:64, 0:n], func=AF.Copy),
                          r=[b_pk], pw=[b_kT])
                else:
                    rope(kTa[0:64, t0:t0 + n], b_kT, pk, b_pk, t0 - CTX, n, True)
                P.act(lambda h, t0=t0, n=n: h.activation(out=sqt[:, 0:n], in_=kTa[0:64, t0:t0 + n], func=AF.Square),
                      r=[b_kT], w=[b_sqt])
                pn, b_pn = bank()
                P.pe(lambda h, pn=pn, n=n: h.matmul(pn[:, 0:n], lhsT=ones_bf[0:64, :], rhs=sqt[:, 0:n], start=True, stop=True),
                     r=[b_sqt, b_c2], w=[b_pn])
                P.dve(lambda h, pn=pn, n=n, tgi=tgi: h.tensor_reduce(out=kmaxp[:, tgi:tgi + 1], in_=pn[:, 0:n], axis=AX.X, op=ALU.max),
                      r=[b_pn], **({"w": [b_kmax]} if tgi == 0 else {"pw": [b_kmax]}))
                for bi in range(n // 128):
                    blk = t0 // 128 + bi
                    pv, b_pv = bank()
                    for k in range(8):
                        P.pe(lambda h, pv=pv, k=k, blk=blk: h.matmul(
                            pv[:, 0:64], lhsT=uTa[:, k, blk * 128:(blk + 1) * 128], rhs=wkv[:, k, 128:192],
                            start=(k == 0), stop=(k == 7)),
                            r=[b_wkv, b_uTa[tgi]], **({"w": [b_pv]} if k == 0 else {"pw": [b_pv]}))
                    P.act(lambda h, pv=pv, blk=blk: h.activation(out=Va[:, blk, 0:64], in_=pv[:, 0:64], func=AF.Copy),
                          r=[b_pv], pw=[b_V])
            P.dve(lambda h: h.tensor_reduce(out=kmax2[:, :], in_=kmaxp[:, 0:5], axis=AX.X, op=ALU.max), r=[b_kmax], w=[b_kmax])
            P.dve(lambda h: h.tensor_scalar(out=kmax2[:, :], in0=kmax2[:, :], scalar1=1.05, scalar2=None, op0=ALU.mult), r=[b_kmax], w=[b_kmax])
            for tgi, (t0, n, isctx) in enumerate(TGS):
                if K.att_stage < 2:
                    break
                pns = {}

                def q_chain(r):
                    pn, b_pn = pns[r]
                    t3 = tmp[2]
                    P.act(lambda h, pn=pn, n=n: h.activation(out=t3[:, 0:n], in_=pn[:, 0:n], func=AF.Ln, scale=kmax2[:, 0:1], bias=epsv[:, 1:2]),
                          r=[b_pn, b_kmax, b_c2], w=[b_tmp[2]])
                    P.act(lambda h, n=n: h.activation(out=t3[:, 0:n], in_=t3[:, 0:n], func=AF.Exp, scale=0.5), r=[b_tmp[2]], w=[b_tmp[2]])
                    P.dve(lambda h, r=r, n=n: h.tensor_scalar(out=mneg[:, r, 0:n], in0=t3[:, 0:n], scalar1=-1.0, scalar2=None, op0=ALU.mult),
                          r=[b_tmp[2]], **({"w": [b_mneg]} if r == 0 else {"pw": [b_mneg]}))
                    hd = g * 4 + r
                    P.act(lambda h, r=r, n=n, hd=hd: h.activation(out=skt[64:128, r, 0:n], in_=mneg[64:128, r, 0:n], func=AF.Exp,
                                                                   scale=0.125, bias=sinkv[64:128, hd:hd + 1]),
                          r=[b_mneg, b_const], **({"w": [b_skt]} if r == 0 else {"pw": [b_skt]}))

                for r in range(4):
                    pq, b_pq = bank()
                    for k in range(8):
                        P.pe(lambda h, pq=pq, k=k, r=r, t0=t0, n=n: h.matmul(
                            pq[:, 0:n], lhsT=wqq[:, k, r * 128:(r + 1) * 128], rhs=uTa[:, k, t0:t0 + n], start=(k == 0), stop=(k == 7)),
                            r=[b_wq, b_uTa[tgi]], **({"w": [b_pq]} if k == 0 else {"pw": [b_pq]}))
                    P.act(lambda h, pq=pq, r=r, n=n: h.activation(out=qpa[0:64, r, 0:n], in_=pq[0:64, 0:n], func=AF.Copy),
                          r=[b_pq], **({"w": [b_qpa]} if r == 0 else {"pw": [b_qpa]}))
                    if not isctx:
                        rope(qa[0:64, r, 0:n], b_qa, pq, b_pq, t0 - CTX, n, r != 0)
                    P.act(lambda h, r=r, n=n: h.activation(out=sqt[:, 0:n], in_=qpa[0:64, r, 0:n], func=AF.Square),
                          r=[b_qpa], w=[b_sqt])
                    pn, b_pn = bank()
                    pns[r] = (pn, b_pn)
                    P.pe(lambda h, pn=pn, n=n: h.matmul(pn[:, 0:n], lhsT=ones_bf[0:64, :], rhs=sqt[:, 0:n], start=True, stop=True),
                         r=[b_sqt, b_c2], w=[b_pn])
                    if r >= 1:
                        q_chain(r - 1)
                q_chain(3)
                P.dve(lambda h, n=n: h.tensor_copy(out=qpa[64:65, :, 0:n], in_=mneg[64:65, :, 0:n]), r=[b_mneg], pw=[b_qpa])
                if not isctx:
                    P.dve(lambda h, n=n: h.tensor_copy(out=qa[64:65, :, 0:n], in_=mneg[64:65, :, 0:n]), r=[b_mneg], pw=[b_qa])
                if K.att_stage < 3:
                    continue
                for qb in range(n // 128):
                    qs = slice(qb * 128, (qb + 1) * 128)
                    if isctx:
                        chunks = [(0, qpa, b_qpa, None), (1, qpa, b_qpa, None)]
                    else:
                        j = (t0 - CTX) // 128 + qb
                        chunks = []
                        if j > 0:
                            chunks.append((2 + j - 1, qa, b_qa, mprev))
                        chunks.append((2 + j, qa, b_qa, None))
                        if j < 15:
                            chunks.append((2 + j + 1, qa, b_qa, mnext))
                        chunks += [(0, qpa, b_qpa, None), (1, qpa, b_qpa, None)]
                    po, b_po = pbank[6 + qbc[0] % 2], b_bank[6 + qbc[0] % 2]
                    qbc[0] += 1
                    nch = len(chunks)
                    pss = [None] * nch

                    def emit_norm(po=po, b_po=b_po, qs=qs, qb=qb):
                        P.dve(lambda h, po=po, qs=qs: h.tensor_tensor(out=rden[0:64, :].rearrange("p (r q) -> p r q", r=4),
                                                                      in0=po[64:128, :].rearrange("p (r q) -> p r q", r=4),
                                                                      in1=skt[64:128, :, qs], op=ALU.add),
                              r=[b_po, b_skt], w=[b_rden])
                        P.dve(lambda h, po=po: h.tensor_copy(out=oun[0:64, :], in_=po[0:64, :]), r=[b_po], w=[b_oun])
                        P.act(lambda h: h.activation(out=rden[0:64, :], in_=rden[0:64, :], func=AF.Ln), r=[b_rden], w=[b_rden])
                        P.act(lambda h: h.activation(out=rden[0:64, :], in_=rden[0:64, :], func=AF.Exp, scale=-1.0), r=[b_rden], w=[b_rden])
                        for par in range(2):
                            P.dve(lambda h, qs=qs, par=par: h.tensor_tensor(
                                out=OT[par * 64:(par + 1) * 64, :, qs],
                                in0=oun[0:64, :].rearrange("p (a b q) -> p a b q", a=2, b=2)[:, :, par, :],
                                in1=rden[0:64, :].rearrange("p (a b q) -> p a b q", a=2, b=2)[:, :, par, :], op=ALU.mult),
                                r=[b_oun, b_rden], **({"w": [b_OT]} if (qb == 0 and par == 0) else {"pw": [b_OT]}))


                    def emit_S(ci):
                        kb, qt, bqt, msk = chunks[ci]
                        bi_ = psc[0] % 6
                        psc[0] += 1
                        ps_, b_ps = pbank[bi_], b_bank[bi_]
                        pss[ci] = (ps_, b_ps)
                        P.pe(lambda h, ps_=ps_, kb=kb, qt=qt, qs=qs, msk=msk: h.matmul(
                            ps_[:, :], lhsT=kTa[0:65, kb * 128:(kb + 1) * 128], rhs=qt[0:65, :, qs],
                            start=True, stop=(msk is None)),
                            r=[b_kT, bqt], w=[b_ps])
                        if msk is not None:
                            P.pe(lambda h, ps_=ps_, msk=msk: h.matmul(ps_[:, :], lhsT=ident[:, :], rhs=msk[:, :], start=False, stop=True),
                                 r=[b_const], pw=[b_ps])

                    emit_S(0)
                    for ci in range(nch):
                        if ci + 1 < nch:
                            emit_S(ci + 1)
                        kb = chunks[ci][0]
                        ps_, b_ps = pss[ci]
                        pi = ptc[0] % 3
                        ptc[0] += 1
                        pt_, bpt = PT[pi], b_PT[pi]
                        P.act(lambda h, ps_=ps_, pt_=pt_: h.activation(out=pt_[:, :], in_=ps_[:, :], func=AF.Exp, scale=0.125),
                              r=[b_ps], w=[bpt])
                        P.pe(lambda h, po=po, kb=kb, pt_=pt_, ci=ci, nch=nch: h.matmul(
                            po[:, :], lhsT=Va[:, kb, :], rhs=pt_[:, :], start=(ci == 0), stop=(ci == nch - 1)),
                            r=[b_V, bpt], **({"w": [b_po]} if ci == 0 else {"pw": [b_po]}))
                        if ci == 1 and len(pending_norm) > 0:
                            pending_norm.pop(0)()
                    pending_norm.append(emit_norm)
                while pending_norm:
                    pending_norm.pop(0)()
                if K.att_stage < 4:
                    continue
                rr = 2 if isctx else rr_l
                for kd in range(8):
                    py, b_py = bank()
                    for r in range(2):
                        P.pe(lambda h, py=py, r=r, kd=kd, n=n: h.matmul(
                            py[:, 0:n], lhsT=wo[:, r, kd * 128:(kd + 1) * 128], rhs=OT[:, r, 0:n],
                            start=(r == 0), stop=(r == 1)),
                            r=[b_wo, b_OT], **({"w": [b_py]} if r == 0 else {"pw": [b_py]}))
                    P.dve(lambda h, py=py, kd=kd, t0=t0, n=n, rr=rr: h.scalar_tensor_tensor(
                        out=hs[:, kd, t0:t0 + n], in0=py[:, 0:n], scalar=mod(layer, 2, kd, rr), in1=hs[:, kd, t0:t0 + n],
                        op0=ALU.mult, op1=ALU.add),
                        r=[b_py, b_hs[0][tgi], b_modv[layer]], w=[b_hs[0][tgi]])
        P.barrier()
        for tgi in range(len(TGS)):
            layer_norm(tgi, layer, 0)
        P.barrier()


    so = [ARENA + 36864]

    def salloc(name, shape, dt, n=1):
        nb = int(np.prod(shape[1:])) * (4 if dt == F32 else 2)
        nb = (nb + 31) // 32 * 32
        ts = [P.sb(name, shape, dt, so[0] + i * nb) for i in range(n)]
        so[0] += nb * n
        return ts if n > 1 else ts[0]

    s_wxbc = salloc("s_wxbc", [128, 8, 768], BF16)
    s_wdt = salloc("s_wdt", [128, 8, 16], BF16)
    s_raw = salloc("s_raw", [128, 520], F32, 3)
    s_acc = salloc("s_acc", [128, 512], F32, 3)
    s_th = salloc("s_th", [128, 512], F32, 3)
    s_xo = salloc("s_xo", [128, 512], BF16, 3)
    s_t1 = salloc("s_t1", [128, 18, 16], F32)
    S1_END = so[0]
    so[0] = ARENA + 36864
    s_wso = salloc("s_wso", [128, 4, 1024], BF16)
    s_Sbin = salloc("s_Sbin", [128, 16, 512], BF16)
    s_Sf = salloc("s_Sf", [128, 512], F32)
    s_Sfb = salloc("s_Sfb", [128, 512], BF16)
    s_xw = salloc("s_xw", [128, 512], BF16, 2)
    s_cbm = salloc("s_cbm", [128, 128], BF16, 2)
    s_arg = salloc("s_arg", [128, 4, 128], BF16, 2)
    s_M = salloc("s_M", [128, 4, 128], BF16, 4)
    s_ya = salloc("s_ya", [128, 512], F32)
    s_yb = salloc("s_yb", [128, 512], F32)
    s_Sb = s_yb
    s_yg = s_ya
    s_yn = salloc("s_yn", [128, 512], BF16, 2)
    s_gz = salloc("s_gz", [128, 512], BF16)
    s_ynT = salloc("s_ynT", [128, 4, 512], BF16)
    s_ss = salloc("s_ss", [128, 4], F32)
    S2_END = so[0]
    so[0] = max(S1_END, S2_END)
    s_wz = salloc("s_wz", [128, 8, 512], BF16)
    s_xtok = salloc("s_xtok", [128, 18, 512], BF16)
    s_btok = salloc("s_btok", [128, 18, 128], BF16)
    s_BT = salloc("s_BT", [128, T], BF16)
    s_CT = salloc("s_CT", [128, T], BF16)
    s_dt = salloc("s_dt", [128, 18, 16], F32)
    s_a = salloc("s_a", [128, 18, 16], F32)
    s_cs = salloc("s_cs", [128, 18, 16], F32)
    s_tot = salloc("s_tot", [128, 18, 16], F32)
    s_E = salloc("s_E", [128, 18, 16], F32)
    s_dec = salloc("s_dec", [128, 18, 16], F32)
    s_cw = salloc("s_cw", [128, 36], F32)
    s_sb = salloc("s_sb", [128, 40], F32)
    s_ng = salloc("s_ng", [128, 4], F32)
    s_tri = salloc("s_tri", [128, 256], F32)
    s_one = salloc("s_one", [128, 2], F32)
    assert so[0] <= 212832, so[0]
    bs = {nm: P.buf(nm) for nm in ("xtok", "btok", "BT", "CT", "gz", "dt", "a", "cs", "tot", "E", "dec", "t1", "t2", "par",
                                   "wxbc", "wz", "wdt", "raw0", "raw1", "raw2", "acc0", "acc1", "acc2", "th0", "th1", "th2", "xo0", "xo1", "xo2", "arg0", "arg1", "xdt0", "xdt1", "wso", "Sbin", "Sf", "Sb",
                                   "Sfb", "xw0", "xw1", "cbm0", "cbm1", "arg", "L", "M0", "M1", "M2", "M3", "ya", "yb", "yg",
                                   "yn0", "yn1", "ynT", "ss")}
    sc_ = {"raw": 0, "xo": 0, "xw": 0, "M": 0, "acc": 0, "arg": 0}
    tri = s_tri[:, 0:128]
    trirev = s_tri[:, 128:256]

    def halo_ap(base, stride):
        a = base.ap
        return bass.AP(base.tensor, base.offset, [list(a[0]), [stride, 2], [1, 2]])

    def bc8(ap2):
        return ap2.unsqueeze(2).to_broadcast([128, 8, 64])

    def ssm(layer, b):
        rr_l = b
        for tgi, (t0, n, isctx) in enumerate(TGS):
            rr = 2 if isctx else rr_l
            for k in range(8):
                P.act(lambda h, k=k, t0=t0, n=n, rr=rr: h.activation(
                    out=uTa[:, k, t0:t0 + n], in_=hs[:, k, t0:t0 + n], func=AF.Identity,
                    scale=mod(layer, 1, k, rr), bias=mod(layer, 0, k, rr)),
                    r=[b_hs[0][tgi], b_modv[layer]], **({"w": [b_uTa[tgi]]} if k == 0 else {"pw": [b_uTa[tgi]]}))
        P.sp(lambda h: h.dma_start(out=s_tri[:, :], in_=tri_d[:, :]), dma_w=bs["par"])
        P.pool(lambda h: h.memset(s_one[:, :], 1.0), w=[bs["t2"]])
        t2v = None
        for g in range(4):
            P.sp(lambda h, g=g: h.dma_start(out=s_cw[:, :], in_=cw_d[g]), dma_w=bs["par"])
            P.sp(lambda h, g=g: h.dma_start(out=s_sb[:, :], in_=sb_d[g]), dma_w=bs["par"])
            P.sp(lambda h, g=g: h.dma_start(out=s_ng[:, :], in_=ng_d[g]), dma_w=bs["par"])
            P.pool(lambda h, g=g: h.dma_start(out=s_wxbc[:, :, :], in_=wxbc_d[g]), dma_w=bs["wxbc"])
            P.pool(lambda h, g=g: h.dma_start(out=s_wz[:, :, :], in_=wz_d[g]), dma_w=bs["wz"])
            P.pool(lambda h, g=g: h.dma_start(out=s_wdt[:, :, :], in_=wdt_d[g]), dma_w=bs["wdt"])
            P.dve(lambda h: h.tensor_scalar(out=s_cw[:, :], in0=s_cw[:, :], scalar1=0.5, scalar2=None, op0=ALU.mult), r=[bs["par"]], w=[bs["par"]])
            P.act(lambda h: h.activation(out=s_sb[:, 16:32], in_=s_sb[:, 16:32], func=AF.Exp), r=[bs["par"]], w=[bs["par"]])
            P.dve(lambda h: h.tensor_scalar(out=s_sb[:, 16:32], in0=s_sb[:, 16:32], scalar1=-1.0, scalar2=None, op0=ALU.mult), r=[bs["par"]], w=[bs["par"]])
            def A1(tile):
                tgi, c = tile["tgi"], tile["c"]
                t0, n, isctx = TGS[tgi]
                seg0, seg1 = (0, CTX) if isctx else (CTX, T)
                ri = sc_["raw"] % 3
                sc_["raw"] += 1
                raw, braw = s_raw[ri], bs[f"raw{ri}"]
                ai = sc_["acc"] % 3
                sc_["acc"] += 1
                acc_, bacc = s_acc[ai], bs[f"acc{ai}"]
                th_, bth = s_th[ai], bs[f"th{ai}"]
                tile.update(raw=raw, braw=braw, acc=acc_, bacc=bacc, th=th_, bth=bth)
                pm_, b_pm_ = bank()
                for k in range(8):
                    P.pe(lambda h, k=k: h.matmul(pm_[:, 0:n], lhsT=s_wxbc[:, k, c * 128:(c + 1) * 128], rhs=uTa[:, k, t0:t0 + n],
                                                 start=(k == 0), stop=(k == 7)),
                         r=[bs["wxbc"], b_uTa[tgi]], **({"w": [b_pm_]} if k == 0 else {"pw": [b_pm_]}))
                P.act(lambda h: h.activation(out=raw[:, 2:2 + n], in_=pm_[:, 0:n], func=AF.Copy), r=[b_pm_], w=[braw])
                hasl = (t0 - 2) >= seg0
                hasr = (t0 + n + 2) <= seg1
                if not hasl:
                    P.pool(lambda h: h.memset(raw[:, 0:2], 0.0), pw=[braw])
                if not hasr:
                    P.pool(lambda h: h.memset(raw[:, 2 + n:4 + n], 0.0), pw=[braw])
                if hasl or hasr:
                    ph, b_ph = bank()
                    hbufs = []
                    if hasl:
                        hbufs.append(b_uTa[[i for i, (a0, an, _) in enumerate(TGS) if a0 <= t0 - 2 < a0 + an][0]])
                    if hasr:
                        hbufs.append(b_uTa[[i for i, (a0, an, _) in enumerate(TGS) if a0 <= t0 + n < a0 + an][0]])
                    for k in range(8):
                        if hasl and hasr:
                            rhs_fn = lambda k=k: halo_ap(uTa[:, k, t0 - 2:t0], n + 2)
                            ncol = 4
                        elif hasl:
                            rhs_fn = lambda k=k: uTa[:, k, t0 - 2:t0]
                            ncol = 2
                        else:
                            rhs_fn = lambda k=k: uTa[:, k, t0 + n:t0 + n + 2]
                            ncol = 2
                        P.pe(lambda h, k=k, rhs_fn=rhs_fn, ncol=ncol: h.matmul(
                            ph[:, 0:ncol], lhsT=s_wxbc[:, k, c * 128:(c + 1) * 128], rhs=rhs_fn(), start=(k == 0), stop=(k == 7)),
                            r=[bs["wxbc"]] + hbufs, **({"w": [b_ph]} if k == 0 else {"pw": [b_ph]}))
                    if hasl:
                        P.act(lambda h: h.activation(out=raw[:, 0:2], in_=ph[:, 0:2], func=AF.Copy), r=[b_ph], pw=[braw])
                    if hasr:
                        o_ = 2 if hasl else 0
                        P.act(lambda h: h.activation(out=raw[:, 2 + n:4 + n], in_=ph[:, o_:o_ + 2], func=AF.Copy), r=[b_ph], pw=[braw])
                P.act(lambda h: h.activation(out=acc_[:, 0:n], in_=raw[:, 0:n], func=AF.Identity,
                                             scale=s_cw[:, c * 6:c * 6 + 1], bias=s_cw[:, c * 6 + 5:c * 6 + 6]),
                      r=[braw, bs["par"]], w=[bacc])

            def A2(tile):
                c = tile["c"]
                t0, n, isctx = TGS[tile["tgi"]]
                raw, braw, acc_, bacc = tile["raw"], tile["braw"], tile["acc"], tile["bacc"]
                for j in range(1, 5):
                    P.dve(lambda h, j=j: h.scalar_tensor_tensor(
                        out=acc_[:, 0:n], in0=raw[:, j:j + n], scalar=s_cw[:, c * 6 + j:c * 6 + j + 1], in1=acc_[:, 0:n],
                        op0=ALU.mult, op1=ALU.add), r=[braw, bs["par"], bacc], w=[bacc])

            def A3(tile):
                t0, n, isctx = TGS[tile["tgi"]]
                acc_, bacc, th_, bth = tile["acc"], tile["bacc"], tile["th"], tile["bth"]
                P.act(lambda h: h.activation(out=th_[:, 0:n], in_=acc_[:, 0:n], func=AF.Tanh), r=[bacc], w=[bth])

            def A4(tile):
                tgi, c = tile["tgi"], tile["c"]
                t0, n, isctx = TGS[tgi]
                acc_, bacc, th_, bth = tile["acc"], tile["bacc"], tile["th"], tile["bth"]
                if c <= 4:
                    xi = sc_["xo"] % 3
                    sc_["xo"] += 1
                    xo, bxo = s_xo[xi], bs[f"xo{xi}"]
                    P.dve(lambda h: h.scalar_tensor_tensor(out=xo[:, 0:n], in0=th_[:, 0:n], scalar=1.0, in1=acc_[:, 0:n],
                                                           op0=ALU.add, op1=ALU.mult), r=[bth, bacc], w=[bxo])
                    if c == 4:
                        P.act(lambda h: h.activation(out=s_BT[:, t0:t0 + n], in_=xo[:, 0:n], func=AF.Copy), r=[bxo], pw=[bs["BT"]])
                    for bi in range(n // 128):
                        blk = t0 // 128 + bi
                        ptr, b_ptr = bank()
                        P.pe(lambda h, ptr=ptr, bi=bi: h.matmul(ptr[:, 0:128], lhsT=xo[:, bi * 128:(bi + 1) * 128], rhs=ident[:, :], start=True, stop=True),
                             r=[bxo, b_const], w=[b_ptr])
                        if c < 4:
                            P.act(lambda h, ptr=ptr, blk=blk: h.activation(out=s_xtok[:, blk, c * 128:(c + 1) * 128], in_=ptr[:, 0:128], func=AF.Copy),
                                  r=[b_ptr], pw=[bs["xtok"]])
                        else:
                            P.act(lambda h, ptr=ptr, blk=blk: h.activation(out=s_btok[:, blk, :], in_=ptr[:, 0:128], func=AF.Copy),
                                  r=[b_ptr], pw=[bs["btok"]])
                else:
                    P.dve(lambda h: h.scalar_tensor_tensor(out=s_CT[:, t0:t0 + n], in0=th_[:, 0:n], scalar=1.0, in1=acc_[:, 0:n],
                                                           op0=ALU.add, op1=ALU.mult), r=[bth, bacc], pw=[bs["CT"]])
                if c == 5:
                    for bi in range(n // 128):
                        blk = t0 // 128 + bi
                        pd, b_pd = bank()
                        for k in range(8):
                            P.pe(lambda h, pd=pd, k=k, blk=blk: h.matmul(pd[:, 0:16], lhsT=uTa[:, k, blk * 128:(blk + 1) * 128], rhs=s_wdt[:, k, :],
                                                                         start=(k == 0), stop=(k == 7)),
                                 r=[bs["wdt"], b_uTa[tgi]], **({"w": [b_pd]} if k == 0 else {"pw": [b_pd]}))
                        P.dve(lambda h, pd=pd, blk=blk: h.tensor_tensor(out=s_dt[:, blk, :], in0=pd[:, 0:16], in1=s_sb[:, 0:16], op=ALU.add),
                              r=[b_pd, bs["par"]], pw=[bs["dt"]])

            tiles = [dict(tgi=tgi, c=c) for tgi in range(len(TGS)) for c in range(6)]
            A1(tiles[0])
            for i, tile in enumerate(tiles):
                A2(tile)
                if i + 1 < len(tiles):
                    A1(tiles[i + 1])
                A3(tile)
                if i >= 1:
                    A4(tiles[i - 1])
            A4(tiles[-1])
            dtv, t1v = s_dt[:, :, :], s_t1[:, :, :]
            P.dve(lambda h: h.tensor_scalar(out=t1v, in0=dtv, scalar1=-1.0, scalar2=None, op0=ALU.mult), r=[bs["dt"]], w=[bs["t1"]])
            P.dve(lambda h: h.tensor_tensor(out=t1v, in0=t1v, in1=dtv, op=ALU.max), r=[bs["dt"], bs["t1"]], w=[bs["t1"]])
            P.act(lambda h: h.activation(out=t1v, in_=t1v, func=AF.Exp, scale=-1.0), r=[bs["t1"]], w=[bs["t1"]])
            P.act(lambda h: h.activation(out=t1v, in_=t1v, func=AF.Ln, bias=s_one[:, 0:1], scale=1.0), r=[bs["t1"], bs["t2"]], w=[bs["t1"]])
            P.dve(lambda h: h.scalar_tensor_tensor(out=dtv, in0=dtv, scalar=0.0, in1=t1v, op0=ALU.max, op1=ALU.add), r=[bs["dt"], bs["t1"]], w=[bs["dt"]])
            P.dve(lambda h: h.tensor_tensor(out=s_a[:, :, :], in0=dtv, in1=s_sb[:, 16:32].unsqueeze(1).to_broadcast([128, 18, 16]), op=ALU.mult),
                  r=[bs["dt"], bs["par"]], w=[bs["a"]])
            for blk in range(18):
                pc, b_pc = bank()
                P.pe(lambda h, pc=pc, blk=blk: h.matmul(pc[:, 0:8], lhsT=tri, rhs=s_a[:, blk, 0:8], start=True, stop=True), r=[bs["a"], bs["par"]], w=[b_pc])
                P.pe(lambda h, pc=pc, blk=blk: h.matmul(pc[:, 8:16], lhsT=trirev, rhs=s_a[:, blk, 8:16], start=True, stop=True), r=[bs["a"], bs["par"]], pw=[b_pc])
                P.pe(lambda h, pc=pc, blk=blk: h.matmul(pc[:, 16:32], lhsT=ones_f[:, :], rhs=s_a[:, blk, :], start=True, stop=True), r=[bs["a"], b_c2], pw=[b_pc])
                P.act(lambda h, pc=pc, blk=blk: h.activation(out=s_cs[:, blk, :], in_=pc[:, 0:16], func=AF.Copy), r=[b_pc], pw=[bs["cs"]])
                P.act(lambda h, pc=pc, blk=blk: h.activation(out=s_tot[:, blk, :], in_=pc[:, 16:32], func=AF.Copy, scale=float(D)), r=[b_pc], pw=[bs["tot"]])
            P.act(lambda h: h.activation(out=s_E[:, :, :], in_=s_cs[:, :, :], func=AF.Exp), r=[bs["cs"]], w=[bs["E"]])
            P.dve(lambda h: h.tensor_tensor(out=s_dec[:, :, :], in0=s_tot[:, :, :], in1=s_cs[:, :, :], op=ALU.subtract), r=[bs["tot"], bs["cs"]], w=[bs["dec"]])
            P.act(lambda h: h.activation(out=s_dec[:, :, :], in_=s_dec[:, :, :], func=AF.Exp), r=[bs["dec"]], w=[bs["dec"]])
            P.dve(lambda h: h.tensor_tensor(out=s_dec[:, :, :], in0=s_dec[:, :, :], in1=s_dt[:, :, :], op=ALU.mult), r=[bs["dec"], bs["dt"]], w=[bs["dec"]])
            P.act(lambda h: h.activation(out=s_tot[:, :, :], in_=s_tot[:, :, :], func=AF.Exp), r=[bs["tot"]], w=[bs["tot"]])
            P.act(lambda h: h.activation(out=s_dt[:, :, :], in_=s_dt[:, :, :], func=AF.Ln), r=[bs["dt"], bs["dec"]], w=[bs["dt"]])
            P.dve(lambda h: h.tensor_tensor(out=s_dt[:, :, :], in0=s_dt[:, :, :], in1=s_cs[:, :, :], op=ALU.subtract), r=[bs["dt"], bs["cs"]], w=[bs["dt"]])
            P.barrier()
            P.pool(lambda h, g=g: h.dma_start(out=s_wso[:, :, :], in_=wso_d[g]), dma_w=bs["wso"])
            for c in range(4):
                P.dve(lambda h, c=c: h.tensor_scalar(out=s_wso[:, c, :], in0=s_wso[:, c, :], scalar1=s_ng[:, c:c + 1], scalar2=None, op0=ALU.mult),
                      r=[bs["wso"], bs["par"]], w=[bs["wso"]])
            P.pool(lambda h: h.memset(s_Sf[:, :], 0.0), w=[bs["Sf"]])
            P.pool(lambda h: h.memset(s_Sb[:, :], 0.0), w=[bs["Sb"]])

            def su_a(blk, d):
                xi = sc_["xw"] % 2
                sc_["xw"] += 1
                xw, bxw = s_xw[xi], bs[f"xw{xi}"]
                P.dve(lambda h: h.tensor_tensor(out=xw[:, :].rearrange("p (a b) -> p a b", a=8), in0=s_xtok[:, blk, :].rearrange("p (a b) -> p a b", a=8),
                                                in1=bc8(s_dec[:, blk, d * 8:d * 8 + 8]), op=ALU.mult), r=[bs["xtok"], bs["dec"]], w=[bxw])
                pst, b_pst = bank()
                P.pe(lambda h: h.matmul(pst[:, :], lhsT=s_btok[:, blk, :], rhs=xw[:, :], start=True, stop=True), r=[bs["btok"], bxw], w=[b_pst])
                return pst, b_pst

            def su_b(S, bS, blk, d, pst, b_pst):
                P.dve(lambda h: h.tensor_tensor(out=S[:, :].rearrange("p (a b) -> p a b", a=8), in0=S[:, :].rearrange("p (a b) -> p a b", a=8),
                                                in1=bc8(s_tot[:, blk, d * 8:d * 8 + 8]), op=ALU.mult), r=[bS, bs["tot"]], w=[bS])
                P.dve(lambda h: h.tensor_tensor(out=S[:, :], in0=S[:, :], in1=pst[:, :], op=ALU.add), r=[bS, b_pst], w=[bS])

            def state_update(S, bS, blk, d):
                pst, b_pst = su_a(blk, d)
                su_b(S, bS, blk, d, pst, b_pst)

            order = [1, 0] + list(range(17, 1, -1))
            nxt = su_a(order[0], 1)
            for i, blk in enumerate(order):
                cur = nxt
                if i + 1 < len(order):
                    nxt = su_a(order[i + 1], 1)
                if blk >= 2:
                    P.act(lambda h, blk=blk: h.activation(out=s_Sbin[:, blk - 2, :], in_=s_Sb[:, :], func=AF.Copy), r=[bs["Sb"]], pw=[bs["Sbin"]])
                su_b(s_Sb, bs["Sb"], blk, 1, cur[0], cur[1])
            groups = [(half, d) for half in range(2) for d in range(2)]
            st = {}

            def F1(blk):
                cols = slice(blk * 128, (blk + 1) * 128)
                P.act(lambda h: h.activation(out=s_Sfb[:, :], in_=s_Sf[:, :], func=AF.Copy), r=[bs["Sf"]], w=[bs["Sfb"]])
                state_update(s_Sf, bs["Sf"], blk, 0)
                pcb, b_pcb = bank()
                P.pe(lambda h: h.matmul(pcb[:, 0:128], lhsT=s_BT[:, cols], rhs=s_CT[:, cols], start=True, stop=True), r=[bs["BT"], bs["CT"]], w=[b_pcb])
                P.dve(lambda h: h.tensor_tensor(out=s_cbm[0][:, :], in0=pcb[:, 0:128], in1=tri, op=ALU.mult), r=[b_pcb, bs["par"]], w=[bs["cbm0"]])
                P.dve(lambda h: h.tensor_tensor(out=s_cbm[1][:, :], in0=pcb[:, 0:128], in1=trirev, op=ALU.mult), r=[b_pcb, bs["par"]], w=[bs["cbm1"]])
                pabs = []
                for (half, d) in groups:
                    pab, b_pab = bank()
                    pabs.append((pab, b_pab))
                    P.pe(lambda h, pab=pab, d=d: h.matmul(pab[:, :], lhsT=ident[:, :], rhs=(mnext if d == 0 else mprev)[:, :], start=True, stop=False),
                         r=[b_const], w=[b_pab])
                    for ci in range(4):
                        col = d * 8 + half * 4 + ci
                        P.pe(lambda h, pab=pab, ci=ci, col=col, d=d: h.matmul(
                            pab[:, ci * 128:(ci + 1) * 128], lhsT=s_a[:, blk, col:col + 1].to_broadcast([128, 128]),
                            rhs=(tri if d == 0 else trirev), start=False, stop=(ci == 3)),
                            r=[bs["a"], bs["par"]], pw=[b_pab])
                pz, b_pz = bank()
                tgz = 1 + (blk - 2) // 4
                for k in range(8):
                    P.pe(lambda h, k=k: h.matmul(pz[:, :], lhsT=uTa[:, k, cols], rhs=s_wz[:, k, :], start=(k == 0), stop=(k == 7)),
                         r=[bs["wz"], b_uTa[tgz]], **({"w": [b_pz]} if k == 0 else {"pw": [b_pz]}))
                args = [None] * 4
                Ms = [None] * 4

                def emit_exp(gi):
                    half, d = groups[gi]
                    pab, b_pab = pabs[gi]
                    ai = sc_["arg"] % 2
                    sc_["arg"] += 1
                    arg, barg = s_arg[ai], bs[f"arg{ai}"]
                    args[gi] = (arg, barg)
                    for ci in range(4):
                        col = d * 8 + half * 4 + ci
                        P.act(lambda h, ci=ci, col=col: h.activation(
                            out=arg[:, ci, :], in_=pab[:, ci * 128:(ci + 1) * 128], func=AF.Exp, bias=s_dt[:, blk, col:col + 1], scale=1.0),
                            r=[b_pab, bs["dt"]], **({"w": [barg]} if ci == 0 else {"pw": [barg]}))

                def emit_mul(gi):
                    half, d = groups[gi]
                    arg, barg = args[gi]
                    mi = sc_["M"] % 4
                    sc_["M"] += 1
                    Mt, bM = s_M[mi], bs[f"M{mi}"]
                    Ms[gi] = (Mt, bM)
                    P.dve(lambda h: h.tensor_tensor(
                        out=Mt[:, :, :], in0=arg[:, :, :], in1=s_cbm[d][:, :].unsqueeze(1).to_broadcast([128, 4, 128]), op=ALU.mult),
                        r=[barg, bs[f"cbm{d}"]], w=[bM])

                emit_exp(0)
                emit_exp(1)
                emit_mul(0)
                emit_exp(2)
                emit_mul(1)
                emit_exp(3)
                emit_mul(2)
                emit_mul(3)
                P.act(lambda h: h.activation(out=s_yb[:, :], in_=pz[:, :], func=AF.Exp, scale=-1.0), r=[b_pz], w=[bs["yb"]])
                P.act(lambda h: h.activation(out=s_yb[:, :], in_=s_yb[:, :], func=AF.Ln, bias=s_one[:, 0:1], scale=1.0), r=[bs["yb"], bs["t2"]], w=[bs["yb"]])
                P.act(lambda h: h.activation(out=s_yb[:, :], in_=s_yb[:, :], func=AF.Exp, scale=-1.0), r=[bs["yb"]], w=[bs["yb"]])
                P.dve(lambda h: h.tensor_tensor(out=s_gz[:, :], in0=s_yb[:, :], in1=pz[:, :], op=ALU.mult), r=[bs["yb"], b_pz], w=[bs["gz"]])
                pof, b_pof = bank()
                P.pe(lambda h: h.matmul(pof[:, :], lhsT=s_CT[:, cols], rhs=s_Sfb[:, :], start=True, stop=True), r=[bs["CT"], bs["Sfb"]], w=[b_pof])
                pob, b_pob = bank()
                P.pe(lambda h: h.matmul(pob[:, :], lhsT=s_CT[:, cols], rhs=s_Sbin[:, blk - 2, :], start=True, stop=True), r=[bs["CT"], bs["Sbin"]], w=[b_pob])
                P.dve(lambda h: h.tensor_tensor(out=s_ya[:, :].rearrange("p (a b) -> p a b", a=8), in0=pof[:, :].rearrange("p (a b) -> p a b", a=8),
                                                in1=bc8(s_E[:, blk, 0:8]), op=ALU.mult), r=[b_pof, bs["E"]], w=[bs["ya"]])
                P.dve(lambda h: h.tensor_tensor(out=s_yb[:, :].rearrange("p (a b) -> p a b", a=8), in0=pob[:, :].rearrange("p (a b) -> p a b", a=8),
                                                in1=bc8(s_E[:, blk, 8:16]), op=ALU.mult), r=[b_pob, bs["E"]], w=[bs["yb"]])
                P.dve(lambda h: h.tensor_tensor(out=s_ya[:, :], in0=s_ya[:, :], in1=s_yb[:, :], op=ALU.add), r=[bs["ya"], bs["yb"]], w=[bs["ya"]])
                P.dve(lambda h: h.tensor_tensor(out=s_yb[:, :].rearrange("p (a b) -> p a b", a=8), in0=s_xtok[:, blk, :].rearrange("p (a b) -> p a b", a=8),
                                                in1=bc8(s_sb[:, 32:40]), op=ALU.mult), r=[bs["xtok"], bs["par"]], w=[bs["yb"]])
                P.dve(lambda h: h.tensor_tensor(out=s_ya[:, :], in0=s_ya[:, :], in1=s_yb[:, :], op=ALU.add), r=[bs["ya"], bs["yb"]], w=[bs["ya"]])
                st[blk] = Ms

            def F2(blk):
                Ms = st.pop(blk)
                pyd, b_pyd = bank()
                for half in range(2):
                    for ci in range(4):
                        hh = half * 4 + ci
                        for d in range(2):
                            Mt, bM = Ms[half * 2 + d]
                            P.pe(lambda h, Mt=Mt, ci=ci, hh=hh, d=d: h.matmul(
                                pyd[:, hh * 64:(hh + 1) * 64], lhsT=Mt[:, ci, :], rhs=s_xtok[:, blk, hh * 64:(hh + 1) * 64], start=(d == 0), stop=(d == 1)),
                                r=[bM, bs["xtok"]], **({"w": [b_pyd]} if (half == 0 and ci == 0 and d == 0) else {"pw": [b_pyd]}))
                P.dve(lambda h: h.tensor_tensor(out=s_ya[:, :], in0=s_ya[:, :], in1=pyd[:, :], op=ALU.add), r=[bs["ya"], b_pyd], w=[bs["ya"]])
                P.dve(lambda h: h.tensor_tensor(out=s_ya[:, :], in0=s_ya[:, :], in1=s_gz[:, :], op=ALU.mult), r=[bs["ya"], bs["gz"]], w=[bs["ya"]])
                P.act(lambda h: h.activation(out=s_yb[:, :], in_=s_ya[:, :], func=AF.Square, accum_out=s_ss[:, 0:1]), r=[bs["ya"]], w=[bs["yb"], bs["ss"]])
                P.act(lambda h: h.activation(out=s_ss[:, 1:2], in_=s_ss[:, 0:1], func=AF.Ln, scale=1.0 / 512.0, bias=epsv[:, 0:1]), r=[bs["ss"], b_c2], w=[bs["ss"]])
                P.act(lambda h: h.activation(out=s_ss[:, 2:3], in_=s_ss[:, 1:2], func=AF.Exp, scale=-0.5), r=[bs["ss"]], w=[bs["ss"]])
                yn, byn = s_yn[blk % 2], bs[f"yn{blk % 2}"]
                P.dve(lambda h: h.tensor_scalar(out=yn[:, :], in0=s_ya[:, :], scalar1=s_ss[:, 2:3], scalar2=None, op0=ALU.mult),
                      r=[bs["ya"], bs["ss"]], w=[byn])

            def T_(blk):
                j = (blk - 2) % 4
                yn, byn = s_yn[blk % 2], bs[f"yn{blk % 2}"]
                for c in range(4):
                    ptr, b_ptr = bank()
                    P.pe(lambda h, ptr=ptr, c=c: h.matmul(ptr[:, 0:128], lhsT=yn[:, c * 128:(c + 1) * 128], rhs=ident[:, :], start=True, stop=True), r=[byn, b_const], w=[b_ptr])
                    P.act(lambda h, ptr=ptr, c=c: h.activation(out=s_ynT[:, c, j * 128:(j + 1) * 128], in_=ptr[:, 0:128], func=AF.Copy), r=[b_ptr],
                          **({"w": [bs["ynT"]]} if (c == 0 and j == 0) else {"pw": [bs["ynT"]]}))
                if j != 3:
                    return
                tgi = 1 + (blk - 2) // 4
                q0 = TGS[tgi][0]
                for kd in range(8):
                    py, b_py = bank()
                    for c in range(4):
                        P.pe(lambda h, py=py, c=c, kd=kd: h.matmul(py[:, :], lhsT=s_wso[:, c, kd * 128:(kd + 1) * 128], rhs=s_ynT[:, c, :], start=(c == 0), stop=(c == 3)),
                             r=[bs["wso"], bs["ynT"]], **({"w": [b_py]} if c == 0 else {"pw": [b_py]}))
                    P.dve(lambda h, py=py, kd=kd: h.scalar_tensor_tensor(
                        out=hs[:, kd, q0:q0 + 512], in0=py[:, :], scalar=mod(layer, 2, kd, rr_l), in1=hs[:, kd, q0:q0 + 512], op0=ALU.mult, op1=ALU.add),
                        r=[b_py, b_hs[0][tgi], b_modv[layer]], w=[b_hs[0][tgi]])

            state_update(s_Sf, bs["Sf"], 0, 0)
            state_update(s_Sf, bs["Sf"], 1, 0)
            F1(2)
            F2(2)
            for blk in range(3, 18):
                F1(blk)
                T_(blk - 1)
                F2(blk)
            T_(17)
            P.barrier()
        for tgi in range(1, len(TGS)):
            layer_norm(tgi, layer, 0)
        P.barrier()

    b_out = P.buf("outst")
    for b in range(nseq):
        for k in range(8):
            P.sp(lambda h, k=k, b=b: h.dma_start(out=hs[:, k, 0:CTX], in_=cxT[b, k]), dma_w=b_hs[0][0])
            for tgi in range(1, 5):
                t0, n, _ = TGS[tgi]
                P.sp(lambda h, k=k, b=b, t0=t0, n=n: h.dma_start(out=hs[:, k, t0:t0 + n], in_=xT[b, k, :, t0 - CTX:t0 - CTX + n]),
                     dma_w=b_hs[0][tgi])
        for tgi, (t0, n, _) in enumerate(TGS):
            P.dve(lambda h, t0=t0, n=n: h.tensor_scalar(out=hs[:, :, t0:t0 + n], in0=hs[:, :, t0:t0 + n], scalar1=ALPHA, scalar2=None, op0=ALU.mult),
                  r=[b_hs[0][tgi]], w=[b_hs[0][tgi]])
        for layer in range(nlayers):
            last = layer == DEPTH - 1
            flags = dbg if isinstance(dbg, dict) else {}
            if layer % 2 == 0:
                if flags.get("att", True):
                    attention(layer, b)
            else:
                ssm(layer, b)
            for tgi, (t0, n, isctx) in enumerate(TGS):
                if last and isctx:
                    continue
                if flags.get("mlp", True):
                    mlp(tgi, layer, 2 if isctx else b)
                if flags.get("ln", True):
                    layer_norm(tgi, layer, 2, final=last)
            P.barrier()
        if dbg is not None:
            for k in range(8):
                P.sp(lambda h, k=k, b=b: h.dma_start(out=dbgT[b, k], in_=hs[:, k, :]), r=b_hs[0], dma_r=b_out)
        for k in range(8):
            P.sp(lambda h, k=k, b=b: h.dma_start(out=outT[b, k], in_=hs[:, k, CTX:T]), r=b_hs[0], dma_r=b_out)
        P.barrier()
    counts = P.finish([b_out])
    return nc, counts


def _rope_perm():
    idx = np.arange(64)
    a = idx // 32
    half = (idx % 32) // 16
    j = idx % 16
    return a * 32 + (1 - half) * 16 + j


def host_constants():
    bf = ml_dtypes.bfloat16
    c = {}
    c["ident"] = np.eye(128, dtype=np.float32).astype(bf)
    jj = np.arange(128)[:, None]
    ii = np.arange(128)[None, :]
    mp = np.where(jj >= ii, 0.0, NEG).astype(np.float32)
    mn = np.where(jj <= ii, 0.0, NEG).astype(np.float32)
    c["mprev"] = np.tile(mp, (1, 4)).astype(bf)
    c["mnext"] = np.tile(mn, (1, 4)).astype(bf)
    t = np.arange(SEQ)
    row = (t // 64).astype(np.float32)
    col = (t % 64).astype(np.float32)
    inv = (10000.0 ** (-np.arange(0, 32, 2, dtype=np.float32) / 32)).astype(np.float32)
    cosT = np.zeros((64, SEQ), np.float32)
    sinT = np.zeros((64, SEQ), np.float32)
    for a, pos in enumerate((row, col)):
        ang = (pos[None, :] * inv[:, None]).astype(np.float32)
        for half in range(2):
            sl = slice(a * 32 + half * 16, a * 32 + half * 16 + 16)
            cosT[sl] = np.cos(ang)
            sinT[sl] = np.sin(ang) * (-1.0 if half == 0 else 1.0)
    kk = np.arange(128)[:, None]
    ll = np.arange(128)[None, :]
    c["tri"] = np.concatenate([(kk <= ll), (kk >= ll)], axis=1).astype(np.float32)
    c["cossin"] = np.concatenate([cosT, sinT], axis=0)
    return c


def host_weights(inp):
    w = {}
    f = np.float32
    wm = np.asarray(inp["w_mod"], f)
    w["wmod"] = np.ascontiguousarray(wm.reshape(DEPTH, 8, 128, 12, 512).transpose(0, 3, 2, 1, 4))
    w["bmod"] = np.ascontiguousarray(np.asarray(inp["b_mod"], f).reshape(DEPTH, 48, 128).transpose(0, 2, 1))
    lnv = np.stack([np.asarray(inp[k], f) for k in ("ln_mix_g", "ln_mix_b", "ln_ff_g", "ln_ff_b")], axis=1)
    w["lnv"] = np.ascontiguousarray(lnv.reshape(DEPTH, 4, 8, 128).transpose(3, 0, 1, 2).reshape(128, DEPTH * 4 * 8))
    w["sinkb"] = np.ascontiguousarray(np.broadcast_to(np.asarray(inp["att_sink"], f).reshape(1, 16), (128, 16)))
    win = np.asarray(inp["att_w_in"], f)[0]
    perm = _rope_perm()
    wq = win[:, :1024].reshape(8, 128, 4, 4, 64)
    wqq = np.concatenate([wq, wq[..., perm]], axis=-1)
    w["wqq"] = np.ascontiguousarray(wqq.transpose(2, 1, 0, 3, 4).reshape(4, 128, 8, 512))
    wk = win[:, 1024:1280].reshape(8, 128, 4, 64)
    wv = win[:, 1280:1536].reshape(8, 128, 4, 64)
    wkv = np.concatenate([wk, wk[..., perm], wv], axis=-1)
    w["wkv"] = np.ascontiguousarray(wkv.transpose(2, 1, 0, 3))
    wo = np.asarray(inp["att_w_out"], f)[0].reshape(4, 2, 2, 64, 1024)
    w["wo"] = np.ascontiguousarray(wo.transpose(0, 2, 3, 1, 4).reshape(4, 128, 2, 1024))
    sw = np.asarray(inp["ssm_w_in"], f)[0]
    wx = sw[:, 2048:4096].reshape(8, 128, 4, 512)
    wB = sw[:, 4096:4608].reshape(8, 128, 4, 128)
    wC = sw[:, 4608:5120].reshape(8, 128, 4, 128)
    w["wxbc"] = np.ascontiguousarray(np.concatenate([wx, wB, wC], axis=-1).transpose(2, 1, 0, 3))
    w["wz"] = np.ascontiguousarray(sw[:, 0:2048].reshape(8, 128, 4, 512).transpose(2, 1, 0, 3))
    wd = sw[:, 5120:5184].reshape(8, 128, 2, 4, 8)
    w["wdt"] = np.ascontiguousarray(wd.transpose(3, 1, 0, 2, 4).reshape(4, 128, 8, 16))
    w["wso"] = np.ascontiguousarray(np.asarray(inp["ssm_w_out"], f)[0].reshape(4, 4, 128, 1024).transpose(0, 2, 1, 3))
    cw = np.asarray(inp["ssm_conv_w"], f)[0]
    cb = np.asarray(inp["ssm_conv_b"], f)[0]
    convw = np.zeros((4, 128, 6, 6), f)
    for g in range(4):
        chans = [np.arange(g * 512 + c * 128, g * 512 + (c + 1) * 128) for c in range(4)]
        chans.append(np.arange(2048 + g * 128, 2048 + (g + 1) * 128))
        chans.append(np.arange(2560 + g * 128, 2560 + (g + 1) * 128))
        for c, ch in enumerate(chans):
            convw[g, :, c, 0:5] = cw[:, ch].T
            convw[g, :, c, 5] = cb[ch]
    w["convw"] = convw.reshape(4, 128, 36)
    dtb = np.asarray(inp["ssm_dt_bias"], f)[0].reshape(2, 4, 8)
    alog = np.asarray(inp["ssm_a_log"], f)[0].reshape(2, 4, 8)
    dsk = np.asarray(inp["ssm_d"], f)[0].reshape(4, 8)
    ssmb = np.zeros((4, 128, 40), f)
    for g in range(4):
        ssmb[g, :, 0:16] = dtb[:, g, :].reshape(1, 16)
        ssmb[g, :, 16:32] = alog[:, g, :].reshape(1, 16)
        ssmb[g, :, 32:40] = dsk[g].reshape(1, 8)
    w["ssmb"] = ssmb
    ng = np.asarray(inp["ssm_norm_g"], f)[0].reshape(4, 4, 128)
    w["normg"] = np.ascontiguousarray(ng.transpose(0, 2, 1))
    w1 = np.asarray(inp["ff_w1"], f).reshape(DEPTH, 8, 128, 8, 512)
    w["w1"] = np.ascontiguousarray(w1.transpose(0, 3, 2, 1, 4))
    w2 = np.asarray(inp["ff_w2"], f).reshape(DEPTH, 32, 128, 8, 128)
    w["w2"] = np.ascontiguousarray(w2.transpose(0, 3, 2, 1, 4))
    return w


def host_core_inputs(inp, core):
    f = np.float32
    b0 = core * BLOC
    x = np.asarray(inp["x"], f)[b0:b0 + BLOC]
    ctx = np.asarray(inp["ctx"], f)[b0:b0 + BLOC]
    d = {}
    d["xT"] = np.ascontiguousarray(x.transpose(0, 2, 1).reshape(BLOC, 8, 128, SEQ))
    d["cxT"] = np.ascontiguousarray(ctx.transpose(0, 2, 1).reshape(BLOC, 8, 128, CTX))
    cc = np.concatenate([np.asarray(inp["c"], f)[b0:b0 + BLOC], np.asarray(inp["c_ctx"], f)[None]], axis=0)
    d["cT"] = np.ascontiguousarray(cc.reshape(3, 8, 128).transpose(2, 1, 0))
    return d


_CACHE = {}


def kernel(**inputs):
    if "nc" not in _CACHE:
        _CACHE["nc"] = build_program()[0]
    nc = _CACHE["nc"]
    shared = {}
    shared.update(host_constants())
    shared.update(host_weights(inputs))
    in_maps = []
    for core in range(NCORE):
        m = dict(shared)
        m.update(host_core_inputs(inputs, core))
        in_maps.append(m)
    res = run_bass_kernel_spmd(nc, in_maps, core_ids=list(range(NCORE)))
    outs = []
    for core in range(NCORE):
        oT = np.asarray(res.results[core]["outT"]).reshape(BLOC, D, SEQ)
        outs.append(oT.transpose(0, 2, 1))
    return np.ascontiguousarray(np.concatenate(outs, axis=0)).astype(np.float32)
```

```python
from contextlib import ExitStack
import numpy as np
import ml_dtypes
import concourse.bass as bass
import concourse.mybir as mybir
from concourse.bass_utils import run_bass_kernel_spmd

F32 = mybir.dt.float32
BF16 = mybir.dt.bfloat16
AF = mybir.ActivationFunctionType
ALU = mybir.AluOpType
AX = mybir.AxisListType

ENGS = ("pe", "act", "dve", "pool", "sp")

D = 1024
SEQ = 2048
CTX = 256
T = SEQ + CTX
NCORE = 8
BLOC = 2
DEPTH = 2
ALPHA = (2.0 * DEPTH) ** 0.25
LN_EPS = 1e-5
RMS_EPS = 1e-5
NEG = -30000.0
TGS = [(0, 256, True), (256, 512, False), (768, 512, False), (1280, 512, False), (1792, 512, False)]


class Buf:
    __slots__ = ("name", "writers", "readers", "prev_readers", "sem", "dcount", "excl")

    def __init__(self, name, excl=False):
        self.name = name
        self.excl = excl
        self.writers = {}
        self.readers = {}
        self.prev_readers = {}
        self.sem = None
        self.dcount = 0


class Op:
    __slots__ = ("emit", "deps", "signal", "dma", "isnop")

    def __init__(self, emit, deps, dma, isnop=False):
        self.emit = emit
        self.deps = deps
        self.signal = False
        self.dma = dma
        self.isnop = isnop


class Prog:
    def __init__(self, nc):
        self.nc = nc
        self.ops = {e: [] for e in ENGS}
        self.seen = {e: {} for e in ENGS}
        self.dma_bufs = []
        self.nbuf = 0
        self.ntens = 0

    def sb(self, name, shape, dt, off):
        self.ntens += 1
        return self.nc.alloc_sbuf_tensor_at(f"{name}_{self.ntens}", list(shape), dt, offset=off + 16512)

    def buf(self, name=None):
        self.nbuf += 1
        return Buf(name or f"b{self.nbuf}")

    def add(self, eng, emit, r=(), w=(), pw=(), dma_w=None, dma_r=None):
        ops = self.ops[eng]
        idx = len(ops)
        deps = {}

        def need(k, v):
            if deps.get(k, -1) < v:
                deps[k] = v

        mykey = ("E", eng)
        allr = list(r) + ([dma_r] if dma_r is not None else [])
        allw = list(w) + ([dma_w] if dma_w is not None else [])
        for b in allr:
            for k, v in b.writers.items():
                need(k, v)
            if b.excl:
                for k, v in b.readers.items():
                    if k != mykey:
                        need(k, v)
        for b in allw:
            for k, v in b.readers.items():
                need(k, v)
            for k, v in b.writers.items():
                if k == mykey and eng == "pe":
                    continue
                if dma_w is not None and k == ("D", dma_w):
                    continue
                need(k, v)
            if not b.readers:
                for k, v in b.prev_readers.items():
                    need(k, v)
        for b in pw:
            for k, v in b.readers.items():
                need(k, v)
            for k, v in b.prev_readers.items():
                need(k, v)
            for k, v in b.writers.items():
                if k[0] == "D":
                    need(k, v)
        seen = self.seen[eng]
        fdeps = {}
        for k, v in deps.items():
            if seen.get(k, -1) >= v:
                continue
            seen[k] = v
            fdeps[k] = v
            if k[0] == "E":
                self.ops[k[1]][v].signal = True
        is_dma = (dma_w is not None) or (dma_r is not None)
        dbuf = dma_w if dma_w is not None else dma_r
        op = Op(emit, fdeps, dbuf if is_dma else None)
        ops.append(op)
        if is_dma:
            if dbuf.sem is None:
                dbuf.sem = True
                self.dma_bufs.append(dbuf)
            dbuf.dcount += 16
            ev = (("D", dbuf), dbuf.dcount)
        else:
            ev = (mykey, idx)
        for b in allr:
            if b.readers.get(ev[0], -1) < ev[1]:
                b.readers[ev[0]] = ev[1]
        for b in allw:
            if b.readers:
                b.prev_readers = b.readers
            b.readers = {}
            b.writers = {ev[0]: ev[1]}
        for b in pw:
            if b.readers:
                b.prev_readers = b.readers
                b.readers = {}
                b.writers = {}
            if b.writers.get(ev[0], -1) < ev[1]:
                b.writers[ev[0]] = ev[1]
        return op

    def pe(self, emit, **kw): return self.add("pe", emit, **kw)
    def act(self, emit, **kw): return self.add("act", emit, **kw)
    def dve(self, emit, **kw): return self.add("dve", emit, **kw)
    def pool(self, emit, **kw): return self.add("pool", emit, **kw)
    def sp(self, emit, **kw): return self.add("sp", emit, **kw)

    def barrier(self):
        last = {}
        for e in ENGS:
            ops = self.ops[e]
            for i in range(len(ops) - 1, -1, -1):
                if ops[i].dma is None and not ops[i].isnop:
                    last[("E", e)] = i
                    break
        for b in self.dma_bufs:
            last[("D", b)] = b.dcount
        for e in ENGS:
            seen = self.seen[e]
            fdeps = {}
            for k, v in last.items():
                if k == ("E", e):
                    continue
                if seen.get(k, -1) >= v:
                    continue
                seen[k] = v
                fdeps[k] = v
                if k[0] == "E":
                    self.ops[k[1]][v].signal = True
            if fdeps:
                self.ops[e].append(Op(lambda h: h.nop(), fdeps, None, True))

    def finish(self, final_bufs):
        nc = self.nc
        with ExitStack() as es:
            sems = {e: es.enter_context(nc.semaphore(f"s_{e}")) for e in ENGS}
            for i, b in enumerate(self.dma_bufs):
                b.sem = es.enter_context(nc.semaphore(f"d{i}"))
            sigcnt = {}
            for e in ENGS:
                c = 0
                arr = []
                for op in self.ops[e]:
                    if op.signal:
                        c += 1
                    arr.append(c)
                sigcnt[e] = arr
            fin = [(b.sem, b.dcount) for b in final_bufs]

            def run(e, h):
                for op in self.ops[e]:
                    for k, v in op.deps.items():
                        if k[0] == "E":
                            h.wait_ge(sems[k[1]], sigcnt[k[1]][v])
                        else:
                            h.wait_ge(k[1].sem, v)
                    ins = op.emit(h)
                    if op.dma is not None:
                        ins.then_inc(op.dma.sem, 16)
                    elif op.signal:
                        ins.then_inc(sems[e], 1)
                if e == "sp":
                    for s, v in fin:
                        h.wait_ge(s, v)

            with nc.Block() as block:
                @block.tensor
                def _(h): run("pe", h)

                @block.scalar
                def _(h): run("act", h)

                @block.vector
                def _(h): run("dve", h)

                @block.gpsimd
                def _(h): run("pool", h)

                @block.sync
                def _(h): run("sp", h)
        return {e: len(self.ops[e]) for e in ENGS}


class K:
    pass


def build_program(nseq=BLOC, nlayers=DEPTH, dbg=None):
    nc = bass.Bass("TRN2", target_bir_lowering=False)
    P = Prog(nc)

    def din(name, shape, dt=F32):
        return nc.dram_tensor(name, list(shape), dt, kind="ExternalInput").ap()

    xT = din("xT", [BLOC, 8, 128, SEQ])
    cxT = din("cxT", [BLOC, 8, 128, CTX])
    cT = din("cT", [128, 8, 3])
    wmod = din("wmod", [DEPTH, 12, 128, 8, 512])
    bmod = din("bmod", [DEPTH, 128, 48])
    lnv_d = din("lnv", [128, DEPTH * 4 * 8])
    sink_d = din("sinkb", [128, 16])
    ident_d = din("ident", [128, 128], BF16)
    mprev_d = din("mprev", [128, 512], BF16)
    mnext_d = din("mnext", [128, 512], BF16)
    cs_d = din("cossin", [128, SEQ])
    wqq_d = din("wqq", [4, 128, 8, 512])
    wkv_d = din("wkv", [4, 128, 8, 192])
    wo_d = din("wo", [4, 128, 2, 1024])
    wxbc_d = din("wxbc", [4, 128, 8, 768])
    wz_d = din("wz", [4, 128, 8, 512])
    wdt_d = din("wdt", [4, 128, 8, 16])
    wso_d = din("wso", [4, 128, 4, 1024])
    cw_d = din("convw", [4, 128, 36])
    sb_d = din("ssmb", [4, 128, 40])
    ng_d = din("normg", [4, 128, 4])
    tri_d = din("tri", [128, 256])
    w1_d = din("w1", [DEPTH, 8, 128, 8, 512])
    w2_d = din("w2", [DEPTH, 8, 128, 32, 128])
    outT = nc.dram_tensor("outT", [BLOC, 8, 128, SEQ], F32, kind="ExternalOutput").ap()
    if dbg is not None:
        dbgT = nc.dram_tensor("dbgT", [BLOC, 8, 128, T], F32, kind="ExternalOutput").ap()

    HS_OFF = 0
    CONST_OFF = 73728
    ARENA = 78848
    hs = P.sb("hs", [128, 8, T], F32, HS_OFF)
    b_hs = [[P.buf(f"hs{g}") for g in range(len(TGS))]]

    co = [CONST_OFF]

    def calloc(name, shape, dt):
        n = int(np.prod(shape[1:])) * (4 if dt == F32 else 2)
        t = P.sb(name, shape, dt, co[0])
        co[0] += (n + 31) // 32 * 32
        assert co[0] <= ARENA
        return t

    ident = calloc("ident", [128, 128], BF16)
    ones_f = calloc("ones_f", [128, 128], F32)
    ones_bf = calloc("ones_bf", [128, 128], BF16)
    mprev = calloc("mprev", [128, 512], BF16)
    mnext = calloc("mnext", [128, 512], BF16)
    modv = [calloc(f"modv{i}", [128, 48, 3], F32) for i in range(DEPTH)]
    lnv = calloc("lnv", [128, DEPTH * 4 * 8], F32)
    lnva = calloc("lnva", [128, DEPTH * 4 * 8], F32)
    sinkv = calloc("sinkv", [128, 16], F32)
    epsv = calloc("epsv", [128, 2], F32)
    kmaxp = calloc("kmaxp", [128, 8], F32)
    kmax2 = calloc("kmax2", [128, 1], F32)
    b_const = P.buf("const")
    b_modv = [P.buf(f"modv{i}") for i in range(DEPTH)]
    b_kmax = P.buf("kmax")

    pbank = [nc.alloc_psum_tensor(f"pb{i}", [128, 512], F32) for i in range(8)]
    b_bank = [Buf(f"pb{i}", excl=True) for i in range(8)]
    bank_rr = [0]

    def bank():
        i = bank_rr[0]
        bank_rr[0] = (i + 1) % 8
        return pbank[i], b_bank[i]

    P.sp(lambda h: h.dma_start(out=ident[:, :], in_=ident_d[:, :]), dma_w=b_const)
    P.sp(lambda h: h.dma_start(out=mprev[:, :], in_=mprev_d[:, :]), dma_w=b_const)
    P.sp(lambda h: h.dma_start(out=mnext[:, :], in_=mnext_d[:, :]), dma_w=b_const)
    P.sp(lambda h: h.dma_start(out=lnv[:, :], in_=lnv_d[:, :]), dma_w=b_const)
    P.sp(lambda h: h.dma_start(out=sinkv[:, :], in_=sink_d[:, :]), dma_w=b_const)
    b_c2 = P.buf("const2")
    P.pool(lambda h: h.memset(ones_f[:, :], 1.0 / D), pw=[b_c2])
    P.pool(lambda h: h.memset(ones_bf[:, :], 1.0), pw=[b_c2])
    P.pool(lambda h: h.memset(epsv[:, 0:1], LN_EPS), pw=[b_c2])
    P.pool(lambda h: h.memset(epsv[:, 1:2], 1e-20), pw=[b_c2])
    P.dve(lambda h: h.tensor_scalar(out=lnva[:, :], in0=lnv[:, :], scalar1=ALPHA, scalar2=None, op0=ALU.mult),
          r=[b_const], w=[b_c2])

    def lnvec(layer, which, k, alpha):
        t = lnva if alpha else lnv
        c = (layer * 4 + which) * 8 + k
        return t[:, c:c + 1]

    a_cT = P.sb("cTs", [128, 8, 3], F32, ARENA)
    a_e = P.sb("cTe", [128, 8, 3], F32, ARENA + 128)
    a_sc = P.sb("scT", [128, 8, 3], F32, ARENA + 256)
    a_bm = P.sb("bm", [128, 48], F32, ARENA + 384)
    a_w = [P.sb(f"wm{i}", [128, 8, 512], F32, ARENA + 1024 + i * 16384) for i in range(2)]
    b_cT = P.buf("cT")
    b_sc = P.buf("scT")
    b_bm = P.buf("bm")
    b_w = [P.buf("wm0"), P.buf("wm1")]
    P.sp(lambda h: h.dma_start(out=a_cT[:, :, :], in_=cT[:, :, :]), dma_w=b_cT)
    P.act(lambda h: h.activation(out=a_e[:, :, :], in_=a_cT[:, :, :], func=AF.Exp, scale=-1.0), r=[b_cT], w=[b_sc])
    P.dve(lambda h: h.tensor_scalar(out=a_e[:, :, :], in0=a_e[:, :, :], scalar1=1.0, scalar2=None, op0=ALU.add), r=[b_sc], w=[b_sc])
    P.dve(lambda h: h.reciprocal(out=a_e[:, :, :], in_=a_e[:, :, :]), r=[b_sc], w=[b_sc])
    P.dve(lambda h: h.tensor_tensor(out=a_sc[:, :, :], in0=a_e[:, :, :], in1=a_cT[:, :, :], op=ALU.mult), r=[b_sc, b_cT], w=[b_sc])
    wi = 0
    for i in range(nlayers):
        pm, b_pm = bank()
        P.sp(lambda h, i=i: h.dma_start(out=a_bm[:, :], in_=bmod[i]), dma_w=b_bm)
        for ft in range(12):
            wt, bw = a_w[wi % 2], b_w[wi % 2]
            wi += 1
            P.sp(lambda h, wt=wt, i=i, ft=ft: h.dma_start(out=wt[:, :, :], in_=wmod[i, ft]), dma_w=bw)
            for fc in range(4):
                c = ft * 4 + fc
                for k in range(8):
                    P.pe(lambda h, pm=pm, wt=wt, c=c, fc=fc, k=k: h.matmul(
                        pm[:, c * 3:c * 3 + 3], lhsT=wt[:, k, fc * 128:(fc + 1) * 128], rhs=a_sc[:, k, :],
                        start=(k == 0), stop=(k == 7)),
                        r=[bw, b_sc], **({"w": [b_pm]} if (c == 0 and k == 0) else {"pw": [b_pm]}))
        mv = modv[i]
        P.dve(lambda h, mv=mv, pm=pm: h.tensor_tensor(
            out=mv[:, :, :], in0=pm[:, 0:144].rearrange("p (c r) -> p c r", r=3),
            in1=a_bm[:, :].unsqueeze(2).to_broadcast([128, 48, 3]), op=ALU.add),
            r=[b_pm, b_bm], w=[b_modv[i]])
        for which in (1, 4):
            P.dve(lambda h, mv=mv, which=which: h.tensor_scalar(
                out=mv[:, which * 8:(which + 1) * 8, :], in0=mv[:, which * 8:(which + 1) * 8, :],
                scalar1=1.0, scalar2=1.0 / ALPHA, op0=ALU.add, op1=ALU.mult),
                r=[b_modv[i]], w=[b_modv[i]])

    def mod(layer, which, k, rr):
        return modv[layer][:, which * 8 + k, rr:rr + 1]

    P.barrier()

    LN_OFF = ARENA + 110080

    def layer_norm(tgi, layer, which_g, final=False):
        t0, n, _ = TGS[tgi]
        sq = [P.sb("lnsq", [128, 512], F32, LN_OFF + i * 2048) for i in range(2)]
        b_sq = [K.b_lnsq0, K.b_lnsq1]
        st = [P.sb("lnst", [128, 512], F32, LN_OFF + 4096 + i * 2048) for i in range(4)]
        b_st = K.b_lnst
        bh = b_hs[0][tgi]
        p1, b_p1 = bank()
        p2, b_p2 = bank()
        for k in range(8):
            s, bs = sq[k % 2], b_sq[k % 2]
            P.act(lambda h, s=s, k=k: h.activation(out=s[:, 0:n], in_=hs[:, k, t0:t0 + n], func=AF.Square),
                  r=[bh], w=[bs])
            P.pe(lambda h, k=k: h.matmul(p1[:, 0:n], lhsT=ones_f[:, :], rhs=hs[:, k, t0:t0 + n],
                                         start=(k == 0), stop=(k == 7)),
                 r=[bh, b_c2], **({"w": [b_p1]} if k == 0 else {"pw": [b_p1]}))
            P.pe(lambda h, s=s, k=k: h.matmul(p2[:, 0:n], lhsT=ones_f[:, :], rhs=s[:, 0:n],
                                              start=(k == 0), stop=(k == 7)),
                 r=[bs, b_c2], **({"w": [b_p2]} if k == 0 else {"pw": [b_p2]}))
        mean, var, rstd, nmr = st
        P.act(lambda h: h.activation(out=mean[:, 0:n], in_=p1[:, 0:n], func=AF.Copy), r=[b_p1], w=[b_st])
        P.dve(lambda h: h.tensor_tensor(out=var[:, 0:n], in0=mean[:, 0:n], in1=mean[:, 0:n], op=ALU.mult), r=[b_st], w=[b_st])
        P.dve(lambda h: h.tensor_tensor(out=var[:, 0:n], in0=p2[:, 0:n], in1=var[:, 0:n], op=ALU.subtract), r=[b_st, b_p2], w=[b_st])
        P.act(lambda h: h.activation(out=rstd[:, 0:n], in_=var[:, 0:n], func=AF.Ln, bias=epsv[:, 0:1], scale=1.0), r=[b_st, b_c2], w=[b_st])
        P.act(lambda h: h.activation(out=rstd[:, 0:n], in_=rstd[:, 0:n], func=AF.Exp, scale=-0.5), r=[b_st], w=[b_st])
        P.dve(lambda h: h.scalar_tensor_tensor(out=nmr[:, 0:n], in0=mean[:, 0:n], scalar=-1.0, in1=rstd[:, 0:n],
                                               op0=ALU.mult, op1=ALU.mult), r=[b_st], w=[b_st])
        for k in range(8):
            P.dve(lambda h, k=k: h.tensor_tensor(out=hs[:, k, t0:t0 + n], in0=hs[:, k, t0:t0 + n], in1=rstd[:, 0:n], op=ALU.mult),
                  r=[bh, b_st], w=[bh])
            P.dve(lambda h, k=k: h.tensor_tensor(out=hs[:, k, t0:t0 + n], in0=hs[:, k, t0:t0 + n], in1=nmr[:, 0:n], op=ALU.add),
                  r=[bh, b_st], w=[bh])
            P.act(lambda h, k=k: h.activation(out=hs[:, k, t0:t0 + n], in_=hs[:, k, t0:t0 + n], func=AF.Identity,
                                              scale=lnvec(layer, which_g, k, not final), bias=lnvec(layer, which_g + 1, k, not final)),
                  r=[bh, b_c2], w=[bh])

    K.att_stage = dbg.get("stage", 4) if isinstance(dbg, dict) else 4
    K.b_lnsq0 = P.buf("lnsq0")
    K.b_lnsq1 = P.buf("lnsq1")
    K.b_lnst = P.buf("lnst")

    M_UT = ARENA
    M_HID = ARENA + 16384
    M_W1 = M_HID + 32768
    M_W2 = M_W1 + 16384
    M_RT = M_W2 + 16384
    m_uT = [P.sb("m_uT", [128, 8, 512], BF16, M_UT + i * 8192) for i in range(2)]
    m_hid = P.sb("m_hid", [128, 32, 512], BF16, M_HID)
    m_w1 = [P.sb("m_w1", [128, 8, 512], BF16, M_W1 + i * 8192) for i in range(2)]
    m_w2 = [P.sb("m_w2", [128, 32, 128], BF16, M_W2 + i * 8192) for i in range(2)]
    m_rt = [P.sb("m_rt", [128, 512], F32, M_RT + i * 2048) for i in range(2)]
    assert M_RT + 4096 <= LN_OFF
    bm_uT = [P.buf("m_uT0"), P.buf("m_uT1")]
    bm_hid = P.buf("m_hid")
    bm_w1 = [P.buf("m_w10"), P.buf("m_w11")]
    bm_w2 = [P.buf("m_w20"), P.buf("m_w21")]
    bm_rt = [P.buf("m_rt0"), P.buf("m_rt1")]
    cnt = {"ut": 0, "w1": 0, "w2": 0, "rt": 0}

    def mlp(tgi, layer, rr):
        t0, n, _ = TGS[tgi]
        bh = b_hs[0][tgi]
        ui = cnt["ut"] % 2
        cnt["ut"] += 1
        uT, buT = m_uT[ui], bm_uT[ui]
        for k in range(8):
            P.act(lambda h, k=k: h.activation(out=uT[:, k, 0:n], in_=hs[:, k, t0:t0 + n], func=AF.Identity,
                                              scale=mod(layer, 4, k, rr), bias=mod(layer, 3, k, rr)),
                  r=[bh, b_modv[layer]], **({"w": [buT]} if k == 0 else {"pw": [buT]}))
        for fb in range(8):
            wi_ = cnt["w1"] % 2
            cnt["w1"] += 1
            w1t, bw1 = m_w1[wi_], bm_w1[wi_]
            P.pool(lambda h, w1t=w1t, fb=fb: h.dma_start(out=w1t[:, :, :], in_=w1_d[layer, fb]), dma_w=bw1)
            for fc in range(4):
                pb, b_pb = bank()
                for k in range(8):
                    P.pe(lambda h, pb=pb, w1t=w1t, fc=fc, k=k: h.matmul(
                        pb[:, 0:n], lhsT=w1t[:, k, fc * 128:(fc + 1) * 128], rhs=uT[:, k, 0:n],
                        start=(k == 0), stop=(k == 7)),
                        r=[bw1, buT], **({"w": [b_pb]} if k == 0 else {"pw": [b_pb]}))
                ri = cnt["rt"] % 2
                cnt["rt"] += 1
                rt, brt = m_rt[ri], bm_rt[ri]
                P.dve(lambda h, pb=pb, rt=rt: h.tensor_scalar(out=rt[:, 0:n], in0=pb[:, 0:n], scalar1=0.0, scalar2=None, op0=ALU.max),
                      r=[b_pb], w=[brt])
                f = fb * 4 + fc
                P.act(lambda h, rt=rt, f=f: h.activation(out=m_hid[:, f, 0:n], in_=rt[:, 0:n], func=AF.Square),
                      r=[brt], **({"w": [bm_hid]} if f == 0 else {"pw": [bm_hid]}))
        for dc in range(8):
            wi_ = cnt["w2"] % 2
            cnt["w2"] += 1
            w2t, bw2 = m_w2[wi_], bm_w2[wi_]
            P.pool(lambda h, w2t=w2t, dc=dc: h.dma_start(out=w2t[:, :, :], in_=w2_d[layer, dc]), dma_w=bw2)
            pb, b_pb = bank()
            for kf in range(32):
                P.pe(lambda h, pb=pb, w2t=w2t, kf=kf: h.matmul(
                    pb[:, 0:n], lhsT=w2t[:, kf, :], rhs=m_hid[:, kf, 0:n], start=(kf == 0), stop=(kf == 31)),
                    r=[bw2, bm_hid], **({"w": [b_pb]} if kf == 0 else {"pw": [b_pb]}))
            P.dve(lambda h, pb=pb, dc=dc: h.scalar_tensor_tensor(
                out=hs[:, dc, t0:t0 + n], in0=pb[:, 0:n], scalar=mod(layer, 5, dc, rr), in1=hs[:, dc, t0:t0 + n],
                op0=ALU.mult, op1=ALU.add),
                r=[b_pb, bh, b_modv[layer]], w=[bh])

    A_UT = ARENA
    A_COS = A_UT + 36864
    A_W = A_COS + 16384
    A_KT = A_W + 19456
    A_V = A_KT + 4608
    A_Q = A_V + 4608
    A_OT = A_Q + 8192
    A_PT = A_OT + 4096
    A_TMP = A_PT + 3072
    A_MN = A_TMP + 6144
    A_SK = A_MN + 4096
    A_RD = A_SK + 8192
    A_SQ = A_RD + 2048
    A_OUN = A_SQ + 1024
    A_END = A_OUN + 2048
    assert A_END <= 212832, A_END
    uTa = P.sb("uTa", [128, 8, T], BF16, A_UT)
    cstab = P.sb("cstab", [128, SEQ], F32, A_COS)
    wqq = P.sb("wqq", [128, 8, 512], BF16, A_W)
    wkv = P.sb("wkv", [128, 8, 192], BF16, A_W + 8192)
    wo = P.sb("wo", [128, 2, 1024], BF16, A_W + 11264)
    kTa = P.sb("kTa", [65, T], BF16, A_KT)
    Va = P.sb("Va", [128, 18, 128], BF16, A_V)
    qa = P.sb("qa", [65, 4, 512], BF16, A_Q)
    qpa = P.sb("qpa", [65, 4, 512], BF16, A_Q + 4096)
    OT = P.sb("OT", [128, 2, 512], BF16, A_OT)
    PT = [P.sb("PT", [128, 512], BF16, A_PT + i * 1024) for i in range(3)]
    tmp = [P.sb("atmp", [128, 512], F32, A_TMP + i * 2048) for i in range(3)]
    mneg = P.sb("mneg", [128, 4, 512], BF16, A_MN)
    skt = P.sb("skt", [128, 4, 512], F32, A_SK)
    rden = P.sb("rden", [64, 512], F32, A_RD)
    sqt = P.sb("sqt", [64, 512], BF16, A_SQ)
    oun = P.sb("oun", [64, 512], F32, A_OUN)
    b_oun = P.buf("oun")
    qbc = [0]
    psc = [0]
    pending_norm = []
    b_uTa = [P.buf(f"uTa{g}") for g in range(len(TGS))]
    b_rope = P.buf("rope")
    b_wq, b_wkv, b_wo = P.buf("wq"), P.buf("wkv"), P.buf("wo")
    b_kT, b_V = P.buf("kT"), P.buf("V")
    b_qa, b_qpa, b_OT = P.buf("qa"), P.buf("qpa"), P.buf("OT")
    b_PT = [P.buf(f"PT{i}") for i in range(3)]
    b_tmp = [P.buf(f"atmp{i}") for i in range(3)]
    b_mneg, b_skt, b_rden, b_sqt = P.buf("mneg"), P.buf("skt"), P.buf("rden"), P.buf("sqt")
    ptc = [0]

    def rope(dst, bdst, pa, b_pa, lt0, n, pwflag):
        t1, t2 = tmp[0], tmp[1]
        P.dve(lambda h: h.tensor_tensor(out=t1[0:64, 0:n], in0=pa[0:64, 0:n], in1=cstab[0:64, lt0:lt0 + n], op=ALU.mult),
              r=[b_pa, b_rope], w=[b_tmp[0]])
        P.dve(lambda h: h.tensor_tensor(out=t2[0:64, 0:n], in0=pa[64:128, 0:n], in1=cstab[64:128, lt0:lt0 + n], op=ALU.mult),
              r=[b_pa, b_rope], w=[b_tmp[1]])
        P.dve(lambda h: h.tensor_tensor(out=dst, in0=t1[0:64, 0:n], in1=t2[0:64, 0:n], op=ALU.add),
              r=[b_tmp[0], b_tmp[1]], **({"pw": [bdst]} if pwflag else {"w": [bdst]}))

    def attention(layer, b):
        rr_l = b
        for tgi, (t0, n, isctx) in enumerate(TGS):
            rr = 2 if isctx else rr_l
            for k in range(8):
                P.act(lambda h, k=k, t0=t0, n=n, rr=rr: h.activation(
                    out=uTa[:, k, t0:t0 + n], in_=hs[:, k, t0:t0 + n], func=AF.Identity,
                    scale=mod(layer, 1, k, rr), bias=mod(layer, 0, k, rr)),
                    r=[b_hs[0][tgi], b_modv[layer]], **({"w": [b_uTa[tgi]]} if k == 0 else {"pw": [b_uTa[tgi]]}))
        P.sp(lambda h: h.dma_start(out=cstab[:, :], in_=cs_d[:, :]), dma_w=b_rope)
        for g in range(4):
            P.pool(lambda h, g=g: h.dma_start(out=wqq[:, :, :], in_=wqq_d[g]), dma_w=b_wq)
            P.pool(lambda h, g=g: h.dma_start(out=wkv[:, :, :], in_=wkv_d[g]), dma_w=b_wkv)
            P.pool(lambda h, g=g: h.dma_start(out=wo[:, :, :], in_=wo_d[g]), dma_w=b_wo)
            P.pool(lambda h: h.memset(kTa[64:65, :], 1.0), w=[b_kT])
            P.pool(lambda h: h.memset(Va[:, :, 64:128], 1.0), w=[b_V])
            for tgi, (t0, n, isctx) in enumerate(TGS):
                pk, b_pk = bank()
                for k in range(8):
                    P.pe(lambda h, pk=pk, k=k, t0=t0, n=n: h.matmul(
                        pk[:, 0:n], lhsT=wkv[:, k, 0:128], rhs=uTa[:, k, t0:t0 + n], start=(k == 0), stop=(k == 7)),
                        r=[b_wkv, b_uTa[tgi]], **({"w": [b_pk]} if k == 0 else {"pw": [b_pk]}))
                if isctx:
                    P.act(lambda h, pk=pk, t0=t0, n=n: h.activation(out=kTa[0:64, t0:t0 + n], in_=pk[0:64, 0:n], func=AF.Copy),
                          r=[b_pk], pw=[b_kT])
                else:
                    rope(kTa[0:64, t0:t0 + n], b_kT, pk, b_pk, t0 - CTX, n, True)
                P.act(lambda h, t0=t0, n=n: h.activation(out=sqt[:, 0:n], in_=kTa[0:64, t0:t0 + n], func=AF.Square),
                      r=[b_kT], w=[b_sqt])
                pn, b_pn = bank()
                P.pe(lambda h, pn=pn, n=n: h.matmul(pn[:, 0:n], lhsT=ones_bf[0:64, :], rhs=sqt[:, 0:n], start=True, stop=True),
                     r=[b_sqt, b_c2], w=[b_pn])
                P.dve(lambda h, pn=pn, n=n, tgi=tgi: h.tensor_reduce(out=kmaxp[:, tgi:tgi + 1], in_=pn[:, 0:n], axis=AX.X, op=ALU.max),
                      r=[b_pn], **({"w": [b_kmax]} if tgi == 0 else {"pw": [b_kmax]}))
                for bi in range(n // 128):
                    blk = t0 // 128 + bi
                    pv, b_pv = bank()
                    for k in range(8):
                        P.pe(lambda h, pv=pv, k=k, blk=blk: h.matmul(
                            pv[:, 0:64], lhsT=uTa[:, k, blk * 128:(blk + 1) * 128], rhs=wkv[:, k, 128:192],
                            start=(k == 0), stop=(k == 7)),
                            r=[b_wkv, b_uTa[tgi]], **({"w": [b_pv]} if k == 0 else {"pw": [b_pv]}))
                    P.act(lambda h, pv=pv, blk=blk: h.activation(out=Va[:, blk, 0:64], in_=pv[:, 0:64], func=AF.Copy),
                          r=[b_pv], pw=[b_V])
            P.dve(lambda h: h.tensor_reduce(out=kmax2[:, :], in_=kmaxp[:, 0:5], axis=AX.X, op=ALU.max), r=[b_kmax], w=[b_kmax])
            P.dve(lambda h: h.tensor_scalar(out=kmax2[:, :], in0=kmax2[:, :], scalar1=1.05, scalar2=None, op0=ALU.mult), r=[b_kmax], w=[b_kmax])
            for tgi, (t0, n, isctx) in enumerate(TGS):
                if K.att_stage < 2:
                    break
                pns = {}

                def q_chain(r):
                    pn, b_pn = pns[r]
                    t3 = tmp[2]
                    P.act(lambda h, pn=pn, n=n: h.activation(out=t3[:, 0:n], in_=pn[:, 0:n], func=AF.Ln, scale=kmax2[:, 0:1], bias=epsv[:, 1:2]),
                          r=[b_pn, b_kmax, b_c2], w=[b_tmp[2]])
                    P.act(lambda h, n=n: h.activation(out=t3[:, 0:n], in_=t3[:, 0:n], func=AF.Exp, scale=0.5), r=[b_tmp[2]], w=[b_tmp[2]])
                    P.dve(lambda h, r=r, n=n: h.tensor_scalar(out=mneg[:, r, 0:n], in0=t3[:, 0:n], scalar1=-1.0, scalar2=None, op0=ALU.mult),
                          r=[b_tmp[2]], **({"w": [b_mneg]} if r == 0 else {"pw": [b_mneg]}))
                    hd = g * 4 + r
                    P.act(lambda h, r=r, n=n, hd=hd: h.activation(out=skt[64:128, r, 0:n], in_=mneg[64:128, r, 0:n], func=AF.Exp,
                                                                   scale=0.125, bias=sinkv[64:128, hd:hd + 1]),
                          r=[b_mneg, b_const], **({"w": [b_skt]} if r == 0 else {"pw": [b_skt]}))

                for r in range(4):
                    pq, b_pq = bank()
                    for k in range(8):
                        P.pe(lambda h, pq=pq, k=k, r=r, t0=t0, n=n: h.matmul(
                            pq[:, 0:n], lhsT=wqq[:, k, r * 128:(r + 1) * 128], rhs=uTa[:, k, t0:t0 + n], start=(k == 0), stop=(k == 7)),
                            r=[b_wq, b_uTa[tgi]], **({"w": [b_pq]} if k == 0 else {"pw": [b_pq]}))
                    P.act(lambda h, pq=pq, r=r, n=n: h.activation(out=qpa[0:64, r, 0:n], in_=pq[0:64, 0:n], func=AF.Copy),
                          r=[b_pq], **({"w": [b_qpa]} if r == 0 else {"pw": [b_qpa]}))
                    if not isctx:
                        rope(qa[0:64, r, 0:n], b_qa, pq, b_pq, t0 - CTX, n, r != 0)
                    P.act(lambda h, r=r, n=n: h.activation(out=sqt[:, 0:n], in_=qpa[0:64, r, 0:n], func=AF.Square),
                          r=[b_qpa], w=[b_sqt])
                    pn, b_pn = bank()
                    pns[r] = (pn, b_pn)
                    P.pe(lambda h, pn=pn, n=n: h.matmul(pn[:, 0:n], lhsT=ones_bf[0:64, :], rhs=sqt[:, 0:n], start=True, stop=True),
                         r=[b_sqt, b_c2], w=[b_pn])
                    if r >= 1:
                        q_chain(r - 1)
                q_chain(3)
                P.dve(lambda h, n=n: h.tensor_copy(out=qpa[64:65, :, 0:n], in_=mneg[64:65, :, 0:n]), r=[b_mneg], pw=[b_qpa])
                if not isctx:
                    P.dve(lambda h, n=n: h.tensor_copy(out=qa[64:65, :, 0:n], in_=mneg[64:65, :, 0:n]), r=[b_mneg], pw=[b_qa])
                if K.att_stage < 3:
                    continue
                for qb in range(n // 128):
                    qs = slice(qb * 128, (qb + 1) * 128)
                    if isctx:
                        chunks = [(0, qpa, b_qpa, None), (1, qpa, b_qpa, None)]
                    else:
                        j = (t0 - CTX) // 128 + qb
                        chunks = []
                        if j > 0:
                            chunks.append((2 + j - 1, qa, b_qa, mprev))
                        chunks.append((2 + j, qa, b_qa, None))
                        if j < 15:
                            chunks.append((2 + j + 1, qa, b_qa, mnext))
                        chunks += [(0, qpa, b_qpa, None), (1, qpa, b_qpa, None)]
                    po, b_po = pbank[6 + qbc[0] % 2], b_bank[6 + qbc[0] % 2]
                    qbc[0] += 1
                    nch = len(chunks)
                    pss = [None] * nch

                    def emit_norm(po=po, b_po=b_po, qs=qs, qb=qb):
                        P.dve(lambda h, po=po, qs=qs: h.tensor_tensor(out=rden[0:64, :].rearrange("p (r q) -> p r q", r=4),
                                                                      in0=po[64:128, :].rearrange("p (r q) -> p r q", r=4),
                                                                      in1=skt[64:128, :, qs], op=ALU.add),
                              r=[b_po, b_skt], w=[b_rden])
                        P.dve(lambda h, po=po: h.tensor_copy(out=oun[0:64, :], in_=po[0:64, :]), r=[b_po], w=[b_oun])
                        P.act(lambda h: h.activation(out=rden[0:64, :], in_=rden[0:64, :], func=AF.Ln), r=[b_rden], w=[b_rden])
                        P.act(lambda h: h.activation(out=rden[0:64, :], in_=rden[0:64, :], func=AF.Exp, scale=-1.0), r=[b_rden], w=[b_rden])
                        for par in range(2):
                            P.dve(lambda h, qs=qs, par=par: h.tensor_tensor(
                                out=OT[par * 64:(par + 1) * 64, :, qs],
                                in0=oun[0:64, :].rearrange("p (a b q) -> p a b q", a=2, b=2)[:, :, par, :],
                                in1=rden[0:64, :].rearrange("p (a b q) -> p a b q", a=2, b=2)[:, :, par, :], op=ALU.mult),
                                r=[b_oun, b_rden], **({"w": [b_OT]} if (qb == 0 and par == 0) else {"pw": [b_OT]}))


                    def emit_S(ci):
                        kb, qt, bqt, msk = chunks[ci]
                        bi_ = psc[0] % 6
                        psc[0] += 1
                        ps_, b_ps = pbank[bi_], b_bank[bi_]
                        pss[ci] = (ps_, b_ps)
                        P.pe(lambda h, ps_=ps_, kb=kb, qt=qt, qs=qs, msk=msk: h.matmul(
                            ps_[:, :], lhsT=kTa[0:65, kb * 128:(kb + 1) * 128], rhs=qt[0:65, :, qs],
                            start=True, stop=(msk is None)),
                            r=[b_kT, bqt], w=[b_ps])
                        if msk is not None:
                            P.pe(lambda h, ps_=ps_, msk=msk: h.matmul(ps_[:, :], lhsT=ident[:, :], rhs=msk[:, :], start=False, stop=True),
                                 r=[b_const], pw=[b_ps])

                    emit_S(0)
                    for ci in range(nch):
                        if ci + 1 < nch:
                            emit_S(ci + 1)
                        kb = chunks[ci][0]
                        ps_, b_ps = pss[ci]
                        pi = ptc[0] % 3
                        ptc[0] += 1
                        pt_, bpt = PT[pi], b_PT[pi]
                        P.act(lambda h, ps_=ps_, pt_=pt_: h.activation(out=pt_[:, :], in_=ps_[:, :], func=AF.Exp, scale=0.125),
                              r=[b_ps], w=[bpt])
                        P.pe(lambda h, po=po, kb=kb, pt_=pt_, ci=ci, nch=nch: h.matmul(
                            po[:, :], lhsT=Va[:, kb, :], rhs=pt_[:, :], start=(ci == 0), stop=(ci == nch - 1)),
                            r=[b_V, bpt], **({"w": [b_po]} if ci == 0 else {"pw": [b_po]}))
                        if ci == 1 and len(pending_norm) > 0:
                            pending_norm.pop(0)()
                    pending_norm.append(emit_norm)
                while pending_norm:
                    pending_norm.pop(0)()
                if K.att_stage < 4:
                    continue
                rr = 2 if isctx else rr_l
                for kd in range(8):
                    py, b_py = bank()
                    for r in range(2):
                        P.pe(lambda h, py=py, r=r, kd=kd, n=n: h.matmul(
                            py[:, 0:n], lhsT=wo[:, r, kd * 128:(kd + 1) * 128], rhs=OT[:, r, 0:n],
                            start=(r == 0), stop=(r == 1)),
                            r=[b_wo, b_OT], **({"w": [b_py]} if r == 0 else {"pw": [b_py]}))
                    P.dve(lambda h, py=py, kd=kd, t0=t0, n=n, rr=rr: h.scalar_tensor_tensor(
                        out=hs[:, kd, t0:t0 + n], in0=py[:, 0:n], scalar=mod(layer, 2, kd, rr), in1=hs[:, kd, t0:t0 + n],
                        op0=ALU.mult, op1=ALU.add),
                        r=[b_py, b_hs[0][tgi], b_modv[layer]], w=[b_hs[0][tgi]])
        P.barrier()
        for tgi in range(len(TGS)):
            layer_norm(tgi, layer, 0)
        P.barrier()


    so = [ARENA + 36864]

    def salloc(name, shape, dt, n=1):
        nb = int(np.prod(shape[1:])) * (4 if dt == F32 else 2)
        nb = (nb + 31) // 32 * 32
        ts = [P.sb(name, shape, dt, so[0] + i * nb) for i in range(n)]
        so[0] += nb * n
        return ts if n > 1 else ts[0]

    s_wxbc = salloc("s_wxbc", [128, 8, 768], BF16)
    s_wdt = salloc("s_wdt", [128, 8, 16], BF16)
    s_raw = salloc("s_raw", [128, 520], F32, 3)
    s_acc = salloc("s_acc", [128, 512], F32, 3)
    s_th = salloc("s_th", [128, 512], F32, 3)
    s_xo = salloc("s_xo", [128, 512], BF16, 3)
    s_t1 = salloc("s_t1", [128, 18, 16], F32)
    S1_END = so[0]
    so[0] = ARENA + 36864
    s_wso = salloc("s_wso", [128, 4, 1024], BF16)
    s_Sbin = salloc("s_Sbin", [128, 16, 512], BF16)
    s_Sf = salloc("s_Sf", [128, 512], F32)
    s_Sfb = salloc("s_Sfb", [128, 512], BF16)
    s_xw = salloc("s_xw", [128, 512], BF16, 2)
    s_cbm = salloc("s_cbm", [128, 128], BF16, 2)
    s_arg = salloc("s_arg", [128, 4, 128], BF16, 2)
    s_M = salloc("s_M", [128, 4, 128], BF16, 4)
    s_ya = salloc("s_ya", [128, 512], F32)
    s_yb = salloc("s_yb", [128, 512], F32)
    s_Sb = s_yb
    s_yg = s_ya
    s_yn = salloc("s_yn", [128, 512], BF16, 2)
    s_gz = salloc("s_gz", [128, 512], BF16)
    s_ynT = salloc("s_ynT", [128, 4, 512], BF16)
    s_ss = salloc("s_ss", [128, 4], F32)
    S2_END = so[0]
    so[0] = max(S1_END, S2_END)
    s_wz = salloc("s_wz", [128, 8, 512], BF16)
    s_xtok = salloc("s_xtok", [128, 18, 512], BF16)
    s_btok = salloc("s_btok", [128, 18, 128], BF16)
    s_BT = salloc("s_BT", [128, T], BF16)
    s_CT = salloc("s_CT", [128, T], BF16)
    s_dt = salloc("s_dt", [128, 18, 16], F32)
    s_a = salloc("s_a", [128, 18, 16], F32)
    s_cs = salloc("s_cs", [128, 18, 16], F32)
    s_tot = salloc("s_tot", [128, 18, 16], F32)
    s_E = salloc("s_E", [128, 18, 16], F32)
    s_dec = salloc("s_dec", [128, 18, 16], F32)
    s_cw = salloc("s_cw", [128, 36], F32)
    s_sb = salloc("s_sb", [128, 40], F32)
    s_ng = salloc("s_ng", [128, 4], F32)
    s_tri = salloc("s_tri", [128, 256], F32)
    s_one = salloc("s_one", [128, 2], F32)
    assert so[0] <= 212832, so[0]
    bs = {nm: P.buf(nm) for nm in ("xtok", "btok", "BT", "CT", "gz", "dt", "a", "cs", "tot", "E", "dec", "t1", "t2", "par",
                                   "wxbc", "wz", "wdt", "raw0", "raw1", "raw2", "acc0", "acc1", "acc2", "th0", "th1", "th2", "xo0", "xo1", "xo2", "arg0", "arg1", "xdt0", "xdt1", "wso", "Sbin", "Sf", "Sb",
                                   "Sfb", "xw0", "xw1", "cbm0", "cbm1", "arg", "L", "M0", "M1", "M2", "M3", "ya", "yb", "yg",
                                   "yn0", "yn1", "ynT", "ss")}
    sc_ = {"raw": 0, "xo": 0, "xw": 0, "M": 0, "acc": 0, "arg": 0}
    tri = s_tri[:, 0:128]
    trirev = s_tri[:, 128:256]

    def halo_ap(base, stride):
        a = base.ap
        return bass.AP(base.tensor, base.offset, [list(a[0]), [stride, 2], [1, 2]])

    def bc8(ap2):
        return ap2.unsqueeze(2).to_broadcast([128, 8, 64])

    def ssm(layer, b):
        rr_l = b
        for tgi, (t0, n, isctx) in enumerate(TGS):
            rr = 2 if isctx else rr_l
            for k in range(8):
                P.act(lambda h, k=k, t0=t0, n=n, rr=rr: h.activation(
                    out=uTa[:, k, t0:t0 + n], in_=hs[:, k, t0:t0 + n], func=AF.Identity,
                    scale=mod(layer, 1, k, rr), bias=mod(layer, 0, k, rr)),
                    r=[b_hs[0][tgi], b_modv[layer]], **({"w": [b_uTa[tgi]]} if k == 0 else {"pw": [b_uTa[tgi]]}))
        P.sp(lambda h: h.dma_start(out=s_tri[:, :], in_=tri_d[:, :]), dma_w=bs["par"])
        P.pool(lambda h: h.memset(s_one[:, :], 1.0), w=[bs["t2"]])
        t2v = None
        for g in range(4):
            P.sp(lambda h, g=g: h.dma_start(out=s_cw[:, :], in_=cw_d[g]), dma_w=bs["par"])
            P.sp(lambda h, g=g: h.dma_start(out=s_sb[:, :], in_=sb_d[g]), dma_w=bs["par"])
            P.sp(lambda h, g=g: h.dma_start(out=s_ng[:, :], in_=ng_d[g]), dma_w=bs["par"])
            P.pool(lambda h, g=g: h.dma_start(out=s_wxbc[:, :, :], in_=wxbc_d[g]), dma_w=bs["wxbc"])
            P.pool(lambda h, g=g: h.dma_start(out=s_wz[:, :, :], in_=wz_d[g]), dma_w=bs["wz"])
            P.pool(lambda h, g=g: h.dma_start(out=s_wdt[:, :, :], in_=wdt_d[g]), dma_w=bs["wdt"])
            P.dve(lambda h: h.tensor_scalar(out=s_cw[:, :], in0=s_cw[:, :], scalar1=0.5, scalar2=None, op0=ALU.mult), r=[bs["par"]], w=[bs["par"]])
            P.act(lambda h: h.activation(out=s_sb[:, 16:32], in_=s_sb[:, 16:32], func=AF.Exp), r=[bs["par"]], w=[bs["par"]])
            P.dve(lambda h: h.tensor_scalar(out=s_sb[:, 16:32], in0=s_sb[:, 16:32], scalar1=-1.0, scalar2=None, op0=ALU.mult), r=[bs["par"]], w=[bs["par"]])
            def A1(tile):
                tgi, c = tile["tgi"], tile["c"]
                t0, n, isctx = TGS[tgi]
                seg0, seg1 = (0, CTX) if isctx else (CTX, T)
                ri = sc_["raw"] % 3
                sc_["raw"] += 1
                raw, braw = s_raw[ri], bs[f"raw{ri}"]
                ai = sc_["acc"] % 3
                sc_["acc"] += 1
                acc_, bacc = s_acc[ai], bs[f"acc{ai}"]
                th_, bth = s_th[ai], bs[f"th{ai}"]
                tile.update(raw=raw, braw=braw, acc=acc_, bacc=bacc, th=th_, bth=bth)
                pm_, b_pm_ = bank()
                for k in range(8):
                    P.pe(lambda h, k=k: h.matmul(pm_[:, 0:n], lhsT=s_wxbc[:, k, c * 128:(c + 1) * 128], rhs=uTa[:, k, t0:t0 + n],
                                                 start=(k == 0), stop=(k == 7)),
                         r=[bs["wxbc"], b_uTa[tgi]], **({"w": [b_pm_]} if k == 0 else {"pw": [b_pm_]}))
                P.act(lambda h: h.activation(out=raw[:, 2:2 + n], in_=pm_[:, 0:n], func=AF.Copy), r=[b_pm_], w=[braw])
                hasl = (t0 - 2) >= seg0
                hasr = (t0 + n + 2) <= seg1
                if not hasl:
                    P.pool(lambda h: h.memset(raw[:, 0:2], 0.0), pw=[braw])
                if not hasr:
                    P.pool(lambda h: h.memset(raw[:, 2 + n:4 + n], 0.0), pw=[braw])
                if hasl or hasr:
                    ph, b_ph = bank()
                    hbufs = []
                    if hasl:
                        hbufs.append(b_uTa[[i for i, (a0, an, _) in enumerate(TGS) if a0 <= t0 - 2 < a0 + an][0]])
                    if hasr:
                        hbufs.append(b_uTa[[i for i, (a0, an, _) in enumerate(TGS) if a0 <= t0 + n < a0 + an][0]])
                    for k in range(8):
                        if hasl and hasr:
                            rhs_fn = lambda k=k: halo_ap(uTa[:, k, t0 - 2:t0], n + 2)
                            ncol = 4
                        elif hasl:
                            rhs_fn = lambda k=k: uTa[:, k, t0 - 2:t0]
                            ncol = 2
                        else:
                            rhs_fn = lambda k=k: uTa[:, k, t0 + n:t0 + n + 2]
                            ncol = 2
                        P.pe(lambda h, k=k, rhs_fn=rhs_fn, ncol=ncol: h.matmul(
                            ph[:, 0:ncol], lhsT=s_wxbc[:, k, c * 128:(c + 1) * 128], rhs=rhs_fn(), start=(k == 0), stop=(k == 7)),
                            r=[bs["wxbc"]] + hbufs, **({"w": [b_ph]} if k == 0 else {"pw": [b_ph]}))
                    if hasl:
                        P.act(lambda h: h.activation(out=raw[:, 0:2], in_=ph[:, 0:2], func=AF.Copy), r=[b_ph], pw=[braw])
                    if hasr:
                        o_ = 2 if hasl else 0
                        P.act(lambda h: h.activation(out=raw[:, 2 + n:4 + n], in_=ph[:, o_:o_ + 2], func=AF.Copy), r=[b_ph], pw=[braw])
                P.act(lambda h: h.activation(out=acc_[:, 0:n], in_=raw[:, 0:n], func=AF.Identity,
                                             scale=s_cw[:, c * 6:c * 6 + 1], bias=s_cw[:, c * 6 + 5:c * 6 + 6]),
                      r=[braw, bs["par"]], w=[bacc])

            def A2(tile):
                c = tile["c"]
                t0, n, isctx = TGS[tile["tgi"]]
                raw, braw, acc_, bacc = tile["raw"], tile["braw"], tile["acc"], tile["bacc"]
                for j in range(1, 5):
                    P.dve(lambda h, j=j: h.scalar_tensor_tensor(
                        out=acc_[:, 0:n], in0=raw[:, j:j + n], scalar=s_cw[:, c * 6 + j:c * 6 + j + 1], in1=acc_[:, 0:n],
                        op0=ALU.mult, op1=ALU.add), r=[braw, bs["par"], bacc], w=[bacc])

            def A3(tile):
                t0, n, isctx = TGS[tile["tgi"]]
                acc_, bacc, th_, bth = tile["acc"], tile["bacc"], tile["th"], tile["bth"]
                P.act(lambda h: h.activation(out=th_[:, 0:n], in_=acc_[:, 0:n], func=AF.Tanh), r=[bacc], w=[bth])

            def A4(tile):
                tgi, c = tile["tgi"], tile["c"]
                t0, n, isctx = TGS[tgi]
                acc_, bacc, th_, bth = tile["acc"], tile["bacc"], tile["th"], tile["bth"]
                if c <= 4:
                    xi = sc_["xo"] % 3
                    sc_["xo"] += 1
                    xo, bxo = s_xo[xi], bs[f"xo{xi}"]
                    P.dve(lambda h: h.scalar_tensor_tensor(out=xo[:, 0:n], in0=th_[:, 0:n], scalar=1.0, in1=acc_[:, 0:n],
                                                           op0=ALU.add, op1=ALU.mult), r=[bth, bacc], w=[bxo])
                    if c == 4:
                        P.act(lambda h: h.activation(out=s_BT[:, t0:t0 + n], in_=xo[:, 0:n], func=AF.Copy), r=[bxo], pw=[bs["BT"]])
                    for bi in range(n // 128):
                        blk = t0 // 128 + bi
                        ptr, b_ptr = bank()
                        P.pe(lambda h, ptr=ptr, bi=bi: h.matmul(ptr[:, 0:128], lhsT=xo[:, bi * 128:(bi + 1) * 128], rhs=ident[:, :], start=True, stop=True),
                             r=[bxo, b_const], w=[b_ptr])
                        if c < 4:
                            P.act(lambda h, ptr=ptr, blk=blk: h.activation(out=s_xtok[:, blk, c * 128:(c + 1) * 128], in_=ptr[:, 0:128], func=AF.Copy),
                                  r=[b_ptr], pw=[bs["xtok"]])
                        else:
                            P.act(lambda h, ptr=ptr, blk=blk: h.activation(out=s_btok[:, blk, :], in_=ptr[:, 0:128], func=AF.Copy),
                                  r=[b_ptr], pw=[bs["btok"]])
                else:
                    P.dve(lambda h: h.scalar_tensor_tensor(out=s_CT[:, t0:t0 + n], in0=th_[:, 0:n], scalar=1.0, in1=acc_[:, 0:n],
                                                           op0=ALU.add, op1=ALU.mult), r=[bth, bacc], pw=[bs["CT"]])
                if c == 5:
                    for bi in range(n // 128):
                        blk = t0 // 128 + bi
                        pd, b_pd = bank()
                        for k in range(8):
                            P.pe(lambda h, pd=pd, k=k, blk=blk: h.matmul(pd[:, 0:16], lhsT=uTa[:, k, blk * 128:(blk + 1) * 128], rhs=s_wdt[:, k, :],
                                                                         start=(k == 0), stop=(k == 7)),
                                 r=[bs["wdt"], b_uTa[tgi]], **({"w": [b_pd]} if k == 0 else {"pw": [b_pd]}))
                        P.dve(lambda h, pd=pd, blk=blk: h.tensor_tensor(out=s_dt[:, blk, :], in0=pd[:, 0:16], in1=s_sb[:, 0:16], op=ALU.add),
                              r=[b_pd, bs["par"]], pw=[bs["dt"]])

            tiles = [dict(tgi=tgi, c=c) for tgi in range(len(TGS)) for c in range(6)]
            A1(tiles[0])
            for i, tile in enumerate(tiles):
                A2(tile)
                if i + 1 < len(tiles):
                    A1(tiles[i + 1])
                A3(tile)
                if i >= 1:
                    A4(tiles[i - 1])
            A4(tiles[-1])
            dtv, t1v = s_dt[:, :, :], s_t1[:, :, :]
            P.dve(lambda h: h.tensor_scalar(out=t1v, in0=dtv, scalar1=-1.0, scalar2=None, op0=ALU.mult), r=[bs["dt"]], w=[bs["t1"]])
            P.dve(lambda h: h.tensor_tensor(out=t1v, in0=t1v, in1=dtv, op=ALU.max), r=[bs["dt"], bs["t1"]], w=[bs["t1"]])
            P.act(lambda h: h.activation(out=t1v, in_=t1v, func=AF.Exp, scale=-1.0), r=[bs["t1"]], w=[bs["t1"]])
            P.act(lambda h: h.activation(out=t1v, in_=t1v, func=AF.Ln, bias=s_one[:, 0:1], scale=1.0), r=[bs["t1"], bs["t2"]], w=[bs["t1"]])
            P.dve(lambda h: h.scalar_tensor_tensor(out=dtv, in0=dtv, scalar=0.0, in1=t1v, op0=ALU.max, op1=ALU.add), r=[bs["dt"], bs["t1"]], w=[bs["dt"]])
            P.dve(lambda h: h.tensor_tensor(out=s_a[:, :, :], in0=dtv, in1=s_sb[:, 16:32].unsqueeze(1).to_broadcast([128, 18, 16]), op=ALU.mult),
                  r=[bs["dt"], bs["par"]], w=[bs["a"]])
            for blk in range(18):
                pc, b_pc = bank()
                P.pe(lambda h, pc=pc, blk=blk: h.matmul(pc[:, 0:8], lhsT=tri, rhs=s_a[:, blk, 0:8], start=True, stop=True), r=[bs["a"], bs["par"]], w=[b_pc])
                P.pe(lambda h, pc=pc, blk=blk: h.matmul(pc[:, 8:16], lhsT=trirev, rhs=s_a[:, blk, 8:16], start=True, stop=True), r=[bs["a"], bs["par"]], pw=[b_pc])
                P.pe(lambda h, pc=pc, blk=blk: h.matmul(pc[:, 16:32], lhsT=ones_f[:, :], rhs=s_a[:, blk, :], start=True, stop=True), r=[bs["a"], b_c2], pw=[b_pc])
                P.act(lambda h, pc=pc, blk=blk: h.activation(out=s_cs[:, blk, :], in_=pc[:, 0:16], func=AF.Copy), r=[b_pc], pw=[bs["cs"]])
                P.act(lambda h, pc=pc, blk=blk: h.activation(out=s_tot[:, blk, :], in_=pc[:, 16:32], func=AF.Copy, scale=float(D)), r=[b_pc], pw=[bs["tot"]])
            P.act(lambda h: h.activation(out=s_E[:, :, :], in_=s_cs[:, :, :], func=AF.Exp), r=[bs["cs"]], w=[bs["E"]])
            P.dve(lambda h: h.tensor_tensor(out=s_dec[:, :, :], in0=s_tot[:, :, :], in1=s_cs[:, :, :], op=ALU.subtract), r=[bs["tot"], bs["cs"]], w=[bs["dec"]])
            P.act(lambda h: h.activation(out=s_dec[:, :, :], in_=s_dec[:, :, :], func=AF.Exp), r=[bs["dec"]], w=[bs["dec"]])
            P.dve(lambda h: h.tensor_tensor(out=s_dec[:, :, :], in0=s_dec[:, :, :], in1=s_dt[:, :, :], op=ALU.mult), r=[bs["dec"], bs["dt"]], w=[bs["dec"]])
            P.act(lambda h: h.activation(out=s_tot[:, :, :], in_=s_tot[:, :, :], func=AF.Exp), r=[bs["tot"]], w=[bs["tot"]])
            P.act(lambda h: h.activation(out=s_dt[:, :, :], in_=s_dt[:, :, :], func=AF.Ln), r=[bs["dt"], bs["dec"]], w=[bs["dt"]])
            P.dve(lambda h: h.tensor_tensor(out=s_dt[:, :, :], in0=s_dt[:, :, :], in1=s_cs[:, :, :], op=ALU.subtract), r=[bs["dt"], bs["cs"]], w=[bs["dt"]])
            P.barrier()
            P.pool(lambda h, g=g: h.dma_start(out=s_wso[:, :, :], in_=wso_d[g]), dma_w=bs["wso"])
            for c in range(4):
                P.dve(lambda h, c=c: h.tensor_scalar(out=s_wso[:, c, :], in0=s_wso[:, c, :], scalar1=s_ng[:, c:c + 1], scalar2=None, op0=ALU.mult),
                      r=[bs["wso"], bs["par"]], w=[bs["wso"]])
            P.pool(lambda h: h.memset(s_Sf[:, :], 0.0), w=[bs["Sf"]])
            P.pool(lambda h: h.memset(s_Sb[:, :], 0.0), w=[bs["Sb"]])

            def su_a(blk, d):
                xi = sc_["xw"] % 2
                sc_["xw"] += 1
                xw, bxw = s_xw[xi], bs[f"xw{xi}"]
                P.dve(lambda h: h.tensor_tensor(out=xw[:, :].rearrange("p (a b) -> p a b", a=8), in0=s_xtok[:, blk, :].rearrange("p (a b) -> p a b", a=8),
                                                in1=bc8(s_dec[:, blk, d * 8:d * 8 + 8]), op=ALU.mult), r=[bs["xtok"], bs["dec"]], w=[bxw])
                pst, b_pst = bank()
                P.pe(lambda h: h.matmul(pst[:, :], lhsT=s_btok[:, blk, :], rhs=xw[:, :], start=True, stop=True), r=[bs["btok"], bxw], w=[b_pst])
                return pst, b_pst

            def su_b(S, bS, blk, d, pst, b_pst):
                P.dve(lambda h: h.tensor_tensor(out=S[:, :].rearrange("p (a b) -> p a b", a=8), in0=S[:, :].rearrange("p (a b) -> p a b", a=8),
                                                in1=bc8(s_tot[:, blk, d * 8:d * 8 + 8]), op=ALU.mult), r=[bS, bs["tot"]], w=[bS])
                P.dve(lambda h: h.tensor_tensor(out=S[:, :], in0=S[:, :], in1=pst[:, :], op=ALU.add), r=[bS, b_pst], w=[bS])

            def state_update(S, bS, blk, d):
                pst, b_pst = su_a(blk, d)
                su_b(S, bS, blk, d, pst, b_pst)

            order = [1, 0] + list(range(17, 1, -1))
            nxt = su_a(order[0], 1)
            for i, blk in enumerate(order):
                cur = nxt
                if i + 1 < len(order):
                    nxt = su_a(order[i + 1], 1)
                if blk >= 2:
                    P.act(lambda h, blk=blk: h.activation(out=s_Sbin[:, blk - 2, :], in_=s_Sb[:, :], func=AF.Copy), r=[bs["Sb"]], pw=[bs["Sbin"]])
                su_b(s_Sb, bs["Sb"], blk, 1, cur[0], cur[1])
            groups = [(half, d) for half in range(2) for d in range(2)]
            st = {}

            def F1(blk):
                cols = slice(blk * 128, (blk + 1) * 128)
                P.act(lambda h: h.activation(out=s_Sfb[:, :], in_=s_Sf[:, :], func=AF.Copy), r=[bs["Sf"]], w=[bs["Sfb"]])
                pabs = []
                for (half, d) in groups:
                    pab, b_pab = bank()
                    pabs.append((pab, b_pab))
                    P.pe(lambda h, pab=pab, d=d: h.matmul(pab[:, :], lhsT=ident[:, :], rhs=(mnext if d == 0 else mprev)[:, :], start=True, stop=False),
                         r=[b_const], w=[b_pab])
                    for ci in range(4):
                        col = d * 8 + half * 4 + ci
                        P.pe(lambda h, pab=pab, ci=ci, col=col, d=d: h.matmul(
                            pab[:, ci * 128:(ci + 1) * 128], lhsT=s_a[:, blk, col:col + 1].to_broadcast([128, 128]),
                            rhs=(tri if d == 0 else trirev), start=False, stop=(ci == 3)),
                            r=[bs["a"], bs["par"]], pw=[b_pab])
                pz, b_pz = bank()
                tgz = 1 + (blk - 2) // 4
                for k in range(8):
                    P.pe(lambda h, k=k: h.matmul(pz[:, :], lhsT=uTa[:, k, cols], rhs=s_wz[:, k, :], start=(k == 0), stop=(k == 7)),
                         r=[bs["wz"], b_uTa[tgz]], **({"w": [b_pz]} if k == 0 else {"pw": [b_pz]}))
                args = [None] * 4
                Ms = [None] * 4

                def emit_exp(gi):
                    half, d = groups[gi]
                    pab, b_pab = pabs[gi]
                    ai = sc_["arg"] % 2
                    sc_["arg"] += 1
                    arg, barg = s_arg[ai], bs[f"arg{ai}"]
                    args[gi] = (arg, barg)
                    for ci in range(4):
                        col = d * 8 + half * 4 + ci
                        P.act(lambda h, ci=ci, col=col: h.activation(
                            out=arg[:, ci, :], in_=pab[:, ci * 128:(ci + 1) * 128], func=AF.Exp, bias=s_dt[:, blk, col:col + 1], scale=1.0),
                            r=[b_pab, bs["dt"]], **({"w": [barg]} if ci == 0 else {"pw": [barg]}))

                def emit_mul(gi):
                    half, d = groups[gi]
                    arg, barg = args[gi]
                    mi = sc_["M"] % 4
                    sc_["M"] += 1
                    Mt, bM = s_M[mi], bs[f"M{mi}"]
                    Ms[gi] = (Mt, bM)
                    P.dve(lambda h: h.tensor_tensor(
                        out=Mt[:, :, :], in0=arg[:, :, :], in1=s_cbm[d][:, :].unsqueeze(1).to_broadcast([128, 4, 128]), op=ALU.mult),
                        r=[barg, bs[f"cbm{d}"]], w=[bM])

                emit_exp(0)
                emit_exp(1)
                state_update(s_Sf, bs["Sf"], blk, 0)
                pcb, b_pcb = bank()
                P.pe(lambda h: h.matmul(pcb[:, 0:128], lhsT=s_BT[:, cols], rhs=s_CT[:, cols], start=True, stop=True), r=[bs["BT"], bs["CT"]], w=[b_pcb])
                P.dve(lambda h: h.tensor_tensor(out=s_cbm[0][:, :], in0=pcb[:, 0:128], in1=tri, op=ALU.mult), r=[b_pcb, bs["par"]], w=[bs["cbm0"]])
                P.dve(lambda h: h.tensor_tensor(out=s_cbm[1][:, :], in0=pcb[:, 0:128], in1=trirev, op=ALU.mult), r=[b_pcb, bs["par"]], w=[bs["cbm1"]])
                emit_mul(0)
                emit_exp(2)
                emit_mul(1)
                emit_exp(3)
                emit_mul(2)
                emit_mul(3)
                P.act(lambda h: h.activation(out=s_yb[:, :], in_=pz[:, :], func=AF.Exp, scale=-1.0), r=[b_pz], w=[bs["yb"]])
                P.act(lambda h: h.activation(out=s_yb[:, :], in_=s_yb[:, :], func=AF.Ln, bias=s_one[:, 0:1], scale=1.0), r=[bs["yb"], bs["t2"]], w=[bs["yb"]])
                P.act(lambda h: h.activation(out=s_yb[:, :], in_=s_yb[:, :], func=AF.Exp, scale=-1.0), r=[bs["yb"]], w=[bs["yb"]])
                P.dve(lambda h: h.tensor_tensor(out=s_gz[:, :], in0=s_yb[:, :], in1=pz[:, :], op=ALU.mult), r=[bs["yb"], b_pz], w=[bs["gz"]])
                pof, b_pof = bank()
                P.pe(lambda h: h.matmul(pof[:, :], lhsT=s_CT[:, cols], rhs=s_Sfb[:, :], start=True, stop=True), r=[bs["CT"], bs["Sfb"]], w=[b_pof])
                pob, b_pob = bank()
                P.pe(lambda h: h.matmul(pob[:, :], lhsT=s_CT[:, cols], rhs=s_Sbin[:, blk - 2, :], start=True, stop=True), r=[bs["CT"], bs["Sbin"]], w=[b_pob])
                P.dve(lambda h: h.tensor_tensor(out=s_ya[:, :].rearrange("p (a b) -> p a b", a=8), in0=pof[:, :].rearrange("p (a b) -> p a b", a=8),
                                                in1=bc8(s_E[:, blk, 0:8]), op=ALU.mult), r=[b_pof, bs["E"]], w=[bs["ya"]])
                P.dve(lambda h: h.tensor_tensor(out=s_yb[:, :].rearrange("p (a b) -> p a b", a=8), in0=pob[:, :].rearrange("p (a b) -> p a b", a=8),
                                                in1=bc8(s_E[:, blk, 8:16]), op=ALU.mult), r=[b_pob, bs["E"]], w=[bs["yb"]])
                P.dve(lambda h: h.tensor_tensor(out=s_ya[:, :], in0=s_ya[:, :], in1=s_yb[:, :], op=ALU.add), r=[bs["ya"], bs["yb"]], w=[bs["ya"]])
                P.dve(lambda h: h.tensor_tensor(out=s_yb[:, :].rearrange("p (a b) -> p a b", a=8), in0=s_xtok[:, blk, :].rearrange("p (a b) -> p a b", a=8),
                                                in1=bc8(s_sb[:, 32:40]), op=ALU.mult), r=[bs["xtok"], bs["par"]], w=[bs["yb"]])
                P.dve(lambda h: h.tensor_tensor(out=s_ya[:, :], in0=s_ya[:, :], in1=s_yb[:, :], op=ALU.add), r=[bs["ya"], bs["yb"]], w=[bs["ya"]])
                st[blk] = Ms

            def F2(blk):
                Ms = st.pop(blk)
                pyd, b_pyd = bank()
                for half in range(2):
                    for ci in range(4):
                        hh = half * 4 + ci
                        for d in range(2):
                            Mt, bM = Ms[half * 2 + d]
                            P.pe(lambda h, Mt=Mt, ci=ci, hh=hh, d=d: h.matmul(
                                pyd[:, hh * 64:(hh + 1) * 64], lhsT=Mt[:, ci, :], rhs=s_xtok[:, blk, hh * 64:(hh + 1) * 64], start=(d == 0), stop=(d == 1)),
                                r=[bM, bs["xtok"]], **({"w": [b_pyd]} if (half == 0 and ci == 0 and d == 0) else {"pw": [b_pyd]}))
                P.dve(lambda h: h.tensor_tensor(out=s_ya[:, :], in0=s_ya[:, :], in1=pyd[:, :], op=ALU.add), r=[bs["ya"], b_pyd], w=[bs["ya"]])
                P.dve(lambda h: h.tensor_tensor(out=s_ya[:, :], in0=s_ya[:, :], in1=s_gz[:, :], op=ALU.mult), r=[bs["ya"], bs["gz"]], w=[bs["ya"]])
                P.act(lambda h: h.activation(out=s_yb[:, :], in_=s_ya[:, :], func=AF.Square, accum_out=s_ss[:, 0:1]), r=[bs["ya"]], w=[bs["yb"], bs["ss"]])
                P.act(lambda h: h.activation(out=s_ss[:, 1:2], in_=s_ss[:, 0:1], func=AF.Ln, scale=1.0 / 512.0, bias=epsv[:, 0:1]), r=[bs["ss"], b_c2], w=[bs["ss"]])
                P.act(lambda h: h.activation(out=s_ss[:, 2:3], in_=s_ss[:, 1:2], func=AF.Exp, scale=-0.5), r=[bs["ss"]], w=[bs["ss"]])
                yn, byn = s_yn[blk % 2], bs[f"yn{blk % 2}"]
                P.dve(lambda h: h.tensor_scalar(out=yn[:, :], in0=s_ya[:, :], scalar1=s_ss[:, 2:3], scalar2=None, op0=ALU.mult),
                      r=[bs["ya"], bs["ss"]], w=[byn])

            def T_(blk):
                j = (blk - 2) % 4
                yn, byn = s_yn[blk % 2], bs[f"yn{blk % 2}"]
                for c in range(4):
                    ptr, b_ptr = bank()
                    P.pe(lambda h, ptr=ptr, c=c: h.matmul(ptr[:, 0:128], lhsT=yn[:, c * 128:(c + 1) * 128], rhs=ident[:, :], start=True, stop=True), r=[byn, b_const], w=[b_ptr])
                    P.act(lambda h, ptr=ptr, c=c: h.activation(out=s_ynT[:, c, j * 128:(j + 1) * 128], in_=ptr[:, 0:128], func=AF.Copy), r=[b_ptr],
                          **({"w": [bs["ynT"]]} if (c == 0 and j == 0) else {"pw": [bs["ynT"]]}))
                if j != 3:
                    return
                tgi = 1 + (blk - 2) // 4
                q0 = TGS[tgi][0]
                for kd in range(8):
                    py, b_py = bank()
                    for c in range(4):
                        P.pe(lambda h, py=py, c=c, kd=kd: h.matmul(py[:, :], lhsT=s_wso[:, c, kd * 128:(kd + 1) * 128], rhs=s_ynT[:, c, :], start=(c == 0), stop=(c == 3)),
                             r=[bs["wso"], bs["ynT"]], **({"w": [b_py]} if c == 0 else {"pw": [b_py]}))
                    P.dve(lambda h, py=py, kd=kd: h.scalar_tensor_tensor(
                        out=hs[:, kd, q0:q0 + 512], in0=py[:, :], scalar=mod(layer, 2, kd, rr_l), in1=hs[:, kd, q0:q0 + 512], op0=ALU.mult, op1=ALU.add),
                        r=[b_py, b_hs[0][tgi], b_modv[layer]], w=[b_hs[0][tgi]])

            state_update(s_Sf, bs["Sf"], 0, 0)
            state_update(s_Sf, bs["Sf"], 1, 0)
            F1(2)
            F2(2)
            for blk in range(3, 18):
                F1(blk)
                T_(blk - 1)
                F2(blk)
            T_(17)
            P.barrier()
        for tgi in range(1, len(TGS)):
            layer_norm(tgi, layer, 0)
        P.barrier()

    b_out = P.buf("outst")
    for b in range(nseq):
        for k in range(8):
            P.sp(lambda h, k=k, b=b: h.dma_start(out=hs[:, k, 0:CTX], in_=cxT[b, k]), dma_w=b_hs[0][0])
            for tgi in range(1, 5):
                t0, n, _ = TGS[tgi]
                P.sp(lambda h, k=k, b=b, t0=t0, n=n: h.dma_start(out=hs[:, k, t0:t0 + n], in_=xT[b, k, :, t0 - CTX:t0 - CTX + n]),
                     dma_w=b_hs[0][tgi])
        for tgi, (t0, n, _) in enumerate(TGS):
            P.dve(lambda h, t0=t0, n=n: h.tensor_scalar(out=hs[:, :, t0:t0 + n], in0=hs[:, :, t0:t0 + n], scalar1=ALPHA, scalar2=None, op0=ALU.mult),
                  r=[b_hs[0][tgi]], w=[b_hs[0][tgi]])
        for layer in range(nlayers):
            last = layer == DEPTH - 1
            flags = dbg if isinstance(dbg, dict) else {}
            if layer % 2 == 0:
                if flags.get("att", True):
                    attention(layer, b)
            else:
                ssm(layer, b)
            for tgi, (t0, n, isctx) in enumerate(TGS):
                if last and isctx:
                    continue
                if flags.get("mlp", True):
                    mlp(tgi, layer, 2 if isctx else b)
                if flags.get("ln", True):
                    layer_norm(tgi, layer, 2, final=last)
            P.barrier()
        if dbg is not None:
            for k in range(8):
                P.sp(lambda h, k=k, b=b: h.dma_start(out=dbgT[b, k], in_=hs[:, k, :]), r=b_hs[0], dma_r=b_out)
        for k in range(8):
            P.sp(lambda h, k=k, b=b: h.dma_start(out=outT[b, k], in_=hs[:, k, CTX:T]), r=b_hs[0], dma_r=b_out)
        P.barrier()
    counts = P.finish([b_out])
    return nc, counts


def _rope_perm():
    idx = np.arange(64)
    a = idx // 32
    half = (idx % 32) // 16
    j = idx % 16
    return a * 32 + (1 - half) * 16 + j


def host_constants():
    bf = ml_dtypes.bfloat16
    c = {}
    c["ident"] = np.eye(128, dtype=np.float32).astype(bf)
    jj = np.arange(128)[:, None]
    ii = np.arange(128)[None, :]
    mp = np.where(jj >= ii, 0.0, NEG).astype(np.float32)
    mn = np.where(jj <= ii, 0.0, NEG).astype(np.float32)
    c["mprev"] = np.tile(mp, (1, 4)).astype(bf)
    c["mnext"] = np.tile(mn, (1, 4)).astype(bf)
    t = np.arange(SEQ)
    row = (t // 64).astype(np.float32)
    col = (t % 64).astype(np.float32)
    inv = (10000.0 ** (-np.arange(0, 32, 2, dtype=np.float32) / 32)).astype(np.float32)
    cosT = np.zeros((64, SEQ), np.float32)
    sinT = np.zeros((64, SEQ), np.float32)
    for a, pos in enumerate((row, col)):
        ang = (pos[None, :] * inv[:, None]).astype(np.float32)
        for half in range(2):
            sl = slice(a * 32 + half * 16, a * 32 + half * 16 + 16)
            cosT[sl] = np.cos(ang)
            sinT[sl] = np.sin(ang) * (-1.0 if half == 0 else 1.0)
    kk = np.arange(128)[:, None]
    ll = np.arange(128)[None, :]
    c["tri"] = np.concatenate([(kk <= ll), (kk >= ll)], axis=1).astype(np.float32)
    c["cossin"] = np.concatenate([cosT, sinT], axis=0)
    return c


def host_weights(inp):
    w = {}
    f = np.float32
    wm = np.asarray(inp["w_mod"], f)
    w["wmod"] = np.ascontiguousarray(wm.reshape(DEPTH, 8, 128, 12, 512).transpose(0, 3, 2, 1, 4))
    w["bmod"] = np.ascontiguousarray(np.asarray(inp["b_mod"], f).reshape(DEPTH, 48, 128).transpose(0, 2, 1))
    lnv = np.stack([np.asarray(inp[k], f) for k in ("ln_mix_g", "ln_mix_b", "ln_ff_g", "ln_ff_b")], axis=1)
    w["lnv"] = np.ascontiguousarray(lnv.reshape(DEPTH, 4, 8, 128).transpose(3, 0, 1, 2).reshape(128, DEPTH * 4 * 8))
    w["sinkb"] = np.ascontiguousarray(np.broadcast_to(np.asarray(inp["att_sink"], f).reshape(1, 16), (128, 16)))
    win = np.asarray(inp["att_w_in"], f)[0]
    perm = _rope_perm()
    wq = win[:, :1024].reshape(8, 128, 4, 4, 64)
    wqq = np.concatenate([wq, wq[..., perm]], axis=-1)
    w["wqq"] = np.ascontiguousarray(wqq.transpose(2, 1, 0, 3, 4).reshape(4, 128, 8, 512))
    wk = win[:, 1024:1280].reshape(8, 128, 4, 64)
    wv = win[:, 1280:1536].reshape(8, 128, 4, 64)
    wkv = np.concatenate([wk, wk[..., perm], wv], axis=-1)
    w["wkv"] = np.ascontiguousarray(wkv.transpose(2, 1, 0, 3))
    wo = np.asarray(inp["att_w_out"], f)[0].reshape(4, 2, 2, 64, 1024)
    w["wo"] = np.ascontiguousarray(wo.transpose(0, 2, 3, 1, 4).reshape(4, 128, 2, 1024))
    sw = np.asarray(inp["ssm_w_in"], f)[0]
    wx = sw[:, 2048:4096].reshape(8, 128, 4, 512)
    wB = sw[:, 4096:4608].reshape(8, 128, 4, 128)
    wC = sw[:, 4608:5120].reshape(8, 128, 4, 128)
    w["wxbc"] = np.ascontiguousarray(np.concatenate([wx, wB, wC], axis=-1).transpose(2, 1, 0, 3))
    w["wz"] = np.ascontiguousarray(sw[:, 0:2048].reshape(8, 128, 4, 512).transpose(2, 1, 0, 3))
    wd = sw[:, 5120:5184].reshape(8, 128, 2, 4, 8)
    w["wdt"] = np.ascontiguousarray(wd.transpose(3, 1, 0, 2, 4).reshape(4, 128, 8, 16))
    w["wso"] = np.ascontiguousarray(np.asarray(inp["ssm_w_out"], f)[0].reshape(4, 4, 128, 1024).transpose(0, 2, 1, 3))
    cw = np.asarray(inp["ssm_conv_w"], f)[0]
    cb = np.asarray(inp["ssm_conv_b"], f)[0]
    convw = np.zeros((4, 128, 6, 6), f)
    for g in range(4):
        chans = [np.arange(g * 512 + c * 128, g * 512 + (c + 1) * 128) for c in range(4)]
        chans.append(np.arange(2048 + g * 128, 2048 + (g + 1) * 128))
        chans.append(np.arange(2560 + g * 128, 2560 + (g + 1) * 128))
        for c, ch in enumerate(chans):
            convw[g, :, c, 0:5] = cw[:, ch].T
            convw[g, :, c, 5] = cb[ch]
    w["convw"] = convw.reshape(4, 128, 36)
    dtb = np.asarray(inp["ssm_dt_bias"], f)[0].reshape(2, 4, 8)
    alog = np.asarray(inp["ssm_a_log"], f)[0].reshape(2, 4, 8)
    dsk = np.asarray(inp["ssm_d"], f)[0].reshape(4, 8)
    ssmb = np.zeros((4, 128, 40), f)
    for g in range(4):
        ssmb[g, :, 0:16] = dtb[:, g, :].reshape(1, 16)
        ssmb[g, :, 16:32] = alog[:, g, :].reshape(1, 16)
        ssmb[g, :, 32:40] = dsk[g].reshape(1, 8)
    w["ssmb"] = ssmb
    ng = np.asarray(inp["ssm_norm_g"], f)[0].reshape(4, 4, 128)
    w["normg"] = np.ascontiguousarray(ng.transpose(0, 2, 1))
    w1 = np.asarray(inp["ff_w1"], f).reshape(DEPTH, 8, 128, 8, 512)
    w["w1"] = np.ascontiguousarray(w1.transpose(0, 3, 2, 1, 4))
    w2 = np.asarray(inp["ff_w2"], f).reshape(DEPTH, 32, 128, 8, 128)
    w["w2"] = np.ascontiguousarray(w2.transpose(0, 3, 2, 1, 4))
    return w


def host_core_inputs(inp, core):
    f = np.float32
    b0 = core * BLOC
    x = np.asarray(inp["x"], f)[b0:b0 + BLOC]
    ctx = np.asarray(inp["ctx"], f)[b0:b0 + BLOC]
    d = {}
    d["xT"] = np.ascontiguousarray(x.transpose(0, 2, 1).reshape(BLOC, 8, 128, SEQ))
    d["cxT"] = np.ascontiguousarray(ctx.transpose(0, 2, 1).reshape(BLOC, 8, 128, CTX))
    cc = np.concatenate([np.asarray(inp["c"], f)[b0:b0 + BLOC], np.asarray(inp["c_ctx"], f)[None]], axis=0)
    d["cT"] = np.ascontiguousarray(cc.reshape(3, 8, 128).transpose(2, 1, 0))
    return d


_CACHE = {}


def kernel(**inputs):
    if "nc" not in _CACHE:
        _CACHE["nc"] = build_program()[0]
    nc = _CACHE["nc"]
    shared = {}
    shared.update(host_constants())
    shared.update(host_weights(inputs))
    in_maps = []
    for core in range(NCORE):
        m = dict(shared)
        m.update(host_core_inputs(inputs, core))
        in_maps.append(m)
    res = run_bass_kernel_spmd(nc, in_maps, core_ids=list(range(NCORE)))
    outs = []
    for core in range(NCORE):
        oT = np.asarray(res.results[core]["outT"]).reshape(BLOC, D, SEQ)
        outs.append(oT.transpose(0, 2, 1))
    return np.ascontiguousarray(np.concatenate(outs, axis=0)).astype(np.float32)
```

```python
from contextlib import ExitStack
import numpy as np
import ml_dtypes
import concourse.bass as bass
import concourse.mybir as mybir
from concourse.bass_utils import run_bass_kernel_spmd

F32 = mybir.dt.float32
BF16 = mybir.dt.bfloat16
AF = mybir.ActivationFunctionType
ALU = mybir.AluOpType
AX = mybir.AxisListType

ENGS = ("pe", "act", "dve", "pool", "sp")

D = 1024
SEQ = 2048
CTX = 256
T = SEQ + CTX
NCORE = 8
BLOC = 2
DEPTH = 2
ALPHA = (2.0 * DEPTH) ** 0.25
LN_EPS = 1e-5
RMS_EPS = 1e-5
NEG = -30000.0
TGS = [(0, 256, True), (256, 512, False), (768, 512, False), (1280, 512, False), (1792, 512, False)]


class Buf:
    __slots__ = ("name", "writers", "readers", "prev_readers", "sem", "dcount", "excl")

    def __init__(self, name, excl=False):
        self.name = name
        self.excl = excl
        self.writers = {}
        self.readers = {}
        self.prev_readers = {}
        self.sem = None
        self.dcount = 0


class Op:
    __slots__ = ("emit", "deps", "signal", "dma", "isnop")

    def __init__(self, emit, deps, dma, isnop=False):
        self.emit = emit
        self.deps = deps
        self.signal = False
        self.dma = dma
        self.isnop = isnop


class Prog:
    def __init__(self, nc):
        self.nc = nc
        self.ops = {e: [] for e in ENGS}
        self.seen = {e: {} for e in ENGS}
        self.dma_bufs = []
        self.nbuf = 0
        self.ntens = 0

    def sb(self, name, shape, dt, off):
        self.ntens += 1
        return self.nc.alloc_sbuf_tensor_at(f"{name}_{self.ntens}", list(shape), dt, offset=off + 16512)

    def buf(self, name=None):
        self.nbuf += 1
        return Buf(name or f"b{self.nbuf}")

    def add(self, eng, emit, r=(), w=(), pw=(), dma_w=None, dma_r=None):
        ops = self.ops[eng]
        idx = len(ops)
        deps = {}

        def need(k, v):
            if deps.get(k, -1) < v:
                deps[k] = v

        mykey = ("E", eng)
        allr = list(r) + ([dma_r] if dma_r is not None else [])
        allw = list(w) + ([dma_w] if dma_w is not None else [])
        for b in allr:
            for k, v in b.writers.items():
                need(k, v)
            if b.excl:
                for k, v in b.readers.items():
                    if k != mykey:
                        need(k, v)
        for b in allw:
            for k, v in b.readers.items():
                need(k, v)
            for k, v in b.writers.items():
                if k == mykey and eng == "pe":
                    continue
                if dma_w is not None and k == ("D", dma_w):
                    continue
                need(k, v)
            if not b.readers:
                for k, v in b.prev_readers.items():
                    need(k, v)
        for b in pw:
            for k, v in b.readers.items():
                need(k, v)
            for k, v in b.prev_readers.items():
                need(k, v)
            for k, v in b.writers.items():
                if k[0] == "D":
                    need(k, v)
        seen = self.seen[eng]
        fdeps = {}
        for k, v in deps.items():
            if seen.get(k, -1) >= v:
                continue
            seen[k] = v
            fdeps[k] = v
            if k[0] == "E":
                self.ops[k[1]][v].signal = True
        is_dma = (dma_w is not None) or (dma_r is not None)
        dbuf = dma_w if dma_w is not None else dma_r
        op = Op(emit, fdeps, dbuf if is_dma else None)
        ops.append(op)
        if is_dma:
            if dbuf.sem is None:
                dbuf.sem = True
                self.dma_bufs.append(dbuf)
            dbuf.dcount += 16
            ev = (("D", dbuf), dbuf.dcount)
        else:
            ev = (mykey, idx)
        for b in allr:
            if b.readers.get(ev[0], -1) < ev[1]:
                b.readers[ev[0]] = ev[1]
        for b in allw:
            if b.readers:
                b.prev_readers = b.readers
            b.readers = {}
            b.writers = {ev[0]: ev[1]}
        for b in pw:
            if b.readers:
                b.prev_readers = b.readers
                b.readers = {}
                b.writers = {}
            if b.writers.get(ev[0], -1) < ev[1]:
                b.writers[ev[0]] = ev[1]
        return op

    def pe(self, emit, **kw): return self.add("pe", emit, **kw)
    def act(self, emit, **kw): return self.add("act", emit, **kw)
    def dve(self, emit, **kw): return self.add("dve", emit, **kw)
    def pool(self, emit, **kw): return self.add("pool", emit, **kw)
    def sp(self, emit, **kw): return self.add("sp", emit, **kw)

    def barrier(self):
        last = {}
        for e in ENGS:
            ops = self.ops[e]
            for i in range(len(ops) - 1, -1, -1):
                if ops[i].dma is None and not ops[i].isnop:
                    last[("E", e)] = i
                    break
        for b in self.dma_bufs:
            last[("D", b)] = b.dcount
        for e in ENGS:
            seen = self.seen[e]
            fdeps = {}
            for k, v in last.items():
                if k == ("E", e):
                    continue
                if seen.get(k, -1) >= v:
                    continue
                seen[k] = v
                fdeps[k] = v
                if k[0] == "E":
                    self.ops[k[1]][v].signal = True
            if fdeps:
                self.ops[e].append(Op(lambda h: h.nop(), fdeps, None, True))

    def finish(self, final_bufs):
        nc = self.nc
        with ExitStack() as es:
            sems = {e: es.enter_context(nc.semaphore(f"s_{e}")) for e in ENGS}
            for i, b in enumerate(self.dma_bufs):
                b.sem = es.enter_context(nc.semaphore(f"d{i}"))
            sigcnt = {}
            for e in ENGS:
                c = 0
                arr = []
                for op in self.ops[e]:
                    if op.signal:
                        c += 1
                    arr.append(c)
                sigcnt[e] = arr
            fin = [(b.sem, b.dcount) for b in final_bufs]

            def run(e, h):
                for op in self.ops[e]:
                    for k, v in op.deps.items():
                        if k[0] == "E":
                            h.wait_ge(sems[k[1]], sigcnt[k[1]][v])
                        else:
                            h.wait_ge(k[1].sem, v)
                    ins = op.emit(h)
                    if op.dma is not None:
                        ins.then_inc(op.dma.sem, 16)
                    elif op.signal:
                        ins.then_inc(sems[e], 1)
                if e == "sp":
                    for s, v in fin:
                        h.wait_ge(s, v)

            with nc.Block() as block:
                @block.tensor
                def _(h): run("pe", h)

                @block.scalar
                def _(h): run("act", h)

                @block.vector
                def _(h): run("dve", h)

                @block.gpsimd
                def _(h): run("pool", h)

                @block.sync
                def _(h): run("sp", h)
        return {e: len(self.ops[e]) for e in ENGS}


class K:
    pass


def build_program(nseq=BLOC, nlayers=DEPTH, dbg=None):
    nc = bass.Bass("TRN2", target_bir_lowering=False)
    P = Prog(nc)

    def din(name, shape, dt=F32):
        return nc.dram_tensor(name, list(shape), dt, kind="ExternalInput").ap()

    xT = din("xT", [BLOC, 8, 128, SEQ])
    cxT = din("cxT", [BLOC, 8, 128, CTX])
    cT = din("cT", [128, 8, 3])
    wmod = din("wmod", [DEPTH, 12, 128, 8, 512])
    bmod = din("bmod", [DEPTH, 128, 48])
    lnv_d = din("lnv", [128, DEPTH * 4 * 8])
    sink_d = din("sinkb", [128, 16])
    ident_d = din("ident", [128, 128], BF16)
    mprev_d = din("mprev", [128, 512], BF16)
    mnext_d = din("mnext", [128, 512], BF16)
    cs_d = din("cossin", [128, SEQ])
    wqq_d = din("wqq", [4, 128, 8, 512])
    wkv_d = din("wkv", [4, 128, 8, 192])
    wo_d = din("wo", [4, 128, 2, 1024])
    wxbc_d = din("wxbc", [4, 128, 8, 768])
    wz_d = din("wz", [4, 128, 8, 512])
    wdt_d = din("wdt", [4, 128, 8, 16])
    wso_d = din("wso", [4, 128, 4, 1024])
    cw_d = din("convw", [4, 128, 36])
    sb_d = din("ssmb", [4, 128, 40])
    ng_d = din("normg", [4, 128, 4])
    tri_d = din("tri", [128, 256])
    w1_d = din("w1", [DEPTH, 8, 128, 8, 512])
    w2_d = din("w2", [DEPTH, 8, 128, 32, 128])
    outT = nc.dram_tensor("outT", [BLOC, 8, 128, SEQ], F32, kind="ExternalOutput").ap()
    if dbg is not None:
        dbgT = nc.dram_tensor("dbgT", [BLOC, 8, 128, T], F32, kind="ExternalOutput").ap()

    HS_OFF = 0
    CONST_OFF = 73728
    ARENA = 78848
    hs = P.sb("hs", [128, 8, T], F32, HS_OFF)
    b_hs = [[P.buf(f"hs{g}") for g in range(len(TGS))]]

    co = [CONST_OFF]

    def calloc(name, shape, dt):
        n = int(np.prod(shape[1:])) * (4 if dt == F32 else 2)
        t = P.sb(name, shape, dt, co[0])
        co[0] += (n + 31) // 32 * 32
        assert co[0] <= ARENA
        return t

    ident = calloc("ident", [128, 128], BF16)
    ones_f = calloc("ones_f", [128, 128], F32)
    ones_bf = calloc("ones_bf", [128, 128], BF16)
    mprev = calloc("mprev", [128, 512], BF16)
    mnext = calloc("mnext", [128, 512], BF16)
    modv = [calloc(f"modv{i}", [128, 48, 3], F32) for i in range(DEPTH)]
    lnv = calloc("lnv", [128, DEPTH * 4 * 8], F32)
    lnva = calloc("lnva", [128, DEPTH * 4 * 8], F32)
    sinkv = calloc("sinkv", [128, 16], F32)
    epsv = calloc("epsv", [128, 2], F32)
    kmaxp = calloc("kmaxp", [128, 8], F32)
    kmax2 = calloc("kmax2", [128, 1], F32)
    b_const = P.buf("const")
    b_modv = [P.buf(f"modv{i}") for i in range(DEPTH)]
    b_kmax = P.buf("kmax")

    pbank = [nc.alloc_psum_tensor(f"pb{i}", [128, 512], F32) for i in range(8)]
    b_bank = [Buf(f"pb{i}", excl=True) for i in range(8)]
    bank_rr = [0]

    def bank():
        i = bank_rr[0]
        bank_rr[0] = (i + 1) % 8
        return pbank[i], b_bank[i]

    P.sp(lambda h: h.dma_start(out=ident[:, :], in_=ident_d[:, :]), dma_w=b_const)
    P.sp(lambda h: h.dma_start(out=mprev[:, :], in_=mprev_d[:, :]), dma_w=b_const)
    P.sp(lambda h: h.dma_start(out=mnext[:, :], in_=mnext_d[:, :]), dma_w=b_const)
    P.sp(lambda h: h.dma_start(out=lnv[:, :], in_=lnv_d[:, :]), dma_w=b_const)
    P.sp(lambda h: h.dma_start(out=sinkv[:, :], in_=sink_d[:, :]), dma_w=b_const)
    b_c2 = P.buf("const2")
    P.pool(lambda h: h.memset(ones_f[:, :], 1.0 / D), pw=[b_c2])
    P.pool(lambda h: h.memset(ones_bf[:, :], 1.0), pw=[b_c2])
    P.pool(lambda h: h.memset(epsv[:, 0:1], LN_EPS), pw=[b_c2])
    P.pool(lambda h: h.memset(epsv[:, 1:2], 1e-20), pw=[b_c2])
    P.dve(lambda h: h.tensor_scalar(out=lnva[:, :], in0=lnv[:, :], scalar1=ALPHA, scalar2=None, op0=ALU.mult),
          r=[b_const], w=[b_c2])

    def lnvec(layer, which, k, alpha):
        t = lnva if alpha else lnv
        c = (layer * 4 + which) * 8 + k
        return t[:, c:c + 1]

    a_cT = P.sb("cTs", [128, 8, 3], F32, ARENA)
    a_e = P.sb("cTe", [128, 8, 3], F32, ARENA + 128)
    a_sc = P.sb("scT", [128, 8, 3], F32, ARENA + 256)
    a_bm = P.sb("bm", [128, 48], F32, ARENA + 384)
    a_w = [P.sb(f"wm{i}", [128, 8, 512], F32, ARENA + 1024 + i * 16384) for i in range(2)]
    b_cT = P.buf("cT")
    b_sc = P.buf("scT")
    b_bm = P.buf("bm")
    b_w = [P.buf("wm0"), P.buf("wm1")]
    P.sp(lambda h: h.dma_start(out=a_cT[:, :, :], in_=cT[:, :, :]), dma_w=b_cT)
    P.act(lambda h: h.activation(out=a_e[:, :, :], in_=a_cT[:, :, :], func=AF.Exp, scale=-1.0), r=[b_cT], w=[b_sc])
    P.dve(lambda h: h.tensor_scalar(out=a_e[:, :, :], in0=a_e[:, :, :], scalar1=1.0, scalar2=None, op0=ALU.add), r=[b_sc], w=[b_sc])
    P.dve(lambda h: h.reciprocal(out=a_e[:, :, :], in_=a_e[:, :, :]), r=[b_sc], w=[b_sc])
    P.dve(lambda h: h.tensor_tensor(out=a_sc[:, :, :], in0=a_e[:, :, :], in1=a_cT[:, :, :], op=ALU.mult), r=[b_sc, b_cT], w=[b_sc])
    wi = 0
    for i in range(nlayers):
        pm, b_pm = bank()
        P.sp(lambda h, i=i: h.dma_start(out=a_bm[:, :], in_=bmod[i]), dma_w=b_bm)
        for ft in range(12):
            wt, bw = a_w[wi % 2], b_w[wi % 2]
            wi += 1
            P.sp(lambda h, wt=wt, i=i, ft=ft: h.dma_start(out=wt[:, :, :], in_=wmod[i, ft]), dma_w=bw)
            for fc in range(4):
                c = ft * 4 + fc
                for k in range(8):
                    P.pe(lambda h, pm=pm, wt=wt, c=c, fc=fc, k=k: h.matmul(
                        pm[:, c * 3:c * 3 + 3], lhsT=wt[:, k, fc * 128:(fc + 1) * 128], rhs=a_sc[:, k, :],
                        start=(k == 0), stop=(k == 7)),
                        r=[bw, b_sc], **({"w": [b_pm]} if (c == 0 and k == 0) else {"pw": [b_pm]}))
        mv = modv[i]
        P.dve(lambda h, mv=mv, pm=pm: h.tensor_tensor(
            out=mv[:, :, :], in0=pm[:, 0:144].rearrange("p (c r) -> p c r", r=3),
            in1=a_bm[:, :].unsqueeze(2).to_broadcast([128, 48, 3]), op=ALU.add),
            r=[b_pm, b_bm], w=[b_modv[i]])
        for which in (1, 4):
            P.dve(lambda h, mv=mv, which=which: h.tensor_scalar(
                out=mv[:, which * 8:(which + 1) * 8, :], in0=mv[:, which * 8:(which + 1) * 8, :],
                scalar1=1.0, scalar2=1.0 / ALPHA, op0=ALU.add, op1=ALU.mult),
                r=[b_modv[i]], w=[b_modv[i]])

    def mod(layer, which, k, rr):
        return modv[layer][:, which * 8 + k, rr:rr + 1]

    P.barrier()

    LN_OFF = ARENA + 110080

    def layer_norm(tgi, layer, which_g, final=False):
        t0, n, _ = TGS[tgi]
        sq = [P.sb("lnsq", [128, 512], F32, LN_OFF + i * 2048) for i in range(2)]
        b_sq = [K.b_lnsq0, K.b_lnsq1]
        st = [P.sb("lnst", [128, 512], F32, LN_OFF + 4096 + i * 2048) for i in range(4)]
        b_st = K.b_lnst
        bh = b_hs[0][tgi]
        p1, b_p1 = bank()
        p2, b_p2 = bank()
        for k in range(8):
            s, bs = sq[k % 2], b_sq[k % 2]
            P.act(lambda h, s=s, k=k: h.activation(out=s[:, 0:n], in_=hs[:, k, t0:t0 + n], func=AF.Square),
                  r=[bh], w=[bs])
            P.pe(lambda h, k=k: h.matmul(p1[:, 0:n], lhsT=ones_f[:, :], rhs=hs[:, k, t0:t0 + n],
                                         start=(k == 0), stop=(k == 7)),
                 r=[bh, b_c2], **({"w": [b_p1]} if k == 0 else {"pw": [b_p1]}))
            P.pe(lambda h, s=s, k=k: h.matmul(p2[:, 0:n], lhsT=ones_f[:, :], rhs=s[:, 0:n],
                                              start=(k == 0), stop=(k == 7)),
                 r=[bs, b_c2], **({"w": [b_p2]} if k == 0 else {"pw": [b_p2]}))
        mean, var, rstd, nmr = st
        P.act(lambda h: h.activation(out=mean[:, 0:n], in_=p1[:, 0:n], func=AF.Copy), r=[b_p1], w=[b_st])
        P.dve(lambda h: h.tensor_tensor(out=var[:, 0:n], in0=mean[:, 0:n], in1=mean[:, 0:n], op=ALU.mult), r=[b_st], w=[b_st])
        P.dve(lambda h: h.tensor_tensor(out=var[:, 0:n], in0=p2[:, 0:n], in1=var[:, 0:n], op=ALU.subtract), r=[b_st, b_p2], w=[b_st])
        P.act(lambda h: h.activation(out=rstd[:, 0:n], in_=var[:, 0:n], func=AF.Ln, bias=epsv[:, 0:1], scale=1.0), r=[b_st, b_c2], w=[b_st])
        P.act(lambda h: h.activation(out=rstd[:, 0:n], in_=rstd[:, 0:n], func=AF.Exp, scale=-0.5), r=[b_st], w=[b_st])
        P.dve(lambda h: h.scalar_tensor_tensor(out=nmr[:, 0:n], in0=mean[:, 0:n], scalar=-1.0, in1=rstd[:, 0:n],
                                               op0=ALU.mult, op1=ALU.mult), r=[b_st], w=[b_st])
        for k in range(8):
            P.dve(lambda h, k=k: h.tensor_tensor(out=hs[:, k, t0:t0 + n], in0=hs[:, k, t0:t0 + n], in1=rstd[:, 0:n], op=ALU.mult),
                  r=[bh, b_st], w=[bh])
            P.dve(lambda h, k=k: h.tensor_tensor(out=hs[:, k, t0:t0 + n], in0=hs[:, k, t0:t0 + n], in1=nmr[:, 0:n], op=ALU.add),
                  r=[bh, b_st], w=[bh])
            P.act(lambda h, k=k: h.activation(out=hs[:, k, t0:t0 + n], in_=hs[:, k, t0:t0 + n], func=AF.Identity,
                                              scale=lnvec(layer, which_g, k, not final), bias=lnvec(layer, which_g + 1, k, not final)),
                  r=[bh, b_c2], w=[bh])

    K.att_stage = dbg.get("stage", 4) if isinstance(dbg, dict) else 4
    K.b_lnsq0 = P.buf("lnsq0")
    K.b_lnsq1 = P.buf("lnsq1")
    K.b_lnst = P.buf("lnst")

    M_UT = ARENA
    M_HID = ARENA + 16384
    M_W1 = M_HID + 32768
    M_W2 = M_W1 + 16384
    M_RT = M_W2 + 16384
    m_uT = [P.sb("m_uT", [128, 8, 512], BF16, M_UT + i * 8192) for i in range(2)]
    m_hid = P.sb("m_hid", [128, 32, 512], BF16, M_HID)
    m_w1 = [P.sb("m_w1", [128, 8, 512], BF16, M_W1 + i * 8192) for i in range(2)]
    m_w2 = [P.sb("m_w2", [128, 32, 128], BF16, M_W2 + i * 8192) for i in range(2)]
    m_rt = [P.sb("m_rt", [128, 512], F32, M_RT + i * 2048) for i in range(2)]
    assert M_RT + 4096 <= LN_OFF
    bm_uT = [P.buf("m_uT0"), P.buf("m_uT1")]
    bm_hid = P.buf("m_hid")
    bm_w1 = [P.buf("m_w10"), P.buf("m_w11")]
    bm_w2 = [P.buf("m_w20"), P.buf("m_w21")]
    bm_rt = [P.buf("m_rt0"), P.buf("m_rt1")]
    cnt = {"ut": 0, "w1": 0, "w2": 0, "rt": 0}

    def mlp(tgi, layer, rr):
        t0, n, _ = TGS[tgi]
        bh = b_hs[0][tgi]
        ui = cnt["ut"] % 2
        cnt["ut"] += 1
        uT, buT = m_uT[ui], bm_uT[ui]
        for k in range(8):
            P.act(lambda h, k=k: h.activation(out=uT[:, k, 0:n], in_=hs[:, k, t0:t0 + n], func=AF.Identity,
                                              scale=mod(layer, 4, k, rr), bias=mod(layer, 3, k, rr)),
                  r=[bh, b_modv[layer]], **({"w": [buT]} if k == 0 else {"pw": [buT]}))
        for fb in range(8):
            wi_ = cnt["w1"] % 2
            cnt["w1"] += 1
            w1t, bw1 = m_w1[wi_], bm_w1[wi_]
            P.pool(lambda h, w1t=w1t, fb=fb: h.dma_start(out=w1t[:, :, :], in_=w1_d[layer, fb]), dma_w=bw1)
            for fc in range(4):
                pb, b_pb = bank()
                for k in range(8):
                    P.pe(lambda h, pb=pb, w1t=w1t, fc=fc, k=k: h.matmul(
                        pb[:, 0:n], lhsT=w1t[:, k, fc * 128:(fc + 1) * 128], rhs=uT[:, k, 0:n],
                        start=(k == 0), stop=(k == 7)),
                        r=[bw1, buT], **({"w": [b_pb]} if k == 0 else {"pw": [b_pb]}))
                ri = cnt["rt"] % 2
                cnt["rt"] += 1
                rt, brt = m_rt[ri], bm_rt[ri]
                P.dve(lambda h, pb=pb, rt=rt: h.tensor_scalar(out=rt[:, 0:n], in0=pb[:, 0:n], scalar1=0.0, scalar2=None, op0=ALU.max),
                      r=[b_pb], w=[brt])
                f = fb * 4 + fc
                P.act(lambda h, rt=rt, f=f: h.activation(out=m_hid[:, f, 0:n], in_=rt[:, 0:n], func=AF.Square),
                      r=[brt], **({"w": [bm_hid]} if f == 0 else {"pw": [bm_hid]}))
        for dc in range(8):
            wi_ = cnt["w2"] % 2
            cnt["w2"] += 1
            w2t, bw2 = m_w2[wi_], bm_w2[wi_]
            P.pool(lambda h, w2t=w2t, dc=dc: h.dma_start(out=w2t[:, :, :], in_=w2_d[layer, dc]), dma_w=bw2)
            pb, b_pb = bank()
            for kf in range(32):
                P.pe(lambda h, pb=pb, w2t=w2t, kf=kf: h.matmul(
                    pb[:, 0:n], lhsT=w2t[:, kf, :], rhs=m_hid[:, kf, 0:n], start=(kf == 0), stop=(kf == 31)),
                    r=[bw2, bm_hid], **({"w": [b_pb]} if kf == 0 else {"pw": [b_pb]}))
            P.dve(lambda h, pb=pb, dc=dc: h.scalar_tensor_tensor(
                out=hs[:, dc, t0:t0 + n], in0=pb[:, 0:n], scalar=mod(layer, 5, dc, rr), in1=hs[:, dc, t0:t0 + n],
                op0=ALU.mult, op1=ALU.add),
                r=[b_pb, bh, b_modv[layer]], w=[bh])

    A_UT = ARENA
    A_COS = A_UT + 36864
    A_W = A_COS + 16384
    A_KT = A_W + 19456
    A_V = A_KT + 4608
    A_Q = A_V + 4608
    A_OT = A_Q + 8192
    A_PT = A_OT + 4096
    A_TMP = A_PT + 3072
    A_MN = A_TMP + 6144
    A_SK = A_MN + 4096
    A_RD = A_SK + 8192
    A_SQ = A_RD + 2048
    A_OUN = A_SQ + 1024
    A_END = A_OUN + 2048
    assert A_END <= 212832, A_END
    uTa = P.sb("uTa", [128, 8, T], BF16, A_UT)
    cstab = P.sb("cstab", [128, SEQ], F32, A_COS)
    wqq = P.sb("wqq", [128, 8, 512], BF16, A_W)
    wkv = P.sb("wkv", [128, 8, 192], BF16, A_W + 8192)
    wo = P.sb("wo", [128, 2, 1024], BF16, A_W + 11264)
    kTa = P.sb("kTa", [65, T], BF16, A_KT)
    Va = P.sb("Va", [128, 18, 128], BF16, A_V)
    qa = P.sb("qa", [65, 4, 512], BF16, A_Q)
    qpa = P.sb("qpa", [65, 4, 512], BF16, A_Q + 4096)
    OT = P.sb("OT", [128, 2, 512], BF16, A_OT)
    PT = [P.sb("PT", [128, 512], BF16, A_PT + i * 1024) for i in range(3)]
    tmp = [P.sb("atmp", [128, 512], F32, A_TMP + i * 2048) for i in range(3)]
    mneg = P.sb("mneg", [128, 4, 512], BF16, A_MN)
    skt = P.sb("skt", [128, 4, 512], F32, A_SK)
    rden = P.sb("rden", [64, 512], F32, A_RD)
    sqt = P.sb("sqt", [64, 512], BF16, A_SQ)
    oun = P.sb("oun", [64, 512], F32, A_OUN)
    b_oun = P.buf("oun")
    qbc = [0]
    psc = [0]
    pending_norm = []
    b_uTa = [P.buf(f"uTa{g}") for g in range(len(TGS))]
    b_rope = P.buf("rope")
    b_wq, b_wkv, b_wo = P.buf("wq"), P.buf("wkv"), P.buf("wo")
    b_kT, b_V = P.buf("kT"), P.buf("V")
    b_qa, b_qpa, b_OT = P.buf("qa"), P.buf("qpa"), P.buf("OT")
    b_PT = [P.buf(f"PT{i}") for i in range(3)]
    b_tmp = [P.buf(f"atmp{i}") for i in range(3)]
    b_mneg, b_skt, b_rden, b_sqt = P.buf("mneg"), P.buf("skt"), P.buf("rden"), P.buf("sqt")
    ptc = [0]

    def rope(dst, bdst, pa, b_pa, lt0, n, pwflag):
        t1, t2 = tmp[0], tmp[1]
        P.dve(lambda h: h.tensor_tensor(out=t1[0:64, 0:n], in0=pa[0:64, 0:n], in1=cstab[0:64, lt0:lt0 + n], op=ALU.mult),
              r=[b_pa, b_rope], w=[b_tmp[0]])
        P.dve(lambda h: h.tensor_tensor(out=t2[0:64, 0:n], in0=pa[64:128, 0:n], in1=cstab[64:128, lt0:lt0 + n], op=ALU.mult),
              r=[b_pa, b_rope], w=[b_tmp[1]])
        P.dve(lambda h: h.tensor_tensor(out=dst, in0=t1[0:64, 0:n], in1=t2[0:64, 0:n], op=ALU.add),
              r=[b_tmp[0], b_tmp[1]], **({"pw": [bdst]} if pwflag else {"w": [bdst]}))

    def attention(layer, b):
        rr_l = b
        for tgi, (t0, n, isctx) in enumerate(TGS):
            rr = 2 if isctx else rr_l
            for k in range(8):
                P.act(lambda h, k=k, t0=t0, n=n, rr=rr: h.activation(
                    out=uTa[:, k, t0:t0 + n], in_=hs[:, k, t0:t0 + n], func=AF.Identity,
                    scale=mod(layer, 1, k, rr), bias=mod(layer, 0, k, rr)),
                    r=[b_hs[0][tgi], b_modv[layer]], **({"w": [b_uTa[tgi]]} if k == 0 else {"pw": [b_uTa[tgi]]}))
        P.sp(lambda h: h.dma_start(out=cstab[:, :], in_=cs_d[:, :]), dma_w=b_rope)
        for g in range(4):
            P.pool(lambda h, g=g: h.dma_start(out=wqq[:, :, :], in_=wqq_d[g]), dma_w=b_wq)
            P.pool(lambda h, g=g: h.dma_start(out=wkv[:, :, :], in_=wkv_d[g]), dma_w=b_wkv)
            P.pool(lambda h, g=g: h.dma_start(out=wo[:, :, :], in_=wo_d[g]), dma_w=b_wo)
            P.pool(lambda h: h.memset(kTa[64:65, :], 1.0), w=[b_kT])
            P.pool(lambda h: h.memset(Va[:, :, 64:128], 1.0), w=[b_V])
            for tgi, (t0, n, isctx) in enumerate(TGS):
                pk, b_pk = bank()
                for k in range(8):
                    P.pe(lambda h, pk=pk, k=k, t0=t0, n=n: h.matmul(
                        pk[:, 0:n], lhsT=wkv[:, k, 0:128], rhs=uTa[:, k, t0:t0 + n], start=(k == 0), stop=(k == 7)),
                        r=[b_wkv, b_uTa[tgi]], **({"w": [b_pk]} if k == 0 else {"pw": [b_pk]}))
                if isctx:
                    P.act(lambda h, pk=pk, t0=t0, n=n: h.activation(out=kTa[0:64, t0:t0 + n], in_=pk[0:64, 0:n], func=AF.Copy),
                          r=[b_pk], pw=[b_kT])
                else:
                    rope(kTa[0:64, t0:t0 + n], b_kT, pk, b_pk, t0 - CTX, n, True)
                P.act(lambda h, t0=t0, n=n: h.activation(out=sqt[:, 0:n], in_=kTa[0:64, t0:t0 + n], func=AF.Square),
                      r=[b_kT], w=[b_sqt])
                pn, b_pn = bank()
                P.pe(lambda h, pn=pn, n=n: h.matmul(pn[:, 0:n], lhsT=ones_bf[0:64, :], rhs=sqt[:, 0:n], start=True, stop=True),
                     r=[b_sqt, b_c2], w=[b_pn])
                P.dve(lambda h, pn=pn, n=n, tgi=tgi: h.tensor_reduce(out=kmaxp[:, tgi:tgi + 1], in_=pn[:, 0:n], axis=AX.X, op=ALU.max),
                      r=[b_pn], **({"w": [b_kmax]} if tgi == 0 else {"pw": [b_kmax]}))
                for bi in range(n // 128):
                    blk = t0 // 128 + bi
                    pv, b_pv = bank()
                    for k in range(8):
                        P.pe(lambda h, pv=pv, k=k, blk=blk: h.matmul(
                            pv[:, 0:64], lhsT=uTa[:, k, blk * 128:(blk + 1) * 128], rhs=wkv[:, k, 128:192],
                            start=(k == 0), stop=(k == 7)),
                            r=[b_wkv, b_uTa[tgi]], **({"w": [b_pv]} if k == 0 else {"pw": [b_pv]}))
                    P.act(lambda h, pv=pv, blk=blk: h.activation(out=Va[:, blk, 0:64], in_=pv[:, 0:64], func=AF.Copy),
                          r=[b_pv], pw=[b_V])
            P.dve(lambda h: h.tensor_reduce(out=kmax2[:, :], in_=kmaxp[:, 0:5], axis=AX.X, op=ALU.max), r=[b_kmax], w=[b_kmax])
            P.dve(lambda h: h.tensor_scalar(out=kmax2[:, :], in0=kmax2[:, :], scalar1=1.05, scalar2=None, op0=ALU.mult), r=[b_kmax], w=[b_kmax])
            for tgi, (t0, n, isctx) in enumerate(TGS):
                if K.att_stage < 2:
                    break
                pns = {}

                def q_chain(r):
                    pn, b_pn = pns[r]
                    t3 = tmp[2]
                    P.act(lambda h, pn=pn, n=n: h.activation(out=t3[:, 0:n], in_=pn[:, 0:n], func=AF.Ln, scale=kmax2[:, 0:1], bias=epsv[:, 1:2]),
                          r=[b_pn, b_kmax, b_c2], w=[b_tmp[2]])
                    P.act(lambda h, n=n: h.activation(out=t3[:, 0:n], in_=t3[:, 0:n], func=AF.Exp, scale=0.5), r=[b_tmp[2]], w=[b_tmp[2]])
                    P.dve(lambda h, r=r, n=n: h.tensor_scalar(out=mneg[:, r, 0:n], in0=t3[:, 0:n], scalar1=-1.0, scalar2=None, op0=ALU.mult),
                          r=[b_tmp[2]], **({"w": [b_mneg]} if r == 0 else {"pw": [b_mneg]}))
                    hd = g * 4 + r
                    P.act(lambda h, r=r, n=n, hd=hd: h.activation(out=skt[64:128, r, 0:n], in_=mneg[64:128, r, 0:n], func=AF.Exp,
                                                                   scale=0.125, bias=sinkv[64:128, hd:hd + 1]),
                          r=[b_mneg, b_const], **({"w": [b_skt]} if r == 0 else {"pw": [b_skt]}))

                pqs = {}

                def q_mm(r):
                    pq, b_pq = bank()
                    pqs[r] = (pq, b_pq)
                    for k in range(8):
                        P.pe(lambda h, pq=pq, k=k, r=r, t0=t0, n=n: h.matmul(
                            pq[:, 0:n], lhsT=wqq[:, k, r * 128:(r + 1) * 128], rhs=uTa[:, k, t0:t0 + n], start=(k == 0), stop=(k == 7)),
                            r=[b_wq, b_uTa[tgi]], **({"w": [b_pq]} if k == 0 else {"pw": [b_pq]}))

                q_mm(0)
                for r in range(4):
                    if r + 1 < 4:
                        q_mm(r + 1)
                    pq, b_pq = pqs[r]
                    P.act(lambda h, pq=pq, r=r, n=n: h.activation(out=qpa[0:64, r, 0:n], in_=pq[0:64, 0:n], func=AF.Copy),
                          r=[b_pq], **({"w": [b_qpa]} if r == 0 else {"pw": [b_qpa]}))
                    if not isctx:
                        rope(qa[0:64, r, 0:n], b_qa, pq, b_pq, t0 - CTX, n, r != 0)
                    P.act(lambda h, r=r, n=n: h.activation(out=sqt[:, 0:n], in_=qpa[0:64, r, 0:n], func=AF.Square),
                          r=[b_qpa], w=[b_sqt])
                    pn, b_pn = bank()
                    pns[r] = (pn, b_pn)
                    P.pe(lambda h, pn=pn, n=n: h.matmul(pn[:, 0:n], lhsT=ones_bf[0:64, :], rhs=sqt[:, 0:n], start=True, stop=True),
                         r=[b_sqt, b_c2], w=[b_pn])
                    if r >= 1:
                        q_chain(r - 1)
                q_chain(3)
                P.dve(lambda h, n=n: h.tensor_copy(out=qpa[64:65, :, 0:n], in_=mneg[64:65, :, 0:n]), r=[b_mneg], pw=[b_qpa])
                if not isctx:
                    P.dve(lambda h, n=n: h.tensor_copy(out=qa[64:65, :, 0:n], in_=mneg[64:65, :, 0:n]), r=[b_mneg], pw=[b_qa])
                if K.att_stage < 3:
                    continue
                for qb in range(n // 128):
                    qs = slice(qb * 128, (qb + 1) * 128)
                    if isctx:
                        chunks = [(0, qpa, b_qpa, None), (1, qpa, b_qpa, None)]
                    else:
                        j = (t0 - CTX) // 128 + qb
                        chunks = []
                        if j > 0:
                            chunks.append((2 + j - 1, qa, b_qa, mprev))
                        chunks.append((2 + j, qa, b_qa, None))
                        if j < 15:
                            chunks.append((2 + j + 1, qa, b_qa, mnext))
                        chunks += [(0, qpa, b_qpa, None), (1, qpa, b_qpa, None)]
                    po, b_po = pbank[6 + qbc[0] % 2], b_bank[6 + qbc[0] % 2]
                    qbc[0] += 1
                    nch = len(chunks)
                    pss = [None] * nch

                    def emit_norm(po=po, b_po=b_po, qs=qs, qb=qb):
                        P.dve(lambda h, po=po, qs=qs: h.tensor_tensor(out=rden[0:64, :].rearrange("p (r q) -> p r q", r=4),
                                                                      in0=po[64:128, :].rearrange("p (r q) -> p r q", r=4),
                                                                      in1=skt[64:128, :, qs], op=ALU.add),
                              r=[b_po, b_skt], w=[b_rden])
                        P.dve(lambda h, po=po: h.tensor_copy(out=oun[0:64, :], in_=po[0:64, :]), r=[b_po], w=[b_oun])
                        P.act(lambda h: h.activation(out=rden[0:64, :], in_=rden[0:64, :], func=AF.Ln), r=[b_rden], w=[b_rden])
                        P.act(lambda h: h.activation(out=rden[0:64, :], in_=rden[0:64, :], func=AF.Exp, scale=-1.0), r=[b_rden], w=[b_rden])
                        for par in range(2):
                            P.dve(lambda h, qs=qs, par=par: h.tensor_tensor(
                                out=OT[par * 64:(par + 1) * 64, :, qs],
                                in0=oun[0:64, :].rearrange("p (a b q) -> p a b q", a=2, b=2)[:, :, par, :],
                                in1=rden[0:64, :].rearrange("p (a b q) -> p a b q", a=2, b=2)[:, :, par, :], op=ALU.mult),
                                r=[b_oun, b_rden], **({"w": [b_OT]} if (qb == 0 and par == 0) else {"pw": [b_OT]}))


                    def emit_S(ci):
                        kb, qt, bqt, msk = chunks[ci]
                        bi_ = psc[0] % 6
                        psc[0] += 1
                        ps_, b_ps = pbank[bi_], b_bank[bi_]
                        pss[ci] = (ps_, b_ps)
                        P.pe(lambda h, ps_=ps_, kb=kb, qt=qt, qs=qs, msk=msk: h.matmul(
                            ps_[:, :], lhsT=kTa[0:65, kb * 128:(kb + 1) * 128], rhs=qt[0:65, :, qs],
                            start=True, stop=(msk is None)),
                            r=[b_kT, bqt], w=[b_ps])
                        if msk is not None:
                            P.pe(lambda h, ps_=ps_, msk=msk: h.matmul(ps_[:, :], lhsT=ident[:, :], rhs=msk[:, :], start=False, stop=True),
                                 r=[b_const], pw=[b_ps])

                    emit_S(0)
                    for ci in range(nch):
                        if ci + 1 < nch:
                            emit_S(ci + 1)
                        kb = chunks[ci][0]
                        ps_, b_ps = pss[ci]
                        pi = ptc[0] % 3
                        ptc[0] += 1
                        pt_, bpt = PT[pi], b_PT[pi]
                        P.act(lambda h, ps_=ps_, pt_=pt_: h.activation(out=pt_[:, :], in_=ps_[:, :], func=AF.Exp, scale=0.125),
                              r=[b_ps], w=[bpt])
                        P.pe(lambda h, po=po, kb=kb, pt_=pt_, ci=ci, nch=nch: h.matmul(
                            po[:, :], lhsT=Va[:, kb, :], rhs=pt_[:, :], start=(ci == 0), stop=(ci == nch - 1)),
                            r=[b_V, bpt], **({"w": [b_po]} if ci == 0 else {"pw": [b_po]}))
                        if ci == 1 and len(pending_norm) > 0:
                            pending_norm.pop(0)()
                    pending_norm.append(emit_norm)
                while pending_norm:
                    pending_norm.pop(0)()
                if K.att_stage < 4:
                    continue
                rr = 2 if isctx else rr_l
                for kd in range(8):
                    py, b_py = bank()
                    for r in range(2):
                        P.pe(lambda h, py=py, r=r, kd=kd, n=n: h.matmul(
                            py[:, 0:n], lhsT=wo[:, r, kd * 128:(kd + 1) * 128], rhs=OT[:, r, 0:n],
                            start=(r == 0), stop=(r == 1)),
                            r=[b_wo, b_OT], **({"w": [b_py]} if r == 0 else {"pw": [b_py]}))
                    P.dve(lambda h, py=py, kd=kd, t0=t0, n=n, rr=rr: h.scalar_tensor_tensor(
                        out=hs[:, kd, t0:t0 + n], in0=py[:, 0:n], scalar=mod(layer, 2, kd, rr), in1=hs[:, kd, t0:t0 + n],
                        op0=ALU.mult, op1=ALU.add),
                        r=[b_py, b_hs[0][tgi], b_modv[layer]], w=[b_hs[0][tgi]])
        P.barrier()
        for tgi in range(len(TGS)):
            layer_norm(tgi, layer, 0)
        P.barrier()


    so = [ARENA + 36864]

    def salloc(name, shape, dt, n=1):
        nb = int(np.prod(shape[1:])) * (4 if dt == F32 else 2)
        nb = (nb + 31) // 32 * 32
        ts = [P.sb(name, shape, dt, so[0] + i * nb) for i in range(n)]
        so[0] += nb * n
        return ts if n > 1 else ts[0]

    s_wxbc = salloc("s_wxbc", [128, 8, 768], BF16)
    s_wdt = salloc("s_wdt", [128, 8, 16], BF16)
    s_raw = salloc("s_raw", [128, 520], F32, 3)
    s_acc = salloc("s_acc", [128, 512], F32, 3)
    s_th = salloc("s_th", [128, 512], F32, 3)
    s_xo = salloc("s_xo", [128, 512], BF16, 3)
    s_t1 = salloc("s_t1", [128, 18, 16], F32)
    S1_END = so[0]
    so[0] = ARENA + 36864
    s_wso = salloc("s_wso", [128, 4, 1024], BF16)
    s_Sbin = salloc("s_Sbin", [128, 16, 512], BF16)
    s_Sf = salloc("s_Sf", [128, 512], F32)
    s_Sfb = salloc("s_Sfb", [128, 512], BF16)
    s_xw = salloc("s_xw", [128, 512], BF16, 2)
    s_cbm = salloc("s_cbm", [128, 128], BF16, 2)
    s_arg = salloc("s_arg", [128, 4, 128], BF16, 2)
    s_M = salloc("s_M", [128, 4, 128], BF16, 4)
    s_ya = salloc("s_ya", [128, 512], F32)
    s_yb = salloc("s_yb", [128, 512], F32)
    s_Sb = s_yb
    s_yg = s_ya
    s_yn = salloc("s_yn", [128, 512], BF16, 2)
    s_gz = salloc("s_gz", [128, 512], BF16)
    s_ynT = salloc("s_ynT", [128, 4, 512], BF16)
    s_ss = salloc("s_ss", [128, 4], F32)
    S2_END = so[0]
    so[0] = max(S1_END, S2_END)
    s_wz = salloc("s_wz", [128, 8, 512], BF16)
    s_xtok = salloc("s_xtok", [128, 18, 512], BF16)
    s_btok = salloc("s_btok", [128, 18, 128], BF16)
    s_BT = salloc("s_BT", [128, T], BF16)
    s_CT = salloc("s_CT", [128, T], BF16)
    s_dt = salloc("s_dt", [128, 18, 16], F32)
    s_a = salloc("s_a", [128, 18, 16], F32)
    s_cs = salloc("s_cs", [128, 18, 16], F32)
    s_tot = salloc("s_tot", [128, 18, 16], F32)
    s_E = salloc("s_E", [128, 18, 16], F32)
    s_dec = salloc("s_dec", [128, 18, 16], F32)
    s_cw = salloc("s_cw", [128, 36], F32)
    s_sb = salloc("s_sb", [128, 40], F32)
    s_ng = salloc("s_ng", [128, 4], F32)
    s_tri = salloc("s_tri", [128, 256], F32)
    s_one = salloc("s_one", [128, 2], F32)
    assert so[0] <= 212832, so[0]
    bs = {nm: P.buf(nm) for nm in ("xtok", "btok", "BT", "CT", "gz", "dt", "a", "cs", "tot", "E", "dec", "t1", "t2", "par",
                                   "wxbc", "wz", "wdt", "raw0", "raw1", "raw2", "acc0", "acc1", "acc2", "th0", "th1", "th2", "xo0", "xo1", "xo2", "arg0", "arg1", "xdt0", "xdt1", "wso", "Sbin", "Sf", "Sb",
                                   "Sfb", "xw0", "xw1", "cbm0", "cbm1", "arg", "L", "M0", "M1", "M2", "M3", "ya", "yb", "yg",
                                   "yn0", "yn1", "ynT", "ss")}
    sc_ = {"raw": 0, "xo": 0, "xw": 0, "M": 0, "acc": 0, "arg": 0}
    tri = s_tri[:, 0:128]
    trirev = s_tri[:, 128:256]

    def halo_ap(base, stride):
        a = base.ap
        return bass.AP(base.tensor, base.offset, [list(a[0]), [stride, 2], [1, 2]])

    def bc8(ap2):
        return ap2.unsqueeze(2).to_broadcast([128, 8, 64])

    def ssm(layer, b):
        rr_l = b
        for tgi, (t0, n, isctx) in enumerate(TGS):
            rr = 2 if isctx else rr_l
            for k in range(8):
                P.act(lambda h, k=k, t0=t0, n=n, rr=rr: h.activation(
                    out=uTa[:, k, t0:t0 + n], in_=hs[:, k, t0:t0 + n], func=AF.Identity,
                    scale=mod(layer, 1, k, rr), bias=mod(layer, 0, k, rr)),
                    r=[b_hs[0][tgi], b_modv[layer]], **({"w": [b_uTa[tgi]]} if k == 0 else {"pw": [b_uTa[tgi]]}))
        P.sp(lambda h: h.dma_start(out=s_tri[:, :], in_=tri_d[:, :]), dma_w=bs["par"])
        P.pool(lambda h: h.memset(s_one[:, :], 1.0), w=[bs["t2"]])
        t2v = None
        for g in range(4):
            P.sp(lambda h, g=g: h.dma_start(out=s_cw[:, :], in_=cw_d[g]), dma_w=bs["par"])
            P.sp(lambda h, g=g: h.dma_start(out=s_sb[:, :], in_=sb_d[g]), dma_w=bs["par"])
            P.sp(lambda h, g=g: h.dma_start(out=s_ng[:, :], in_=ng_d[g]), dma_w=bs["par"])
            P.pool(lambda h, g=g: h.dma_start(out=s_wxbc[:, :, :], in_=wxbc_d[g]), dma_w=bs["wxbc"])
            P.pool(lambda h, g=g: h.dma_start(out=s_wz[:, :, :], in_=wz_d[g]), dma_w=bs["wz"])
            P.pool(lambda h, g=g: h.dma_start(out=s_wdt[:, :, :], in_=wdt_d[g]), dma_w=bs["wdt"])
            P.dve(lambda h: h.tensor_scalar(out=s_cw[:, :], in0=s_cw[:, :], scalar1=0.5, scalar2=None, op0=ALU.mult), r=[bs["par"]], w=[bs["par"]])
            P.act(lambda h: h.activation(out=s_sb[:, 16:32], in_=s_sb[:, 16:32], func=AF.Exp), r=[bs["par"]], w=[bs["par"]])
            P.dve(lambda h: h.tensor_scalar(out=s_sb[:, 16:32], in0=s_sb[:, 16:32], scalar1=-1.0, scalar2=None, op0=ALU.mult), r=[bs["par"]], w=[bs["par"]])
            def A1(tile):
                tgi, c = tile["tgi"], tile["c"]
                t0, n, isctx = TGS[tgi]
                seg0, seg1 = (0, CTX) if isctx else (CTX, T)
                ri = sc_["raw"] % 3
                sc_["raw"] += 1
                raw, braw = s_raw[ri], bs[f"raw{ri}"]
                ai = sc_["acc"] % 3
                sc_["acc"] += 1
                acc_, bacc = s_acc[ai], bs[f"acc{ai}"]
                th_, bth = s_th[ai], bs[f"th{ai}"]
                tile.update(raw=raw, braw=braw, acc=acc_, bacc=bacc, th=th_, bth=bth)
                pm_, b_pm_ = bank()
                for k in range(8):
                    P.pe(lambda h, k=k: h.matmul(pm_[:, 0:n], lhsT=s_wxbc[:, k, c * 128:(c + 1) * 128], rhs=uTa[:, k, t0:t0 + n],
                                                 start=(k == 0), stop=(k == 7)),
                         r=[bs["wxbc"], b_uTa[tgi]], **({"w": [b_pm_]} if k == 0 else {"pw": [b_pm_]}))
                P.act(lambda h: h.activation(out=raw[:, 2:2 + n], in_=pm_[:, 0:n], func=AF.Copy), r=[b_pm_], w=[braw])
                hasl = (t0 - 2) >= seg0
                hasr = (t0 + n + 2) <= seg1
                if not hasl:
                    P.pool(lambda h: h.memset(raw[:, 0:2], 0.0), pw=[braw])
                if not hasr:
                    P.pool(lambda h: h.memset(raw[:, 2 + n:4 + n], 0.0), pw=[braw])
                if hasl or hasr:
                    ph, b_ph = bank()
                    hbufs = []
                    if hasl:
                        hbufs.append(b_uTa[[i for i, (a0, an, _) in enumerate(TGS) if a0 <= t0 - 2 < a0 + an][0]])
                    if hasr:
                        hbufs.append(b_uTa[[i for i, (a0, an, _) in enumerate(TGS) if a0 <= t0 + n < a0 + an][0]])
                    for k in range(8):
                        if hasl and hasr:
                            rhs_fn = lambda k=k: halo_ap(uTa[:, k, t0 - 2:t0], n + 2)
                            ncol = 4
                        elif hasl:
                            rhs_fn = lambda k=k: uTa[:, k, t0 - 2:t0]
                            ncol = 2
                        else:
                            rhs_fn = lambda k=k: uTa[:, k, t0 + n:t0 + n + 2]
                            ncol = 2
                        P.pe(lambda h, k=k, rhs_fn=rhs_fn, ncol=ncol: h.matmul(
                            ph[:, 0:ncol], lhsT=s_wxbc[:, k, c * 128:(c + 1) * 128], rhs=rhs_fn(), start=(k == 0), stop=(k == 7)),
                            r=[bs["wxbc"]] + hbufs, **({"w": [b_ph]} if k == 0 else {"pw": [b_ph]}))
                    if hasl:
                        P.act(lambda h: h.activation(out=raw[:, 0:2], in_=ph[:, 0:2], func=AF.Copy), r=[b_ph], pw=[braw])
                    if hasr:
                        o_ = 2 if hasl else 0
                        P.act(lambda h: h.activation(out=raw[:, 2 + n:4 + n], in_=ph[:, o_:o_ + 2], func=AF.Copy), r=[b_ph], pw=[braw])
                P.act(lambda h: h.activation(out=acc_[:, 0:n], in_=raw[:, 0:n], func=AF.Identity,
                                             scale=s_cw[:, c * 6:c * 6 + 1], bias=s_cw[:, c * 6 + 5:c * 6 + 6]),
                      r=[braw, bs["par"]], w=[bacc])

            def A2(tile):
                c = tile["c"]
                t0, n, isctx = TGS[tile["tgi"]]
                raw, braw, acc_, bacc = tile["raw"], tile["braw"], tile["acc"], tile["bacc"]
                for j in range(1, 5):
                    P.dve(lambda h, j=j: h.scalar_tensor_tensor(
                        out=acc_[:, 0:n], in0=raw[:, j:j + n], scalar=s_cw[:, c * 6 + j:c * 6 + j + 1], in1=acc_[:, 0:n],
                        op0=ALU.mult, op1=ALU.add), r=[braw, bs["par"], bacc], w=[bacc])

            def A3(tile):
                t0, n, isctx = TGS[tile["tgi"]]
                acc_, bacc, th_, bth = tile["acc"], tile["bacc"], tile["th"], tile["bth"]
                P.act(lambda h: h.activation(out=th_[:, 0:n], in_=acc_[:, 0:n], func=AF.Tanh), r=[bacc], w=[bth])

            def A4(tile):
                tgi, c = tile["tgi"], tile["c"]
                t0, n, isctx = TGS[tgi]
                acc_, bacc, th_, bth = tile["acc"], tile["bacc"], tile["th"], tile["bth"]
                if c <= 4:
                    xi = sc_["xo"] % 3
                    sc_["xo"] += 1
                    xo, bxo = s_xo[xi], bs[f"xo{xi}"]
                    P.dve(lambda h: h.scalar_tensor_tensor(out=xo[:, 0:n], in0=th_[:, 0:n], scalar=1.0, in1=acc_[:, 0:n],
                                                           op0=ALU.add, op1=ALU.mult), r=[bth, bacc], w=[bxo])
                    if c == 4:
                        P.act(lambda h: h.activation(out=s_BT[:, t0:t0 + n], in_=xo[:, 0:n], func=AF.Copy), r=[bxo], pw=[bs["BT"]])
                    for bi in range(n // 128):
                        blk = t0 // 128 + bi
                        ptr, b_ptr = bank()
                        P.pe(lambda h, ptr=ptr, bi=bi: h.matmul(ptr[:, 0:128], lhsT=xo[:, bi * 128:(bi + 1) * 128], rhs=ident[:, :], start=True, stop=True),
                             r=[bxo, b_const], w=[b_ptr])
                        if c < 4:
                            P.act(lambda h, ptr=ptr, blk=blk: h.activation(out=s_xtok[:, blk, c * 128:(c + 1) * 128], in_=ptr[:, 0:128], func=AF.Copy),
                                  r=[b_ptr], pw=[bs["xtok"]])
                        else:
                            P.act(lambda h, ptr=ptr, blk=blk: h.activation(out=s_btok[:, blk, :], in_=ptr[:, 0:128], func=AF.Copy),
                                  r=[b_ptr], pw=[bs["btok"]])
                else:
                    P.dve(lambda h: h.scalar_tensor_tensor(out=s_CT[:, t0:t0 + n], in0=th_[:, 0:n], scalar=1.0, in1=acc_[:, 0:n],
                                                           op0=ALU.add, op1=ALU.mult), r=[bth, bacc], pw=[bs["CT"]])
                if c == 5:
                    for bi in range(n // 128):
                        blk = t0 // 128 + bi
                        pd, b_pd = bank()
                        for k in range(8):
                            P.pe(lambda h, pd=pd, k=k, blk=blk: h.matmul(pd[:, 0:16], lhsT=uTa[:, k, blk * 128:(blk + 1) * 128], rhs=s_wdt[:, k, :],
                                                                         start=(k == 0), stop=(k == 7)),
                                 r=[bs["wdt"], b_uTa[tgi]], **({"w": [b_pd]} if k == 0 else {"pw": [b_pd]}))
                        P.dve(lambda h, pd=pd, blk=blk: h.tensor_tensor(out=s_dt[:, blk, :], in0=pd[:, 0:16], in1=s_sb[:, 0:16], op=ALU.add),
                              r=[b_pd, bs["par"]], pw=[bs["dt"]])

            tiles = [dict(tgi=tgi, c=c) for tgi in range(len(TGS)) for c in range(6)]
            A1(tiles[0])
            for i, tile in enumerate(tiles):
                A2(tile)
                if i + 1 < len(tiles):
                    A1(tiles[i + 1])
                A3(tile)
                if i >= 1:
                    A4(tiles[i - 1])
            A4(tiles[-1])
            dtv, t1v = s_dt[:, :, :], s_t1[:, :, :]
            P.dve(lambda h: h.tensor_scalar(out=t1v, in0=dtv, scalar1=-1.0, scalar2=None, op0=ALU.mult), r=[bs["dt"]], w=[bs["t1"]])
            P.dve(lambda h: h.tensor_tensor(out=t1v, in0=t1v, in1=dtv, op=ALU.max), r=[bs["dt"], bs["t1"]], w=[bs["t1"]])
            P.act(lambda h: h.activation(out=t1v, in_=t1v, func=AF.Exp, scale=-1.0), r=[bs["t1"]], w=[bs["t1"]])
            P.act(lambda h: h.activation(out=t1v, in_=t1v, func=AF.Ln, bias=s_one[:, 0:1], scale=1.0), r=[bs["t1"], bs["t2"]], w=[bs["t1"]])
            P.dve(lambda h: h.scalar_tensor_tensor(out=dtv, in0=dtv, scalar=0.0, in1=t1v, op0=ALU.max, op1=ALU.add), r=[bs["dt"], bs["t1"]], w=[bs["dt"]])
            P.dve(lambda h: h.tensor_tensor(out=s_a[:, :, :], in0=dtv, in1=s_sb[:, 16:32].unsqueeze(1).to_broadcast([128, 18, 16]), op=ALU.mult),
                  r=[bs["dt"], bs["par"]], w=[bs["a"]])
            for blk in range(18):
                pc, b_pc = bank()
                P.pe(lambda h, pc=pc, blk=blk: h.matmul(pc[:, 0:8], lhsT=tri, rhs=s_a[:, blk, 0:8], start=True, stop=True), r=[bs["a"], bs["par"]], w=[b_pc])
                P.pe(lambda h, pc=pc, blk=blk: h.matmul(pc[:, 8:16], lhsT=trirev, rhs=s_a[:, blk, 8:16], start=True, stop=True), r=[bs["a"], bs["par"]], pw=[b_pc])
                P.pe(lambda h, pc=pc, blk=blk: h.matmul(pc[:, 16:32], lhsT=ones_f[:, :], rhs=s_a[:, blk, :], start=True, stop=True), r=[bs["a"], b_c2], pw=[b_pc])
                P.act(lambda h, pc=pc, blk=blk: h.activation(out=s_cs[:, blk, :], in_=pc[:, 0:16], func=AF.Copy), r=[b_pc], pw=[bs["cs"]])
                P.act(lambda h, pc=pc, blk=blk: h.activation(out=s_tot[:, blk, :], in_=pc[:, 16:32], func=AF.Copy, scale=float(D)), r=[b_pc], pw=[bs["tot"]])
            P.act(lambda h: h.activation(out=s_E[:, :, :], in_=s_cs[:, :, :], func=AF.Exp), r=[bs["cs"]], w=[bs["E"]])
            P.dve(lambda h: h.tensor_tensor(out=s_dec[:, :, :], in0=s_tot[:, :, :], in1=s_cs[:, :, :], op=ALU.subtract), r=[bs["tot"], bs["cs"]], w=[bs["dec"]])
            P.act(lambda h: h.activation(out=s_dec[:, :, :], in_=s_dec[:, :, :], func=AF.Exp), r=[bs["dec"]], w=[bs["dec"]])
            P.dve(lambda h: h.tensor_tensor(out=s_dec[:, :, :], in0=s_dec[:, :, :], in1=s_dt[:, :, :], op=ALU.mult), r=[bs["dec"], bs["dt"]], w=[bs["dec"]])
            P.act(lambda h: h.activation(out=s_tot[:, :, :], in_=s_tot[:, :, :], func=AF.Exp), r=[bs["tot"]], w=[bs["tot"]])
            P.act(lambda h: h.activation(out=s_dt[:, :, :], in_=s_dt[:, :, :], func=AF.Ln), r=[bs["dt"], bs["dec"]], w=[bs["dt"]])
            P.dve(lambda h: h.tensor_tensor(out=s_dt[:, :, :], in0=s_dt[:, :, :], in1=s_cs[:, :, :], op=ALU.subtract), r=[bs["dt"], bs["cs"]], w=[bs["dt"]])
            P.barrier()
            P.pool(lambda h, g=g: h.dma_start(out=s_wso[:, :, :], in_=wso_d[g]), dma_w=bs["wso"])
            for c in range(4):
                P.dve(lambda h, c=c: h.tensor_scalar(out=s_wso[:, c, :], in0=s_wso[:, c, :], scalar1=s_ng[:, c:c + 1], scalar2=None, op0=ALU.mult),
                      r=[bs["wso"], bs["par"]], w=[bs["wso"]])
            P.pool(lambda h: h.memset(s_Sf[:, :], 0.0), w=[bs["Sf"]])
            P.pool(lambda h: h.memset(s_Sb[:, :], 0.0), w=[bs["Sb"]])

            def su_a(blk, d):
                xi = sc_["xw"] % 2
                sc_["xw"] += 1
                xw, bxw = s_xw[xi], bs[f"xw{xi}"]
                P.dve(lambda h: h.tensor_tensor(out=xw[:, :].rearrange("p (a b) -> p a b", a=8), in0=s_xtok[:, blk, :].rearrange("p (a b) -> p a b", a=8),
                                                in1=bc8(s_dec[:, blk, d * 8:d * 8 + 8]), op=ALU.mult), r=[bs["xtok"], bs["dec"]], w=[bxw])
                pst, b_pst = bank()
                P.pe(lambda h: h.matmul(pst[:, :], lhsT=s_btok[:, blk, :], rhs=xw[:, :], start=True, stop=True), r=[bs["btok"], bxw], w=[b_pst])
                return pst, b_pst

            def su_b(S, bS, blk, d, pst, b_pst):
                P.dve(lambda h: h.tensor_tensor(out=S[:, :].rearrange("p (a b) -> p a b", a=8), in0=S[:, :].rearrange("p (a b) -> p a b", a=8),
                                                in1=bc8(s_tot[:, blk, d * 8:d * 8 + 8]), op=ALU.mult), r=[bS, bs["tot"]], w=[bS])
                P.dve(lambda h: h.tensor_tensor(out=S[:, :], in0=S[:, :], in1=pst[:, :], op=ALU.add), r=[bS, b_pst], w=[bS])

            def state_update(S, bS, blk, d):
                pst, b_pst = su_a(blk, d)
                su_b(S, bS, blk, d, pst, b_pst)

            order = [1, 0] + list(range(17, 1, -1))
            nxt = su_a(order[0], 1)
            for i, blk in enumerate(order):
                cur = nxt
                if i + 1 < len(order):
                    nxt = su_a(order[i + 1], 1)
                if blk >= 2:
                    P.act(lambda h, blk=blk: h.activation(out=s_Sbin[:, blk - 2, :], in_=s_Sb[:, :], func=AF.Copy), r=[bs["Sb"]], pw=[bs["Sbin"]])
                su_b(s_Sb, bs["Sb"], blk, 1, cur[0], cur[1])
            groups = [(half, d) for half in range(2) for d in range(2)]
            st = {}

            def F1(blk):
                cols = slice(blk * 128, (blk + 1) * 128)
                P.act(lambda h: h.activation(out=s_Sfb[:, :], in_=s_Sf[:, :], func=AF.Copy), r=[bs["Sf"]], w=[bs["Sfb"]])
                pabs = []
                for (half, d) in groups:
                    pab, b_pab = bank()
                    pabs.append((pab, b_pab))
                    P.pe(lambda h, pab=pab, d=d: h.matmul(pab[:, :], lhsT=ident[:, :], rhs=(mnext if d == 0 else mprev)[:, :], start=True, stop=False),
                         r=[b_const], w=[b_pab])
                    for ci in range(4):
                        col = d * 8 + half * 4 + ci
                        P.pe(lambda h, pab=pab, ci=ci, col=col, d=d: h.matmul(
                            pab[:, ci * 128:(ci + 1) * 128], lhsT=s_a[:, blk, col:col + 1].to_broadcast([128, 128]),
                            rhs=(tri if d == 0 else trirev), start=False, stop=(ci == 3)),
                            r=[bs["a"], bs["par"]], pw=[b_pab])
                pz, b_pz = bank()
                tgz = 1 + (blk - 2) // 4
                for k in range(8):
                    P.pe(lambda h, k=k: h.matmul(pz[:, :], lhsT=uTa[:, k, cols], rhs=s_wz[:, k, :], start=(k == 0), stop=(k == 7)),
                         r=[bs["wz"], b_uTa[tgz]], **({"w": [b_pz]} if k == 0 else {"pw": [b_pz]}))
                args = [None] * 4
                Ms = [None] * 4

                def emit_exp(gi):
                    half, d = groups[gi]
                    pab, b_pab = pabs[gi]
                    ai = sc_["arg"] % 2
                    sc_["arg"] += 1
                    arg, barg = s_arg[ai], bs[f"arg{ai}"]
                    args[gi] = (arg, barg)
                    for ci in range(4):
                        col = d * 8 + half * 4 + ci
                        P.act(lambda h, ci=ci, col=col: h.activation(
                            out=arg[:, ci, :], in_=pab[:, ci * 128:(ci + 1) * 128], func=AF.Exp, bias=s_dt[:, blk, col:col + 1], scale=1.0),
                            r=[b_pab, bs["dt"]], **({"w": [barg]} if ci == 0 else {"pw": [barg]}))

                def emit_mul(gi):
                    half, d = groups[gi]
                    arg, barg = args[gi]
                    mi = sc_["M"] % 4
                    sc_["M"] += 1
                    Mt, bM = s_M[mi], bs[f"M{mi}"]
                    Ms[gi] = (Mt, bM)
                    P.dve(lambda h: h.tensor_tensor(
                        out=Mt[:, :, :], in0=arg[:, :, :], in1=s_cbm[d][:, :].unsqueeze(1).to_broadcast([128, 4, 128]), op=ALU.mult),
                        r=[barg, bs[f"cbm{d}"]], w=[bM])

                emit_exp(0)
                emit_exp(1)
                state_update(s_Sf, bs["Sf"], blk, 0)
                pcb, b_pcb = bank()
                P.pe(lambda h: h.matmul(pcb[:, 0:128], lhsT=s_BT[:, cols], rhs=s_CT[:, cols], start=True, stop=True), r=[bs["BT"], bs["CT"]], w=[b_pcb])
                P.dve(lambda h: h.tensor_tensor(out=s_cbm[0][:, :], in0=pcb[:, 0:128], in1=tri, op=ALU.mult), r=[b_pcb, bs["par"]], w=[bs["cbm0"]])
                P.dve(lambda h: h.tensor_tensor(out=s_cbm[1][:, :], in0=pcb[:, 0:128], in1=trirev, op=ALU.mult), r=[b_pcb, bs["par"]], w=[bs["cbm1"]])
                emit_mul(0)
                emit_exp(2)
                emit_mul(1)
                emit_exp(3)
                emit_mul(2)
                emit_mul(3)
                P.act(lambda h: h.activation(out=s_yb[:, :], in_=pz[:, :], func=AF.Exp, scale=-1.0), r=[b_pz], w=[bs["yb"]])
                P.act(lambda h: h.activation(out=s_yb[:, :], in_=s_yb[:, :], func=AF.Ln, bias=s_one[:, 0:1], scale=1.0), r=[bs["yb"], bs["t2"]], w=[bs["yb"]])
                P.act(lambda h: h.activation(out=s_yb[:, :], in_=s_yb[:, :], func=AF.Exp, scale=-1.0), r=[bs["yb"]], w=[bs["yb"]])
                P.dve(lambda h: h.tensor_tensor(out=s_gz[:, :], in0=s_yb[:, :], in1=pz[:, :], op=ALU.mult), r=[bs["yb"], b_pz], w=[bs["gz"]])
                pof, b_pof = bank()
                P.pe(lambda h: h.matmul(pof[:, :], lhsT=s_CT[:, cols], rhs=s_Sfb[:, :], start=True, stop=True), r=[bs["CT"], bs["Sfb"]], w=[b_pof])
                pob, b_pob = bank()
                P.pe(lambda h: h.matmul(pob[:, :], lhsT=s_CT[:, cols], rhs=s_Sbin[:, blk - 2, :], start=True, stop=True), r=[bs["CT"], bs["Sbin"]], w=[b_pob])
                P.dve(lambda h: h.tensor_tensor(out=s_ya[:, :].rearrange("p (a b) -> p a b", a=8), in0=pof[:, :].rearrange("p (a b) -> p a b", a=8),
                                                in1=bc8(s_E[:, blk, 0:8]), op=ALU.mult), r=[b_pof, bs["E"]], w=[bs["ya"]])
                P.dve(lambda h: h.tensor_tensor(out=s_yb[:, :].rearrange("p (a b) -> p a b", a=8), in0=pob[:, :].rearrange("p (a b) -> p a b", a=8),
                                                in1=bc8(s_E[:, blk, 8:16]), op=ALU.mult), r=[b_pob, bs["E"]], w=[bs["yb"]])
                P.dve(lambda h: h.tensor_tensor(out=s_ya[:, :], in0=s_ya[:, :], in1=s_yb[:, :], op=ALU.add), r=[bs["ya"], bs["yb"]], w=[bs["ya"]])
                P.dve(lambda h: h.tensor_tensor(out=s_yb[:, :].rearrange("p (a b) -> p a b", a=8), in0=s_xtok[:, blk, :].rearrange("p (a b) -> p a b", a=8),
                                                in1=bc8(s_sb[:, 32:40]), op=ALU.mult), r=[bs["xtok"], bs["par"]], w=[bs["yb"]])
                P.dve(lambda h: h.tensor_tensor(out=s_ya[:, :], in0=s_ya[:, :], in1=s_yb[:, :], op=ALU.add), r=[bs["ya"], bs["yb"]], w=[bs["ya"]])
                st[blk] = Ms

            def F2(blk):
                Ms = st.pop(blk)
                pyd, b_pyd = bank()
                for half in range(2):
                    for ci in range(4):
                        hh = half * 4 + ci
                        for d in range(2):
                            Mt, bM = Ms[half * 2 + d]
                            P.pe(lambda h, Mt=Mt, ci=ci, hh=hh, d=d: h.matmul(
                                pyd[:, hh * 64:(hh + 1) * 64], lhsT=Mt[:, ci, :], rhs=s_xtok[:, blk, hh * 64:(hh + 1) * 64], start=(d == 0), stop=(d == 1)),
                                r=[bM, bs["xtok"]], **({"w": [b_pyd]} if (half == 0 and ci == 0 and d == 0) else {"pw": [b_pyd]}))
                P.dve(lambda h: h.tensor_tensor(out=s_ya[:, :], in0=s_ya[:, :], in1=pyd[:, :], op=ALU.add), r=[bs["ya"], b_pyd], w=[bs["ya"]])
                P.dve(lambda h: h.tensor_tensor(out=s_ya[:, :], in0=s_ya[:, :], in1=s_gz[:, :], op=ALU.mult), r=[bs["ya"], bs["gz"]], w=[bs["ya"]])
                P.act(lambda h: h.activation(out=s_yb[:, :], in_=s_ya[:, :], func=AF.Square, accum_out=s_ss[:, 0:1]), r=[bs["ya"]], w=[bs["yb"], bs["ss"]])
                P.act(lambda h: h.activation(out=s_ss[:, 1:2], in_=s_ss[:, 0:1], func=AF.Ln, scale=1.0 / 512.0, bias=epsv[:, 0:1]), r=[bs["ss"], b_c2], w=[bs["ss"]])
                P.act(lambda h: h.activation(out=s_ss[:, 2:3], in_=s_ss[:, 1:2], func=AF.Exp, scale=-0.5), r=[bs["ss"]], w=[bs["ss"]])
                yn, byn = s_yn[blk % 2], bs[f"yn{blk % 2}"]
                P.dve(lambda h: h.tensor_scalar(out=yn[:, :], in0=s_ya[:, :], scalar1=s_ss[:, 2:3], scalar2=None, op0=ALU.mult),
                      r=[bs["ya"], bs["ss"]], w=[byn])

            def T_(blk):
                j = (blk - 2) % 4
                yn, byn = s_yn[blk % 2], bs[f"yn{blk % 2}"]
                for c in range(4):
                    ptr, b_ptr = bank()
                    P.pe(lambda h, ptr=ptr, c=c: h.matmul(ptr[:, 0:128], lhsT=yn[:, c * 128:(c + 1) * 128], rhs=ident[:, :], start=True, stop=True), r=[byn, b_const], w=[b_ptr])
                    P.act(lambda h, ptr=ptr, c=c: h.activation(out=s_ynT[:, c, j * 128:(j + 1) * 128], in_=ptr[:, 0:128], func=AF.Copy), r=[b_ptr],
                          **({"w": [bs["ynT"]]} if (c == 0 and j == 0) else {"pw": [bs["ynT"]]}))
                if j != 3:
                    return
                tgi = 1 + (blk - 2) // 4
                q0 = TGS[tgi][0]
                for kd in range(8):
                    py, b_py = bank()
                    for c in range(4):
                        P.pe(lambda h, py=py, c=c, kd=kd: h.matmul(py[:, :], lhsT=s_wso[:, c, kd * 128:(kd + 1) * 128], rhs=s_ynT[:, c, :], start=(c == 0), stop=(c == 3)),
                             r=[bs["wso"], bs["ynT"]], **({"w": [b_py]} if c == 0 else {"pw": [b_py]}))
                    P.dve(lambda h, py=py, kd=kd: h.scalar_tensor_tensor(
                        out=hs[:, kd, q0:q0 + 512], in0=py[:, :], scalar=mod(layer, 2, kd, rr_l), in1=hs[:, kd, q0:q0 + 512], op0=ALU.mult, op1=ALU.add),
                        r=[b_py, b_hs[0][tgi], b_modv[layer]], w=[b_hs[0][tgi]])

            state_update(s_Sf, bs["Sf"], 0, 0)
            state_update(s_Sf, bs["Sf"], 1, 0)
            F1(2)
            F2(2)
            for blk in range(3, 18):
                F1(blk)
                T_(blk - 1)
                F2(blk)
            T_(17)
            P.barrier()
        for tgi in range(1, len(TGS)):
            layer_norm(tgi, layer, 0)
        P.barrier()

    b_out = P.buf("outst")
    for b in range(nseq):
        for k in range(8):
            P.sp(lambda h, k=k, b=b: h.dma_start(out=hs[:, k, 0:CTX], in_=cxT[b, k]), dma_w=b_hs[0][0])
            for tgi in range(1, 5):
                t0, n, _ = TGS[tgi]
                P.sp(lambda h, k=k, b=b, t0=t0, n=n: h.dma_start(out=hs[:, k, t0:t0 + n], in_=xT[b, k, :, t0 - CTX:t0 - CTX + n]),
                     dma_w=b_hs[0][tgi])
        for tgi, (t0, n, _) in enumerate(TGS):
            P.dve(lambda h, t0=t0, n=n: h.tensor_scalar(out=hs[:, :, t0:t0 + n], in0=hs[:, :, t0:t0 + n], scalar1=ALPHA, scalar2=None, op0=ALU.mult),
                  r=[b_hs[0][tgi]], w=[b_hs[0][tgi]])
        for layer in range(nlayers):
            last = layer == DEPTH - 1
            flags = dbg if isinstance(dbg, dict) else {}
            if layer % 2 == 0:
                if flags.get("att", True):
                    attention(layer, b)
            else:
                ssm(layer, b)
            for tgi, (t0, n, isctx) in enumerate(TGS):
                if last and isctx:
                    continue
                if flags.get("mlp", True):
                    mlp(tgi, layer, 2 if isctx else b)
                if flags.get("ln", True):
                    layer_norm(tgi, layer, 2, final=last)
            P.barrier()
        if dbg is not None:
            for k in range(8):
                P.sp(lambda h, k=k, b=b: h.dma_start(out=dbgT[b, k], in_=hs[:, k, :]), r=b_hs[0], dma_r=b_out)
        for k in range(8):
            P.sp(lambda h, k=k, b=b: h.dma_start(out=outT[b, k], in_=hs[:, k, CTX:T]), r=b_hs[0], dma_r=b_out)
        P.barrier()
    counts = P.finish([b_out])
    return nc, counts


def _rope_perm():
    idx = np.arange(64)
    a = idx // 32
    half = (idx % 32) // 16
    j = idx % 16
    return a * 32 + (1 - half) * 16 + j


def host_constants():
    bf = ml_dtypes.bfloat16
    c = {}
    c["ident"] = np.eye(128, dtype=np.float32).astype(bf)
    jj = np.arange(128)[:, None]
    ii = np.arange(128)[None, :]
    mp = np.where(jj >= ii, 0.0, NEG).astype(np.float32)
    mn = np.where(jj <= ii, 0.0, NEG).astype(np.float32)
    c["mprev"] = np.tile(mp, (1, 4)).astype(bf)
    c["mnext"] = np.tile(mn, (1, 4)).astype(bf)
    t = np.arange(SEQ)
    row = (t // 64).astype(np.float32)
    col = (t % 64).astype(np.float32)
    inv = (10000.0 ** (-np.arange(0, 32, 2, dtype=np.float32) / 32)).astype(np.float32)
    cosT = np.zeros((64, SEQ), np.float32)
    sinT = np.zeros((64, SEQ), np.float32)
    for a, pos in enumerate((row, col)):
        ang = (pos[None, :] * inv[:, None]).astype(np.float32)
        for half in range(2):
            sl = slice(a * 32 + half * 16, a * 32 + half * 16 + 16)
            cosT[sl] = np.cos(ang)
            sinT[sl] = np.sin(ang) * (-1.0 if half == 0 else 1.0)
    kk = np.arange(128)[:, None]
    ll = np.arange(128)[None, :]
    c["tri"] = np.concatenate([(kk <= ll), (kk >= ll)], axis=1).astype(np.float32)
    c["cossin"] = np.concatenate([cosT, sinT], axis=0)
    return c


def host_weights(inp):
    w = {}
    f = np.float32
    wm = np.asarray(inp["w_mod"], f)
    w["wmod"] = np.ascontiguousarray(wm.reshape(DEPTH, 8, 128, 12, 512).transpose(0, 3, 2, 1, 4))
    w["bmod"] = np.ascontiguousarray(np.asarray(inp["b_mod"], f).reshape(DEPTH, 48, 128).transpose(0, 2, 1))
    lnv = np.stack([np.asarray(inp[k], f) for k in ("ln_mix_g", "ln_mix_b", "ln_ff_g", "ln_ff_b")], axis=1)
    w["lnv"] = np.ascontiguousarray(lnv.reshape(DEPTH, 4, 8, 128).transpose(3, 0, 1, 2).reshape(128, DEPTH * 4 * 8))
    w["sinkb"] = np.ascontiguousarray(np.broadcast_to(np.asarray(inp["att_sink"], f).reshape(1, 16), (128, 16)))
    win = np.asarray(inp["att_w_in"], f)[0]
    perm = _rope_perm()
    wq = win[:, :1024].reshape(8, 128, 4, 4, 64)
    wqq = np.concatenate([wq, wq[..., perm]], axis=-1)
    w["wqq"] = np.ascontiguousarray(wqq.transpose(2, 1, 0, 3, 4).reshape(4, 128, 8, 512))
    wk = win[:, 1024:1280].reshape(8, 128, 4, 64)
    wv = win[:, 1280:1536].reshape(8, 128, 4, 64)
    wkv = np.concatenate([wk, wk[..., perm], wv], axis=-1)
    w["wkv"] = np.ascontiguousarray(wkv.transpose(2, 1, 0, 3))
    wo = np.asarray(inp["att_w_out"], f)[0].reshape(4, 2, 2, 64, 1024)
    w["wo"] = np.ascontiguousarray(wo.transpose(0, 2, 3, 1, 4).reshape(4, 128, 2, 1024))
    sw = np.asarray(inp["ssm_w_in"], f)[0]
    wx = sw[:, 2048:4096].reshape(8, 128, 4, 512)
    wB = sw[:, 4096:4608].reshape(8, 128, 4, 128)
    wC = sw[:, 4608:5120].reshape(8, 128, 4, 128)
    w["wxbc"] = np.ascontiguousarray(np.concatenate([wx, wB, wC], axis=-1).transpose(2, 1, 0, 3))
    w["wz"] = np.ascontiguousarray(sw[:, 0:2048].reshape(8, 128, 4, 512).transpose(2, 1, 0, 3))
    wd = sw[:, 5120:5184].reshape(8, 128, 2, 4, 8)
    w["wdt"] = np.ascontiguousarray(wd.transpose(3, 1, 0, 2, 4).reshape(4, 128, 8, 16))
    w["wso"] = np.ascontiguousarray(np.asarray(inp["ssm_w_out"], f)[0].reshape(4, 4, 128, 1024).transpose(0, 2, 1, 3))
    cw = np.asarray(inp["ssm_conv_w"], f)[0]
    cb = np.asarray(inp["ssm_conv_b"], f)[0]
    convw = np.zeros((4, 128, 6, 6), f)
    for g in range(4):
        chans = [np.arange(g * 512 + c * 128, g * 512 + (c + 1) * 128) for c in range(4)]
        chans.append(np.arange(2048 + g * 128, 2048 + (g + 1) * 128))
        chans.append(np.arange(2560 + g * 128, 2560 + (g + 1) * 128))
        for c, ch in enumerate(chans):
            convw[g, :, c, 0:5] = cw[:, ch].T
            convw[g, :, c, 5] = cb[ch]
    w["convw"] = convw.reshape(4, 128, 36)
    dtb = np.asarray(inp["ssm_dt_bias"], f)[0].reshape(2, 4, 8)
    alog = np.asarray(inp["ssm_a_log"], f)[0].reshape(2, 4, 8)
    dsk = np.asarray(inp["ssm_d"], f)[0].reshape(4, 8)
    ssmb = np.zeros((4, 128, 40), f)
    for g in range(4):
        ssmb[g, :, 0:16] = dtb[:, g, :].reshape(1, 16)
        ssmb[g, :, 16:32] = alog[:, g, :].reshape(1, 16)
        ssmb[g, :, 32:40] = dsk[g].reshape(1, 8)
    w["ssmb"] = ssmb
    ng = np.asarray(inp["ssm_norm_g"], f)[0].reshape(4, 4, 128)
    w["normg"] = np.ascontiguousarray(ng.transpose(0, 2, 1))
    w1 = np.asarray(inp["ff_w1"], f).reshape(DEPTH, 8, 128, 8, 512)
    w["w1"] = np.ascontiguousarray(w1.transpose(0, 3, 2, 1, 4))
    w2 = np.asarray(inp["ff_w2"], f).reshape(DEPTH, 32, 128, 8, 128)
    w["w2"] = np.ascontiguousarray(w2.transpose(0, 3, 2, 1, 4))
    return w


def host_core_inputs(inp, core):
    f = np.float32
    b0 = core * BLOC
    x = np.asarray(inp["x"], f)[b0:b0 + BLOC]
    ctx = np.asarray(inp["ctx"], f)[b0:b0 + BLOC]
    d = {}
    d["xT"] = np.ascontiguousarray(x.transpose(0, 2, 1).reshape(BLOC, 8, 128, SEQ))
    d["cxT"] = np.ascontiguousarray(ctx.transpose(0, 2, 1).reshape(BLOC, 8, 128, CTX))
    cc = np.concatenate([np.asarray(inp["c"], f)[b0:b0 + BLOC], np.asarray(inp["c_ctx"], f)[None]], axis=0)
    d["cT"] = np.ascontiguousarray(cc.reshape(3, 8, 128).transpose(2, 1, 0))
    return d


_CACHE = {}


def kernel(**inputs):
    if "nc" not in _CACHE:
        _CACHE["nc"] = build_program()[0]
    nc = _CACHE["nc"]
    shared = {}
    shared.update(host_constants())
    shared.update(host_weights(inputs))
    in_maps = []
    for core in range(NCORE):
        m = dict(shared)
        m.update(host_core_inputs(inputs, core))
        in_maps.append(m)
    res = run_bass_kernel_spmd(nc, in_maps, core_ids=list(range(NCORE)))
    outs = []
    for core in range(NCORE):
        oT = np.asarray(res.results[core]["outT"]).reshape(BLOC, D, SEQ)
        outs.append(oT.transpose(0, 2, 1))
    return np.ascontiguousarray(np.concatenate(outs, axis=0)).astype(np.float32)
```

```python
from contextlib import ExitStack
import numpy as np
import ml_dtypes
import concourse.bass as bass
import concourse.mybir as mybir
from concourse.bass_utils import run_bass_kernel_spmd

F32 = mybir.dt.float32
BF16 = mybir.dt.bfloat16
AF = mybir.ActivationFunctionType
ALU = mybir.AluOpType
AX = mybir.AxisListType

ENGS = ("pe", "act", "dve", "pool", "sp")

D = 1024
SEQ = 2048
CTX = 256
T = SEQ + CTX
NCORE = 8
BLOC = 2
DEPTH = 2
ALPHA = (2.0 * DEPTH) ** 0.25
LN_EPS = 1e-5
RMS_EPS = 1e-5
NEG = -30000.0
TGS = [(0, 256, True), (256, 512, False), (768, 512, False), (1280, 512, False), (1792, 512, False)]


class Buf:
    __slots__ = ("name", "writers", "readers", "prev_readers", "sem", "dcount", "excl")

    def __init__(self, name, excl=False):
        self.name = name
        self.excl = excl
        self.writers = {}
        self.readers = {}
        self.prev_readers = {}
        self.sem = None
        self.dcount = 0


class Op:
    __slots__ = ("emit", "deps", "signal", "dma", "isnop")

    def __init__(self, emit, deps, dma, isnop=False):
        self.emit = emit
        self.deps = deps
        self.signal = False
        self.dma = dma
        self.isnop = isnop


class Prog:
    def __init__(self, nc):
        self.nc = nc
        self.ops = {e: [] for e in ENGS}
        self.seen = {e: {} for e in ENGS}
        self.dma_bufs = []
        self.nbuf = 0
        self.ntens = 0

    def sb(self, name, shape, dt, off):
        self.ntens += 1
        return self.nc.alloc_sbuf_tensor_at(f"{name}_{self.ntens}", list(shape), dt, offset=off + 16512)

    def buf(self, name=None):
        self.nbuf += 1
        return Buf(name or f"b{self.nbuf}")

    def add(self, eng, emit, r=(), w=(), pw=(), dma_w=None, dma_r=None):
        ops = self.ops[eng]
        idx = len(ops)
        deps = {}

        def need(k, v):
            if deps.get(k, -1) < v:
                deps[k] = v

        mykey = ("E", eng)
        allr = list(r) + ([dma_r] if dma_r is not None else [])
        allw = list(w) + ([dma_w] if dma_w is not None else [])
        for b in allr:
            for k, v in b.writers.items():
                need(k, v)
            if b.excl:
                for k, v in b.readers.items():
                    if k != mykey:
                        need(k, v)
        for b in allw:
            for k, v in b.readers.items():
                need(k, v)
            for k, v in b.writers.items():
                if k == mykey and eng == "pe":
                    continue
                if dma_w is not None and k == ("D", dma_w):
                    continue
                need(k, v)
            if not b.readers:
                for k, v in b.prev_readers.items():
                    need(k, v)
        for b in pw:
            for k, v in b.readers.items():
                need(k, v)
            for k, v in b.prev_readers.items():
                need(k, v)
            for k, v in b.writers.items():
                if k[0] == "D":
                    need(k, v)
        seen = self.seen[eng]
        fdeps = {}
        for k, v in deps.items():
            if seen.get(k, -1) >= v:
                continue
            seen[k] = v
            fdeps[k] = v
            if k[0] == "E":
                self.ops[k[1]][v].signal = True
        is_dma = (dma_w is not None) or (dma_r is not None)
        dbuf = dma_w if dma_w is not None else dma_r
        op = Op(emit, fdeps, dbuf if is_dma else None)
        ops.append(op)
        if is_dma:
            if dbuf.sem is None:
                dbuf.sem = True
                self.dma_bufs.append(dbuf)
            dbuf.dcount += 16
            ev = (("D", dbuf), dbuf.dcount)
        else:
            ev = (mykey, idx)
        for b in allr:
            if b.readers.get(ev[0], -1) < ev[1]:
                b.readers[ev[0]] = ev[1]
        for b in allw:
            if b.readers:
                b.prev_readers = b.readers
            b.readers = {}
            b.writers = {ev[0]: ev[1]}
        for b in pw:
            if b.readers:
                b.prev_readers = b.readers
                b.readers = {}
                b.writers = {}
            if b.writers.get(ev[0], -1) < ev[1]:
                b.writers[ev[0]] = ev[1]
        return op

    def pe(self, emit, **kw): return self.add("pe", emit, **kw)
    def act(self, emit, **kw): return self.add("act", emit, **kw)
    def dve(self, emit, **kw): return self.add("dve", emit, **kw)
    def pool(self, emit, **kw): return self.add("pool", emit, **kw)
    def sp(self, emit, **kw): return self.add("sp", emit, **kw)

    def barrier(self):
        last = {}
        for e in ENGS:
            ops = self.ops[e]
            for i in range(len(ops) - 1, -1, -1):
                if ops[i].dma is None and not ops[i].isnop:
                    last[("E", e)] = i
                    break
        for b in self.dma_bufs:
            last[("D", b)] = b.dcount
        for e in ENGS:
            seen = self.seen[e]
            fdeps = {}
            for k, v in last.items():
                if k == ("E", e):
                    continue
                if seen.get(k, -1) >= v:
                    continue
                seen[k] = v
                fdeps[k] = v
                if k[0] == "E":
                    self.ops[k[1]][v].signal = True
            if fdeps:
                self.ops[e].append(Op(lambda h: h.nop(), fdeps, None, True))

    def finish(self, final_bufs):
        nc = self.nc
        with ExitStack() as es:
            sems = {e: es.enter_context(nc.semaphore(f"s_{e}")) for e in ENGS}
            for i, b in enumerate(self.dma_bufs):
                b.sem = es.enter_context(nc.semaphore(f"d{i}"))
            sigcnt = {}
            for e in ENGS:
                c = 0
                arr = []
                for op in self.ops[e]:
                    if op.signal:
                        c += 1
                    arr.append(c)
                sigcnt[e] = arr
            fin = [(b.sem, b.dcount) for b in final_bufs]

            def run(e, h):
                for op in self.ops[e]:
                    for k, v in op.deps.items():
                        if k[0] == "E":
                            h.wait_ge(sems[k[1]], sigcnt[k[1]][v])
                        else:
                            h.wait_ge(k[1].sem, v)
                    ins = op.emit(h)
                    if op.dma is not None:
                        ins.then_inc(op.dma.sem, 16)
                    elif op.signal:
                        ins.then_inc(sems[e], 1)
                if e == "sp":
                    for s, v in fin:
                        h.wait_ge(s, v)

            with nc.Block() as block:
                @block.tensor
                def _(h): run("pe", h)

                @block.scalar
                def _(h): run("act", h)

                @block.vector
                def _(h): run("dve", h)

                @block.gpsimd
                def _(h): run("pool", h)

                @block.sync
                def _(h): run("sp", h)
        return {e: len(self.ops[e]) for e in ENGS}


class K:
    pass


def build_program(nseq=BLOC, nlayers=DEPTH, dbg=None):
    nc = bass.Bass("TRN2", target_bir_lowering=False)
    P = Prog(nc)

    def din(name, shape, dt=F32):
        return nc.dram_tensor(name, list(shape), dt, kind="ExternalInput").ap()

    xT = din("xT", [BLOC, 8, 128, SEQ])
    cxT = din("cxT", [BLOC, 8, 128, CTX])
    cT = din("cT", [128, 8, 3])
    wmod = din("wmod", [DEPTH, 12, 128, 8, 512])
    bmod = din("bmod", [DEPTH, 128, 48])
    lnv_d = din("lnv", [128, DEPTH * 4 * 8])
    sink_d = din("sinkb", [128, 16])
    ident_d = din("ident", [128, 128], BF16)
    mprev_d = din("mprev", [128, 512], BF16)
    mnext_d = din("mnext", [128, 512], BF16)
    cs_d = din("cossin", [128, SEQ])
    wqq_d = din("wqq", [4, 128, 8, 512])
    wkv_d = din("wkv", [4, 128, 8, 192])
    wo_d = din("wo", [4, 128, 2, 1024])
    wxbc_d = din("wxbc", [4, 128, 8, 768])
    wz_d = din("wz", [4, 128, 8, 512])
    wdt_d = din("wdt", [4, 128, 8, 16])
    wso_d = din("wso", [4, 128, 4, 1024])
    cw_d = din("convw", [4, 128, 36])
    sb_d = din("ssmb", [4, 128, 40])
    ng_d = din("normg", [4, 128, 4])
    tri_d = din("tri", [128, 256])
    w1_d = din("w1", [DEPTH, 8, 128, 8, 512])
    w2_d = din("w2", [DEPTH, 8, 128, 32, 128])
    outT = nc.dram_tensor("outT", [BLOC, 8, 128, SEQ], F32, kind="ExternalOutput").ap()
    if dbg is not None:
        dbgT = nc.dram_tensor("dbgT", [BLOC, 8, 128, T], F32, kind="ExternalOutput").ap()

    HS_OFF = 0
    CONST_OFF = 73728
    ARENA = 78848
    hs = P.sb("hs", [128, 8, T], F32, HS_OFF)
    b_hs = [[P.buf(f"hs{g}") for g in range(len(TGS))]]

    co = [CONST_OFF]

    def calloc(name, shape, dt):
        n = int(np.prod(shape[1:])) * (4 if dt == F32 else 2)
        t = P.sb(name, shape, dt, co[0])
        co[0] += (n + 31) // 32 * 32
        assert co[0] <= ARENA
        return t

    ident = calloc("ident", [128, 128], BF16)
    ones_f = calloc("ones_f", [128, 128], F32)
    ones_bf = calloc("ones_bf", [128, 128], BF16)
    mprev = calloc("mprev", [128, 512], BF16)
    mnext = calloc("mnext", [128, 512], BF16)
    modv = [calloc(f"modv{i}", [128, 48, 3], F32) for i in range(DEPTH)]
    lnv = calloc("lnv", [128, DEPTH * 4 * 8], F32)
    lnva = calloc("lnva", [128, DEPTH * 4 * 8], F32)
    sinkv = calloc("sinkv", [128, 16], F32)
    epsv = calloc("epsv", [128, 2], F32)
    kmaxp = calloc("kmaxp", [128, 8], F32)
    kmax2 = calloc("kmax2", [128, 1], F32)
    b_const = P.buf("const")
    b_modv = [P.buf(f"modv{i}") for i in range(DEPTH)]
    b_kmax = P.buf("kmax")

    pbank = [nc.alloc_psum_tensor(f"pb{i}", [128, 512], F32) for i in range(8)]
    b_bank = [Buf(f"pb{i}", excl=True) for i in range(8)]
    bank_rr = [0]

    def bank():
        i = bank_rr[0]
        bank_rr[0] = (i + 1) % 8
        return pbank[i], b_bank[i]

    P.sp(lambda h: h.dma_start(out=ident[:, :], in_=ident_d[:, :]), dma_w=b_const)
    P.sp(lambda h: h.dma_start(out=mprev[:, :], in_=mprev_d[:, :]), dma_w=b_const)
    P.sp(lambda h: h.dma_start(out=mnext[:, :], in_=mnext_d[:, :]), dma_w=b_const)
    P.sp(lambda h: h.dma_start(out=lnv[:, :], in_=lnv_d[:, :]), dma_w=b_const)
    P.sp(lambda h: h.dma_start(out=sinkv[:, :], in_=sink_d[:, :]), dma_w=b_const)
    b_c2 = P.buf("const2")
    P.pool(lambda h: h.memset(ones_f[:, :], 1.0 / D), pw=[b_c2])
    P.pool(lambda h: h.memset(ones_bf[:, :], 1.0), pw=[b_c2])
    P.pool(lambda h: h.memset(epsv[:, 0:1], LN_EPS), pw=[b_c2])
    P.pool(lambda h: h.memset(epsv[:, 1:2], 1e-20), pw=[b_c2])
    P.dve(lambda h: h.tensor_scalar(out=lnva[:, :], in0=lnv[:, :], scalar1=ALPHA, scalar2=None, op0=ALU.mult),
          r=[b_const], w=[b_c2])

    def lnvec(layer, which, k, alpha):
        t = lnva if alpha else lnv
        c = (layer * 4 + which) * 8 + k
        return t[:, c:c + 1]

    a_cT = P.sb("cTs", [128, 8, 3], F32, ARENA)
    a_e = P.sb("cTe", [128, 8, 3], F32, ARENA + 128)
    a_sc = P.sb("scT", [128, 8, 3], F32, ARENA + 256)
    a_bm = P.sb("bm", [128, 48], F32, ARENA + 384)
    a_w = [P.sb(f"wm{i}", [128, 8, 512], F32, ARENA + 1024 + i * 16384) for i in range(2)]
    b_cT = P.buf("cT")
    b_sc = P.buf("scT")
    b_bm = P.buf("bm")
    b_w = [P.buf("wm0"), P.buf("wm1")]
    P.sp(lambda h: h.dma_start(out=a_cT[:, :, :], in_=cT[:, :, :]), dma_w=b_cT)
    P.act(lambda h: h.activation(out=a_e[:, :, :], in_=a_cT[:, :, :], func=AF.Exp, scale=-1.0), r=[b_cT], w=[b_sc])
    P.dve(lambda h: h.tensor_scalar(out=a_e[:, :, :], in0=a_e[:, :, :], scalar1=1.0, scalar2=None, op0=ALU.add), r=[b_sc], w=[b_sc])
    P.dve(lambda h: h.reciprocal(out=a_e[:, :, :], in_=a_e[:, :, :]), r=[b_sc], w=[b_sc])
    P.dve(lambda h: h.tensor_tensor(out=a_sc[:, :, :], in0=a_e[:, :, :], in1=a_cT[:, :, :], op=ALU.mult), r=[b_sc, b_cT], w=[b_sc])
    wi = 0
    for i in range(nlayers):
        pm, b_pm = bank()
        P.sp(lambda h, i=i: h.dma_start(out=a_bm[:, :], in_=bmod[i]), dma_w=b_bm)
        for ft in range(12):
            wt, bw = a_w[wi % 2], b_w[wi % 2]
            wi += 1
            P.sp(lambda h, wt=wt, i=i, ft=ft: h.dma_start(out=wt[:, :, :], in_=wmod[i, ft]), dma_w=bw)
            for fc in range(4):
                c = ft * 4 + fc
                for k in range(8):
                    P.pe(lambda h, pm=pm, wt=wt, c=c, fc=fc, k=k: h.matmul(
                        pm[:, c * 3:c * 3 + 3], lhsT=wt[:, k, fc * 128:(fc + 1) * 128], rhs=a_sc[:, k, :],
                        start=(k == 0), stop=(k == 7)),
                        r=[bw, b_sc], **({"w": [b_pm]} if (c == 0 and k == 0) else {"pw": [b_pm]}))
        mv = modv[i]
        P.dve(lambda h, mv=mv, pm=pm: h.tensor_tensor(
            out=mv[:, :, :], in0=pm[:, 0:144].rearrange("p (c r) -> p c r", r=3),
            in1=a_bm[:, :].unsqueeze(2).to_broadcast([128, 48, 3]), op=ALU.add),
            r=[b_pm, b_bm], w=[b_modv[i]])
        for which in (1, 4):
            P.dve(lambda h, mv=mv, which=which: h.tensor_scalar(
                out=mv[:, which * 8:(which + 1) * 8, :], in0=mv[:, which * 8:(which + 1) * 8, :],
                scalar1=1.0, scalar2=1.0 / ALPHA, op0=ALU.add, op1=ALU.mult),
                r=[b_modv[i]], w=[b_modv[i]])

    def mod(layer, which, k, rr):
        return modv[layer][:, which * 8 + k, rr:rr + 1]

    P.barrier()

    LN_OFF = ARENA + 110080

    def layer_norm(tgi, layer, which_g, final=False):
        t0, n, _ = TGS[tgi]
        sq = [P.sb("lnsq", [128, 512], F32, LN_OFF + i * 2048) for i in range(2)]
        b_sq = [K.b_lnsq0, K.b_lnsq1]
        st = [P.sb("lnst", [128, 512], F32, LN_OFF + 4096 + i * 2048) for i in range(4)]
        b_st = K.b_lnst
        bh = b_hs[0][tgi]
        p1, b_p1 = bank()
        p2, b_p2 = bank()
        for k in range(8):
            s, bs = sq[k % 2], b_sq[k % 2]
            P.act(lambda h, s=s, k=k: h.activation(out=s[:, 0:n], in_=hs[:, k, t0:t0 + n], func=AF.Square),
                  r=[bh], w=[bs])
            P.pe(lambda h, k=k: h.matmul(p1[:, 0:n], lhsT=ones_f[:, :], rhs=hs[:, k, t0:t0 + n],
                                         start=(k == 0), stop=(k == 7)),
                 r=[bh, b_c2], **({"w": [b_p1]} if k == 0 else {"pw": [b_p1]}))
            P.pe(lambda h, s=s, k=k: h.matmul(p2[:, 0:n], lhsT=ones_f[:, :], rhs=s[:, 0:n],
                                              start=(k == 0), stop=(k == 7)),
                 r=[bs, b_c2], **({"w": [b_p2]} if k == 0 else {"pw": [b_p2]}))
        mean, var, rstd, nmr = st
        P.act(lambda h: h.activation(out=mean[:, 0:n], in_=p1[:, 0:n], func=AF.Copy), r=[b_p1], w=[b_st])
        P.dve(lambda h: h.tensor_tensor(out=var[:, 0:n], in0=mean[:, 0:n], in1=mean[:, 0:n], op=ALU.mult), r=[b_st], w=[b_st])
        P.dve(lambda h: h.tensor_tensor(out=var[:, 0:n], in0=p2[:, 0:n], in1=var[:, 0:n], op=ALU.subtract), r=[b_st, b_p2], w=[b_st])
        P.act(lambda h: h.activation(out=rstd[:, 0:n], in_=var[:, 0:n], func=AF.Ln, bias=epsv[:, 0:1], scale=1.0), r=[b_st, b_c2], w=[b_st])
        P.act(lambda h: h.activation(out=rstd[:, 0:n], in_=rstd[:, 0:n], func=AF.Exp, scale=-0.5), r=[b_st], w=[b_st])
        P.dve(lambda h: h.scalar_tensor_tensor(out=nmr[:, 0:n], in0=mean[:, 0:n], scalar=-1.0, in1=rstd[:, 0:n],
                                               op0=ALU.mult, op1=ALU.mult), r=[b_st], w=[b_st])
        for k in range(8):
            P.dve(lambda h, k=k: h.tensor_tensor(out=hs[:, k, t0:t0 + n], in0=hs[:, k, t0:t0 + n], in1=rstd[:, 0:n], op=ALU.mult),
                  r=[bh, b_st], w=[bh])
            P.dve(lambda h, k=k: h.tensor_tensor(out=hs[:, k, t0:t0 + n], in0=hs[:, k, t0:t0 + n], in1=nmr[:, 0:n], op=ALU.add),
                  r=[bh, b_st], w=[bh])
            P.act(lambda h, k=k: h.activation(out=hs[:, k, t0:t0 + n], in_=hs[:, k, t0:t0 + n], func=AF.Identity,
                                              scale=lnvec(layer, which_g, k, not final), bias=lnvec(layer, which_g + 1, k, not final)),
                  r=[bh, b_c2], w=[bh])

    K.att_stage = dbg.get("stage", 4) if isinstance(dbg, dict) else 4
    K.b_lnsq0 = P.buf("lnsq0")
    K.b_lnsq1 = P.buf("lnsq1")
    K.b_lnst = P.buf("lnst")

    M_UT = ARENA
    M_HID = ARENA + 16384
    M_W1 = M_HID + 32768
    M_W2 = M_W1 + 16384
    M_RT = M_W2 + 16384
    m_uT = [P.sb("m_uT", [128, 8, 512], BF16, M_UT + i * 8192) for i in range(2)]
    m_hid = P.sb("m_hid", [128, 32, 512], BF16, M_HID)
    m_w1 = [P.sb("m_w1", [128, 8, 512], BF16, M_W1 + i * 8192) for i in range(2)]
    m_w2 = [P.sb("m_w2", [128, 32, 128], BF16, M_W2 + i * 8192) for i in range(2)]
    m_rt = [P.sb("m_rt", [128, 512], F32, M_RT + i * 2048) for i in range(2)]
    assert M_RT + 4096 <= LN_OFF
    bm_uT = [P.buf("m_uT0"), P.buf("m_uT1")]
    bm_hid = P.buf("m_hid")
    bm_w1 = [P.buf("m_w10"), P.buf("m_w11")]
    bm_w2 = [P.buf("m_w20"), P.buf("m_w21")]
    bm_rt = [P.buf("m_rt0"), P.buf("m_rt1")]
    cnt = {"ut": 0, "w1": 0, "w2": 0, "rt": 0}

    def mlp(tgi, layer, rr):
        t0, n, _ = TGS[tgi]
        bh = b_hs[0][tgi]
        ui = cnt["ut"] % 2
        cnt["ut"] += 1
        uT, buT = m_uT[ui], bm_uT[ui]
        for k in range(8):
            P.act(lambda h, k=k: h.activation(out=uT[:, k, 0:n], in_=hs[:, k, t0:t0 + n], func=AF.Identity,
                                              scale=mod(layer, 4, k, rr), bias=mod(layer, 3, k, rr)),
                  r=[bh, b_modv[layer]], **({"w": [buT]} if k == 0 else {"pw": [buT]}))
        for fb in range(8):
            wi_ = cnt["w1"] % 2
            cnt["w1"] += 1
            w1t, bw1 = m_w1[wi_], bm_w1[wi_]
            P.pool(lambda h, w1t=w1t, fb=fb: h.dma_start(out=w1t[:, :, :], in_=w1_d[layer, fb]), dma_w=bw1)
            for fc in range(4):
                pb, b_pb = bank()
                for k in range(8):
                    P.pe(lambda h, pb=pb, w1t=w1t, fc=fc, k=k: h.matmul(
                        pb[:, 0:n], lhsT=w1t[:, k, fc * 128:(fc + 1) * 128], rhs=uT[:, k, 0:n],
                        start=(k == 0), stop=(k == 7)),
                        r=[bw1, buT], **({"w": [b_pb]} if k == 0 else {"pw": [b_pb]}))
                ri = cnt["rt"] % 2
                cnt["rt"] += 1
                rt, brt = m_rt[ri], bm_rt[ri]
                P.dve(lambda h, pb=pb, rt=rt: h.tensor_scalar(out=rt[:, 0:n], in0=pb[:, 0:n], scalar1=0.0, scalar2=None, op0=ALU.max),
                      r=[b_pb], w=[brt])
                f = fb * 4 + fc
                P.act(lambda h, rt=rt, f=f: h.activation(out=m_hid[:, f, 0:n], in_=rt[:, 0:n], func=AF.Square),
                      r=[brt], **({"w": [bm_hid]} if f == 0 else {"pw": [bm_hid]}))
        for dc in range(8):
            wi_ = cnt["w2"] % 2
            cnt["w2"] += 1
            w2t, bw2 = m_w2[wi_], bm_w2[wi_]
            P.pool(lambda h, w2t=w2t, dc=dc: h.dma_start(out=w2t[:, :, :], in_=w2_d[layer, dc]), dma_w=bw2)
            pb, b_pb = bank()
            for kf in range(32):
                P.pe(lambda h, pb=pb, w2t=w2t, kf=kf: h.matmul(
                    pb[:, 0:n], lhsT=w2t[:, kf, :], rhs=m_hid[:, kf, 0:n], start=(kf == 0), stop=(kf == 31)),
                    r=[bw2, bm_hid], **({"w": [b_pb]} if kf == 0 else {"pw": [b_pb]}))
            P.dve(lambda h, pb=pb, dc=dc: h.scalar_tensor_tensor(
                out=hs[:, dc, t0:t0 + n], in0=pb[:, 0:n], scalar=mod(layer, 5, dc, rr), in1=hs[:, dc, t0:t0 + n],
                op0=ALU.mult, op1=ALU.add),
                r=[b_pb, bh, b_modv[layer]], w=[bh])

    A_UT = ARENA
    A_COS = A_UT + 36864
    A_W = A_COS + 16384
    A_KT = A_W + 19456
    A_V = A_KT + 4608
    A_Q = A_V + 4608
    A_OT = A_Q + 8192
    A_PT = A_OT + 4096
    A_TMP = A_PT + 3072
    A_MN = A_TMP + 6144
    A_SK = A_MN + 4096
    A_RD = A_SK + 8192
    A_SQ = A_RD + 2048
    A_OUN = A_SQ + 1024
    A_END = A_OUN + 2048
    assert A_END <= 212832, A_END
    uTa = P.sb("uTa", [128, 8, T], BF16, A_UT)
    cstab = P.sb("cstab", [128, SEQ], F32, A_COS)
    wqq = P.sb("wqq", [128, 8, 512], BF16, A_W)
    wkv = P.sb("wkv", [128, 8, 192], BF16, A_W + 8192)
    wo = P.sb("wo", [128, 2, 1024], BF16, A_W + 11264)
    kTa = P.sb("kTa", [65, T], BF16, A_KT)
    Va = P.sb("Va", [128, 18, 128], BF16, A_V)
    qa = P.sb("qa", [65, 4, 512], BF16, A_Q)
    qpa = P.sb("qpa", [65, 4, 512], BF16, A_Q + 4096)
    OT = P.sb("OT", [128, 2, 512], BF16, A_OT)
    PT = [P.sb("PT", [128, 512], BF16, A_PT + i * 1024) for i in range(3)]
    tmp = [P.sb("atmp", [128, 512], F32, A_TMP + i * 2048) for i in range(3)]
    mneg = P.sb("mneg", [128, 4, 512], BF16, A_MN)
    skt = P.sb("skt", [128, 4, 512], F32, A_SK)
    rden = P.sb("rden", [64, 512], F32, A_RD)
    sqt = P.sb("sqt", [64, 512], BF16, A_SQ)
    oun = P.sb("oun", [64, 512], F32, A_OUN)
    b_oun = P.buf("oun")
    qbc = [0]
    psc = [0]
    pending_norm = []
    b_uTa = [P.buf(f"uTa{g}") for g in range(len(TGS))]
    b_rope = P.buf("rope")
    b_wq, b_wkv, b_wo = P.buf("wq"), P.buf("wkv"), P.buf("wo")
    b_kT, b_V = P.buf("kT"), P.buf("V")
    b_qa, b_qpa, b_OT = P.buf("qa"), P.buf("qpa"), P.buf("OT")
    b_PT = [P.buf(f"PT{i}") for i in range(3)]
    b_tmp = [P.buf(f"atmp{i}") for i in range(3)]
    b_mneg, b_skt, b_rden, b_sqt = P.buf("mneg"), P.buf("skt"), P.buf("rden"), P.buf("sqt")
    ptc = [0]

    def rope(dst, bdst, pa, b_pa, lt0, n, pwflag):
        t1, t2 = tmp[0], tmp[1]
        P.dve(lambda h: h.tensor_tensor(out=t1[0:64, 0:n], in0=pa[0:64, 0:n], in1=cstab[0:64, lt0:lt0 + n], op=ALU.mult),
              r=[b_pa, b_rope], w=[b_tmp[0]])
        P.dve(lambda h: h.tensor_tensor(out=t2[0:64, 0:n], in0=pa[64:128, 0:n], in1=cstab[64:128, lt0:lt0 + n], op=ALU.mult),
              r=[b_pa, b_rope], w=[b_tmp[1]])
        P.dve(lambda h: h.tensor_tensor(out=dst, in0=t1[0:64, 0:n], in1=t2[0:64, 0:n], op=ALU.add),
              r=[b_tmp[0], b_tmp[1]], **({"pw": [bdst]} if pwflag else {"w": [bdst]}))

    def attention(layer, b):
        rr_l = b
        for tgi, (t0, n, isctx) in enumerate(TGS):
            rr = 2 if isctx else rr_l
            for k in range(8):
                P.act(lambda h, k=k, t0=t0, n=n, rr=rr: h.activation(
                    out=uTa[:, k, t0:t0 + n], in_=hs[:, k, t0:t0 + n], func=AF.Identity,
                    scale=mod(layer, 1, k, rr), bias=mod(layer, 0, k, rr)),
                    r=[b_hs[0][tgi], b_modv[layer]], **({"w": [b_uTa[tgi]]} if k == 0 else {"pw": [b_uTa[tgi]]}))
        P.sp(lambda h: h.dma_start(out=cstab[:, :], in_=cs_d[:, :]), dma_w=b_rope)
        for g in range(4):
            P.pool(lambda h, g=g: h.dma_start(out=wqq[:, :, :], in_=wqq_d[g]), dma_w=b_wq)
            P.pool(lambda h, g=g: h.dma_start(out=wkv[:, :, :], in_=wkv_d[g]), dma_w=b_wkv)
            P.pool(lambda h, g=g: h.dma_start(out=wo[:, :, :], in_=wo_d[g]), dma_w=b_wo)
            P.pool(lambda h: h.memset(kTa[64:65, :], 1.0), w=[b_kT])
            P.pool(lambda h: h.memset(Va[:, :, 64:128], 1.0), w=[b_V])
            for tgi, (t0, n, isctx) in enumerate(TGS):
                pk, b_pk = bank()
                for k in range(8):
                    P.pe(lambda h, pk=pk, k=k, t0=t0, n=n: h.matmul(
                        pk[:, 0:n], lhsT=wkv[:, k, 0:128], rhs=uTa[:, k, t0:t0 + n], start=(k == 0), stop=(k == 7)),
                        r=[b_wkv, b_uTa[tgi]], **({"w": [b_pk]} if k == 0 else {"pw": [b_pk]}))
                if isctx:
                    P.act(lambda h, pk=pk, t0=t0, n=n: h.activation(out=kTa[0:64, t0:t0 + n], in_=pk[0:64, 0:n], func=AF.Copy),
                          r=[b_pk], pw=[b_kT])
                else:
                    rope(kTa[0:64, t0:t0 + n], b_kT, pk, b_pk, t0 - CTX, n, True)
                for bi in range(n // 128):
                    blk = t0 // 128 + bi
                    pv, b_pv = bank()
                    for k in range(8):
                        P.pe(lambda h, pv=pv, k=k, blk=blk: h.matmul(
                            pv[:, 0:64], lhsT=uTa[:, k, blk * 128:(blk + 1) * 128], rhs=wkv[:, k, 128:192],
                            start=(k == 0), stop=(k == 7)),
                            r=[b_wkv, b_uTa[tgi]], **({"w": [b_pv]} if k == 0 else {"pw": [b_pv]}))
                    P.act(lambda h, pv=pv, blk=blk: h.activation(out=Va[:, blk, 0:64], in_=pv[:, 0:64], func=AF.Copy),
                          r=[b_pv], pw=[b_V])
                P.act(lambda h, t0=t0, n=n: h.activation(out=sqt[:, 0:n], in_=kTa[0:64, t0:t0 + n], func=AF.Square),
                      r=[b_kT], w=[b_sqt])
                pn, b_pn = bank()
                P.pe(lambda h, pn=pn, n=n: h.matmul(pn[:, 0:n], lhsT=ones_bf[0:64, :], rhs=sqt[:, 0:n], start=True, stop=True),
                     r=[b_sqt, b_c2], w=[b_pn])
                P.dve(lambda h, pn=pn, n=n, tgi=tgi: h.tensor_reduce(out=kmaxp[:, tgi:tgi + 1], in_=pn[:, 0:n], axis=AX.X, op=ALU.max),
                      r=[b_pn], **({"w": [b_kmax]} if tgi == 0 else {"pw": [b_kmax]}))
            P.dve(lambda h: h.tensor_reduce(out=kmax2[:, :], in_=kmaxp[:, 0:5], axis=AX.X, op=ALU.max), r=[b_kmax], w=[b_kmax])
            P.dve(lambda h: h.tensor_scalar(out=kmax2[:, :], in0=kmax2[:, :], scalar1=1.05, scalar2=None, op0=ALU.mult), r=[b_kmax], w=[b_kmax])
            for tgi, (t0, n, isctx) in enumerate(TGS):
                if K.att_stage < 2:
                    break
                pns = {}

                def q_chain(r):
                    pn, b_pn = pns[r]
                    t3 = tmp[2]
                    P.act(lambda h, pn=pn, n=n: h.activation(out=t3[:, 0:n], in_=pn[:, 0:n], func=AF.Ln, scale=kmax2[:, 0:1], bias=epsv[:, 1:2]),
                          r=[b_pn, b_kmax, b_c2], w=[b_tmp[2]])
                    P.act(lambda h, n=n: h.activation(out=t3[:, 0:n], in_=t3[:, 0:n], func=AF.Exp, scale=0.5), r=[b_tmp[2]], w=[b_tmp[2]])
                    P.dve(lambda h, r=r, n=n: h.tensor_scalar(out=mneg[:, r, 0:n], in0=t3[:, 0:n], scalar1=-1.0, scalar2=None, op0=ALU.mult),
                          r=[b_tmp[2]], **({"w": [b_mneg]} if r == 0 else {"pw": [b_mneg]}))
                    hd = g * 4 + r
                    P.act(lambda h, r=r, n=n, hd=hd: h.activation(out=skt[64:128, r, 0:n], in_=mneg[64:128, r, 0:n], func=AF.Exp,
                                                                   scale=0.125, bias=sinkv[64:128, hd:hd + 1]),
                          r=[b_mneg, b_const], **({"w": [b_skt]} if r == 0 else {"pw": [b_skt]}))

                pqs = {}

                def q_mm(r):
                    pq, b_pq = bank()
                    pqs[r] = (pq, b_pq)
                    for k in range(8):
                        P.pe(lambda h, pq=pq, k=k, r=r, t0=t0, n=n: h.matmul(
                            pq[:, 0:n], lhsT=wqq[:, k, r * 128:(r + 1) * 128], rhs=uTa[:, k, t0:t0 + n], start=(k == 0), stop=(k == 7)),
                            r=[b_wq, b_uTa[tgi]], **({"w": [b_pq]} if k == 0 else {"pw": [b_pq]}))

                q_mm(0)
                for r in range(4):
                    if r + 1 < 4:
                        q_mm(r + 1)
                    pq, b_pq = pqs[r]
                    P.act(lambda h, pq=pq, r=r, n=n: h.activation(out=qpa[0:64, r, 0:n], in_=pq[0:64, 0:n], func=AF.Copy),
                          r=[b_pq], **({"w": [b_qpa]} if r == 0 else {"pw": [b_qpa]}))
                    if not isctx:
                        rope(qa[0:64, r, 0:n], b_qa, pq, b_pq, t0 - CTX, n, r != 0)
                    P.act(lambda h, r=r, n=n: h.activation(out=sqt[:, 0:n], in_=qpa[0:64, r, 0:n], func=AF.Square),
                          r=[b_qpa], w=[b_sqt])
                    pn, b_pn = bank()
                    pns[r] = (pn, b_pn)
                    P.pe(lambda h, pn=pn, n=n: h.matmul(pn[:, 0:n], lhsT=ones_bf[0:64, :], rhs=sqt[:, 0:n], start=True, stop=True),
                         r=[b_sqt, b_c2], w=[b_pn])
                    if r >= 1:
                        q_chain(r - 1)
                q_chain(3)
                P.dve(lambda h, n=n: h.tensor_copy(out=qpa[64:65, :, 0:n], in_=mneg[64:65, :, 0:n]), r=[b_mneg], pw=[b_qpa])
                if not isctx:
                    P.dve(lambda h, n=n: h.tensor_copy(out=qa[64:65, :, 0:n], in_=mneg[64:65, :, 0:n]), r=[b_mneg], pw=[b_qa])
                if K.att_stage < 3:
                    continue
                for qb in range(n // 128):
                    qs = slice(qb * 128, (qb + 1) * 128)
                    if isctx:
                        chunks = [(0, qpa, b_qpa, None), (1, qpa, b_qpa, None)]
                    else:
                        j = (t0 - CTX) // 128 + qb
                        chunks = []
                        if j > 0:
                            chunks.append((2 + j - 1, qa, b_qa, mprev))
                        chunks.append((2 + j, qa, b_qa, None))
                        if j < 15:
                            chunks.append((2 + j + 1, qa, b_qa, mnext))
                        chunks += [(0, qpa, b_qpa, None), (1, qpa, b_qpa, None)]
                    po, b_po = pbank[6 + qbc[0] % 2], b_bank[6 + qbc[0] % 2]
                    qbc[0] += 1
                    nch = len(chunks)
                    pss = [None] * nch

                    def emit_norm(po=po, b_po=b_po, qs=qs, qb=qb):
                        P.dve(lambda h, po=po, qs=qs: h.tensor_tensor(out=rden[0:64, :].rearrange("p (r q) -> p r q", r=4),
                                                                      in0=po[64:128, :].rearrange("p (r q) -> p r q", r=4),
                                                                      in1=skt[64:128, :, qs], op=ALU.add),
                              r=[b_po, b_skt], w=[b_rden])
                        P.dve(lambda h, po=po: h.tensor_copy(out=oun[0:64, :], in_=po[0:64, :]), r=[b_po], w=[b_oun])
                        P.act(lambda h: h.activation(out=rden[0:64, :], in_=rden[0:64, :], func=AF.Ln), r=[b_rden], w=[b_rden])
                        P.act(lambda h: h.activation(out=rden[0:64, :], in_=rden[0:64, :], func=AF.Exp, scale=-1.0), r=[b_rden], w=[b_rden])
                        for par in range(2):
                            P.dve(lambda h, qs=qs, par=par: h.tensor_tensor(
                                out=OT[par * 64:(par + 1) * 64, :, qs],
                                in0=oun[0:64, :].rearrange("p (a b q) -> p a b q", a=2, b=2)[:, :, par, :],
                                in1=rden[0:64, :].rearrange("p (a b q) -> p a b q", a=2, b=2)[:, :, par, :], op=ALU.mult),
                                r=[b_oun, b_rden], **({"w": [b_OT]} if (qb == 0 and par == 0) else {"pw": [b_OT]}))


                    def emit_S(ci):
                        kb, qt, bqt, msk = chunks[ci]
                        bi_ = psc[0] % 6
                        psc[0] += 1
                        ps_, b_ps = pbank[bi_], b_bank[bi_]
                        pss[ci] = (ps_, b_ps)
                        P.pe(lambda h, ps_=ps_, kb=kb, qt=qt, qs=qs, msk=msk: h.matmul(
                            ps_[:, :], lhsT=kTa[0:65, kb * 128:(kb + 1) * 128], rhs=qt[0:65, :, qs],
                            start=True, stop=(msk is None)),
                            r=[b_kT, bqt], w=[b_ps])
                        if msk is not None:
                            P.pe(lambda h, ps_=ps_, msk=msk: h.matmul(ps_[:, :], lhsT=ident[:, :], rhs=msk[:, :], start=False, stop=True),
                                 r=[b_const], pw=[b_ps])

                    emit_S(0)
                    for ci in range(nch):
                        if ci + 1 < nch:
                            emit_S(ci + 1)
                        kb = chunks[ci][0]
                        ps_, b_ps = pss[ci]
                        pi = ptc[0] % 3
                        ptc[0] += 1
                        pt_, bpt = PT[pi], b_PT[pi]
                        P.act(lambda h, ps_=ps_, pt_=pt_: h.activation(out=pt_[:, :], in_=ps_[:, :], func=AF.Exp, scale=0.125),
                              r=[b_ps], w=[bpt])
                        P.pe(lambda h, po=po, kb=kb, pt_=pt_, ci=ci, nch=nch: h.matmul(
                            po[:, :], lhsT=Va[:, kb, :], rhs=pt_[:, :], start=(ci == 0), stop=(ci == nch - 1)),
                            r=[b_V, bpt], **({"w": [b_po]} if ci == 0 else {"pw": [b_po]}))
                        if ci == 1 and len(pending_norm) > 0:
                            pending_norm.pop(0)()
                    pending_norm.append(emit_norm)
                while pending_norm:
                    pending_norm.pop(0)()
                if K.att_stage < 4:
                    continue
                rr = 2 if isctx else rr_l
                for kd in range(8):
                    py, b_py = bank()
                    for r in range(2):
                        P.pe(lambda h, py=py, r=r, kd=kd, n=n: h.matmul(
                            py[:, 0:n], lhsT=wo[:, r, kd * 128:(kd + 1) * 128], rhs=OT[:, r, 0:n],
                            start=(r == 0), stop=(r == 1)),
                            r=[b_wo, b_OT], **({"w": [b_py]} if r == 0 else {"pw": [b_py]}))
                    P.dve(lambda h, py=py, kd=kd, t0=t0, n=n, rr=rr: h.scalar_tensor_tensor(
                        out=hs[:, kd, t0:t0 + n], in0=py[:, 0:n], scalar=mod(layer, 2, kd, rr), in1=hs[:, kd, t0:t0 + n],
                        op0=ALU.mult, op1=ALU.add),
                        r=[b_py, b_hs[0][tgi], b_modv[layer]], w=[b_hs[0][tgi]])
        P.barrier()
        for tgi in range(len(TGS)):
            layer_norm(tgi, layer, 0)
        P.barrier()


    so = [ARENA + 36864]

    def salloc(name, shape, dt, n=1):
        nb = int(np.prod(shape[1:])) * (4 if dt == F32 else 2)
        nb = (nb + 31) // 32 * 32
        ts = [P.sb(name, shape, dt, so[0] + i * nb) for i in range(n)]
        so[0] += nb * n
        return ts if n > 1 else ts[0]

    s_wxbc = salloc("s_wxbc", [128, 8, 768], BF16)
    s_wdt = salloc("s_wdt", [128, 8, 16], BF16)
    s_raw = salloc("s_raw", [128, 520], F32, 3)
    s_acc = salloc("s_acc", [128, 512], F32, 3)
    s_th = salloc("s_th", [128, 512], F32, 3)
    s_xo = salloc("s_xo", [128, 512], BF16, 3)
    s_t1 = salloc("s_t1", [128, 18, 16], F32)
    S1_END = so[0]
    so[0] = ARENA + 36864
    s_wso = salloc("s_wso", [128, 4, 1024], BF16)
    s_Sbin = salloc("s_Sbin", [128, 16, 512], BF16)
    s_Sf = salloc("s_Sf", [128, 512], F32)
    s_Sfb = salloc("s_Sfb", [128, 512], BF16)
    s_xw = salloc("s_xw", [128, 512], BF16, 2)
    s_cbm = salloc("s_cbm", [128, 128], BF16, 2)
    s_arg = salloc("s_arg", [128, 4, 128], BF16, 2)
    s_M = salloc("s_M", [128, 4, 128], BF16, 4)
    s_ya = salloc("s_ya", [128, 512], F32)
    s_yb = salloc("s_yb", [128, 512], F32)
    s_Sb = s_yb
    s_yg = s_ya
    s_yn = salloc("s_yn", [128, 512], BF16, 2)
    s_gz = salloc("s_gz", [128, 512], BF16)
    s_ynT = salloc("s_ynT", [128, 4, 512], BF16)
    s_ss = salloc("s_ss", [128, 4], F32)
    S2_END = so[0]
    so[0] = max(S1_END, S2_END)
    s_wz = salloc("s_wz", [128, 8, 512], BF16)
    s_xtok = salloc("s_xtok", [128, 18, 512], BF16)
    s_btok = salloc("s_btok", [128, 18, 128], BF16)
    s_BT = salloc("s_BT", [128, T], BF16)
    s_CT = salloc("s_CT", [128, T], BF16)
    s_dt = salloc("s_dt", [128, 18, 16], F32)
    s_a = salloc("s_a", [128, 18, 16], F32)
    s_cs = salloc("s_cs", [128, 18, 16], F32)
    s_tot = salloc("s_tot", [128, 18, 16], F32)
    s_E = salloc("s_E", [128, 18, 16], F32)
    s_dec = salloc("s_dec", [128, 18, 16], F32)
    s_cw = salloc("s_cw", [128, 36], F32)
    s_sb = salloc("s_sb", [128, 40], F32)
    s_ng = salloc("s_ng", [128, 4], F32)
    s_tri = salloc("s_tri", [128, 256], F32)
    s_one = salloc("s_one", [128, 2], F32)
    assert so[0] <= 212832, so[0]
    bs = {nm: P.buf(nm) for nm in ("xtok", "btok", "BT", "CT", "gz", "dt", "a", "cs", "tot", "E", "dec", "t1", "t2", "par",
                                   "wxbc", "wz", "wdt", "raw0", "raw1", "raw2", "acc0", "acc1", "acc2", "th0", "th1", "th2", "xo0", "xo1", "xo2", "arg0", "arg1", "xdt0", "xdt1", "wso", "Sbin", "Sf", "Sb",
                                   "Sfb", "xw0", "xw1", "cbm0", "cbm1", "arg", "L", "M0", "M1", "M2", "M3", "ya", "yb", "yg",
                                   "yn0", "yn1", "ynT", "ss")}
    sc_ = {"raw": 0, "xo": 0, "xw": 0, "M": 0, "acc": 0, "arg": 0}
    tri = s_tri[:, 0:128]
    trirev = s_tri[:, 128:256]

    def halo_ap(base, stride):
        a = base.ap
        return bass.AP(base.tensor, base.offset, [list(a[0]), [stride, 2], [1, 2]])

    def bc8(ap2):
        return ap2.unsqueeze(2).to_broadcast([128, 8, 64])

    def ssm(layer, b):
        rr_l = b
        for tgi, (t0, n, isctx) in enumerate(TGS):
            rr = 2 if isctx else rr_l
            for k in range(8):
                P.act(lambda h, k=k, t0=t0, n=n, rr=rr: h.activation(
                    out=uTa[:, k, t0:t0 + n], in_=hs[:, k, t0:t0 + n], func=AF.Identity,
                    scale=mod(layer, 1, k, rr), bias=mod(layer, 0, k, rr)),
                    r=[b_hs[0][tgi], b_modv[layer]], **({"w": [b_uTa[tgi]]} if k == 0 else {"pw": [b_uTa[tgi]]}))
        P.sp(lambda h: h.dma_start(out=s_tri[:, :], in_=tri_d[:, :]), dma_w=bs["par"])
        P.pool(lambda h: h.memset(s_one[:, :], 1.0), w=[bs["t2"]])
        t2v = None
        for g in range(4):
            P.sp(lambda h, g=g: h.dma_start(out=s_cw[:, :], in_=cw_d[g]), dma_w=bs["par"])
            P.sp(lambda h, g=g: h.dma_start(out=s_sb[:, :], in_=sb_d[g]), dma_w=bs["par"])
            P.sp(lambda h, g=g: h.dma_start(out=s_ng[:, :], in_=ng_d[g]), dma_w=bs["par"])
            P.pool(lambda h, g=g: h.dma_start(out=s_wxbc[:, :, :], in_=wxbc_d[g]), dma_w=bs["wxbc"])
            P.pool(lambda h, g=g: h.dma_start(out=s_wz[:, :, :], in_=wz_d[g]), dma_w=bs["wz"])
            P.pool(lambda h, g=g: h.dma_start(out=s_wdt[:, :, :], in_=wdt_d[g]), dma_w=bs["wdt"])
            P.dve(lambda h: h.tensor_scalar(out=s_cw[:, :], in0=s_cw[:, :], scalar1=0.5, scalar2=None, op0=ALU.mult), r=[bs["par"]], w=[bs["par"]])
            P.act(lambda h: h.activation(out=s_sb[:, 16:32], in_=s_sb[:, 16:32], func=AF.Exp), r=[bs["par"]], w=[bs["par"]])
            P.dve(lambda h: h.tensor_scalar(out=s_sb[:, 16:32], in0=s_sb[:, 16:32], scalar1=-1.0, scalar2=None, op0=ALU.mult), r=[bs["par"]], w=[bs["par"]])
            def A1(tile):
                tgi, c = tile["tgi"], tile["c"]
                t0, n, isctx = TGS[tgi]
                seg0, seg1 = (0, CTX) if isctx else (CTX, T)
                ri = sc_["raw"] % 3
                sc_["raw"] += 1
                raw, braw = s_raw[ri], bs[f"raw{ri}"]
                ai = sc_["acc"] % 3
                sc_["acc"] += 1
                acc_, bacc = s_acc[ai], bs[f"acc{ai}"]
                th_, bth = s_th[ai], bs[f"th{ai}"]
                tile.update(raw=raw, braw=braw, acc=acc_, bacc=bacc, th=th_, bth=bth)
                pm_, b_pm_ = bank()
                for k in range(8):
                    P.pe(lambda h, k=k: h.matmul(pm_[:, 0:n], lhsT=s_wxbc[:, k, c * 128:(c + 1) * 128], rhs=uTa[:, k, t0:t0 + n],
                                                 start=(k == 0), stop=(k == 7)),
                         r=[bs["wxbc"], b_uTa[tgi]], **({"w": [b_pm_]} if k == 0 else {"pw": [b_pm_]}))
                P.act(lambda h: h.activation(out=raw[:, 2:2 + n], in_=pm_[:, 0:n], func=AF.Copy), r=[b_pm_], w=[braw])
                hasl = (t0 - 2) >= seg0
                hasr = (t0 + n + 2) <= seg1
                if not hasl:
                    P.pool(lambda h: h.memset(raw[:, 0:2], 0.0), pw=[braw])
                if not hasr:
                    P.pool(lambda h: h.memset(raw[:, 2 + n:4 + n], 0.0), pw=[braw])
                if hasl or hasr:
                    ph, b_ph = bank()
                    hbufs = []
                    if hasl:
                        hbufs.append(b_uTa[[i for i, (a0, an, _) in enumerate(TGS) if a0 <= t0 - 2 < a0 + an][0]])
                    if hasr:
                        hbufs.append(b_uTa[[i for i, (a0, an, _) in enumerate(TGS) if a0 <= t0 + n < a0 + an][0]])
                    for k in range(8):
                        if hasl and hasr:
                            rhs_fn = lambda k=k: halo_ap(uTa[:, k, t0 - 2:t0], n + 2)
                            ncol = 4
                        elif hasl:
                            rhs_fn = lambda k=k: uTa[:, k, t0 - 2:t0]
                            ncol = 2
                        else:
                            rhs_fn = lambda k=k: uTa[:, k, t0 + n:t0 + n + 2]
                            ncol = 2
                        P.pe(lambda h, k=k, rhs_fn=rhs_fn, ncol=ncol: h.matmul(
                            ph[:, 0:ncol], lhsT=s_wxbc[:, k, c * 128:(c + 1) * 128], rhs=rhs_fn(), start=(k == 0), stop=(k == 7)),
                            r=[bs["wxbc"]] + hbufs, **({"w": [b_ph]} if k == 0 else {"pw": [b_ph]}))
                    if hasl:
                        P.act(lambda h: h.activation(out=raw[:, 0:2], in_=ph[:, 0:2], func=AF.Copy), r=[b_ph], pw=[braw])
                    if hasr:
                        o_ = 2 if hasl else 0
                        P.act(lambda h: h.activation(out=raw[:, 2 + n:4 + n], in_=ph[:, o_:o_ + 2], func=AF.Copy), r=[b_ph], pw=[braw])
                P.act(lambda h: h.activation(out=acc_[:, 0:n], in_=raw[:, 0:n], func=AF.Identity,
                                             scale=s_cw[:, c * 6:c * 6 + 1], bias=s_cw[:, c * 6 + 5:c * 6 + 6]),
                      r=[braw, bs["par"]], w=[bacc])

            def A2(tile):
                c = tile["c"]
                t0, n, isctx = TGS[tile["tgi"]]
                raw, braw, acc_, bacc = tile["raw"], tile["braw"], tile["acc"], tile["bacc"]
                for j in range(1, 5):
                    P.dve(lambda h, j=j: h.scalar_tensor_tensor(
                        out=acc_[:, 0:n], in0=raw[:, j:j + n], scalar=s_cw[:, c * 6 + j:c * 6 + j + 1], in1=acc_[:, 0:n],
                        op0=ALU.mult, op1=ALU.add), r=[braw, bs["par"], bacc], w=[bacc])

            def A3(tile):
                t0, n, isctx = TGS[tile["tgi"]]
                acc_, bacc, th_, bth = tile["acc"], tile["bacc"], tile["th"], tile["bth"]
                P.act(lambda h: h.activation(out=th_[:, 0:n], in_=acc_[:, 0:n], func=AF.Tanh), r=[bacc], w=[bth])

            def A4(tile):
                tgi, c = tile["tgi"], tile["c"]
                t0, n, isctx = TGS[tgi]
                acc_, bacc, th_, bth = tile["acc"], tile["bacc"], tile["th"], tile["bth"]
                if c <= 4:
                    xi = sc_["xo"] % 3
                    sc_["xo"] += 1
                    xo, bxo = s_xo[xi], bs[f"xo{xi}"]
                    P.dve(lambda h: h.scalar_tensor_tensor(out=xo[:, 0:n], in0=th_[:, 0:n], scalar=1.0, in1=acc_[:, 0:n],
                                                           op0=ALU.add, op1=ALU.mult), r=[bth, bacc], w=[bxo])
                    if c == 4:
                        P.act(lambda h: h.activation(out=s_BT[:, t0:t0 + n], in_=xo[:, 0:n], func=AF.Copy), r=[bxo], pw=[bs["BT"]])
                    for bi in range(n // 128):
                        blk = t0 // 128 + bi
                        ptr, b_ptr = bank()
                        P.pe(lambda h, ptr=ptr, bi=bi: h.matmul(ptr[:, 0:128], lhsT=xo[:, bi * 128:(bi + 1) * 128], rhs=ident[:, :], start=True, stop=True),
                             r=[bxo, b_const], w=[b_ptr])
                        if c < 4:
                            P.act(lambda h, ptr=ptr, blk=blk: h.activation(out=s_xtok[:, blk, c * 128:(c + 1) * 128], in_=ptr[:, 0:128], func=AF.Copy),
                                  r=[b_ptr], pw=[bs["xtok"]])
                        else:
                            P.act(lambda h, ptr=ptr, blk=blk: h.activation(out=s_btok[:, blk, :], in_=ptr[:, 0:128], func=AF.Copy),
                                  r=[b_ptr], pw=[bs["btok"]])
                else:
                    P.dve(lambda h: h.scalar_tensor_tensor(out=s_CT[:, t0:t0 + n], in0=th_[:, 0:n], scalar=1.0, in1=acc_[:, 0:n],
                                                           op0=ALU.add, op1=ALU.mult), r=[bth, bacc], pw=[bs["CT"]])
                if c == 5:
                    for bi in range(n // 128):
                        blk = t0 // 128 + bi
                        pd, b_pd = bank()
                        for k in range(8):
                            P.pe(lambda h, pd=pd, k=k, blk=blk: h.matmul(pd[:, 0:16], lhsT=uTa[:, k, blk * 128:(blk + 1) * 128], rhs=s_wdt[:, k, :],
                                                                         start=(k == 0), stop=(k == 7)),
                                 r=[bs["wdt"], b_uTa[tgi]], **({"w": [b_pd]} if k == 0 else {"pw": [b_pd]}))
                        P.dve(lambda h, pd=pd, blk=blk: h.tensor_tensor(out=s_dt[:, blk, :], in0=pd[:, 0:16], in1=s_sb[:, 0:16], op=ALU.add),
                              r=[b_pd, bs["par"]], pw=[bs["dt"]])

            tiles = [dict(tgi=tgi, c=c) for tgi in range(len(TGS)) for c in range(6)]
            A1(tiles[0])
            for i, tile in enumerate(tiles):
                A2(tile)
                if i + 1 < len(tiles):
                    A1(tiles[i + 1])
                A3(tile)
                if i >= 1:
                    A4(tiles[i - 1])
            A4(tiles[-1])
            dtv, t1v = s_dt[:, :, :], s_t1[:, :, :]
            P.dve(lambda h: h.tensor_scalar(out=t1v, in0=dtv, scalar1=-1.0, scalar2=None, op0=ALU.mult), r=[bs["dt"]], w=[bs["t1"]])
            P.dve(lambda h: h.tensor_tensor(out=t1v, in0=t1v, in1=dtv, op=ALU.max), r=[bs["dt"], bs["t1"]], w=[bs["t1"]])
            P.act(lambda h: h.activation(out=t1v, in_=t1v, func=AF.Exp, scale=-1.0), r=[bs["t1"]], w=[bs["t1"]])
            P.act(lambda h: h.activation(out=t1v, in_=t1v, func=AF.Ln, bias=s_one[:, 0:1], scale=1.0), r=[bs["t1"], bs["t2"]], w=[bs["t1"]])
            P.dve(lambda h: h.scalar_tensor_tensor(out=dtv, in0=dtv, scalar=0.0, in1=t1v, op0=ALU.max, op1=ALU.add), r=[bs["dt"], bs["t1"]], w=[bs["dt"]])
            P.dve(lambda h: h.tensor_tensor(out=s_a[:, :, :], in0=dtv, in1=s_sb[:, 16:32].unsqueeze(1).to_broadcast([128, 18, 16]), op=ALU.mult),
                  r=[bs["dt"], bs["par"]], w=[bs["a"]])
            for blk in range(18):
                pc, b_pc = bank()
                P.pe(lambda h, pc=pc, blk=blk: h.matmul(pc[:, 0:8], lhsT=tri, rhs=s_a[:, blk, 0:8], start=True, stop=True), r=[bs["a"], bs["par"]], w=[b_pc])
                P.pe(lambda h, pc=pc, blk=blk: h.matmul(pc[:, 8:16], lhsT=trirev, rhs=s_a[:, blk, 8:16], start=True, stop=True), r=[bs["a"], bs["par"]], pw=[b_pc])
                P.pe(lambda h, pc=pc, blk=blk: h.matmul(pc[:, 16:32], lhsT=ones_f[:, :], rhs=s_a[:, blk, :], start=True, stop=True), r=[bs["a"], b_c2], pw=[b_pc])
                P.act(lambda h, pc=pc, blk=blk: h.activation(out=s_cs[:, blk, :], in_=pc[:, 0:16], func=AF.Copy), r=[b_pc], pw=[bs["cs"]])
                P.act(lambda h, pc=pc, blk=blk: h.activation(out=s_tot[:, blk, :], in_=pc[:, 16:32], func=AF.Copy, scale=float(D)), r=[b_pc], pw=[bs["tot"]])
            P.act(lambda h: h.activation(out=s_E[:, :, :], in_=s_cs[:, :, :], func=AF.Exp), r=[bs["cs"]], w=[bs["E"]])
            P.dve(lambda h: h.tensor_tensor(out=s_dec[:, :, :], in0=s_tot[:, :, :], in1=s_cs[:, :, :], op=ALU.subtract), r=[bs["tot"], bs["cs"]], w=[bs["dec"]])
            P.act(lambda h: h.activation(out=s_dec[:, :, :], in_=s_dec[:, :, :], func=AF.Exp), r=[bs["dec"]], w=[bs["dec"]])
            P.dve(lambda h: h.tensor_tensor(out=s_dec[:, :, :], in0=s_dec[:, :, :], in1=s_dt[:, :, :], op=ALU.mult), r=[bs["dec"], bs["dt"]], w=[bs["dec"]])
            P.act(lambda h: h.activation(out=s_tot[:, :, :], in_=s_tot[:, :, :], func=AF.Exp), r=[bs["tot"]], w=[bs["tot"]])
            P.act(lambda h: h.activation(out=s_dt[:, :, :], in_=s_dt[:, :, :], func=AF.Ln), r=[bs["dt"], bs["dec"]], w=[bs["dt"]])
            P.dve(lambda h: h.tensor_tensor(out=s_dt[:, :, :], in0=s_dt[:, :, :], in1=s_cs[:, :, :], op=ALU.subtract), r=[bs["dt"], bs["cs"]], w=[bs["dt"]])
            P.barrier()
            P.pool(lambda h, g=g: h.dma_start(out=s_wso[:, :, :], in_=wso_d[g]), dma_w=bs["wso"])
            for c in range(4):
                P.dve(lambda h, c=c: h.tensor_scalar(out=s_wso[:, c, :], in0=s_wso[:, c, :], scalar1=s_ng[:, c:c + 1], scalar2=None, op0=ALU.mult),
                      r=[bs["wso"], bs["par"]], w=[bs["wso"]])
            P.pool(lambda h: h.memset(s_Sf[:, :], 0.0), w=[bs["Sf"]])
            P.pool(lambda h: h.memset(s_Sb[:, :], 0.0), w=[bs["Sb"]])

            def su_a(blk, d):
                xi = sc_["xw"] % 2
                sc_["xw"] += 1
                xw, bxw = s_xw[xi], bs[f"xw{xi}"]
                P.dve(lambda h: h.tensor_tensor(out=xw[:, :].rearrange("p (a b) -> p a b", a=8), in0=s_xtok[:, blk, :].rearrange("p (a b) -> p a b", a=8),
                                                in1=bc8(s_dec[:, blk, d * 8:d * 8 + 8]), op=ALU.mult), r=[bs["xtok"], bs["dec"]], w=[bxw])
                pst, b_pst = bank()
                P.pe(lambda h: h.matmul(pst[:, :], lhsT=s_btok[:, blk, :], rhs=xw[:, :], start=True, stop=True), r=[bs["btok"], bxw], w=[b_pst])
                return pst, b_pst

            def su_b(S, bS, blk, d, pst, b_pst):
                P.dve(lambda h: h.tensor_tensor(out=S[:, :].rearrange("p (a b) -> p a b", a=8), in0=S[:, :].rearrange("p (a b) -> p a b", a=8),
                                                in1=bc8(s_tot[:, blk, d * 8:d * 8 + 8]), op=ALU.mult), r=[bS, bs["tot"]], w=[bS])
                P.dve(lambda h: h.tensor_tensor(out=S[:, :], in0=S[:, :], in1=pst[:, :], op=ALU.add), r=[bS, b_pst], w=[bS])

            def state_update(S, bS, blk, d):
                pst, b_pst = su_a(blk, d)
                su_b(S, bS, blk, d, pst, b_pst)

            order = [1, 0] + list(range(17, 1, -1))
            nxt = su_a(order[0], 1)
            for i, blk in enumerate(order):
                cur = nxt
                if i + 1 < len(order):
                    nxt = su_a(order[i + 1], 1)
                if blk >= 2:
                    P.act(lambda h, blk=blk: h.activation(out=s_Sbin[:, blk - 2, :], in_=s_Sb[:, :], func=AF.Copy), r=[bs["Sb"]], pw=[bs["Sbin"]])
                su_b(s_Sb, bs["Sb"], blk, 1, cur[0], cur[1])
            groups = [(half, d) for half in range(2) for d in range(2)]
            st = {}

            def F1(blk):
                cols = slice(blk * 128, (blk + 1) * 128)
                P.act(lambda h: h.activation(out=s_Sfb[:, :], in_=s_Sf[:, :], func=AF.Copy), r=[bs["Sf"]], w=[bs["Sfb"]])
                pabs = []
                for (half, d) in groups:
                    pab, b_pab = bank()
                    pabs.append((pab, b_pab))
                    P.pe(lambda h, pab=pab, d=d: h.matmul(pab[:, :], lhsT=ident[:, :], rhs=(mnext if d == 0 else mprev)[:, :], start=True, stop=False),
                         r=[b_const], w=[b_pab])
                    for ci in range(4):
                        col = d * 8 + half * 4 + ci
                        P.pe(lambda h, pab=pab, ci=ci, col=col, d=d: h.matmul(
                            pab[:, ci * 128:(ci + 1) * 128], lhsT=s_a[:, blk, col:col + 1].to_broadcast([128, 128]),
                            rhs=(tri if d == 0 else trirev), start=False, stop=(ci == 3)),
                            r=[bs["a"], bs["par"]], pw=[b_pab])
                pz, b_pz = bank()
                tgz = 1 + (blk - 2) // 4
                for k in range(8):
                    P.pe(lambda h, k=k: h.matmul(pz[:, :], lhsT=uTa[:, k, cols], rhs=s_wz[:, k, :], start=(k == 0), stop=(k == 7)),
                         r=[bs["wz"], b_uTa[tgz]], **({"w": [b_pz]} if k == 0 else {"pw": [b_pz]}))
                args = [None] * 4
                Ms = [None] * 4

                def emit_exp(gi):
                    half, d = groups[gi]
                    pab, b_pab = pabs[gi]
                    ai = sc_["arg"] % 2
                    sc_["arg"] += 1
                    arg, barg = s_arg[ai], bs[f"arg{ai}"]
                    args[gi] = (arg, barg)
                    for ci in range(4):
                        col = d * 8 + half * 4 + ci
                        P.act(lambda h, ci=ci, col=col: h.activation(
                            out=arg[:, ci, :], in_=pab[:, ci * 128:(ci + 1) * 128], func=AF.Exp, bias=s_dt[:, blk, col:col + 1], scale=1.0),
                            r=[b_pab, bs["dt"]], **({"w": [barg]} if ci == 0 else {"pw": [barg]}))

                def emit_mul(gi):
                    half, d = groups[gi]
                    arg, barg = args[gi]
                    mi = sc_["M"] % 4
                    sc_["M"] += 1
                    Mt, bM = s_M[mi], bs[f"M{mi}"]
                    Ms[gi] = (Mt, bM)
                    P.dve(lambda h: h.tensor_tensor(
                        out=Mt[:, :, :], in0=arg[:, :, :], in1=s_cbm[d][:, :].unsqueeze(1).to_broadcast([128, 4, 128]), op=ALU.mult),
                        r=[barg, bs[f"cbm{d}"]], w=[bM])

                emit_exp(0)
                emit_exp(1)
                state_update(s_Sf, bs["Sf"], blk, 0)
                pcb, b_pcb = bank()
                P.pe(lambda h: h.matmul(pcb[:, 0:128], lhsT=s_BT[:, cols], rhs=s_CT[:, cols], start=True, stop=True), r=[bs["BT"], bs["CT"]], w=[b_pcb])
                P.dve(lambda h: h.tensor_tensor(out=s_cbm[0][:, :], in0=pcb[:, 0:128], in1=tri, op=ALU.mult), r=[b_pcb, bs["par"]], w=[bs["cbm0"]])
                P.dve(lambda h: h.tensor_tensor(out=s_cbm[1][:, :], in0=pcb[:, 0:128], in1=trirev, op=ALU.mult), r=[b_pcb, bs["par"]], w=[bs["cbm1"]])
                emit_mul(0)
                emit_exp(2)
                emit_mul(1)
                emit_exp(3)
                emit_mul(2)
                emit_mul(3)
                P.act(lambda h: h.activation(out=s_yb[:, :], in_=pz[:, :], func=AF.Exp, scale=-1.0), r=[b_pz], w=[bs["yb"]])
                P.act(lambda h: h.activation(out=s_yb[:, :], in_=s_yb[:, :], func=AF.Ln, bias=s_one[:, 0:1], scale=1.0), r=[bs["yb"], bs["t2"]], w=[bs["yb"]])
                P.act(lambda h: h.activation(out=s_yb[:, :], in_=s_yb[:, :], func=AF.Exp, scale=-1.0), r=[bs["yb"]], w=[bs["yb"]])
                P.dve(lambda h: h.tensor_tensor(out=s_gz[:, :], in0=s_yb[:, :], in1=pz[:, :], op=ALU.mult), r=[bs["yb"], b_pz], w=[bs["gz"]])
                pof, b_pof = bank()
                P.pe(lambda h: h.matmul(pof[:, :], lhsT=s_CT[:, cols], rhs=s_Sfb[:, :], start=True, stop=True), r=[bs["CT"], bs["Sfb"]], w=[b_pof])
                pob, b_pob = bank()
                P.pe(lambda h: h.matmul(pob[:, :], lhsT=s_CT[:, cols], rhs=s_Sbin[:, blk - 2, :], start=True, stop=True), r=[bs["CT"], bs["Sbin"]], w=[b_pob])
                P.dve(lambda h: h.tensor_tensor(out=s_ya[:, :].rearrange("p (a b) -> p a b", a=8), in0=pof[:, :].rearrange("p (a b) -> p a b", a=8),
                                                in1=bc8(s_E[:, blk, 0:8]), op=ALU.mult), r=[b_pof, bs["E"]], w=[bs["ya"]])
                P.dve(lambda h: h.tensor_tensor(out=s_yb[:, :].rearrange("p (a b) -> p a b", a=8), in0=pob[:, :].rearrange("p (a b) -> p a b", a=8),
                                                in1=bc8(s_E[:, blk, 8:16]), op=ALU.mult), r=[b_pob, bs["E"]], w=[bs["yb"]])
                P.dve(lambda h: h.tensor_tensor(out=s_ya[:, :], in0=s_ya[:, :], in1=s_yb[:, :], op=ALU.add), r=[bs["ya"], bs["yb"]], w=[bs["ya"]])
                P.dve(lambda h: h.tensor_tensor(out=s_yb[:, :].rearrange("p (a b) -> p a b", a=8), in0=s_xtok[:, blk, :].rearrange("p (a b) -> p a b", a=8),
                                                in1=bc8(s_sb[:, 32:40]), op=ALU.mult), r=[bs["xtok"], bs["par"]], w=[bs["yb"]])
                P.dve(lambda h: h.tensor_tensor(out=s_ya[:, :], in0=s_ya[:, :], in1=s_yb[:, :], op=ALU.add), r=[bs["ya"], bs["yb"]], w=[bs["ya"]])
                st[blk] = Ms

            def F2(blk):
                Ms = st.pop(blk)
                pyd, b_pyd = bank()
                for half in range(2):
                    for ci in range(4):
                        hh = half * 4 + ci
                        for d in range(2):
                            Mt, bM = Ms[half * 2 + d]
                            P.pe(lambda h, Mt=Mt, ci=ci, hh=hh, d=d: h.matmul(
                                pyd[:, hh * 64:(hh + 1) * 64], lhsT=Mt[:, ci, :], rhs=s_xtok[:, blk, hh * 64:(hh + 1) * 64], start=(d == 0), stop=(d == 1)),
                                r=[bM, bs["xtok"]], **({"w": [b_pyd]} if (half == 0 and ci == 0 and d == 0) else {"pw": [b_pyd]}))
                P.dve(lambda h: h.tensor_tensor(out=s_ya[:, :], in0=s_ya[:, :], in1=pyd[:, :], op=ALU.add), r=[bs["ya"], b_pyd], w=[bs["ya"]])
                P.dve(lambda h: h.tensor_tensor(out=s_ya[:, :], in0=s_ya[:, :], in1=s_gz[:, :], op=ALU.mult), r=[bs["ya"], bs["gz"]], w=[bs["ya"]])
                P.act(lambda h: h.activation(out=s_yb[:, :], in_=s_ya[:, :], func=AF.Square, accum_out=s_ss[:, 0:1]), r=[bs["ya"]], w=[bs["yb"], bs["ss"]])
                P.act(lambda h: h.activation(out=s_ss[:, 1:2], in_=s_ss[:, 0:1], func=AF.Ln, scale=1.0 / 512.0, bias=epsv[:, 0:1]), r=[bs["ss"], b_c2], w=[bs["ss"]])
                P.act(lambda h: h.activation(out=s_ss[:, 2:3], in_=s_ss[:, 1:2], func=AF.Exp, scale=-0.5), r=[bs["ss"]], w=[bs["ss"]])
                yn, byn = s_yn[blk % 2], bs[f"yn{blk % 2}"]
                P.dve(lambda h: h.tensor_scalar(out=yn[:, :], in0=s_ya[:, :], scalar1=s_ss[:, 2:3], scalar2=None, op0=ALU.mult),
                      r=[bs["ya"], bs["ss"]], w=[byn])

            def T_(blk):
                j = (blk - 2) % 4
                yn, byn = s_yn[blk % 2], bs[f"yn{blk % 2}"]
                for c in range(4):
                    ptr, b_ptr = bank()
                    P.pe(lambda h, ptr=ptr, c=c: h.matmul(ptr[:, 0:128], lhsT=yn[:, c * 128:(c + 1) * 128], rhs=ident[:, :], start=True, stop=True), r=[byn, b_const], w=[b_ptr])
                    P.act(lambda h, ptr=ptr, c=c: h.activation(out=s_ynT[:, c, j * 128:(j + 1) * 128], in_=ptr[:, 0:128], func=AF.Copy), r=[b_ptr],
                          **({"w": [bs["ynT"]]} if (c == 0 and j == 0) else {"pw": [bs["ynT"]]}))
                if j != 3:
                    return
                tgi = 1 + (blk - 2) // 4
                q0 = TGS[tgi][0]
                for kd in range(8):
                    py, b_py = bank()
                    for c in range(4):
                        P.pe(lambda h, py=py, c=c, kd=kd: h.matmul(py[:, :], lhsT=s_wso[:, c, kd * 128:(kd + 1) * 128], rhs=s_ynT[:, c, :], start=(c == 0), stop=(c == 3)),
                             r=[bs["wso"], bs["ynT"]], **({"w": [b_py]} if c == 0 else {"pw": [b_py]}))
                    P.dve(lambda h, py=py, kd=kd: h.scalar_tensor_tensor(
                        out=hs[:, kd, q0:q0 + 512], in0=py[:, :], scalar=mod(layer, 2, kd, rr_l), in1=hs[:, kd, q0:q0 + 512], op0=ALU.mult, op1=ALU.add),
                        r=[b_py, b_hs[0][tgi], b_modv[layer]], w=[b_hs[0][tgi]])

            state_update(s_Sf, bs["Sf"], 0, 0)
            state_update(s_Sf, bs["Sf"], 1, 0)
            F1(2)
            F2(2)
            for blk in range(3, 18):
                F1(blk)
                T_(blk - 1)
                F2(blk)
            T_(17)
            P.barrier()
        for tgi in range(1, len(TGS)):
            layer_norm(tgi, layer, 0)
        P.barrier()

    b_out = P.buf("outst")
    for b in range(nseq):
        for k in range(8):
            P.sp(lambda h, k=k, b=b: h.dma_start(out=hs[:, k, 0:CTX], in_=cxT[b, k]), dma_w=b_hs[0][0])
            for tgi in range(1, 5):
                t0, n, _ = TGS[tgi]
                P.sp(lambda h, k=k, b=b, t0=t0, n=n: h.dma_start(out=hs[:, k, t0:t0 + n], in_=xT[b, k, :, t0 - CTX:t0 - CTX + n]),
                     dma_w=b_hs[0][tgi])
        for tgi, (t0, n, _) in enumerate(TGS):
            P.dve(lambda h, t0=t0, n=n: h.tensor_scalar(out=hs[:, :, t0:t0 + n], in0=hs[:, :, t0:t0 + n], scalar1=ALPHA, scalar2=None, op0=ALU.mult),
                  r=[b_hs[0][tgi]], w=[b_hs[0][tgi]])
        for layer in range(nlayers):
            last = layer == DEPTH - 1
            flags = dbg if isinstance(dbg, dict) else {}
            if layer % 2 == 0:
                if flags.get("att", True):
                    attention(layer, b)
            else:
                ssm(layer, b)
            for tgi, (t0, n, isctx) in enumerate(TGS):
                if last and isctx:
                    continue
                if flags.get("mlp", True):
                    mlp(tgi, layer, 2 if isctx else b)
                if flags.get("ln", True):
                    layer_norm(tgi, layer, 2, final=last)
            P.barrier()
        if dbg is not None:
            for k in range(8):
                P.sp(lambda h, k=k, b=b: h.dma_start(out=dbgT[b, k], in_=hs[:, k, :]), r=b_hs[0], dma_r=b_out)
        for k in range(8):
            P.sp(lambda h, k=k, b=b: h.dma_start(out=outT[b, k], in_=hs[:, k, CTX:T]), r=b_hs[0], dma_r=b_out)
        P.barrier()
    counts = P.finish([b_out])
    return nc, counts


def _rope_perm():
    idx = np.arange(64)
    a = idx // 32
    half = (idx % 32) // 16
    j = idx % 16
    return a * 32 + (1 - half) * 16 + j


def host_constants():
    bf = ml_dtypes.bfloat16
    c = {}
    c["ident"] = np.eye(128, dtype=np.float32).astype(bf)
    jj = np.arange(128)[:, None]
    ii = np.arange(128)[None, :]
    mp = np.where(jj >= ii, 0.0, NEG).astype(np.float32)
    mn = np.where(jj <= ii, 0.0, NEG).astype(np.float32)
    c["mprev"] = np.tile(mp, (1, 4)).astype(bf)
    c["mnext"] = np.tile(mn, (1, 4)).astype(bf)
    t = np.arange(SEQ)
    row = (t // 64).astype(np.float32)
    col = (t % 64).astype(np.float32)
    inv = (10000.0 ** (-np.arange(0, 32, 2, dtype=np.float32) / 32)).astype(np.float32)
    cosT = np.zeros((64, SEQ), np.float32)
    sinT = np.zeros((64, SEQ), np.float32)
    for a, pos in enumerate((row, col)):
        ang = (pos[None, :] * inv[:, None]).astype(np.float32)
        for half in range(2):
            sl = slice(a * 32 + half * 16, a * 32 + half * 16 + 16)
            cosT[sl] = np.cos(ang)
            sinT[sl] = np.sin(ang) * (-1.0 if half == 0 else 1.0)
    kk = np.arange(128)[:, None]
    ll = np.arange(128)[None, :]
    c["tri"] = np.concatenate([(kk <= ll), (kk >= ll)], axis=1).astype(np.float32)
    c["cossin"] = np.concatenate([cosT, sinT], axis=0)
    return c


def host_weights(inp):
    w = {}
    f = np.float32
    wm = np.asarray(inp["w_mod"], f)
    w["wmod"] = np.ascontiguousarray(wm.reshape(DEPTH, 8, 128, 12, 512).transpose(0, 3, 2, 1, 4))
    w["bmod"] = np.ascontiguousarray(np.asarray(inp["b_mod"], f).reshape(DEPTH, 48, 128).transpose(0, 2, 1))
    lnv = np.stack([np.asarray(inp[k], f) for k in ("ln_mix_g", "ln_mix_b", "ln_ff_g", "ln_ff_b")], axis=1)
    w["lnv"] = np.ascontiguousarray(lnv.reshape(DEPTH, 4, 8, 128).transpose(3, 0, 1, 2).reshape(128, DEPTH * 4 * 8))
    w["sinkb"] = np.ascontiguousarray(np.broadcast_to(np.asarray(inp["att_sink"], f).reshape(1, 16), (128, 16)))
    win = np.asarray(inp["att_w_in"], f)[0]
    perm = _rope_perm()
    wq = win[:, :1024].reshape(8, 128, 4, 4, 64)
    wqq = np.concatenate([wq, wq[..., perm]], axis=-1)
    w["wqq"] = np.ascontiguousarray(wqq.transpose(2, 1, 0, 3, 4).reshape(4, 128, 8, 512))
    wk = win[:, 1024:1280].reshape(8, 128, 4, 64)
    wv = win[:, 1280:1536].reshape(8, 128, 4, 64)
    wkv = np.concatenate([wk, wk[..., perm], wv], axis=-1)
    w["wkv"] = np.ascontiguousarray(wkv.transpose(2, 1, 0, 3))
    wo = np.asarray(inp["att_w_out"], f)[0].reshape(4, 2, 2, 64, 1024)
    w["wo"] = np.ascontiguousarray(wo.transpose(0, 2, 3, 1, 4).reshape(4, 128, 2, 1024))
    sw = np.asarray(inp["ssm_w_in"], f)[0]
    wx = sw[:, 2048:4096].reshape(8, 128, 4, 512)
    wB = sw[:, 4096:4608].reshape(8, 128, 4, 128)
    wC = sw[:, 4608:5120].reshape(8, 128, 4, 128)
    w["wxbc"] = np.ascontiguousarray(np.concatenate([wx, wB, wC], axis=-1).transpose(2, 1, 0, 3))
    w["wz"] = np.ascontiguousarray(sw[:, 0:2048].reshape(8, 128, 4, 512).transpose(2, 1, 0, 3))
    wd = sw[:, 5120:5184].reshape(8, 128, 2, 4, 8)
    w["wdt"] = np.ascontiguousarray(wd.transpose(3, 1, 0, 2, 4).reshape(4, 128, 8, 16))
    w["wso"] = np.ascontiguousarray(np.asarray(inp["ssm_w_out"], f)[0].reshape(4, 4, 128, 1024).transpose(0, 2, 1, 3))
    cw = np.asarray(inp["ssm_conv_w"], f)[0]
    cb = np.asarray(inp["ssm_conv_b"], f)[0]
    convw = np.zeros((4, 128, 6, 6), f)
    for g in range(4):
        chans = [np.arange(g * 512 + c * 128, g * 512 + (c + 1) * 128) for c in range(4)]
        chans.append(np.arange(2048 + g * 128, 2048 + (g + 1) * 128))
        chans.append(np.arange(2560 + g * 128, 2560 + (g + 1) * 128))
        for c, ch in enumerate(chans):
            convw[g, :, c, 0:5] = cw[:, ch].T
            convw[g, :, c, 5] = cb[ch]
    w["convw"] = convw.reshape(4, 128, 36)
    dtb = np.asarray(inp["ssm_dt_bias"], f)[0].reshape(2, 4, 8)
    alog = np.asarray(inp["ssm_a_log"], f)[0].reshape(2, 4, 8)
    dsk = np.asarray(inp["ssm_d"], f)[0].reshape(4, 8)
    ssmb = np.zeros((4, 128, 40), f)
    for g in range(4):
        ssmb[g, :, 0:16] = dtb[:, g, :].reshape(1, 16)
        ssmb[g, :, 16:32] = alog[:, g, :].reshape(1, 16)
        ssmb[g, :, 32:40] = dsk[g].reshape(1, 8)
    w["ssmb"] = ssmb
    ng = np.asarray(inp["ssm_norm_g"], f)[0].reshape(4, 4, 128)
    w["normg"] = np.ascontiguousarray(ng.transpose(0, 2, 1))
    w1 = np.asarray(inp["ff_w1"], f).reshape(DEPTH, 8, 128, 8, 512)
    w["w1"] = np.ascontiguousarray(w1.transpose(0, 3, 2, 1, 4))
    w2 = np.asarray(inp["ff_w2"], f).reshape(DEPTH, 32, 128, 8, 128)
    w["w2"] = np.ascontiguousarray(w2.transpose(0, 3, 2, 1, 4))
    return w


def host_core_inputs(inp, core):
    f = np.float32
    b0 = core * BLOC
    x = np.asarray(inp["x"], f)[b0:b0 + BLOC]
    ctx = np.asarray(inp["ctx"], f)[b0:b0 + BLOC]
    d = {}
    d["xT"] = np.ascontiguousarray(x.transpose(0, 2, 1).reshape(BLOC, 8, 128, SEQ))
    d["cxT"] = np.ascontiguousarray(ctx.transpose(0, 2, 1).reshape(BLOC, 8, 128, CTX))
    cc = np.concatenate([np.asarray(inp["c"], f)[b0:b0 + BLOC], np.asarray(inp["c_ctx"], f)[None]], axis=0)
    d["cT"] = np.ascontiguousarray(cc.reshape(3, 8, 128).transpose(2, 1, 0))
    return d


_CACHE = {}


def kernel(**inputs):
    if "nc" not in _CACHE:
        _CACHE["nc"] = build_program()[0]
    nc = _CACHE["nc"]
    shared = {}
    shared.update(host_constants())
    shared.update(host_weights(inputs))
    in_maps = []
    for core in range(NCORE):
        m = dict(shared)
        m.update(host_core_inputs(inputs, core))
        in_maps.append(m)
    res = run_bass_kernel_spmd(nc, in_maps, core_ids=list(range(NCORE)))
    outs = []
    for core in range(NCORE):
        oT = np.asarray(res.results[core]["outT"]).reshape(BLOC, D, SEQ)
        outs.append(oT.transpose(0, 2, 1))
    return np.ascontiguousarray(np.concatenate(outs, axis=0)).astype(np.float32)
```
